# Optimizing a Trainium2 kernel written in Bass

```python
import math
import jax, jax.numpy as jnp
from jax import lax
import numpy as np

D_MODEL = 1024
BATCH = 8
SEQ = 2048
DEPTH = 2

N_A_LAYERS = DEPTH // 2
N_B_LAYERS = DEPTH - N_A_LAYERS
CONV_WIDTH = 3
D_FF = 4 * D_MODEL
HEAD_DIM = 64
N_HEADS = D_MODEL // HEAD_DIM
N_KV_GROUPS = 4
HEADS_PER_GROUP = N_HEADS // N_KV_GROUPS
N_BRANCH = 3
CMP_BLOCK = 32
CMP_STRIDE = 16
CMP_HIDDEN = 4 * HEAD_DIM
SEL_BLOCK = 64
N_SEL = 16
WINDOW = 512
Q_BLOCK = 64
EPS = 1e-6
NEG = -1e30
BIG = 1e30

kernel_name = "yoco_shortconv_nsa_hybrid"


def rms_norm(x, gain):
    xf = x.astype(jnp.float32)
    y = xf * lax.rsqrt(jnp.mean(xf * xf, axis=-1, keepdims=True) + EPS)
    return (y * gain.astype(jnp.float32)).astype(x.dtype)


def modulate(h, shift, scale):
    return h * (1 + scale[:, None, :]) + shift[:, None, :]


def masked_softmax(s, mask):
    s = jnp.where(mask, s.astype(jnp.float32), NEG)
    p = jnp.where(mask, jnp.exp(s - jnp.max(s, axis=-1, keepdims=True)), 0.0)
    return p / jnp.maximum(jnp.sum(p, axis=-1, keepdims=True), 1e-30)


def short_conv_mixer(h, w_in, conv_w, w_out):
    gate_b, gate_c, u = jnp.split(h @ w_in, 3, axis=-1)
    v = gate_c * u
    v = lax.conv_general_dilated(
        v, conv_w[:, None, :], window_strides=(1,),
        padding=[(CONV_WIDTH - 1, 0)],
        dimension_numbers=("NWC", "WIO", "NWC"),
        feature_group_count=v.shape[-1])
    return (gate_b * v) @ w_out


def squared_relu_mlp(h, w1, w2):
    return jnp.square(jax.nn.relu(h @ w1)) @ w2


def n_compressed(seq):
    return (seq - CMP_BLOCK) // CMP_STRIDE + 1


def cmp_to_sel_matrix(seq):
    n_cmp, n_sel = n_compressed(seq), seq // SEL_BLOCK
    c0 = np.arange(n_cmp)[:, None] * CMP_STRIDE
    s0 = np.arange(n_sel)[None, :] * SEL_BLOCK
    ov = np.minimum(c0 + CMP_BLOCK, s0 + SEL_BLOCK) - np.maximum(c0, s0)
    return (np.clip(ov, 0, None) / CMP_BLOCK).astype(np.float32)


def shared_kv(h, w_kv, k_gain, cmp_pe, cmp_w1, cmp_w2):
    B, S, _ = h.shape
    kv = (h @ w_kv).reshape(B, S, 2 * N_BRANCH, N_KV_GROUPS, HEAD_DIM)
    kv = kv.transpose(2, 0, 3, 1, 4)
    kc_raw, vc_raw, ks, vs, kw, vw = kv[0], kv[1], kv[2], kv[3], kv[4], kv[5]
    n_cmp = n_compressed(S)
    idx = np.arange(n_cmp)[:, None] * CMP_STRIDE + np.arange(CMP_BLOCK)[None, :]

    def compress(t, pe, w1, w2):
        blk = t[:, :, idx, :] + pe
        blk = blk.reshape(B, N_KV_GROUPS, n_cmp, CMP_BLOCK * HEAD_DIM)
        return jax.nn.gelu(blk @ w1) @ w2

    kc = rms_norm(compress(kc_raw, cmp_pe[0], cmp_w1[0], cmp_w2[0]), k_gain[0])
    vc = compress(vc_raw, cmp_pe[1], cmp_w1[1], cmp_w2[1])
    ks = rms_norm(ks, k_gain[1])
    kw = rms_norm(kw, k_gain[2])
    pad = ((0, 0), (0, 0), (WINDOW, 0), (0, 0))
    return kc, vc, ks, vs, jnp.pad(kw, pad), jnp.pad(vw, pad)


def nsa_attention(h, w_qg, q_gain, w_o, kc, vc, ks, vs, kw, vw):
    B, S, _ = h.shape
    G, Hg = N_KV_GROUPS, HEADS_PER_GROUP
    qg = h @ w_qg
    q = qg[..., :N_HEADS * HEAD_DIM].reshape(B, S, N_HEADS, HEAD_DIM)
    q = rms_norm(q, q_gain) * (HEAD_DIM ** -0.5)
    q = q.reshape(B, S, G, Hg, HEAD_DIM).transpose(0, 2, 3, 1, 4)
    gates = jax.nn.sigmoid(qg[..., N_HEADS * HEAD_DIM:].astype(jnp.float32))
    gates = gates.reshape(B, S, G, Hg, N_BRANCH).transpose(0, 2, 3, 1, 4)

    n_cmp = kc.shape[2]
    n_sel = S // SEL_BLOCK
    n_top = min(N_SEL, n_sel)
    cmp_map = jnp.asarray(cmp_to_sel_matrix(S))
    cmp_end = jnp.arange(n_cmp) * CMP_STRIDE + CMP_BLOCK - 1
    sel_ids = jnp.arange(n_sel)
    b_ix = jnp.arange(B)[:, None, None, None]
    g_ix = jnp.arange(G)[None, :, None, None]

    def query_block(s0):
        qb = lax.dynamic_slice_in_dim(q, s0, Q_BLOCK, axis=3)
        gb = lax.dynamic_slice_in_dim(gates, s0, Q_BLOCK, axis=3)
        t = s0 + jnp.arange(Q_BLOCK)
        sc = jnp.einsum("bgkqd,bgnd->bgkqn", qb, kc)
        p_cmp = masked_softmax(sc, cmp_end[None, :] <= t[:, None])
        o_cmp = jnp.einsum("bgkqn,bgnd->bgkqd", p_cmp.astype(vc.dtype), vc)
        imp = jnp.einsum("bgkqn,nj->bgqj", p_cmp, cmp_map)
        cur = t[:, None] // SEL_BLOCK
        forced = (sel_ids[None] == 0) | (sel_ids[None] == cur) | (sel_ids[None] == cur - 1)
        causal = sel_ids[None] * SEL_BLOCK <= t[:, None]
        score = jnp.where(forced, BIG, jnp.where(causal, imp, NEG))
        _, top = lax.top_k(score, n_top)
        tok = (top[..., None] * SEL_BLOCK + jnp.arange(SEL_BLOCK)).reshape(
            B, G, Q_BLOCK, n_top * SEL_BLOCK)
        k_sel = ks[b_ix, g_ix, tok]
        v_sel = vs[b_ix, g_ix, tok]
        ss = jnp.einsum("bgkqd,bgqmd->bgkqm", qb, k_sel)
        p_sel = masked_softmax(ss, (tok <= t[:, None])[:, :, None])
        o_sel = jnp.einsum("bgkqm,bgqmd->bgkqd", p_sel.astype(v_sel.dtype), v_sel)
        k_win = lax.dynamic_slice_in_dim(kw, s0, WINDOW + Q_BLOCK, axis=2)
        v_win = lax.dynamic_slice_in_dim(vw, s0, WINDOW + Q_BLOCK, axis=2)
        kpos = s0 - WINDOW + jnp.arange(WINDOW + Q_BLOCK)
        wmask = ((kpos[None] <= t[:, None]) & (kpos[None] > t[:, None] - WINDOW)
                 & (kpos[None] >= 0))
        sw = jnp.einsum("bgkqd,bgnd->bgkqn", qb, k_win)
        p_win = masked_softmax(sw, wmask)
        o_win = jnp.einsum("bgkqn,bgnd->bgkqd", p_win.astype(v_win.dtype), v_win)
        o = gb[..., 0:1] * o_cmp + gb[..., 1:2] * o_sel + gb[..., 2:3] * o_win
        return o.astype(qb.dtype)

    starts = jnp.arange(S // Q_BLOCK) * Q_BLOCK
    out = lax.map(query_block, starts)
    out = out.transpose(1, 0, 4, 2, 3, 5).reshape(B, S, N_HEADS * HEAD_DIM)
    return out @ w_o


def setup_inputs(seed: int = 0) -> dict:
    key = jax.random.key(seed)
    ks = jax.random.split(key, 24)
    D, G = D_MODEL, N_KV_GROUPS
    nrm = jax.random.normal
    f32 = jnp.float32
    return {
        "x": nrm(ks[0], (BATCH, SEQ, D), f32),
        "c": nrm(ks[1], (BATCH, D), f32),
        "norm_gain": 1.0 + 0.1 * nrm(ks[2], (DEPTH, 2, D), f32),
        "w_ada": 0.5 * D ** -0.5 * nrm(ks[3], (DEPTH, D, 6 * D), f32),
        "b_ada": 0.01 * nrm(ks[4], (DEPTH, 6 * D), f32),
        "w_a_in": D ** -0.5 * nrm(ks[5], (N_A_LAYERS, D, 3 * D), f32),
        "conv_w": CONV_WIDTH ** -0.5 * nrm(ks[6], (N_A_LAYERS, CONV_WIDTH, D), f32),
        "w_a_out": D ** -0.5 * nrm(ks[7], (N_A_LAYERS, D, D), f32),
        "w_qg": D ** -0.5 * nrm(ks[8], (N_B_LAYERS, D, N_HEADS * HEAD_DIM + N_BRANCH * N_HEADS), f32),
        "q_gain": 1.0 + 0.1 * nrm(ks[9], (N_B_LAYERS, HEAD_DIM), f32),
        "w_o": (N_HEADS * HEAD_DIM) ** -0.5 * nrm(ks[10], (N_B_LAYERS, N_HEADS * HEAD_DIM, D), f32),
        "kv_norm_gain": 1.0 + 0.1 * nrm(ks[11], (D,), f32),
        "w_ada_kv": 0.5 * D ** -0.5 * nrm(ks[12], (D, 2 * D), f32),
        "b_ada_kv": 0.01 * nrm(ks[13], (2 * D,), f32),
        "w_kv": D ** -0.5 * nrm(ks[14], (D, 2 * N_BRANCH * G * HEAD_DIM), f32),
        "k_gain": 1.0 + 0.1 * nrm(ks[15], (N_BRANCH, HEAD_DIM), f32),
        "cmp_pe": 0.2 * nrm(ks[16], (2, CMP_BLOCK, HEAD_DIM), f32),
        "cmp_w1": (CMP_BLOCK * HEAD_DIM) ** -0.5 * nrm(ks[17], (2, CMP_BLOCK * HEAD_DIM, CMP_HIDDEN), f32),
        "cmp_w2": CMP_HIDDEN ** -0.5 * nrm(ks[18], (2, CMP_HIDDEN, HEAD_DIM), f32),
        "w_mlp1": D ** -0.5 * nrm(ks[19], (DEPTH, D, D_FF), f32),
        "w_mlp2": D_FF ** -0.5 * nrm(ks[20], (DEPTH, D_FF, D), f32),
    }


def reference(x, c, norm_gain, w_ada, b_ada, w_a_in, conv_w, w_a_out, w_qg, q_gain, w_o,
              kv_norm_gain, w_ada_kv, b_ada_kv, w_kv, k_gain, cmp_pe, cmp_w1, cmp_w2,
              w_mlp1, w_mlp2):
    c_act = jax.nn.silu(c)
    kvs = None
    for i in range(DEPTH):
        mod = c_act @ w_ada[i] + b_ada[i]
        sh1, sc1, g1, sh2, sc2, g2 = jnp.split(mod, 6, axis=-1)
        if i == N_A_LAYERS:
            sh_kv, sc_kv = jnp.split(c_act @ w_ada_kv + b_ada_kv, 2, axis=-1)
            h_kv = modulate(rms_norm(x, kv_norm_gain), sh_kv, sc_kv)
            kvs = shared_kv(h_kv, w_kv, k_gain, cmp_pe, cmp_w1, cmp_w2)
        h = modulate(rms_norm(x, norm_gain[i, 0]), sh1, sc1)
        if i < N_A_LAYERS:
            mix = short_conv_mixer(h, w_a_in[i], conv_w[i], w_a_out[i])
        else:
            j = i - N_A_LAYERS
            mix = nsa_attention(h, w_qg[j], q_gain[j], w_o[j], *kvs)
        x = x + g1[:, None, :] * mix
        h = modulate(rms_norm(x, norm_gain[i, 1]), sh2, sc2)
        x = x + g2[:, None, :] * squared_relu_mlp(h, w_mlp1[i], w_mlp2[i])
    return x
```

```python
import numpy as np
from contextlib import ExitStack
import concourse.bass as bass
import concourse.mybir as mybir
from concourse.bass_utils import run_bass_kernel_spmd

F32 = mybir.dt.float32
BF16 = mybir.dt.bfloat16
AF = mybir.ActivationFunctionType
ALU = mybir.AluOpType

D = 1024
T = 2048
KC = 8
NTG = 4
TGW = 512
EPS = 1e-6
N_CORES = 8

V_NG = 0
V_KVG = 4
V_BADA = 5
V_BKV = 17
V_CONV = 19
NV = 22

DEBUG_STOP = None


class Tok:
    __slots__ = ("w", "rs", "rdma", "excl")

    def __init__(self, excl=False):
        self.w = None
        self.rs = {}
        self.rdma = []
        self.excl = excl


class Op:
    __slots__ = ("eng", "fn", "deps", "dma", "signal", "sigval", "dmaval")

    def __init__(self, eng, fn, dma):
        self.eng = eng
        self.fn = fn
        self.dma = dma
        self.deps = ()
        self.signal = False
        self.sigval = 0
        self.dmaval = 0


ENGS = ["pe", "act", "dve", "pool", "sp"]


class Sched:
    def __init__(self, nc):
        self.nc = nc
        self.ops = {e: [] for e in ENGS}
        self.dma_count = {}
        self.dma_last = {}

    def op(self, eng, fn, reads=(), writes=(), dma=None):
        o = Op(eng, fn, dma)
        ex = [t for t in reads if t.excl]
        if ex:
            reads = [t for t in reads if not t.excl]
            writes = list(writes) + ex
        deps = set()
        for t in reads:
            if t.w is not None:
                deps.add(t.w)
        for t in writes:
            if t.w is not None:
                deps.add(t.w)
            deps.update(t.rs.values())
            deps.update(t.rdma)
        if dma is not None:
            deps = {d for d in deps if d.dma != dma}
        o.deps = tuple(deps)
        for t in reads:
            if dma is not None:
                t.rdma.append(o)
            else:
                t.rs[eng] = o
        for t in writes:
            t.w = o
            t.rs = {}
            t.rdma = []
        if dma is not None:
            c = self.dma_count.get(dma, 0) + 1
            self.dma_count[dma] = c
            o.dmaval = 16 * c
            self.dma_last[dma] = o
        self.ops[eng].append(o)
        return o

    def barrier(self, engines=("pe", "act", "dve", "sp", "pool")):
        lasts = []
        for e in ENGS:
            for o in reversed(self.ops[e]):
                if o.dma is None and o.fn is not None:
                    lasts.append(o)
                    break
        lasts += list(self.dma_last.values())
        for e in engines:
            o = Op(e, None, None)
            o.deps = tuple(lasts)
            self.ops[e].append(o)

    def emit(self, final_dma_keys=()):
        nc = self.nc
        for e in ENGS:
            for o in self.ops[e]:
                for d in o.deps:
                    if d.dma is None:
                        if d.eng == "pe" and o.eng == "pe" and o.dma is None and o.fn is not None:
                            continue
                        d.signal = True
        for e in ENGS:
            c = 0
            for o in self.ops[e]:
                if o.dma is None and o.signal:
                    c += 1
                    o.sigval = c
        with ExitStack() as es:
            esem = {e: es.enter_context(nc.semaphore("s_" + e)) for e in ENGS}
            dsem = {k: es.enter_context(nc.semaphore("d_" + str(k))) for k in self.dma_count}
            block = es.enter_context(nc.Block())

            def run(e, eng):
                waited = {}
                for o in self.ops[e]:
                    need = {}
                    for d in o.deps:
                        if d.dma is not None:
                            key, sem, val = ("d", d.dma), dsem[d.dma], d.dmaval
                        else:
                            if d.eng == "pe" and e == "pe" and o.dma is None and o.fn is not None:
                                continue
                            key, sem, val = ("e", d.eng), esem[d.eng], d.sigval
                        if val > need.get(key, (None, 0))[1]:
                            need[key] = (sem, val)
                    for key, (sem, val) in need.items():
                        if waited.get(key, 0) >= val:
                            continue
                        waited[key] = val
                        eng.wait_ge(sem, val)
                    if o.fn is None:
                        continue
                    ins = o.fn(eng)
                    if o.dma is not None:
                        ins.then_inc(dsem[o.dma], 16)
                    elif o.signal:
                        ins.then_inc(esem[e], 1)
                if e == "sp":
                    for k in final_dma_keys:
                        eng.wait_ge(dsem[k], 16 * self.dma_count[k])

            block.sync(lambda eng: run("sp", eng))
            block.scalar(lambda eng: run("act", eng))
            block.vector(lambda eng: run("dve", eng))
            block.gpsimd(lambda eng: run("pool", eng))
            block.tensor(lambda eng: run("pe", eng))


class StopBuild(Exception):
    pass


class TT:
    def __init__(self, ap):
        self.ap = ap
        self.toks = {}

    def t(self, *key):
        tk = self.toks.get(key)
        if tk is None:
            tk = self.toks[key] = Tok()
        return tk

    def all(self):
        return list(self.toks.values())


def build_program(stop=None):
    nc = bass.Bass("TRN2", target_bir_lowering=False)

    def din(name, shape):
        return nc.dram_tensor(name, list(shape), F32, kind="ExternalInput").ap()

    d_x = din("xT", [128, KC, T])
    d_c = din("cT", [128, KC])
    d_vecs = din("vecs", [128, NV, KC])
    d_consts = din("consts", [128, 256])
    d_w_ada = din("w_ada", [2, D, 6 * D])
    d_w_a_in = din("w_a_in", [1, D, 3 * D])
    d_w_a_out = din("w_a_out", [1, D, D])
    d_w_mlp1 = din("w_mlp1", [2, D, 4 * D])
    d_w_mlp2 = din("w_mlp2", [2, 4 * D, D])
    d_w_ada_kv = din("w_ada_kv", [D, 2 * D])
    d_w_kv = din("w_kv", [D, 1536])
    d_w_qg = din("w_qg", [1, D, 1072])
    d_w_o = din("w_o", [1, D, D])
    d_cmp_w1 = din("cmp_w1", [2, 2048, 256])
    d_cmp_w2 = din("cmp_w2", [2, 256, 64])
    d_hvecs = din("hvecs", [128, 4])
    d_peT = din("peT", [128, 64])
    d_amask = din("amask", [128, 256 + 2048])
    d_e128 = din("e128", [128, 2048])
    d_cmap = din("cmap", [128, 40])
    d_force = din("force", [128, 256])
    d_gsel = din("gsel", [48, 3072])
    d_bd = din("bdones", [128, 128])
    d_y = nc.dram_tensor("yT", [128, KC, T], F32, kind="ExternalOutput").ap()
    d_xs = nc.dram_tensor("xs_scratch", [128, KC, T], F32).ap()

    S = Sched(nc)
    es = ExitStack()
    with es:
        def sb(name, shape, dt):
            return es.enter_context(nc.sbuf_tensor(name, list(shape), dt))

        XR = sb("XR", [128, KC * T], F32)
        HR = sb("HR", [128, KC * T], BF16)
        AR = sb("AR", [128, KC * T], BF16)
        RING = [sb("RING%d" % i, [128, 8192], BF16) for i in range(2)]
        ring_t = [Tok(), Tok()]
        rstd_s = sb("rstd", [128, T], F32)
        TMPF = [sb("tmpf%d" % i, [128, TGW], F32) for i in range(3)]
        tmpf_t = [Tok() for _ in TMPF]
        TMPB = [sb("tmpb%d" % i, [128, TGW], BF16) for i in range(3)]
        tmpb_t = [Tok() for _ in TMPB]
        SCR = sb("SCR", [128, 11048], BF16)
        ones_b = sb("ones_b", [128, 128], BF16)
        vecs = sb("vecs_s", [128, NV, KC], F32)
        c_f = sb("c_f", [128, KC], F32)
        cact = sb("cact", [128, KC], BF16)
        modv = sb("modv", [128, 3, 48], F32)
        der = sb("der", [128, 8, KC], F32)
        hvecs = sb("hvecs_s", [128, 4], F32)
        gq = sb("gq", [128, 1], F32)
        ident_b = sb("ident_b", [128, 128], BF16)
        bd_ones = sb("bd_ones", [128, 128], BF16)
        cbias = sb("cbias", [128, 4], F32)
        kcT = sb("kcT", [128, 4, 128], BF16)
        vcA = sb("vcA", [128, 4, 128], BF16)
        rd4 = sb("rd4", [128, 4], F32)
        scA = sb("scA", [128, 32], F32)
        scB = sb("scB", [128, 32], F32)
        m8a = sb("m8a", [128, 8], F32)
        m8b = sb("m8b", [128, 8], F32)
        selm = sb("selm", [128, 32], BF16)
        ACC = [sb("acc%d" % i, [128, 256], F32) for i in range(2)]
        PS = [es.enter_context(nc.psum_tensor("ps%d" % i, [128, TGW], F32)) for i in range(8)]
        ps_t = [Tok(excl=True) for _ in PS]

        X = TT(XR[:].rearrange("p (k t) -> p k t", k=KC))
        H = TT(HR[:].rearrange("p (k t) -> p k t", k=KC))
        A = TT(AR[:].rearrange("p (k t) -> p k t", k=KC))
        t_const = Tok()
        eps_t = sb("eps_t", [128, 1], F32)
        t_eps = Tok()
        S.op("dve", lambda e: e.memset(eps_t[:], EPS), writes=[t_eps])
        t_vecs = Tok()
        t_c = Tok()
        t_cact = Tok()
        t_modv = [Tok(), Tok(), Tok()]
        t_der = Tok()
        t_rstd = [Tok() for _ in range(NTG)]

        cnt = {"ps": 0, "ring": 0, "tf": 0, "tb": 0}

        POOLS = {"ALL": list(range(8)), "P6": [0, 1, 2, 3, 4, 5], "S": [0, 1, 2, 3], "O": [4, 5], "M": [6, 7]}
        pcnt = {k: 0 for k in POOLS}

        defpool = {"v": "ALL"}

        def next_ps(pool=None):
            pool = pool or defpool["v"]
            lst = POOLS[pool]
            i = lst[pcnt[pool] % len(lst)]
            pcnt[pool] += 1
            return PS[i], ps_t[i]

        def next_tf():
            i = cnt["tf"] % len(TMPF)
            cnt["tf"] += 1
            return TMPF[i], tmpf_t[i]

        def next_tb():
            i = cnt["tb"] % len(TMPB)
            cnt["tb"] += 1
            return TMPB[i], tmpb_t[i]

        def tsl(tg):
            return slice(tg * TGW, (tg + 1) * TGW)

        S.op("pool", lambda e: e.dma_start(out=ones_b[:], in_=d_consts[:, 128:256]), writes=[t_const], dma="cb")
        S.op("sp", lambda e: e.dma_start(out=vecs[:], in_=d_vecs), writes=[t_vecs], dma="c")
        S.op("sp", lambda e: e.dma_start(out=c_f[:], in_=d_c), writes=[t_c], dma="c")
        for tg in range(NTG):
            for kh in range(2):
                ks = slice(kh * 4, kh * 4 + 4)
                S.op("sp", (lambda e, tg=tg, ks=ks: e.dma_start(out=X.ap[:, ks, tsl(tg)], in_=d_x[:, ks, tsl(tg)])),
                     writes=[X.t(k, tg) for k in range(kh * 4, kh * 4 + 4)], dma="x")

        S.op("act", lambda e: e.activation(out=cact[:], in_=c_f[:], func=AF.Silu), reads=[t_c], writes=[t_cact])

        def load_w(src_aps, dst_view_fn):
            i = cnt["ring"] % 2
            cnt["ring"] += 1
            slot = RING[i]
            for dst_fn, src in src_aps:
                S.op("pool", (lambda e, dst_fn=dst_fn, src=src, slot=slot: e.dma_start(out=dst_fn(slot), in_=src)),
                     writes=[ring_t[i]], dma="r%d" % i)
            return dst_view_fn(slot), ring_t[i]

        def load_w_std(w2d, c0, ncols, k0=0):
            src = w2d.rearrange("(k p) n -> p k n", p=128)

            def view(slot):
                return slot[:, 0:8 * ncols].rearrange("p (k c) -> p k c", k=8)
            pieces = []
            for kh in range(2):
                ks = slice(kh * 4, kh * 4 + 4)
                pieces.append(((lambda slot, ks=ks: view(slot)[:, ks, :]), src[:, k0 + kh * 4:k0 + kh * 4 + 4, c0:c0 + ncols]))
            return load_w(pieces, view)

        def ada_matvec(w2d, ncb, bias_idx, mi):
            ps, pt = next_ps()
            for cb in range(ncb):
                wv, wt = load_w_std(w2d, cb * 1024, 1024)
                for oc in range(8):
                    col = cb * 8 + oc
                    for k in range(KC):
                        S.op("pe", (lambda e, ps=ps, wv=wv, oc=oc, k=k, col=col: e.matmul(
                            ps[:, col:col + 1], wv[:, k, oc * 128:(oc + 1) * 128], cact[:, k:k + 1],
                            start=(k == 0), stop=(k == KC - 1))), reads=[wt, t_cact], writes=[pt])
            n = ncb * 8
            S.op("dve", (lambda e, ps=ps, n=n: e.tensor_tensor(
                out=modv[:, mi, 0:n], in0=ps[:, 0:n],
                in1=vecs[:, bias_idx:bias_idx + ncb, :].rearrange("p a b -> p (a b)"), op=ALU.add)),
                reads=[pt, t_vecs], writes=[t_modv[mi]])

        def mod_part(mi, part):
            return modv[:, mi, part * 8:(part + 1) * 8]

        def derive(mi, part_sc, gain_idx, dst):
            S.op("dve", lambda e: e.tensor_scalar(out=der[:, dst, :], in0=mod_part(mi, part_sc), scalar1=1.0, scalar2=1.0,
                                                  op0=ALU.add, op1=ALU.mult), reads=[t_modv[mi]], writes=[t_der])
            S.op("dve", lambda e: e.tensor_tensor(out=der[:, dst, :], in0=der[:, dst, :], in1=vecs[:, gain_idx, :], op=ALU.mult),
                 reads=[t_der, t_vecs], writes=[t_der])

        def compute_rstd():
            for tg in range(NTG):
                ps, pt = next_ps()
                for k in range(KC):
                    tb, tbt = next_tb()
                    S.op("act", (lambda e, tb=tb, k=k, tg=tg: e.activation(out=tb[:], in_=X.ap[:, k, tsl(tg)], func=AF.Square)),
                         reads=[X.t(k, tg)], writes=[tbt])
                    S.op("pe", (lambda e, ps=ps, tb=tb, k=k: e.matmul(ps[:], ones_b[:], tb[:], start=(k == 0), stop=(k == KC - 1))),
                         reads=[tbt, t_const], writes=[pt])
                tf, tft = next_tf()
                S.op("act", (lambda e, ps=ps, tf=tf: e.activation(out=tf[:], in_=ps[:], func=AF.Sqrt, bias=eps_t[:, 0:1], scale=1.0 / D)),
                     reads=[pt, t_eps], writes=[tft])
                S.op("dve", (lambda e, tf=tf, tg=tg: e.reciprocal(out=rstd_s[:, tsl(tg)], in_=tf[:])), reads=[tft], writes=[t_rstd[tg]])

        def norm_mod(dst, a_ap, b_ap, extra_reads):
            for tg in range(NTG):
                for k in range(KC):
                    tf, tft = next_tf()
                    S.op("dve", (lambda e, tf=tf, k=k, tg=tg: e.tensor_tensor(out=tf[:], in0=X.ap[:, k, tsl(tg)], in1=rstd_s[:, tsl(tg)], op=ALU.mult)),
                         reads=[X.t(k, tg), t_rstd[tg]], writes=[tft])
                    S.op("act", (lambda e, tf=tf, k=k, tg=tg: e.activation(out=dst.ap[:, k, tsl(tg)], in_=tf[:], func=AF.Identity,
                                                                            bias=b_ap[:, k:k + 1], scale=a_ap[:, k:k + 1])),
                         reads=[tft] + extra_reads, writes=[dst.t(k, tg)])

        def proj(wv, wt, src, n_oc, evac, tgs=range(NTG), oc_cols=None):
            for tg in tgs:
                for oc in range(n_oc):
                    ps, pt = next_ps()
                    for k in range(KC):
                        lhs = wv[:, k, oc * 128:(oc + 1) * 128] if oc_cols is None else oc_cols(wv, k, oc)
                        S.op("pe", (lambda e, ps=ps, lhs=lhs, k=k, tg=tg: e.matmul(ps[:], lhs, src.ap[:, k, tsl(tg)],
                                                                                   start=(k == 0), stop=(k == KC - 1))),
                             reads=[wt, src.t(k, tg)], writes=[pt])
                    evac(oc, tg, ps, pt)

        def resid_evac(g_ap, extra_reads):
            def ev(oc, tg, ps, pt):
                S.op("dve", (lambda e: e.scalar_tensor_tensor(out=X.ap[:, oc, tsl(tg)], in0=ps[:], scalar=g_ap[:, oc:oc + 1],
                                                              in1=X.ap[:, oc, tsl(tg)], op0=ALU.mult, op1=ALU.add)),
                     reads=[pt, X.t(oc, tg)] + extra_reads, writes=[X.t(oc, tg)])
            return ev

        def mlp(layer, mi):
            compute_rstd()
            derive(mi, 4, V_NG + 2 * layer + 1, 1)
            norm_mod(H, der[:, 1, :], mod_part(mi, 3), [t_der, t_modv[mi]])
            g2 = mod_part(mi, 5)
            for hb in range(4):
                wv, wt = load_w_std(d_w_mlp1[layer], hb * 1024, 1024)

                def ev1(oc, tg, ps, pt):
                    tf, tft = next_tf()
                    S.op("act", (lambda e: e.activation(out=tf[:], in_=ps[:], func=AF.Relu)), reads=[pt], writes=[tft])
                    S.op("dve", (lambda e: e.tensor_tensor(out=A.ap[:, oc, tsl(tg)], in0=tf[:], in1=tf[:], op=ALU.mult)),
                         reads=[tft], writes=[A.t(oc, tg)])
                proj(wv, wt, H, 8, ev1)
                wv2, wt2 = load_w_std(d_w_mlp2[layer], 0, 1024, k0=hb * 8)
                proj(wv2, wt2, A, 8, resid_evac(g2, [t_modv[mi]]))

        ada_matvec(d_w_ada[0], 6, V_BADA, 0)
        compute_rstd()
        derive(0, 1, V_NG + 0, 0)
        norm_mod(H, der[:, 0, :], mod_part(0, 0), [t_der, t_modv[0]])

        gbv = SCR[:, 0:4096].rearrange("p (j t) -> p j t", j=2)
        vv = SCR[:, 4096:4096 + 2 * 2056].rearrange("p (j t) -> p j t", j=2)
        t_gb = [[Tok() for _ in range(NTG)] for _ in range(2)]
        t_v = [[Tok() for _ in range(NTG)] for _ in range(2)]
        t_vhalo = Tok()
        S.op("dve", lambda e: e.memset(vv[:, :, 0:2], 0.0), writes=[t_vhalo])
        w_in_v = d_w_a_in[0].rearrange("(k p) (s c) -> p k s c", p=128, s=3)
        g1 = mod_part(0, 2)
        def mixer_j(wv, wt, jj, j):
            jb = j % 2
            for tg in range(NTG):
                pss = []
                for s in range(3):
                    ps, pt = next_ps()
                    for k in range(KC):
                        S.op("pe", (lambda e, ps=ps, s=s, k=k, tg=tg: e.matmul(ps[:], wv[:, k, s, jj * 128:(jj + 1) * 128], H.ap[:, k, tsl(tg)],
                                                                               start=(k == 0), stop=(k == KC - 1))),
                             reads=[wt, H.t(k, tg)], writes=[pt])
                    pss.append((ps, pt))
                (psb, ptb), (psc, ptc), (psu, ptu) = pss
                S.op("act", (lambda e, psb=psb, tg=tg: e.activation(out=gbv[:, jb, tsl(tg)], in_=psb[:], func=AF.Copy)),
                     reads=[ptb], writes=[t_gb[jb][tg]])
                tb, tbt = next_tb()
                S.op("act", (lambda e, psc=psc, tb=tb: e.activation(out=tb[:], in_=psc[:], func=AF.Copy)), reads=[ptc], writes=[tbt])
                S.op("dve", (lambda e, psu=psu, tb=tb, tg=tg: e.tensor_tensor(out=vv[:, jb, 2 + tg * TGW:2 + (tg + 1) * TGW], in0=psu[:], in1=tb[:], op=ALU.mult)),
                     reads=[ptu, tbt], writes=[t_v[jb][tg]])
            for tg in range(NTG):
                tf, tft = next_tf()
                rd = [t_v[jb][tg], t_vhalo, t_vecs] + ([t_v[jb][tg - 1]] if tg > 0 else [])
                b0 = tg * TGW
                S.op("dve", (lambda e, tf=tf, b0=b0: e.tensor_scalar(out=tf[:], in0=vv[:, jb, b0 + 2:b0 + 2 + TGW], scalar1=vecs[:, V_CONV + 2, j:j + 1], scalar2=None, op0=ALU.mult)),
                     reads=rd, writes=[tft])
                S.op("dve", (lambda e, tf=tf, b0=b0: e.scalar_tensor_tensor(out=tf[:], in0=vv[:, jb, b0 + 1:b0 + 1 + TGW], scalar=vecs[:, V_CONV + 1, j:j + 1], in1=tf[:], op0=ALU.mult, op1=ALU.add)),
                     reads=rd + [tft], writes=[tft])
                S.op("dve", (lambda e, tf=tf, b0=b0: e.scalar_tensor_tensor(out=tf[:], in0=vv[:, jb, b0:b0 + TGW], scalar=vecs[:, V_CONV + 0, j:j + 1], in1=tf[:], op0=ALU.mult, op1=ALU.add)),
                     reads=rd + [tft], writes=[tft])
                S.op("dve", (lambda e, tf=tf, tg=tg: e.tensor_tensor(out=A.ap[:, j, tsl(tg)], in0=tf[:], in1=gbv[:, jb, tsl(tg)], op=ALU.mult)),
                     reads=[tft, t_gb[jb][tg]], writes=[A.t(j, tg)])

        for jp in range(4):
            def view(slot):
                return slot[:, 0:6144].rearrange("p (k s c) -> p k s c", k=8, s=3)
            pieces = []
            for s3 in range(3):
                pieces.append(((lambda slot, s3=s3: view(slot)[:, :, s3, :]), w_in_v[:, :, s3, jp * 256:(jp + 1) * 256]))
            wv, wt = load_w(pieces, view)
            for jj in range(2):
                mixer_j(wv, wt, jj, 2 * jp + jj)
        wv, wt = load_w_std(d_w_a_out[0], 0, 1024)
        proj(wv, wt, A, 8, resid_evac(g1, [t_modv[0]]))
        if stop != "mix0":
            mlp(0, 0)
        def chk(name, dumps):
            if stop != name:
                return
            S.barrier()
            for ap, dst in dumps:
                S.op("pool", (lambda e, ap=ap, dst=dst: e.dma_start(out=dst, in_=ap)), dma="out")
            raise StopBuild()

        if stop not in ("mix0", "l0"):
          try:
            G4 = 4
            defpool["v"] = "P6"
            t_l1c = Tok()
            wqg_g = SCR[:, 8192:8576].rearrange("p (k c) -> p k c", k=8)
            cw2k = SCR[:, 8576:8832].rearrange("p (c d) -> p c d", c=2)
            cw2v = SCR[:, 8832:8960].rearrange("p (c d) -> p c d", c=2)
            peT_b = SCR[:, 8960:9024]
            S.op("sp", lambda e: e.dma_start(out=hvecs[:], in_=d_hvecs), writes=[t_l1c], dma="c")
            S.op("pool", lambda e: e.dma_start(out=ident_b[:], in_=d_consts[:, 0:128]), writes=[t_l1c], dma="cb")
            S.op("pool", lambda e: e.dma_start(out=bd_ones[:], in_=d_bd), writes=[t_l1c], dma="cb")
            S.op("pool", lambda e: e.dma_start(out=peT_b, in_=d_peT), writes=[t_l1c], dma="cb")
            S.op("pool", lambda e: e.dma_start(out=cw2k[:, :, 0:64], in_=d_cmp_w2[0].rearrange("(c p) d -> p c d", p=128)), writes=[t_l1c], dma="cb")
            S.op("pool", lambda e: e.dma_start(out=cw2k[:, :, 64:128], in_=d_cmp_w2[0].rearrange("(c p) d -> p c d", p=128)), writes=[t_l1c], dma="cb")
            S.op("pool", lambda e: e.dma_start(out=cw2v, in_=d_cmp_w2[1].rearrange("(c p) d -> p c d", p=128)), writes=[t_l1c], dma="cb")
            S.op("pool", lambda e: e.dma_start(out=wqg_g, in_=d_w_qg[0].rearrange("(k p) n -> p k n", p=128)[:, :, 1024:1072]), writes=[t_l1c], dma="cb")
            t_gq = Tok()
            S.op("dve", lambda e: e.tensor_scalar(out=gq[:], in0=hvecs[:, 0:1], scalar1=0.125, scalar2=None, op0=ALU.mult), reads=[t_l1c], writes=[t_gq])

            ada_matvec(d_w_ada[1], 6, V_BADA + 6, 1)
            ada_matvec(d_w_ada_kv, 2, V_BKV, 2)
            compute_rstd()
            derive(2, 1, V_KVG, 2)
            derive(1, 1, V_NG + 2, 3)
            norm_mod(H, der[:, 2, :], mod_part(2, 0), [t_der, t_modv[2]])
            norm_mod(A, der[:, 3, :], mod_part(1, 0), [t_der, t_modv[1]])
            xs_t = [Tok() for _ in range(NTG)]
            for tg in range(NTG):
                for kh in range(2):
                    ks = slice(kh * 4, kh * 4 + 4)
                    S.op("sp", (lambda e, tg=tg, ks=ks: e.dma_start(out=d_xs[:, ks, tsl(tg)], in_=X.ap[:, ks, tsl(tg)])),
                         reads=[X.t(k, tg) for k in range(kh * 4, kh * 4 + 4)], writes=[xs_t[tg]], dma="xs")

            def head_norm(ps, pt, ncol, gain_ap, gain_reads, dst_ap, dst_toks):
                tb, tbt = next_tb()
                S.op("act", (lambda e: e.activation(out=tb[:, 0:ncol], in_=ps[:, 0:ncol], func=AF.Square)), reads=[pt], writes=[tbt])
                ps2, pt2 = next_ps("M")
                S.op("pe", (lambda e: e.matmul(ps2[:, 0:ncol], bd_ones[:], tb[:, 0:ncol], start=True, stop=True)), reads=[tbt, t_l1c], writes=[pt2])
                tf, tft = next_tf()
                S.op("act", (lambda e: e.activation(out=tf[:, 0:ncol], in_=ps2[:, 0:ncol], func=AF.Sqrt, bias=eps_t[:, 0:1], scale=1.0 / 64)),
                     reads=[pt2, t_eps], writes=[tft])
                tf2, tft2 = next_tf()
                S.op("dve", (lambda e: e.reciprocal(out=tf2[:, 0:ncol], in_=tf[:, 0:ncol])), reads=[tft], writes=[tft2])
                S.op("dve", (lambda e: e.scalar_tensor_tensor(out=dst_ap, in0=ps[:, 0:ncol], scalar=gain_ap, in1=tf2[:, 0:ncol], op0=ALU.mult, op1=ALU.mult)),
                     reads=[pt, tft2] + gain_reads, writes=dst_toks)

            RAW = TT(SCR[:, 0:8192].rearrange("p (c t) -> p c t", c=4))
            wv, wt = load_w_std(d_w_kv, 0, 512)

            def ev_raw(oc, tg, ps, pt):
                S.op("act", (lambda e: e.activation(out=RAW.ap[:, oc, tsl(tg)], in_=ps[:], func=AF.Copy)), reads=[pt], writes=[RAW.t(oc, tg)])
            proj(wv, wt, H, 4, ev_raw)

            chk("raw", [(RAW.ap, d_y[:, 0:4, :])])
            t_kc = [Tok() for _ in range(G4)]
            t_vc = [Tok() for _ in range(G4)]
            t_vc_ones = Tok()
            S.op("dve", lambda e: e.memset(vcA[:, :, 64:128], 1.0), writes=[t_vc_ones])
            t_cb = Tok()
            for kv in range(2):
                def view1(slot):
                    return slot[:, 0:8192].rearrange("p (l h) -> p l h", l=32)
                src1 = d_cmp_w1[kv].rearrange("(l d) h -> d l h", d=64)
                pieces = [((lambda slot: view1(slot)[0:64, :, :]), src1), ((lambda slot: view1(slot)[64:128, :, :]), src1)]
                cwv, cwt = load_w(pieces, view1)
                psb, ptb = next_ps()
                for hc in range(2):
                    for l in range(32):
                        S.op("pe", (lambda e, hc=hc, l=l, cwv=cwv, psb=psb, kv=kv: e.matmul(
                            psb[:, hc:hc + 1], cwv[0:64, l, hc * 128:(hc + 1) * 128], peT_b[0:64, kv * 32 + l:kv * 32 + l + 1],
                            start=(l == 0), stop=(l == 31))), reads=[cwt, t_l1c], writes=[ptb])
                S.op("dve", (lambda e, psb=psb, kv=kv: e.tensor_copy(out=cbias[:, 2 * kv:2 * kv + 2], in_=psb[:, 0:2])), reads=[ptb], writes=[t_cb])
                for g in range(G4):
                    base = (g % 2) * 64
                    c = kv * 2 + g // 2
                    hids = []
                    for hc in range(2):
                        ps, pt = next_ps()
                        for l in range(32):
                            S.op("pe", (lambda e, ps=ps, l=l, hc=hc, cwv=cwv, base=base, c=c: e.matmul(
                                ps[:, 0:127], cwv[base:base + 64, l, hc * 128:(hc + 1) * 128],
                                RAW.ap[base:base + 64, c, l:l + 16 * 126 + 1:16], start=(l == 0), stop=(l == 31))),
                                reads=[cwt] + [RAW.t(c, tg) for tg in range(NTG)], writes=[pt])
                        z, zt = next_tf()
                        S.op("act", (lambda e, ps=ps, z=z, hc=hc, kv=kv: e.activation(out=z[:, 0:127], in_=ps[:, 0:127], func=AF.Identity,
                                                                                      bias=cbias[:, 2 * kv + hc:2 * kv + hc + 1], scale=1.0)),
                             reads=[pt, t_cb], writes=[zt])
                        u, ut = next_tf()
                        S.op("dve", (lambda e, z=z, u=u: e.tensor_tensor(out=u[:, 0:127], in0=z[:, 0:127], in1=z[:, 0:127], op=ALU.mult)), reads=[zt], writes=[ut])
                        S.op("dve", (lambda e, u=u: e.tensor_scalar(out=u[:, 0:127], in0=u[:, 0:127], scalar1=0.044715, scalar2=1.0, op0=ALU.mult, op1=ALU.add)),
                             reads=[ut], writes=[ut])
                        S.op("dve", (lambda e, z=z, u=u: e.tensor_tensor(out=u[:, 0:127], in0=u[:, 0:127], in1=z[:, 0:127], op=ALU.mult)), reads=[ut, zt], writes=[ut])
                        S.op("act", (lambda e, u=u: e.activation(out=u[:, 0:127], in_=u[:, 0:127], func=AF.Sigmoid, scale=1.5957691216057308)), reads=[ut], writes=[ut])
                        hb_, hbt = next_tb()
                        S.op("dve", (lambda e, z=z, u=u, hb_=hb_: e.tensor_tensor(out=hb_[:, 0:127], in0=u[:, 0:127], in1=z[:, 0:127], op=ALU.mult)), reads=[ut, zt], writes=[hbt])
                        hids.append((hb_, hbt))
                    if kv == 0:
                        ps, pt = next_ps()
                        for hc in range(2):
                            S.op("pe", (lambda e, ps=ps, hc=hc, hb_=hids[hc][0]: e.matmul(ps[:, 0:127], cw2k[:, hc, :], hb_[:, 0:127], start=(hc == 0), stop=(hc == 1))),
                                 reads=[hids[hc][1], t_l1c], writes=[pt])
                        head_norm(ps, pt, 127, hvecs[:, 1:2], [t_l1c], kcT[:, g, 0:127], [t_kc[g]])
                    else:
                        ps, pt = next_ps()
                        for hc in range(2):
                            S.op("pe", (lambda e, ps=ps, hc=hc, hb_=hids[hc][0]: e.matmul(ps[0:127, 0:64], hb_[:, 0:127], cw2v[:, hc, :], start=(hc == 0), stop=(hc == 1))),
                                 reads=[hids[hc][1], t_l1c], writes=[pt])
                        S.op("act", (lambda e, ps=ps, g=g: e.activation(out=vcA[0:127, g, 0:64], in_=ps[0:127, 0:64], func=AF.Copy)), reads=[pt], writes=[t_vc[g]])

            chk("cmp", [(kcT[:].rearrange("p g n -> p (g n)"), d_y[:, 0, 0:512]), (vcA[:].rearrange("p g n -> p (g n)"), d_y[:, 1, 0:512])])
            S.barrier()
            XB = XR[:].bitcast(BF16)
            QT = TT(XB[:, 0:16384].rearrange("p (h t) -> p h t", h=8))
            KST = TT(XB[:, 16384:24576].rearrange("p (g t) -> p g t", g=4))
            KWT = TT(XB[:, 24576:32768].rearrange("p (g t) -> p g t", g=4))
            RB = rstd_s[:].bitcast(BF16)
            SIGG = TT(RB[0:48, 0:T])

            wv, wt = load_w_std(d_w_qg[0], 0, 1024)

            def ev_q(oc, tg, ps, pt):
                head_norm(ps, pt, TGW, gq[:, 0:1], [t_gq], QT.ap[:, oc, tsl(tg)], [QT.t(oc, tg)])
            proj(wv, wt, A, 8, ev_q)
            for tg in range(NTG):
                ps, pt = next_ps()
                for k in range(KC):
                    S.op("pe", (lambda e, ps=ps, k=k, tg=tg: e.matmul(ps[0:48, :], wqg_g[:, k, :], A.ap[:, k, tsl(tg)], start=(k == 0), stop=(k == KC - 1))),
                         reads=[t_l1c, A.t(k, tg)], writes=[pt])
                S.op("act", (lambda e, ps=ps, tg=tg: e.activation(out=SIGG.ap[:, tsl(tg)], in_=ps[0:48, :], func=AF.Sigmoid)), reads=[pt], writes=[SIGG.t(tg)])

            for typ, dst, gi in ((2, KST, 2), (4, KWT, 3)):
                def viewk(slot):
                    return slot[:, 0:4096].rearrange("p (k g r d) -> p k g r d", k=8, g=4, r=2)
                srck = d_w_kv.rearrange("(k p) n -> p k n", p=128)[:, :, typ * 256:(typ + 1) * 256].rearrange("p k (g d) -> p k g d", g=4)
                pieces = [((lambda slot, r=r, gg=gg: viewk(slot)[:, :, gg, r, :]), srck[:, :, gg, :]) for r in range(2) for gg in range(4)]
                kv_, kt_ = load_w(pieces, viewk)

                def ev_k(oc, tg, ps, pt, dst=dst, gi=gi):
                    head_norm(ps, pt, TGW, hvecs[:, gi:gi + 1], [t_l1c], dst.ap[:, oc, tsl(tg)], [dst.t(oc, tg)])
                proj(kv_, kt_, H, 4, ev_k, oc_cols=(lambda wv_, k, oc: wv_[:, k, oc, :, :].rearrange("p r d -> p (r d)")))
            chk("q", [(QT.ap, d_y)])
            chk("k", [(KST.ap, d_y[:, 0:4, :]), (KWT.ap, d_y[:, 4:8, :]), ])
            S.barrier()
            AB = AR[:]
            VS = TT(AB[:, 0:8192].rearrange("p (t g c) -> p t g c", t=16, g=4))
            VW = TT(AB[:, 8192:16384].rearrange("p (t g c) -> p t g c", t=16, g=4))
            t_vones = Tok()
            S.op("dve", lambda e: e.memset(VS.ap[:, :, :, 64:128], 1.0), writes=[t_vones])
            S.op("dve", lambda e: e.memset(VW.ap[:, :, :, 64:128], 1.0), writes=[t_vones])

            def viewv(slot):
                return slot[:, 0:4096].rearrange("p (k s c) -> p k s c", k=8, s=2)
            srcv = d_w_kv.rearrange("(k p) n -> p k n", p=128)
            pieces = [((lambda slot: viewv(slot)[:, :, 0, :]), srcv[:, :, 768:1024]), ((lambda slot: viewv(slot)[:, :, 1, :]), srcv[:, :, 1280:1536])]
            vv_, vt_ = load_w(pieces, viewv)
            for tt in range(16):
                ps, pt = next_ps()
                tg = tt // 4
                for k in range(KC):
                    S.op("pe", (lambda e, ps=ps, k=k, tt=tt: e.matmul(ps[:], H.ap[:, k, tt * 128:(tt + 1) * 128], vv_[:, k, :, :].rearrange("p s c -> p (s c)"),
                                                                    start=(k == 0), stop=(k == KC - 1))), reads=[vt_, H.t(k, tg)], writes=[pt])
                S.op("act", (lambda e, ps=ps, tt=tt: e.activation(out=VS.ap[:, tt, :, 0:64], in_=ps[:, 0:256].rearrange("p (g d) -> p g d", g=4), func=AF.Copy)),
                     reads=[pt, t_vones], writes=[VS.t(tt)])
                S.op("dve", (lambda e, ps=ps, tt=tt: e.tensor_copy(out=VW.ap[:, tt, :, 0:64], in_=ps[:, 256:512].rearrange("p (g d) -> p g d", g=4))),
                     reads=[pt, t_vones], writes=[VW.t(tt)])
            chk("v", [(AB[:, 0:8192], d_y[:, 0:4, :].rearrange("p a b -> p (a b)")), (AB[:, 8192:16384], d_y[:, 4:8, :].rearrange("p a b -> p (a b)"))])
            S.barrier()

            t_ac = Tok()
            tri_b = SCR[:, 0:128]
            anti_b = SCR[:, 128:256]
            cmpm = SCR[:, 256:2304]
            e128 = SCR[:, 2304:4352].rearrange("p (k j) -> p k j", k=16)
            cmap = SCR[:, 4352:4392]
            force = SCR[:, 4392:4904].bitcast(F32).rearrange("p (q j) -> p q j", q=8)
            gsel = SCR[0:48, 4904:7976].rearrange("p (h b m) -> p h b m", h=8, b=3)
            PTS = [SCR[:, 7976 + i * 1024:7976 + (i + 1) * 1024].rearrange("p (k r c) -> p k r c", k=2, r=2) for i in range(3)]
            pts_t = [Tok() for _ in PTS]
            S.op("pool", lambda e: e.dma_start(out=SCR[:, 0:2304], in_=d_amask), writes=[t_ac], dma="cb")
            S.op("pool", lambda e: e.dma_start(out=SCR[:, 2304:4352], in_=d_e128), writes=[t_ac], dma="cb")
            S.op("pool", lambda e: e.dma_start(out=cmap, in_=d_cmap), writes=[t_ac], dma="cb")
            S.op("sp", lambda e: e.dma_start(out=SCR[:, 4392:4904].bitcast(F32), in_=d_force), writes=[t_ac], dma="c")
            S.op("pool", lambda e: e.dma_start(out=SCR[0:48, 4904:7976], in_=d_gsel), writes=[t_ac], dma="cb")
            HB = HR[:]
            OTG = TT(HB[:, 0:4096].rearrange("p (h t) -> p h t", h=8))
            GBC = HB[:, 4096:7168].rearrange("p (h b t) -> p h b t", h=2, b=3)
            t_gbc = Tok()
            XST = TT(HB[:, 7168:15360].bitcast(F32).rearrange("p (k t) -> p k t", k=8))
            BIAS = [HB[:, 15360 + i * 256:15360 + (i + 1) * 256].rearrange("p (r q) -> p r q", r=2) for i in range(2)]
            bias_t = [Tok(), Tok()]
            for i in range(2):
                S.op("dve", (lambda e, i=i: e.memset(BIAS[i], 0.0)), writes=[bias_t[i]])
            wo_v, wo_t = load_w_std(d_w_o[0], 0, 1024)
            ptc = {"n": 0, "b": 0, "a": 0}

            def next_pt():
                i = ptc["n"] % 3
                ptc["n"] += 1
                return PTS[i], pts_t[i]

            def qk_tile(bank, bt, colbase, KT, g, kt0, nk, par, qt, mask, bias_i):
                lhs_k = KT.ap[par * 64:(par + 1) * 64, g, kt0:kt0 + nk] if KT is not None else kcT[par * 64:(par + 1) * 64, g, 0:127]
                k_reads = [KT.t(g, kt0 // TGW)] if KT is not None else [t_kc[g]]
                qsl = slice(qt * 128, (qt + 1) * 128)
                q_reads = [QT.t(2 * g, qt // 4), QT.t(2 * g + 1, qt // 4)]
                if mask is None:
                    S.op("pe", (lambda e: e.matmul(bank[0:nk, colbase:colbase + 256], lhs_k, QT.ap[par * 64:(par + 1) * 64, 2 * g:2 * g + 2, qsl], start=True, stop=True)),
                         reads=k_reads + q_reads, writes=[bt])
                elif mask == "bias":
                    kt = kt0 // 128
                    S.op("pe", (lambda e: e.matmul(bank[:, colbase:colbase + 256], e128[:, kt, :], BIAS[bias_i].rearrange("p r q -> p (r q)"), start=True, stop=False)),
                         reads=[t_ac, bias_t[bias_i]], writes=[bt])
                    S.op("pe", (lambda e: e.matmul(bank[:, colbase:colbase + 256], lhs_k, QT.ap[par * 64:(par + 1) * 64, 2 * g:2 * g + 2, qsl], start=False, stop=True)),
                         reads=k_reads + q_reads, writes=[bt])
                else:
                    for hpl in range(2):
                        cs = slice(colbase + hpl * 128, colbase + (hpl + 1) * 128)
                        S.op("pe", (lambda e, cs=cs: e.matmul(bank[0:nk, cs], ident_b[:, 0:nk], mask, start=True, stop=False)), reads=[t_ac, t_l1c], writes=[bt])
                        S.op("pe", (lambda e, cs=cs, hpl=hpl: e.matmul(bank[0:nk, cs], lhs_k, QT.ap[par * 64:(par + 1) * 64, 2 * g + hpl, qsl], start=False, stop=True)),
                             reads=k_reads + q_reads, writes=[bt])

            def attn_branch(g, qt, kind, bias_i):
                ps_o, pt_o = next_ps("O")
                if kind == "cmp":
                    bA, tA = next_ps("S")
                    bB, tB = next_ps("S")
                    PT, ptt = next_pt()
                    for par, bank, bt in ((0, bA, tA), (1, bB, tB)):
                        qk_tile(bank, bt, 0, None, g, 0, 127, par, qt, cmpm[:, qt * 128:(qt + 1) * 128], None)
                        S.op("act", (lambda e, par=par, bank=bank: e.activation(out=PT[0:127, 0, par, :], in_=bank[0:127, 0:256], func=AF.Exp)),
                             reads=[bt], writes=[ptt])
                    S.op("pe", (lambda e: e.matmul(ps_o[:], vcA[0:127, g, :], PT[0:127, 0, :, :].rearrange("p r c -> p (r c)"), start=True, stop=True)),
                         reads=[ptt, t_vc[g], t_vc_ones], writes=[pt_o])
                    return ps_o, pt_o, (PT, ptt)
                KT, V = (KST, VS) if kind == "sel" else (KWT, VW)
                kts = list(range(0, qt + 1)) if kind == "sel" else list(range(max(0, qt - 4), qt + 1))
                for pi in range(0, len(kts), 2):
                    pair = kts[pi:pi + 2]
                    npair = len(pair)
                    bA, tA = next_ps("S")
                    bB, tB = next_ps("S")
                    PT, ptt = next_pt()
                    for par, bank, bt in ((0, bA, tA), (1, bB, tB)):
                        for ktp, kt in enumerate(pair):
                            if kt == qt:
                                mask = tri_b
                            elif kind == "win" and kt == qt - 4:
                                mask = anti_b
                            elif kind == "sel" and qt >= 8:
                                mask = "bias"
                            else:
                                mask = None
                            qk_tile(bank, bt, ktp * 256, KT, g, kt * 128, 128, par, qt, mask, bias_i)
                        S.op("act", (lambda e, par=par, bank=bank, PT=PT, npair=npair: e.activation(
                            out=PT[:, 0:npair, par, :], in_=bank[:, 0:npair * 256].rearrange("p (k c) -> p k c", k=npair), func=AF.Exp)),
                            reads=[bt], writes=[ptt])
                    for ktp, kt in enumerate(pair):
                        S.op("pe", (lambda e, ktp=ktp, kt=kt, PT=PT: e.matmul(ps_o[:], V.ap[:, kt, g, :], PT[:, ktp, :, :].rearrange("p r c -> p (r c)"),
                                                                             start=(kt == kts[0]), stop=(kt == kts[-1]))),
                             reads=[ptt, V.t(kt), t_vones], writes=[pt_o])
                return ps_o, pt_o, None

            def combine(g, qt, br, ps_o, pt_o, acc, acct):
                qi = qt % 4
                tf, tft = next_tf()
                S.op("dve", (lambda e: e.tensor_scalar(out=tf[64:128, :], in0=ps_o[64:128, :], scalar1=1e-30, scalar2=None, op0=ALU.max)), reads=[pt_o], writes=[tft])
                tf2, tft2 = next_tf()
                S.op("dve", (lambda e: e.reciprocal(out=tf2[64:128, :], in_=tf[64:128, :])), reads=[tft], writes=[tft2])
                on, ont = next_tf()
                for par in range(2):
                    S.op("dve", (lambda e, par=par: e.tensor_tensor(out=on[par * 64:(par + 1) * 64, 0:256], in0=ps_o[0:64, par * 256:(par + 1) * 256],
                                                                     in1=tf2[64:128, par * 256:(par + 1) * 256], op=ALU.mult)), reads=[pt_o, tft2], writes=[ont])
                onv = on[:, 0:256].rearrange("p (h q) -> p h q", h=2)
                accv = acc[:].rearrange("p (h q) -> p h q", h=2)
                Gv = GBC[:, :, br, qi * 128:(qi + 1) * 128]
                if br == 0:
                    S.op("dve", (lambda e: e.tensor_tensor(out=accv, in0=onv, in1=Gv, op=ALU.mult)), reads=[ont, t_gbc], writes=[acct])
                else:
                    S.op("dve", (lambda e: e.tensor_tensor(out=onv, in0=onv, in1=Gv, op=ALU.mult)), reads=[ont, t_gbc], writes=[ont])
                    if br == 1:
                        S.op("dve", (lambda e: e.tensor_tensor(out=accv, in0=accv, in1=onv, op=ALU.add)), reads=[ont, acct], writes=[acct])
                    else:
                        S.op("dve", (lambda e: e.tensor_tensor(out=OTG.ap[:, 2 * g:2 * g + 2, qi * 128:(qi + 1) * 128], in0=accv, in1=onv, op=ALU.add)),
                             reads=[ont, acct], writes=[OTG.t(2 * g, 0), OTG.t(2 * g + 1, 0)])

            def topk_bias(g, qt, PT, ptt, bias_i):
                ps_i, pt_i = next_ps("M")
                for c in range(4):
                    S.op("pe", (lambda e, c=c: e.matmul(ps_i[:, c * 33:(c + 1) * 33], PT[0:127, 0, c // 2, (c % 2) * 128:(c % 2 + 1) * 128], cmap[0:127, 0:33],
                                                          start=True, stop=True)), reads=[ptt, t_ac], writes=[pt_i])
                psv = ps_i[:, 0:132].rearrange("p (c j) -> p c j", c=4)
                S.op("dve", (lambda e: e.reciprocal(out=rd4[:], in_=psv[:, :, 32])), reads=[pt_i], writes=[t_tk])
                S.op("dve", (lambda e: e.tensor_scalar(out=scA[:], in0=psv[:, 0, 0:32], scalar1=rd4[:, 0:1], scalar2=None, op0=ALU.mult)), reads=[pt_i, t_tk], writes=[t_tk])
                for c in range(1, 4):
                    S.op("dve", (lambda e, c=c: e.scalar_tensor_tensor(out=scA[:], in0=psv[:, c, 0:32], scalar=rd4[:, c:c + 1], in1=scA[:], op0=ALU.mult, op1=ALU.add)),
                         reads=[pt_i, t_tk], writes=[t_tk])
                S.op("dve", (lambda e: e.tensor_tensor(out=scA[:], in0=scA[:], in1=force[:, qt - 8, :], op=ALU.add)), reads=[t_tk, t_ac], writes=[t_tk])
                S.op("dve", (lambda e: e.max(out=m8a[:], in_=scA[:])), reads=[t_tk], writes=[t_tk])
                S.op("dve", (lambda e: e.match_replace(out=scB[:], in_to_replace=m8a[:], in_values=scA[:], imm_value=-1e30)), reads=[t_tk], writes=[t_tk])
                S.op("dve", (lambda e: e.max(out=m8b[:], in_=scB[:])), reads=[t_tk], writes=[t_tk])
                S.op("dve", (lambda e: e.tensor_scalar(out=selm[:], in0=scA[:], scalar1=m8b[:, 7:8], scalar2=1.0, op0=ALU.is_ge, op1=ALU.subtract)), reads=[t_tk], writes=[t_tk])
                ps_m, pt_m = next_ps("M")
                S.op("pe", (lambda e: e.matmul(ps_m[0:32, 0:128], selm[:], ident_b[:], start=True, stop=True)), reads=[t_tk, t_l1c], writes=[pt_m])
                S.op("act", (lambda e: e.activation(out=BIAS[bias_i][0:32, :, :], in_=ps_m[0:32, 0:128].unsqueeze(1).to_broadcast([32, 2, 128]), func=AF.Copy, scale=30000.0)),
                     reads=[pt_m], writes=[bias_t[bias_i]])

            on_t = [Tok(), Tok()]
            acc_t = [Tok(), Tok()]
            t_tk = Tok()
            g1 = mod_part(1, 2)
            for tg in range(NTG):
                for kh in range(2):
                    ks = slice(kh * 4, kh * 4 + 4)
                    S.op("sp", (lambda e, tg=tg, ks=ks: e.dma_start(out=XST.ap[:, ks, :], in_=d_xs[:, ks, tsl(tg)])),
                         reads=[xs_t[tg]], writes=[XST.t(k) for k in range(kh * 4, kh * 4 + 4)], dma="xr")
                for g in range(G4):
                    for hpl in range(2):
                        for br in range(3):
                            ps, pt = next_ps("M")
                            S.op("pe", (lambda e, ps=ps, hpl=hpl, br=br, g=g, tg=tg: e.matmul(ps[:], gsel[:, 2 * g + hpl, br, :], SIGG.ap[:, tsl(tg)], start=True, stop=True)),
                                 reads=[t_ac, SIGG.t(tg)], writes=[pt])
                            S.op("act", (lambda e, ps=ps, hpl=hpl, br=br: e.activation(out=GBC[:, hpl, br, :], in_=ps[:], func=AF.Copy)), reads=[pt], writes=[t_gbc])
                    for qi in range(4):
                        qt = tg * 4 + qi
                        ai = ptc["b"] % 2
                        ptc["b"] += 1
                        acc, acct = ACC[ai], acc_t[ai]
                        ps_o, pt_o, pc = attn_branch(g, qt, "cmp", ai)
                        if qt >= 8:
                            topk_bias(g, qt, pc[0], pc[1], ai)
                        combine(g, qt, 0, ps_o, pt_o, acc, acct)
                        ps_o, pt_o, _ = attn_branch(g, qt, "sel", ai)
                        combine(g, qt, 1, ps_o, pt_o, acc, acct)
                        ps_o, pt_o, _ = attn_branch(g, qt, "win", ai)
                        combine(g, qt, 2, ps_o, pt_o, acc, acct)

                def ev_o(oc, tg_, ps, pt, tg=tg):
                    S.op("dve", (lambda e: e.scalar_tensor_tensor(out=XST.ap[:, oc, :], in0=ps[:], scalar=g1[:, oc:oc + 1], in1=XST.ap[:, oc, :], op0=ALU.mult, op1=ALU.add)),
                         reads=[pt, XST.t(oc), t_modv[1]], writes=[XST.t(oc)])
                proj(wo_v, wo_t, OTG, 8, ev_o, tgs=[0])
                for kh in range(2):
                    ks = slice(kh * 4, kh * 4 + 4)
                    S.op("sp", (lambda e, tg=tg, ks=ks: e.dma_start(out=d_xs[:, ks, tsl(tg)], in_=XST.ap[:, ks, :])),
                         reads=[XST.t(k) for k in range(kh * 4, kh * 4 + 4)], writes=[xs_t[tg]], dma="xw")
            S.barrier()
            for tg in range(NTG):
                for kh in range(2):
                    ks = slice(kh * 4, kh * 4 + 4)
                    S.op("sp", (lambda e, tg=tg, ks=ks: e.dma_start(out=X.ap[:, ks, tsl(tg)], in_=d_xs[:, ks, tsl(tg)])),
                         reads=[xs_t[tg]], writes=[X.t(k, tg) for k in range(kh * 4, kh * 4 + 4)], dma="x2")
            defpool["v"] = "ALL"
            if stop != "mix1":
                mlp(1, 1)
          except StopBuild:
            S.emit(final_dma_keys=["out"])
            return nc

        for tg in range(NTG):
            for kh in range(2):
                ks = slice(kh * 4, kh * 4 + 4)
                S.op("sp", (lambda e, tg=tg, ks=ks: e.dma_start(out=d_y[:, ks, tsl(tg)], in_=X.ap[:, ks, tsl(tg)])),
                     reads=[X.t(k, tg) for k in range(kh * 4, kh * 4 + 4)], dma="out")
        S.emit(final_dma_keys=["out"])
    return nc


def _fm(v):
    return np.ascontiguousarray(v.reshape(KC, 128).T)


def _const_tables():
    f32 = np.float32
    NEGM = -30000.0
    j = np.arange(128)[:, None]
    t = np.arange(128)[None, :]
    tri = np.where(j <= t, 0.0, NEGM)
    anti = np.where(j > t, 0.0, NEGM)
    n = np.arange(128)[:, None]
    tt = np.arange(T)[None, :]
    cmpm = np.where((16 * n + 31 <= tt) & (n < 127), 0.0, NEGM)
    amask = np.concatenate([tri, anti, cmpm], axis=1).astype(f32)
    e128 = np.zeros((128, 16, 128), f32)
    for kt in range(16):
        for jj in range(128):
            e128[2 * kt + jj // 64, kt, jj] = 1.0
    cmap = np.zeros((128, 40), f32)
    c0 = np.arange(127)[:, None] * 16
    s0 = np.arange(32)[None, :] * 64
    ov = np.minimum(c0 + 32, s0 + 64) - np.maximum(c0, s0)
    cmap[:127, :32] = np.clip(ov, 0, None) / 32.0
    cmap[:127, 32] = 1.0
    force = np.zeros((128, 8, 32), f32)
    for q in range(8):
        tq = 128 * (q + 8) + np.arange(128)
        cur = tq // 64
        jb = np.arange(32)[None, :]
        forced = (jb == 0) | (jb == cur[:, None]) | (jb == cur[:, None] - 1)
        force[:, q, :] = np.where(forced, 1e4, np.where(jb > cur[:, None], -1e4, 0.0))
    gsel = np.zeros((48, 8, 3, 128), f32)
    for hp in range(8):
        for br in range(3):
            for m in range(128):
                gsel[(2 * hp + m // 64) * 3 + br, hp, br, m] = 1.0
    p = np.arange(128)
    bd = (p[:, None] // 64 == p[None, :] // 64).astype(f32)
    return {"amask": amask, "e128": e128.reshape(128, 2048), "cmap": cmap, "force": force.reshape(128, 256),
            "gsel": gsel.reshape(48, 3072), "bdones": bd}


def prep_inputs(inputs):
    f32 = np.float32
    g = {k: np.asarray(v, dtype=f32) for k, v in inputs.items()}
    vecs = np.zeros((128, NV, KC), f32)
    for i in range(2):
        for j in range(2):
            vecs[:, V_NG + 2 * i + j, :] = _fm(g["norm_gain"][i, j])
        for part in range(6):
            vecs[:, V_BADA + 6 * i + part, :] = _fm(g["b_ada"][i, part * D:(part + 1) * D])
    vecs[:, V_KVG, :] = _fm(g["kv_norm_gain"])
    for part in range(2):
        vecs[:, V_BKV + part, :] = _fm(g["b_ada_kv"][part * D:(part + 1) * D])
    for j in range(3):
        vecs[:, V_CONV + j, :] = _fm(g["conv_w"][0, j])
    consts = np.concatenate([np.eye(128, dtype=f32), np.ones((128, 128), f32)], axis=1)
    p64 = np.arange(128) % 64
    hvecs = np.zeros((128, 4), f32)
    hvecs[:, 0] = g["q_gain"][0, p64]
    for i in range(3):
        hvecs[:, 1 + i] = g["k_gain"][i, p64]
    peT = np.zeros((128, 64), f32)
    for kv in range(2):
        peT[:, kv * 32:(kv + 1) * 32] = g["cmp_pe"][kv][:, p64].T
    shared = {
        "vecs": vecs, "consts": consts, "hvecs": hvecs, "peT": peT,
        "w_ada": g["w_ada"], "w_a_in": g["w_a_in"], "w_a_out": g["w_a_out"],
        "w_mlp1": g["w_mlp1"], "w_mlp2": g["w_mlp2"],
        "w_ada_kv": g["w_ada_kv"], "w_kv": g["w_kv"], "w_qg": g["w_qg"], "w_o": g["w_o"],
        "cmp_w1": g["cmp_w1"], "cmp_w2": g["cmp_w2"],
    }
    shared.update(_const_tables())
    in_maps = []
    for b in range(N_CORES):
        m = dict(shared)
        xT = g["x"][b].T.reshape(KC, 128, T).transpose(1, 0, 2)
        m["xT"] = np.ascontiguousarray(xT)
        m["cT"] = _fm(g["c"][b])
        in_maps.append(m)
    return in_maps


def post_outputs(results):
    outs = []
    for r in results:
        yT = np.asarray(r["yT"])
        outs.append(yT.transpose(2, 1, 0).reshape(T, D))
    return np.stack(outs, axis=0).astype(np.float32)


def kernel(**inputs):
    in_maps = prep_inputs(inputs)
    nc = build_program(DEBUG_STOP)
    res = run_bass_kernel_spmd(nc, in_maps, core_ids=list(range(N_CORES)))
    return post_outputs(res.results)
```

```python
import numpy as np
from contextlib import ExitStack
import concourse.bass as bass
import concourse.mybir as mybir
from concourse.bass_utils import run_bass_kernel_spmd

F32 = mybir.dt.float32
BF16 = mybir.dt.bfloat16
AF = mybir.ActivationFunctionType
ALU = mybir.AluOpType

D = 1024
T = 2048
KC = 8
NTG = 4
TGW = 512
EPS = 1e-6
N_CORES = 8

V_NG = 0
V_KVG = 4
V_BADA = 5
V_BKV = 17
V_CONV = 19
NV = 22

DEBUG_STOP = None


class Tok:
    __slots__ = ("w", "rs", "rdma", "excl")

    def __init__(self, excl=False):
        self.w = None
        self.rs = {}
        self.rdma = []
        self.excl = excl


class Op:
    __slots__ = ("eng", "fn", "deps", "dma", "signal", "sigval", "dmaval", "dsem")

    def __init__(self, eng, fn, dma):
        self.eng = eng
        self.fn = fn
        self.dma = dma
        self.deps = ()
        self.signal = False
        self.sigval = 0
        self.dmaval = 0
        self.dsem = None


ENGS = ["pe", "act", "dve", "pool", "sp"]
DMA_SEMS = 16


class Sched:
    def __init__(self, nc):
        self.nc = nc
        self.ops = {e: [] for e in ENGS}
        self.dma_hist = {e: [] for e in ENGS}

    def op(self, eng, fn, reads=(), writes=(), dma=None):
        o = Op(eng, fn, dma)
        ex = [t for t in reads if t.excl]
        if ex:
            reads = [t for t in reads if not t.excl]
            writes = list(writes) + ex
        deps = set()
        for t in reads:
            if t.w is not None:
                deps.add(t.w)
        for t in writes:
            if t.w is not None:
                deps.add(t.w)
            deps.update(t.rs.values())
            deps.update(t.rdma)
        if dma is not None:
            deps = {d for d in deps if d.dma != dma}
            hist = self.dma_hist[eng]
            n = len(hist)
            o.dsem = (eng, n % DMA_SEMS)
            o.dmaval = 16 * (n // DMA_SEMS + 1)
            if n >= DMA_SEMS:
                deps.add(hist[n - DMA_SEMS])
            hist.append(o)
        o.deps = tuple(deps)
        for t in reads:
            if dma is not None:
                t.rdma.append(o)
            else:
                t.rs[eng] = o
        for t in writes:
            t.w = o
            t.rs = {}
            t.rdma = []
        self.ops[eng].append(o)
        return o

    def barrier(self, engines=("pe", "act", "dve", "sp", "pool")):
        lasts = []
        for e in ENGS:
            for o in reversed(self.ops[e]):
                if o.dma is None and o.fn is not None:
                    lasts.append(o)
                    break
            lasts += self.dma_hist[e][-DMA_SEMS:]
        for e in engines:
            o = Op(e, None, None)
            o.deps = tuple(lasts)
            self.ops[e].append(o)

    def emit(self, final_dma_keys=()):
        nc = self.nc
        for e in ENGS:
            for o in self.ops[e]:
                for d in o.deps:
                    if d.dma is None:
                        if d.eng == "pe" and o.eng == "pe" and o.dma is None and o.fn is not None:
                            continue
                        d.signal = True
        for e in ENGS:
            c = 0
            for o in self.ops[e]:
                if o.dma is None and o.signal:
                    c += 1
                    o.sigval = c
        with ExitStack() as es:
            esem = {e: es.enter_context(nc.semaphore("s_" + e)) for e in ENGS}
            dsem = {}
            for e in ENGS:
                for i in range(min(DMA_SEMS, len(self.dma_hist[e]))):
                    dsem[(e, i)] = es.enter_context(nc.semaphore("d_%s%d" % (e, i)))
            block = es.enter_context(nc.Block())

            def run(e, eng):
                waited = {}
                for o in self.ops[e]:
                    need = {}
                    for d in o.deps:
                        if d.dma is not None:
                            key, sem, val = ("d",) + d.dsem, dsem[d.dsem], d.dmaval
                        else:
                            if d.eng == "pe" and e == "pe" and o.dma is None and o.fn is not None:
                                continue
                            key, sem, val = ("e", d.eng), esem[d.eng], d.sigval
                        if val > need.get(key, (None, 0))[1]:
                            need[key] = (sem, val)
                    for key, (sem, val) in need.items():
                        if waited.get(key, 0) >= val:
                            continue
                        waited[key] = val
                        eng.wait_ge(sem, val)
                    if o.fn is None:
                        continue
                    ins = o.fn(eng)
                    if o.dma is not None:
                        ins.then_inc(dsem[o.dsem], 16)
                    elif o.signal:
                        ins.then_inc(esem[e], 1)
                if e == "sp":
                    fin = {}
                    for q in ENGS:
                        for d in self.dma_hist[q]:
                            if d.dma in final_dma_keys:
                                fin[d.dsem] = max(fin.get(d.dsem, 0), d.dmaval)
                    for k, v in fin.items():
                        if waited.get(("d",) + k, 0) < v:
                            eng.wait_ge(dsem[k], v)

            block.sync(lambda eng: run("sp", eng))
            block.scalar(lambda eng: run("act", eng))
            block.vector(lambda eng: run("dve", eng))
            block.gpsimd(lambda eng: run("pool", eng))
            block.tensor(lambda eng: run("pe", eng))


class StopBuild(Exception):
    pass


class TT:
    def __init__(self, ap):
        self.ap = ap
        self.toks = {}

    def t(self, *key):
        tk = self.toks.get(key)
        if tk is None:
            tk = self.toks[key] = Tok()
        return tk

    def all(self):
        return list(self.toks.values())


def build_program(stop=None):
    nc = bass.Bass("TRN2", target_bir_lowering=False)

    def din(name, shape):
        return nc.dram_tensor(name, list(shape), F32, kind="ExternalInput").ap()

    d_x = din("xT", [128, KC, T])
    d_c = din("cT", [128, KC])
    d_vecs = din("vecs", [128, NV, KC])
    d_consts = din("consts", [128, 256])
    d_w_ada = din("w_ada", [2, D, 6 * D])
    d_w_a_in = din("w_a_in", [1, D, 3 * D])
    d_w_a_out = din("w_a_out", [1, D, D])
    d_w_mlp1 = din("w_mlp1", [2, D, 4 * D])
    d_w_mlp2 = din("w_mlp2", [2, 4 * D, D])
    d_w_ada_kv = din("w_ada_kv", [D, 2 * D])
    d_w_kv = din("w_kv", [D, 1536])
    d_w_qg = din("w_qg", [1, D, 1072])
    d_w_o = din("w_o", [1, D, D])
    d_cmp_w1 = din("cmp_w1", [2, 2048, 256])
    d_cmp_w2 = din("cmp_w2", [2, 256, 64])
    d_hvecs = din("hvecs", [128, 4])
    d_peT = din("peT", [128, 64])
    d_amask = din("amask", [128, 256 + 2048])
    d_e128 = din("e128", [128, 2048])
    d_cmap = din("cmap", [128, 40])
    d_force = din("force", [128, 256])
    d_gsel = din("gsel", [48, 3072])
    d_bd = din("bdones", [128, 128])
    d_y = nc.dram_tensor("yT", [128, KC, T], F32, kind="ExternalOutput").ap()
    d_xs = nc.dram_tensor("xs_scratch", [128, KC, T], F32).ap()

    S = Sched(nc)
    es = ExitStack()
    with es:
        def sb(name, shape, dt):
            return es.enter_context(nc.sbuf_tensor(name, list(shape), dt))

        XR = sb("XR", [128, KC * T], F32)
        HR = sb("HR", [128, KC * T], BF16)
        AR = sb("AR", [128, KC * T], BF16)
        RING = [sb("RING%d" % i, [128, 8192], BF16) for i in range(2)]
        ring_t = [Tok(), Tok()]
        rstd_s = sb("rstd", [128, T], F32)
        TMPF = [sb("tmpf%d" % i, [128, TGW], F32) for i in range(3)]
        tmpf_t = [Tok() for _ in TMPF]
        TMPB = [sb("tmpb%d" % i, [128, TGW], BF16) for i in range(3)]
        tmpb_t = [Tok() for _ in TMPB]
        SCR = sb("SCR", [128, 11048], BF16)
        ones_b = sb("ones_b", [128, 128], BF16)
        vecs = sb("vecs_s", [128, NV, KC], F32)
        c_f = sb("c_f", [128, KC], F32)
        cact = sb("cact", [128, KC], BF16)
        modv = sb("modv", [128, 3, 48], F32)
        der = sb("der", [128, 8, KC], F32)
        hvecs = sb("hvecs_s", [128, 4], F32)
        gq = sb("gq", [128, 1], F32)
        ident_b = sb("ident_b", [128, 128], BF16)
        bd_ones = sb("bd_ones", [128, 128], BF16)
        cbias = sb("cbias", [128, 4], F32)
        kcT = sb("kcT", [128, 4, 128], BF16)
        vcA = sb("vcA", [128, 4, 128], BF16)
        rd4 = sb("rd4", [128, 4], F32)
        scA = sb("scA", [128, 32], F32)
        scB = sb("scB", [128, 32], F32)
        m8a = sb("m8a", [128, 8], F32)
        m8b = sb("m8b", [128, 8], F32)
        selm = sb("selm", [128, 32], BF16)
        ACC = [sb("acc%d" % i, [128, 256], F32) for i in range(2)]
        PS = [es.enter_context(nc.psum_tensor("ps%d" % i, [128, TGW], F32)) for i in range(8)]
        ps_t = [Tok(excl=True) for _ in PS]

        X = TT(XR[:].rearrange("p (k t) -> p k t", k=KC))
        H = TT(HR[:].rearrange("p (k t) -> p k t", k=KC))
        A = TT(AR[:].rearrange("p (k t) -> p k t", k=KC))
        t_const = Tok()
        eps_t = sb("eps_t", [128, 2], F32)
        t_eps = Tok()
        S.op("dve", lambda e: e.memset(eps_t[:, 0:1], EPS), writes=[t_eps])
        S.op("dve", lambda e: e.memset(eps_t[:, 1:2], 1e-30), writes=[t_eps])
        t_vecs = Tok()
        t_c = Tok()
        t_cact = Tok()
        t_modv = [Tok(), Tok(), Tok()]
        t_der = Tok()
        t_rstd = [Tok() for _ in range(NTG)]

        cnt = {"ps": 0, "ring": 0, "tf": 0, "tb": 0}

        POOLS = {"ALL": list(range(8)), "P6": [0, 1, 2, 3, 4, 5], "S": [0, 1, 2, 3], "O": [4, 5], "M": [6, 7]}
        pcnt = {k: 0 for k in POOLS}

        defpool = {"v": "ALL"}

        def next_ps(pool=None):
            pool = pool or defpool["v"]
            lst = POOLS[pool]
            i = lst[pcnt[pool] % len(lst)]
            pcnt[pool] += 1
            return PS[i], ps_t[i]

        def next_tf():
            i = cnt["tf"] % len(TMPF)
            cnt["tf"] += 1
            return TMPF[i], tmpf_t[i]

        def next_tb():
            i = cnt["tb"] % len(TMPB)
            cnt["tb"] += 1
            return TMPB[i], tmpb_t[i]

        def tsl(tg):
            return slice(tg * TGW, (tg + 1) * TGW)

        S.op("pool", lambda e: e.dma_start(out=ones_b[:], in_=d_consts[:, 128:256]), writes=[t_const], dma="cb")
        S.op("sp", lambda e: e.dma_start(out=vecs[:], in_=d_vecs), writes=[t_vecs], dma="c")
        S.op("sp", lambda e: e.dma_start(out=c_f[:], in_=d_c), writes=[t_c], dma="c")
        for tg in range(NTG):
            for kh in range(2):
                ks = slice(kh * 4, kh * 4 + 4)
                S.op("sp", (lambda e, tg=tg, ks=ks: e.dma_start(out=X.ap[:, ks, tsl(tg)], in_=d_x[:, ks, tsl(tg)])),
                     writes=[X.t(k, tg) for k in range(kh * 4, kh * 4 + 4)], dma="x")

        S.op("act", lambda e: e.activation(out=cact[:], in_=c_f[:], func=AF.Silu), reads=[t_c], writes=[t_cact])

        def load_w(src_aps, dst_view_fn):
            i = cnt["ring"] % 2
            cnt["ring"] += 1
            slot = RING[i]
            for dst_fn, src in src_aps:
                S.op("pool", (lambda e, dst_fn=dst_fn, src=src, slot=slot: e.dma_start(out=dst_fn(slot), in_=src)),
                     writes=[ring_t[i]], dma="r%d" % i)
            return dst_view_fn(slot), ring_t[i]

        def load_w_std(w2d, c0, ncols, k0=0):
            src = w2d.rearrange("(k p) n -> p k n", p=128)

            def view(slot):
                return slot[:, 0:8 * ncols].rearrange("p (k c) -> p k c", k=8)
            pieces = []
            for kh in range(2):
                ks = slice(kh * 4, kh * 4 + 4)
                pieces.append(((lambda slot, ks=ks: view(slot)[:, ks, :]), src[:, k0 + kh * 4:k0 + kh * 4 + 4, c0:c0 + ncols]))
            return load_w(pieces, view)

        def ada_matvec(w2d, ncb, bias_idx, mi):
            ps, pt = next_ps()
            for cb in range(ncb):
                wv, wt = load_w_std(w2d, cb * 1024, 1024)
                for oc in range(8):
                    col = cb * 8 + oc
                    for k in range(KC):
                        S.op("pe", (lambda e, ps=ps, wv=wv, oc=oc, k=k, col=col: e.matmul(
                            ps[:, col:col + 1], wv[:, k, oc * 128:(oc + 1) * 128], cact[:, k:k + 1],
                            start=(k == 0), stop=(k == KC - 1))), reads=[wt, t_cact], writes=[pt])
            n = ncb * 8
            S.op("dve", (lambda e, ps=ps, n=n: e.tensor_tensor(
                out=modv[:, mi, 0:n], in0=ps[:, 0:n],
                in1=vecs[:, bias_idx:bias_idx + ncb, :].rearrange("p a b -> p (a b)"), op=ALU.add)),
                reads=[pt, t_vecs], writes=[t_modv[mi]])

        def mod_part(mi, part):
            return modv[:, mi, part * 8:(part + 1) * 8]

        def derive(mi, part_sc, gain_idx, dst):
            S.op("dve", lambda e: e.tensor_scalar(out=der[:, dst, :], in0=mod_part(mi, part_sc), scalar1=1.0, scalar2=1.0,
                                                  op0=ALU.add, op1=ALU.mult), reads=[t_modv[mi]], writes=[t_der])
            S.op("dve", lambda e: e.tensor_tensor(out=der[:, dst, :], in0=der[:, dst, :], in1=vecs[:, gain_idx, :], op=ALU.mult),
                 reads=[t_der, t_vecs], writes=[t_der])

        def compute_rstd():
            for tg in range(NTG):
                ps, pt = next_ps()
                for k in range(KC):
                    tb, tbt = next_tb()
                    S.op("act", (lambda e, tb=tb, k=k, tg=tg: e.activation(out=tb[:], in_=X.ap[:, k, tsl(tg)], func=AF.Square)),
                         reads=[X.t(k, tg)], writes=[tbt])
                    S.op("pe", (lambda e, ps=ps, tb=tb, k=k: e.matmul(ps[:], ones_b[:], tb[:], start=(k == 0), stop=(k == KC - 1))),
                         reads=[tbt, t_const], writes=[pt])
                tf, tft = next_tf()
                S.op("act", (lambda e, ps=ps, tf=tf: e.activation(out=tf[:], in_=ps[:], func=AF.Ln, bias=eps_t[:, 0:1], scale=1.0 / D)),
                     reads=[pt, t_eps], writes=[tft])
                S.op("act", (lambda e, tf=tf, tg=tg: e.activation(out=rstd_s[:, tsl(tg)], in_=tf[:], func=AF.Exp, scale=-0.5)), reads=[tft], writes=[t_rstd[tg]])

        def norm_mod(dst, a_ap, b_ap, extra_reads):
            for tg in range(NTG):
                for k in range(KC):
                    tf, tft = next_tf()
                    S.op("dve", (lambda e, tf=tf, k=k, tg=tg: e.tensor_tensor(out=tf[:], in0=X.ap[:, k, tsl(tg)], in1=rstd_s[:, tsl(tg)], op=ALU.mult)),
                         reads=[X.t(k, tg), t_rstd[tg]], writes=[tft])
                    S.op("act", (lambda e, tf=tf, k=k, tg=tg: e.activation(out=dst.ap[:, k, tsl(tg)], in_=tf[:], func=AF.Identity,
                                                                            bias=b_ap[:, k:k + 1], scale=a_ap[:, k:k + 1])),
                         reads=[tft] + extra_reads, writes=[dst.t(k, tg)])

        def proj(wv, wt, src, n_oc, evac, tgs=range(NTG), oc_cols=None):
            for tg in tgs:
                for oc in range(n_oc):
                    ps, pt = next_ps()
                    for k in range(KC):
                        lhs = wv[:, k, oc * 128:(oc + 1) * 128] if oc_cols is None else oc_cols(wv, k, oc)
                        S.op("pe", (lambda e, ps=ps, lhs=lhs, k=k, tg=tg: e.matmul(ps[:], lhs, src.ap[:, k, tsl(tg)],
                                                                                   start=(k == 0), stop=(k == KC - 1))),
                             reads=[wt, src.t(k, tg)], writes=[pt])
                    evac(oc, tg, ps, pt)

        def resid_evac(g_ap, extra_reads):
            def ev(oc, tg, ps, pt):
                S.op("dve", (lambda e: e.scalar_tensor_tensor(out=X.ap[:, oc, tsl(tg)], in0=ps[:], scalar=g_ap[:, oc:oc + 1],
                                                              in1=X.ap[:, oc, tsl(tg)], op0=ALU.mult, op1=ALU.add)),
                     reads=[pt, X.t(oc, tg)] + extra_reads, writes=[X.t(oc, tg)])
            return ev

        def mlp(layer, mi):
            compute_rstd()
            derive(mi, 4, V_NG + 2 * layer + 1, 1)
            norm_mod(H, der[:, 1, :], mod_part(mi, 3), [t_der, t_modv[mi]])
            g2 = mod_part(mi, 5)
            for hb in range(4):
                wv, wt = load_w_std(d_w_mlp1[layer], hb * 1024, 1024)

                def ev1(oc, tg, ps, pt):
                    tf, tft = next_tf()
                    S.op("act", (lambda e: e.activation(out=tf[:], in_=ps[:], func=AF.Relu)), reads=[pt], writes=[tft])
                    S.op("dve", (lambda e: e.tensor_tensor(out=A.ap[:, oc, tsl(tg)], in0=tf[:], in1=tf[:], op=ALU.mult)),
                         reads=[tft], writes=[A.t(oc, tg)])
                proj(wv, wt, H, 8, ev1)
                wv2, wt2 = load_w_std(d_w_mlp2[layer], 0, 1024, k0=hb * 8)
                proj(wv2, wt2, A, 8, resid_evac(g2, [t_modv[mi]]))

        ada_matvec(d_w_ada[0], 6, V_BADA, 0)
        compute_rstd()
        derive(0, 1, V_NG + 0, 0)
        norm_mod(H, der[:, 0, :], mod_part(0, 0), [t_der, t_modv[0]])

        gbv = SCR[:, 0:4096].rearrange("p (j t) -> p j t", j=2)
        vv = SCR[:, 4096:4096 + 2 * 2056].rearrange("p (j t) -> p j t", j=2)
        t_gb = [[Tok() for _ in range(NTG)] for _ in range(2)]
        t_v = [[Tok() for _ in range(NTG)] for _ in range(2)]
        t_vhalo = Tok()
        S.op("dve", lambda e: e.memset(vv[:, :, 0:2], 0.0), writes=[t_vhalo])
        w_in_v = d_w_a_in[0].rearrange("(k p) (s c) -> p k s c", p=128, s=3)
        g1 = mod_part(0, 2)
        def mixer_j(wv, wt, jj, j):
            jb = j % 2
            for tg in range(NTG):
                pss = []
                for s in range(3):
                    ps, pt = next_ps()
                    for k in range(KC):
                        S.op("pe", (lambda e, ps=ps, s=s, k=k, tg=tg: e.matmul(ps[:], wv[:, k, s, jj * 128:(jj + 1) * 128], H.ap[:, k, tsl(tg)],
                                                                               start=(k == 0), stop=(k == KC - 1))),
                             reads=[wt, H.t(k, tg)], writes=[pt])
                    pss.append((ps, pt))
                (psb, ptb), (psc, ptc), (psu, ptu) = pss
                S.op("act", (lambda e, psb=psb, tg=tg: e.activation(out=gbv[:, jb, tsl(tg)], in_=psb[:], func=AF.Copy)),
                     reads=[ptb], writes=[t_gb[jb][tg]])
                tb, tbt = next_tb()
                S.op("act", (lambda e, psc=psc, tb=tb: e.activation(out=tb[:], in_=psc[:], func=AF.Copy)), reads=[ptc], writes=[tbt])
                S.op("dve", (lambda e, psu=psu, tb=tb, tg=tg: e.tensor_tensor(out=vv[:, jb, 2 + tg * TGW:2 + (tg + 1) * TGW], in0=psu[:], in1=tb[:], op=ALU.mult)),
                     reads=[ptu, tbt], writes=[t_v[jb][tg]])
            for tg in range(NTG):
                tf, tft = next_tf()
                rd = [t_v[jb][tg], t_vhalo, t_vecs] + ([t_v[jb][tg - 1]] if tg > 0 else [])
                b0 = tg * TGW
                S.op("dve", (lambda e, tf=tf, b0=b0: e.tensor_scalar(out=tf[:], in0=vv[:, jb, b0 + 2:b0 + 2 + TGW], scalar1=vecs[:, V_CONV + 2, j:j + 1], scalar2=None, op0=ALU.mult)),
                     reads=rd, writes=[tft])
                S.op("dve", (lambda e, tf=tf, b0=b0: e.scalar_tensor_tensor(out=tf[:], in0=vv[:, jb, b0 + 1:b0 + 1 + TGW], scalar=vecs[:, V_CONV + 1, j:j + 1], in1=tf[:], op0=ALU.mult, op1=ALU.add)),
                     reads=rd + [tft], writes=[tft])
                S.op("dve", (lambda e, tf=tf, b0=b0: e.scalar_tensor_tensor(out=tf[:], in0=vv[:, jb, b0:b0 + TGW], scalar=vecs[:, V_CONV + 0, j:j + 1], in1=tf[:], op0=ALU.mult, op1=ALU.add)),
                     reads=rd + [tft], writes=[tft])
                S.op("dve", (lambda e, tf=tf, tg=tg: e.tensor_tensor(out=A.ap[:, j, tsl(tg)], in0=tf[:], in1=gbv[:, jb, tsl(tg)], op=ALU.mult)),
                     reads=[tft, t_gb[jb][tg]], writes=[A.t(j, tg)])

        for jp in range(4):
            def view(slot):
                return slot[:, 0:6144].rearrange("p (k s c) -> p k s c", k=8, s=3)
            pieces = []
            for s3 in range(3):
                pieces.append(((lambda slot, s3=s3: view(slot)[:, :, s3, :]), w_in_v[:, :, s3, jp * 256:(jp + 1) * 256]))
            wv, wt = load_w(pieces, view)
            for jj in range(2):
                mixer_j(wv, wt, jj, 2 * jp + jj)
        wv, wt = load_w_std(d_w_a_out[0], 0, 1024)
        proj(wv, wt, A, 8, resid_evac(g1, [t_modv[0]]))
        if stop != "mix0":
            mlp(0, 0)
        def chk(name, dumps):
            if stop != name:
                return
            S.barrier()
            for ap, dst in dumps:
                S.op("pool", (lambda e, ap=ap, dst=dst: e.dma_start(out=dst, in_=ap)), dma="out")
            raise StopBuild()

        if stop not in ("mix0", "l0"):
          try:
            G4 = 4
            defpool["v"] = "P6"
            t_l1c = Tok()
            wqg_g = SCR[:, 8192:8576].rearrange("p (k c) -> p k c", k=8)
            cw2k = SCR[:, 8576:8832].rearrange("p (c d) -> p c d", c=2)
            cw2v = SCR[:, 8832:8960].rearrange("p (c d) -> p c d", c=2)
            peT_b = SCR[:, 8960:9024]
            S.op("sp", lambda e: e.dma_start(out=hvecs[:], in_=d_hvecs), writes=[t_l1c], dma="c")
            S.op("pool", lambda e: e.dma_start(out=ident_b[:], in_=d_consts[:, 0:128]), writes=[t_l1c], dma="cb")
            S.op("pool", lambda e: e.dma_start(out=bd_ones[:], in_=d_bd), writes=[t_l1c], dma="cb")
            S.op("pool", lambda e: e.dma_start(out=peT_b, in_=d_peT), writes=[t_l1c], dma="cb")
            S.op("pool", lambda e: e.dma_start(out=cw2k[:, :, 0:64], in_=d_cmp_w2[0].rearrange("(c p) d -> p c d", p=128)), writes=[t_l1c], dma="cb")
            S.op("pool", lambda e: e.dma_start(out=cw2k[:, :, 64:128], in_=d_cmp_w2[0].rearrange("(c p) d -> p c d", p=128)), writes=[t_l1c], dma="cb")
            S.op("pool", lambda e: e.dma_start(out=cw2v, in_=d_cmp_w2[1].rearrange("(c p) d -> p c d", p=128)), writes=[t_l1c], dma="cb")
            S.op("pool", lambda e: e.dma_start(out=wqg_g, in_=d_w_qg[0].rearrange("(k p) n -> p k n", p=128)[:, :, 1024:1072]), writes=[t_l1c], dma="cb")
            t_gq = Tok()
            S.op("dve", lambda e: e.tensor_scalar(out=gq[:], in0=hvecs[:, 0:1], scalar1=0.125, scalar2=None, op0=ALU.mult), reads=[t_l1c], writes=[t_gq])

            ada_matvec(d_w_ada[1], 6, V_BADA + 6, 1)
            ada_matvec(d_w_ada_kv, 2, V_BKV, 2)
            compute_rstd()
            derive(2, 1, V_KVG, 2)
            derive(1, 1, V_NG + 2, 3)
            norm_mod(H, der[:, 2, :], mod_part(2, 0), [t_der, t_modv[2]])
            norm_mod(A, der[:, 3, :], mod_part(1, 0), [t_der, t_modv[1]])
            xs_t = [Tok() for _ in range(NTG)]
            for tg in range(NTG):
                for kh in range(2):
                    ks = slice(kh * 4, kh * 4 + 4)
                    S.op("sp", (lambda e, tg=tg, ks=ks: e.dma_start(out=d_xs[:, ks, tsl(tg)], in_=X.ap[:, ks, tsl(tg)])),
                         reads=[X.t(k, tg) for k in range(kh * 4, kh * 4 + 4)], writes=[xs_t[tg]], dma="xs")

            def head_norm(ps, pt, ncol, gain_ap, gain_reads, dst_ap, dst_toks):
                tb, tbt = next_tb()
                S.op("act", (lambda e: e.activation(out=tb[:, 0:ncol], in_=ps[:, 0:ncol], func=AF.Square)), reads=[pt], writes=[tbt])
                ps2, pt2 = next_ps("M")
                S.op("pe", (lambda e: e.matmul(ps2[:, 0:ncol], bd_ones[:], tb[:, 0:ncol], start=True, stop=True)), reads=[tbt, t_l1c], writes=[pt2])
                tf, tft = next_tf()
                S.op("act", (lambda e: e.activation(out=tf[:, 0:ncol], in_=ps2[:, 0:ncol], func=AF.Ln, bias=eps_t[:, 0:1], scale=1.0 / 64)),
                     reads=[pt2, t_eps], writes=[tft])
                tf2, tft2 = next_tf()
                S.op("act", (lambda e: e.activation(out=tf2[:, 0:ncol], in_=tf[:, 0:ncol], func=AF.Exp, scale=-0.5)), reads=[tft], writes=[tft2])
                S.op("dve", (lambda e: e.scalar_tensor_tensor(out=dst_ap, in0=ps[:, 0:ncol], scalar=gain_ap, in1=tf2[:, 0:ncol], op0=ALU.mult, op1=ALU.mult)),
                     reads=[pt, tft2] + gain_reads, writes=dst_toks)

            RAW = TT(SCR[:, 0:8192].rearrange("p (c t) -> p c t", c=4))
            wv, wt = load_w_std(d_w_kv, 0, 512)

            def ev_raw(oc, tg, ps, pt):
                S.op("act", (lambda e: e.activation(out=RAW.ap[:, oc, tsl(tg)], in_=ps[:], func=AF.Copy)), reads=[pt], writes=[RAW.t(oc, tg)])
            proj(wv, wt, H, 4, ev_raw)

            chk("raw", [(RAW.ap, d_y[:, 0:4, :])])
            t_kc = [Tok() for _ in range(G4)]
            t_vc = [Tok() for _ in range(G4)]
            t_vc_ones = Tok()
            S.op("dve", lambda e: e.memset(vcA[:, :, 64:128], 1.0), writes=[t_vc_ones])
            t_cb = Tok()
            for kv in range(2):
                def view1(slot):
                    return slot[:, 0:8192].rearrange("p (l h) -> p l h", l=32)
                src1 = d_cmp_w1[kv].rearrange("(l d) h -> d l h", d=64)
                pieces = [((lambda slot: view1(slot)[0:64, :, :]), src1), ((lambda slot: view1(slot)[64:128, :, :]), src1)]
                cwv, cwt = load_w(pieces, view1)
                psb, ptb = next_ps()
                for hc in range(2):
                    for l in range(32):
                        S.op("pe", (lambda e, hc=hc, l=l, cwv=cwv, psb=psb, kv=kv: e.matmul(
                            psb[:, hc:hc + 1], cwv[0:64, l, hc * 128:(hc + 1) * 128], peT_b[0:64, kv * 32 + l:kv * 32 + l + 1],
                            start=(l == 0), stop=(l == 31))), reads=[cwt, t_l1c], writes=[ptb])
                S.op("dve", (lambda e, psb=psb, kv=kv: e.tensor_copy(out=cbias[:, 2 * kv:2 * kv + 2], in_=psb[:, 0:2])), reads=[ptb], writes=[t_cb])
                for g in range(G4):
                    base = (g % 2) * 64
                    c = kv * 2 + g // 2
                    hids = []
                    for hc in range(2):
                        ps, pt = next_ps()
                        for l in range(32):
                            S.op("pe", (lambda e, ps=ps, l=l, hc=hc, cwv=cwv, base=base, c=c: e.matmul(
                                ps[:, 0:127], cwv[base:base + 64, l, hc * 128:(hc + 1) * 128],
                                RAW.ap[base:base + 64, c, l:l + 16 * 126 + 1:16], start=(l == 0), stop=(l == 31))),
                                reads=[cwt] + [RAW.t(c, tg) for tg in range(NTG)], writes=[pt])
                        z, zt = next_tf()
                        S.op("act", (lambda e, ps=ps, z=z, hc=hc, kv=kv: e.activation(out=z[:, 0:127], in_=ps[:, 0:127], func=AF.Identity,
                                                                                      bias=cbias[:, 2 * kv + hc:2 * kv + hc + 1], scale=1.0)),
                             reads=[pt, t_cb], writes=[zt])
                        u, ut = next_tf()
                        S.op("dve", (lambda e, z=z, u=u: e.tensor_tensor(out=u[:, 0:127], in0=z[:, 0:127], in1=z[:, 0:127], op=ALU.mult)), reads=[zt], writes=[ut])
                        S.op("dve", (lambda e, u=u: e.tensor_scalar(out=u[:, 0:127], in0=u[:, 0:127], scalar1=0.044715, scalar2=1.0, op0=ALU.mult, op1=ALU.add)),
                             reads=[ut], writes=[ut])
                        S.op("dve", (lambda e, z=z, u=u: e.tensor_tensor(out=u[:, 0:127], in0=u[:, 0:127], in1=z[:, 0:127], op=ALU.mult)), reads=[ut, zt], writes=[ut])
                        S.op("act", (lambda e, u=u: e.activation(out=u[:, 0:127], in_=u[:, 0:127], func=AF.Sigmoid, scale=1.5957691216057308)), reads=[ut], writes=[ut])
                        hb_, hbt = next_tb()
                        S.op("dve", (lambda e, z=z, u=u, hb_=hb_: e.tensor_tensor(out=hb_[:, 0:127], in0=u[:, 0:127], in1=z[:, 0:127], op=ALU.mult)), reads=[ut, zt], writes=[hbt])
                        hids.append((hb_, hbt))
                    if kv == 0:
                        ps, pt = next_ps()
                        for hc in range(2):
                            S.op("pe", (lambda e, ps=ps, hc=hc, hb_=hids[hc][0]: e.matmul(ps[:, 0:127], cw2k[:, hc, :], hb_[:, 0:127], start=(hc == 0), stop=(hc == 1))),
                                 reads=[hids[hc][1], t_l1c], writes=[pt])
                        head_norm(ps, pt, 127, hvecs[:, 1:2], [t_l1c], kcT[:, g, 0:127], [t_kc[g]])
                    else:
                        ps, pt = next_ps()
                        for hc in range(2):
                            S.op("pe", (lambda e, ps=ps, hc=hc, hb_=hids[hc][0]: e.matmul(ps[0:127, 0:64], hb_[:, 0:127], cw2v[:, hc, :], start=(hc == 0), stop=(hc == 1))),
                                 reads=[hids[hc][1], t_l1c], writes=[pt])
                        S.op("act", (lambda e, ps=ps, g=g: e.activation(out=vcA[0:127, g, 0:64], in_=ps[0:127, 0:64], func=AF.Copy)), reads=[pt], writes=[t_vc[g]])

            chk("cmp", [(kcT[:].rearrange("p g n -> p (g n)"), d_y[:, 0, 0:512]), (vcA[:].rearrange("p g n -> p (g n)"), d_y[:, 1, 0:512])])
            S.barrier()
            XB = XR[:].bitcast(BF16)
            QT = TT(XB[:, 0:16384].rearrange("p (h t) -> p h t", h=8))
            KST = TT(XB[:, 16384:24576].rearrange("p (g t) -> p g t", g=4))
            KWT = TT(XB[:, 24576:32768].rearrange("p (g t) -> p g t", g=4))
            RB = rstd_s[:].bitcast(BF16)
            SIGG = TT(RB[0:48, 0:T])

            wv, wt = load_w_std(d_w_qg[0], 0, 1024)

            def ev_q(oc, tg, ps, pt):
                head_norm(ps, pt, TGW, gq[:, 0:1], [t_gq], QT.ap[:, oc, tsl(tg)], [QT.t(oc, tg)])
            proj(wv, wt, A, 8, ev_q)
            for tg in range(NTG):
                ps, pt = next_ps()
                for k in range(KC):
                    S.op("pe", (lambda e, ps=ps, k=k, tg=tg: e.matmul(ps[0:48, :], wqg_g[:, k, :], A.ap[:, k, tsl(tg)], start=(k == 0), stop=(k == KC - 1))),
                         reads=[t_l1c, A.t(k, tg)], writes=[pt])
                S.op("act", (lambda e, ps=ps, tg=tg: e.activation(out=SIGG.ap[:, tsl(tg)], in_=ps[0:48, :], func=AF.Sigmoid)), reads=[pt], writes=[SIGG.t(tg)])

            for typ, dst, gi in ((2, KST, 2), (4, KWT, 3)):
                def viewk(slot):
                    return slot[:, 0:4096].rearrange("p (k g r d) -> p k g r d", k=8, g=4, r=2)
                srck = d_w_kv.rearrange("(k p) n -> p k n", p=128)[:, :, typ * 256:(typ + 1) * 256].rearrange("p k (g d) -> p k g d", g=4)
                pieces = [((lambda slot, r=r, gg=gg: viewk(slot)[:, :, gg, r, :]), srck[:, :, gg, :]) for r in range(2) for gg in range(4)]
                kv_, kt_ = load_w(pieces, viewk)

                def ev_k(oc, tg, ps, pt, dst=dst, gi=gi):
                    head_norm(ps, pt, TGW, hvecs[:, gi:gi + 1], [t_l1c], dst.ap[:, oc, tsl(tg)], [dst.t(oc, tg)])
                proj(kv_, kt_, H, 4, ev_k, oc_cols=(lambda wv_, k, oc: wv_[:, k, oc, :, :].rearrange("p r d -> p (r d)")))
            chk("q", [(QT.ap, d_y)])
            chk("k", [(KST.ap, d_y[:, 0:4, :]), (KWT.ap, d_y[:, 4:8, :]), ])
            S.barrier()
            AB = AR[:]
            VS = TT(AB[:, 0:8192].rearrange("p (t g c) -> p t g c", t=16, g=4))
            VW = TT(AB[:, 8192:16384].rearrange("p (t g c) -> p t g c", t=16, g=4))
            t_vones = Tok()
            S.op("dve", lambda e: e.memset(VS.ap[:, :, :, 64:128], 1.0), writes=[t_vones])
            S.op("dve", lambda e: e.memset(VW.ap[:, :, :, 64:128], 1.0), writes=[t_vones])

            def viewv(slot):
                return slot[:, 0:4096].rearrange("p (k s c) -> p k s c", k=8, s=2)
            srcv = d_w_kv.rearrange("(k p) n -> p k n", p=128)
            pieces = [((lambda slot: viewv(slot)[:, :, 0, :]), srcv[:, :, 768:1024]), ((lambda slot: viewv(slot)[:, :, 1, :]), srcv[:, :, 1280:1536])]
            vv_, vt_ = load_w(pieces, viewv)
            for tt in range(16):
                ps, pt = next_ps()
                tg = tt // 4
                for k in range(KC):
                    S.op("pe", (lambda e, ps=ps, k=k, tt=tt: e.matmul(ps[:], H.ap[:, k, tt * 128:(tt + 1) * 128], vv_[:, k, :, :].rearrange("p s c -> p (s c)"),
                                                                    start=(k == 0), stop=(k == KC - 1))), reads=[vt_, H.t(k, tg)], writes=[pt])
                S.op("act", (lambda e, ps=ps, tt=tt: e.activation(out=VS.ap[:, tt, :, 0:64], in_=ps[:, 0:256].rearrange("p (g d) -> p g d", g=4), func=AF.Copy)),
                     reads=[pt, t_vones], writes=[VS.t(tt)])
                S.op("dve", (lambda e, ps=ps, tt=tt: e.tensor_copy(out=VW.ap[:, tt, :, 0:64], in_=ps[:, 256:512].rearrange("p (g d) -> p g d", g=4))),
                     reads=[pt, t_vones], writes=[VW.t(tt)])
            chk("v", [(AB[:, 0:8192], d_y[:, 0:4, :].rearrange("p a b -> p (a b)")), (AB[:, 8192:16384], d_y[:, 4:8, :].rearrange("p a b -> p (a b)"))])
            S.barrier()

            t_ac = Tok()
            tri_b = SCR[:, 0:128]
            anti_b = SCR[:, 128:256]
            cmpm = SCR[:, 256:2304]
            e128 = SCR[:, 2304:4352].rearrange("p (k j) -> p k j", k=16)
            cmap = SCR[:, 4352:4392]
            force = SCR[:, 4392:4904].bitcast(F32).rearrange("p (q j) -> p q j", q=8)
            gsel = SCR[0:48, 4904:7976].rearrange("p (h b m) -> p h b m", h=8, b=3)
            PTS = [SCR[:, 7976 + i * 1024:7976 + (i + 1) * 1024].rearrange("p (k r c) -> p k r c", k=2, r=2) for i in range(3)]
            pts_t = [Tok() for _ in PTS]
            S.op("pool", lambda e: e.dma_start(out=SCR[:, 0:2304], in_=d_amask), writes=[t_ac], dma="cb")
            S.op("pool", lambda e: e.dma_start(out=SCR[:, 2304:4352], in_=d_e128), writes=[t_ac], dma="cb")
            S.op("pool", lambda e: e.dma_start(out=cmap, in_=d_cmap), writes=[t_ac], dma="cb")
            S.op("sp", lambda e: e.dma_start(out=SCR[:, 4392:4904].bitcast(F32), in_=d_force), writes=[t_ac], dma="c")
            S.op("pool", lambda e: e.dma_start(out=SCR[0:48, 4904:7976], in_=d_gsel), writes=[t_ac], dma="cb")
            HB = HR[:]
            OTG = TT(HB[:, 0:4096].rearrange("p (h t) -> p h t", h=8))
            GBC = HB[:, 4096:7168].rearrange("p (h b t) -> p h b t", h=2, b=3)
            t_gbc = Tok()
            XST = TT(HB[:, 7168:15360].bitcast(F32).rearrange("p (k t) -> p k t", k=8))
            BIAS = [HB[:, 15360 + i * 256:15360 + (i + 1) * 256].rearrange("p (r q) -> p r q", r=2) for i in range(2)]
            bias_t = [Tok(), Tok()]
            for i in range(2):
                S.op("dve", (lambda e, i=i: e.memset(BIAS[i], 0.0)), writes=[bias_t[i]])
            wo_v, wo_t = load_w_std(d_w_o[0], 0, 1024)
            ptc = {"n": 0, "b": 0, "a": 0}

            def next_pt():
                i = ptc["n"] % 3
                ptc["n"] += 1
                return PTS[i], pts_t[i]

            def qk_tile(bank, bt, colbase, KT, g, kt0, nk, par, qt, mask, bias_i):
                lhs_k = KT.ap[par * 64:(par + 1) * 64, g, kt0:kt0 + nk] if KT is not None else kcT[par * 64:(par + 1) * 64, g, 0:127]
                k_reads = [KT.t(g, kt0 // TGW)] if KT is not None else [t_kc[g]]
                qsl = slice(qt * 128, (qt + 1) * 128)
                q_reads = [QT.t(2 * g, qt // 4), QT.t(2 * g + 1, qt // 4)]
                if mask is None:
                    S.op("pe", (lambda e: e.matmul(bank[0:nk, colbase:colbase + 256], lhs_k, QT.ap[par * 64:(par + 1) * 64, 2 * g:2 * g + 2, qsl], start=True, stop=True)),
                         reads=k_reads + q_reads, writes=[bt])
                elif mask == "bias":
                    kt = kt0 // 128
                    S.op("pe", (lambda e: e.matmul(bank[:, colbase:colbase + 256], e128[:, kt, :], BIAS[bias_i].rearrange("p r q -> p (r q)"), start=True, stop=False)),
                         reads=[t_ac, bias_t[bias_i]], writes=[bt])
                    S.op("pe", (lambda e: e.matmul(bank[:, colbase:colbase + 256], lhs_k, QT.ap[par * 64:(par + 1) * 64, 2 * g:2 * g + 2, qsl], start=False, stop=True)),
                         reads=k_reads + q_reads, writes=[bt])
                else:
                    for hpl in range(2):
                        cs = slice(colbase + hpl * 128, colbase + (hpl + 1) * 128)
                        S.op("pe", (lambda e, cs=cs: e.matmul(bank[0:nk, cs], ident_b[:, 0:nk], mask, start=True, stop=False)), reads=[t_ac, t_l1c], writes=[bt])
                        S.op("pe", (lambda e, cs=cs, hpl=hpl: e.matmul(bank[0:nk, cs], lhs_k, QT.ap[par * 64:(par + 1) * 64, 2 * g + hpl, qsl], start=False, stop=True)),
                             reads=k_reads + q_reads, writes=[bt])

            def branch_steps(g, qt, kind, bias_i, pos, br, acc, acct):
                ps_o, pt_o = next_ps("O")
                out = []
                if kind == "cmp":
                    bA, tA = next_ps("S")
                    bB, tB = next_ps("S")
                    PT, ptt = next_pt()

                    def qk():
                        for par, bank, bt in ((0, bA, tA), (1, bB, tB)):
                            qk_tile(bank, bt, 0, None, g, 0, 127, par, qt, cmpm[:, qt * 128:(qt + 1) * 128], None)
                            S.op("act", (lambda e, par=par, bank=bank: e.activation(out=PT[0:127, 0, par, :], in_=bank[0:127, 0:256], func=AF.Exp)),
                                 reads=[bt], writes=[ptt])
                        if qt >= 8:
                            topk_bias(g, qt, PT, ptt, bias_i)

                    def pv():
                        S.op("pe", (lambda e: e.matmul(ps_o[:], vcA[0:127, g, :], PT[0:127, 0, :, :].rearrange("p r c -> p (r c)"), start=True, stop=True)),
                             reads=[ptt, t_vc[g], t_vc_ones], writes=[pt_o])
                        combine(g, qt, pos, br, ps_o, pt_o, acc, acct)
                    return [(qk, pv)]
                KT, V = (KST, VS) if kind == "sel" else (KWT, VW)
                kts = list(range(0, qt + 1)) if kind == "sel" else list(range(max(0, qt - 4), qt + 1))
                for pi in range(0, len(kts), 2):
                    pair = kts[pi:pi + 2]
                    banks = (next_ps("S"), next_ps("S"))
                    PTp = next_pt()

                    def qk(pair=pair, banks=banks, PTp=PTp):
                        PT, ptt = PTp
                        npair = len(pair)
                        for par, (bank, bt) in enumerate(banks):
                            for ktp, kt in enumerate(pair):
                                if kt == qt:
                                    mask = tri_b
                                elif kind == "win" and kt == qt - 4:
                                    mask = anti_b
                                elif kind == "sel" and qt >= 8:
                                    mask = "bias"
                                else:
                                    mask = None
                                qk_tile(bank, bt, ktp * 256, KT, g, kt * 128, 128, par, qt, mask, bias_i)
                            S.op("act", (lambda e, par=par, bank=bank: e.activation(
                                out=PT[:, 0:npair, par, :], in_=bank[:, 0:npair * 256].rearrange("p (k c) -> p k c", k=npair), func=AF.Exp)),
                                reads=[bt], writes=[ptt])

                    def pv(pair=pair, PTp=PTp, last=(pi + 2 >= len(kts))):
                        PT, ptt = PTp
                        for ktp, kt in enumerate(pair):
                            S.op("pe", (lambda e, ktp=ktp, kt=kt: e.matmul(ps_o[:], V.ap[:, kt, g, :], PT[:, ktp, :, :].rearrange("p r c -> p (r c)"),
                                                                         start=(kt == kts[0]), stop=(kt == kts[-1]))),
                                 reads=[ptt, V.t(kt), t_vones], writes=[pt_o])
                        if last:
                            combine(g, qt, pos, br, ps_o, pt_o, acc, acct)
                    out.append((qk, pv))
                return out

            def combine(g, qt, pos, br, ps_o, pt_o, acc, acct):
                qi = qt % 4
                tf, tft = next_tf()
                S.op("act", (lambda e: e.activation(out=tf[64:128, :], in_=ps_o[64:128, :], func=AF.Ln, bias=eps_t[64:128, 1:2], scale=1.0)), reads=[pt_o, t_eps], writes=[tft])
                tf2, tft2 = next_tf()
                S.op("act", (lambda e: e.activation(out=tf2[64:128, :], in_=tf[64:128, :], func=AF.Exp, scale=-1.0)), reads=[tft], writes=[tft2])
                on, ont = next_tf()
                for par in range(2):
                    S.op("dve", (lambda e, par=par: e.tensor_tensor(out=on[par * 64:(par + 1) * 64, 0:256], in0=ps_o[0:64, par * 256:(par + 1) * 256],
                                                                     in1=tf2[64:128, par * 256:(par + 1) * 256], op=ALU.mult)), reads=[pt_o, tft2], writes=[ont])
                onv = on[:, 0:256].rearrange("p (h q) -> p h q", h=2)
                accv = acc[:].rearrange("p (h q) -> p h q", h=2)
                Gv = GBC[:, :, br, qi * 128:(qi + 1) * 128]
                if pos == 0:
                    S.op("dve", (lambda e: e.tensor_tensor(out=accv, in0=onv, in1=Gv, op=ALU.mult)), reads=[ont, t_gbc], writes=[acct])
                else:
                    S.op("dve", (lambda e: e.tensor_tensor(out=onv, in0=onv, in1=Gv, op=ALU.mult)), reads=[ont, t_gbc], writes=[ont])
                    if pos == 1:
                        S.op("dve", (lambda e: e.tensor_tensor(out=accv, in0=accv, in1=onv, op=ALU.add)), reads=[ont, acct], writes=[acct])
                    else:
                        S.op("dve", (lambda e: e.tensor_tensor(out=OTG.ap[:, 2 * g:2 * g + 2, qi * 128:(qi + 1) * 128], in0=accv, in1=onv, op=ALU.add)),
                             reads=[ont, acct], writes=[OTG.t(2 * g, 0), OTG.t(2 * g + 1, 0)])

            def topk_bias(g, qt, PT, ptt, bias_i):
                ps_i, pt_i = next_ps("M")
                for c in range(4):
                    S.op("pe", (lambda e, c=c: e.matmul(ps_i[:, c * 33:(c + 1) * 33], PT[0:127, 0, c // 2, (c % 2) * 128:(c % 2 + 1) * 128], cmap[0:127, 0:33],
                                                          start=True, stop=True)), reads=[ptt, t_ac], writes=[pt_i])
                psv = ps_i[:, 0:132].rearrange("p (c j) -> p c j", c=4)
                S.op("dve", (lambda e: e.reciprocal(out=rd4[:], in_=psv[:, :, 32])), reads=[pt_i], writes=[t_tk])
                S.op("dve", (lambda e: e.tensor_scalar(out=scA[:], in0=psv[:, 0, 0:32], scalar1=rd4[:, 0:1], scalar2=None, op0=ALU.mult)), reads=[pt_i, t_tk], writes=[t_tk])
                for c in range(1, 4):
                    S.op("dve", (lambda e, c=c: e.scalar_tensor_tensor(out=scA[:], in0=psv[:, c, 0:32], scalar=rd4[:, c:c + 1], in1=scA[:], op0=ALU.mult, op1=ALU.add)),
                         reads=[pt_i, t_tk], writes=[t_tk])
                S.op("dve", (lambda e: e.tensor_tensor(out=scA[:], in0=scA[:], in1=force[:, qt - 8, :], op=ALU.add)), reads=[t_tk, t_ac], writes=[t_tk])
                S.op("dve", (lambda e: e.max(out=m8a[:], in_=scA[:])), reads=[t_tk], writes=[t_tk])
                S.op("dve", (lambda e: e.match_replace(out=scB[:], in_to_replace=m8a[:], in_values=scA[:], imm_value=-1e30)), reads=[t_tk], writes=[t_tk])
                S.op("dve", (lambda e: e.max(out=m8b[:], in_=scB[:])), reads=[t_tk], writes=[t_tk])
                S.op("dve", (lambda e: e.tensor_scalar(out=selm[:], in0=scA[:], scalar1=m8b[:, 7:8], scalar2=1.0, op0=ALU.is_ge, op1=ALU.subtract)), reads=[t_tk], writes=[t_tk])
                ps_m, pt_m = next_ps("M")
                S.op("pe", (lambda e: e.matmul(ps_m[0:32, 0:128], selm[:], ident_b[:], start=True, stop=True)), reads=[t_tk, t_l1c], writes=[pt_m])
                S.op("act", (lambda e: e.activation(out=BIAS[bias_i][0:32, :, :], in_=ps_m[0:32, 0:128].unsqueeze(1).to_broadcast([32, 2, 128]), func=AF.Copy, scale=30000.0)),
                     reads=[pt_m], writes=[bias_t[bias_i]])

            on_t = [Tok(), Tok()]
            acc_t = [Tok(), Tok()]
            t_tk = Tok()
            g1 = mod_part(1, 2)
            steps = []

            def emit_xreload(tg):
                for kh in range(2):
                    ks = slice(kh * 4, kh * 4 + 4)
                    S.op("sp", (lambda e, ks=ks: e.dma_start(out=XST.ap[:, ks, :], in_=d_xs[:, ks, tsl(tg)])),
                         reads=[xs_t[tg]], writes=[XST.t(k) for k in range(kh * 4, kh * 4 + 4)], dma="xr")

            def emit_gbc(tg, g):
                for hpl in range(2):
                    for br in range(3):
                        ps, pt = next_ps("M")
                        S.op("pe", (lambda e, ps=ps, hpl=hpl, br=br: e.matmul(ps[:], gsel[:, 2 * g + hpl, br, :], SIGG.ap[:, tsl(tg)], start=True, stop=True)),
                             reads=[t_ac, SIGG.t(tg)], writes=[pt])
                        S.op("act", (lambda e, ps=ps, hpl=hpl, br=br: e.activation(out=GBC[:, hpl, br, :], in_=ps[:], func=AF.Copy)), reads=[pt], writes=[t_gbc])

            def emit_wo(tg):
                def ev_o(oc, tg_, ps, pt):
                    S.op("dve", (lambda e: e.scalar_tensor_tensor(out=XST.ap[:, oc, :], in0=ps[:], scalar=g1[:, oc:oc + 1], in1=XST.ap[:, oc, :], op0=ALU.mult, op1=ALU.add)),
                         reads=[pt, XST.t(oc), t_modv[1]], writes=[XST.t(oc)])
                defpool["v"] = "M"
                proj(wo_v, wo_t, OTG, 8, ev_o, tgs=[0])
                defpool["v"] = "P6"
                for kh in range(2):
                    ks = slice(kh * 4, kh * 4 + 4)
                    S.op("sp", (lambda e, ks=ks: e.dma_start(out=d_xs[:, ks, tsl(tg)], in_=XST.ap[:, ks, :])),
                         reads=[XST.t(k) for k in range(kh * 4, kh * 4 + 4)], writes=[xs_t[tg]], dma="xw")

            for tg in range(NTG):
                steps.append((None, (lambda tg=tg: emit_xreload(tg))))
                for g in range(G4):
                    steps.append((None, (lambda tg=tg, g=g: emit_gbc(tg, g))))
                    for qi in range(4):
                        qt = tg * 4 + qi
                        ai = ptc["b"] % 2
                        ptc["b"] += 1
                        acc, acct = ACC[ai], acc_t[ai]
                        steps += branch_steps(g, qt, "cmp", ai, 0, 0, acc, acct)
                        steps += branch_steps(g, qt, "win", ai, 1, 2, acc, acct)
                        steps += branch_steps(g, qt, "sel", ai, 2, 1, acc, acct)
                steps.append((None, (lambda tg=tg: emit_wo(tg))))
            prev_pv = None
            for qk_fn, pv_fn in steps:
                if qk_fn is not None:
                    qk_fn()
                if prev_pv is not None:
                    prev_pv()
                prev_pv = pv_fn
            if prev_pv is not None:
                prev_pv()
            S.barrier()
            for tg in range(NTG):
                for kh in range(2):
                    ks = slice(kh * 4, kh * 4 + 4)
                    S.op("sp", (lambda e, tg=tg, ks=ks: e.dma_start(out=X.ap[:, ks, tsl(tg)], in_=d_xs[:, ks, tsl(tg)])),
                         reads=[xs_t[tg]], writes=[X.t(k, tg) for k in range(kh * 4, kh * 4 + 4)], dma="x2")
            defpool["v"] = "ALL"
            if stop != "mix1":
                mlp(1, 1)
          except StopBuild:
            S.emit(final_dma_keys=["out"])
            return nc

        for tg in range(NTG):
            for kh in range(2):
                ks = slice(kh * 4, kh * 4 + 4)
                S.op("sp", (lambda e, tg=tg, ks=ks: e.dma_start(out=d_y[:, ks, tsl(tg)], in_=X.ap[:, ks, tsl(tg)])),
                     reads=[X.t(k, tg) for k in range(kh * 4, kh * 4 + 4)], dma="out")
        S.emit(final_dma_keys=["out"])
    return nc


def _fm(v):
    return np.ascontiguousarray(v.reshape(KC, 128).T)


def _const_tables():
    f32 = np.float32
    NEGM = -30000.0
    j = np.arange(128)[:, None]
    t = np.arange(128)[None, :]
    tri = np.where(j <= t, 0.0, NEGM)
    anti = np.where(j > t, 0.0, NEGM)
    n = np.arange(128)[:, None]
    tt = np.arange(T)[None, :]
    cmpm = np.where((16 * n + 31 <= tt) & (n < 127), 0.0, NEGM)
    amask = np.concatenate([tri, anti, cmpm], axis=1).astype(f32)
    e128 = np.zeros((128, 16, 128), f32)
    for kt in range(16):
        for jj in range(128):
            e128[2 * kt + jj // 64, kt, jj] = 1.0
    cmap = np.zeros((128, 40), f32)
    c0 = np.arange(127)[:, None] * 16
    s0 = np.arange(32)[None, :] * 64
    ov = np.minimum(c0 + 32, s0 + 64) - np.maximum(c0, s0)
    cmap[:127, :32] = np.clip(ov, 0, None) / 32.0
    cmap[:127, 32] = 1.0
    force = np.zeros((128, 8, 32), f32)
    for q in range(8):
        tq = 128 * (q + 8) + np.arange(128)
        cur = tq // 64
        jb = np.arange(32)[None, :]
        forced = (jb == 0) | (jb == cur[:, None]) | (jb == cur[:, None] - 1)
        force[:, q, :] = np.where(forced, 1e4, np.where(jb > cur[:, None], -1e4, 0.0))
    gsel = np.zeros((48, 8, 3, 128), f32)
    for hp in range(8):
        for br in range(3):
            for m in range(128):
                gsel[(2 * hp + m // 64) * 3 + br, hp, br, m] = 1.0
    p = np.arange(128)
    bd = (p[:, None] // 64 == p[None, :] // 64).astype(f32)
    return {"amask": amask, "e128": e128.reshape(128, 2048), "cmap": cmap, "force": force.reshape(128, 256),
            "gsel": gsel.reshape(48, 3072), "bdones": bd}


def prep_inputs(inputs):
    f32 = np.float32
    g = {k: np.asarray(v, dtype=f32) for k, v in inputs.items()}
    vecs = np.zeros((128, NV, KC), f32)
    for i in range(2):
        for j in range(2):
            vecs[:, V_NG + 2 * i + j, :] = _fm(g["norm_gain"][i, j])
        for part in range(6):
            vecs[:, V_BADA + 6 * i + part, :] = _fm(g["b_ada"][i, part * D:(part + 1) * D])
    vecs[:, V_KVG, :] = _fm(g["kv_norm_gain"])
    for part in range(2):
        vecs[:, V_BKV + part, :] = _fm(g["b_ada_kv"][part * D:(part + 1) * D])
    for j in range(3):
        vecs[:, V_CONV + j, :] = _fm(g["conv_w"][0, j])
    consts = np.concatenate([np.eye(128, dtype=f32), np.ones((128, 128), f32)], axis=1)
    p64 = np.arange(128) % 64
    hvecs = np.zeros((128, 4), f32)
    hvecs[:, 0] = g["q_gain"][0, p64]
    for i in range(3):
        hvecs[:, 1 + i] = g["k_gain"][i, p64]
    peT = np.zeros((128, 64), f32)
    for kv in range(2):
        peT[:, kv * 32:(kv + 1) * 32] = g["cmp_pe"][kv][:, p64].T
    shared = {
        "vecs": vecs, "consts": consts, "hvecs": hvecs, "peT": peT,
        "w_ada": g["w_ada"], "w_a_in": g["w_a_in"], "w_a_out": g["w_a_out"],
        "w_mlp1": g["w_mlp1"], "w_mlp2": g["w_mlp2"],
        "w_ada_kv": g["w_ada_kv"], "w_kv": g["w_kv"], "w_qg": g["w_qg"], "w_o": g["w_o"],
        "cmp_w1": g["cmp_w1"], "cmp_w2": g["cmp_w2"],
    }
    shared.update(_const_tables())
    in_maps = []
    for b in range(N_CORES):
        m = dict(shared)
        xT = g["x"][b].T.reshape(KC, 128, T).transpose(1, 0, 2)
        m["xT"] = np.ascontiguousarray(xT)
        m["cT"] = _fm(g["c"][b])
        in_maps.append(m)
    return in_maps


def post_outputs(results):
    outs = []
    for r in results:
        yT = np.asarray(r["yT"])
        outs.append(yT.transpose(2, 1, 0).reshape(T, D))
    return np.stack(outs, axis=0).astype(np.float32)


def kernel(**inputs):
    in_maps = prep_inputs(inputs)
    nc = build_program(DEBUG_STOP)
    res = run_bass_kernel_spmd(nc, in_maps, core_ids=list(range(N_CORES)))
    return post_outputs(res.results)
```

```python
import numpy as np
from contextlib import ExitStack
import concourse.bass as bass
import concourse.mybir as mybir
from concourse.bass_utils import run_bass_kernel_spmd

F32 = mybir.dt.float32
BF16 = mybir.dt.bfloat16
AF = mybir.ActivationFunctionType
ALU = mybir.AluOpType

D = 1024
T = 2048
KC = 8
NTG = 4
TGW = 512
EPS = 1e-6
N_CORES = 8

V_NG = 0
V_KVG = 4
V_BADA = 5
V_BKV = 17
V_CONV = 19
NV = 22

DEBUG_STOP = None


class Tok:
    __slots__ = ("ws", "rs", "rdma", "excl")

    def __init__(self, excl=False):
        self.ws = []
        self.rs = {}
        self.rdma = []
        self.excl = excl


class Op:
    __slots__ = ("eng", "fn", "deps", "dma", "signal", "sigval", "dmaval", "dsem")

    def __init__(self, eng, fn, dma):
        self.eng = eng
        self.fn = fn
        self.dma = dma
        self.deps = ()
        self.signal = False
        self.sigval = 0
        self.dmaval = 0
        self.dsem = None


ENGS = ["pe", "act", "dve", "pool", "sp"]
DMA_SEMS = 16


class Sched:
    def __init__(self, nc):
        self.nc = nc
        self.ops = {e: [] for e in ENGS}
        self.dma_hist = {e: [] for e in ENGS}

    def op(self, eng, fn, reads=(), writes=(), dma=None):
        o = Op(eng, fn, dma)
        ex = [t for t in reads if t.excl]
        if ex:
            reads = [t for t in reads if not t.excl]
            writes = list(writes) + ex
        deps = set()
        for t in reads:
            deps.update(t.ws)
        for t in writes:
            deps.update(t.ws)
            deps.update(t.rs.values())
            deps.update(t.rdma)
        if dma is not None:
            deps = {d for d in deps if d.dma != dma}
            hist = self.dma_hist[eng]
            n = len(hist)
            o.dsem = (eng, n % DMA_SEMS)
            o.dmaval = 16 * (n // DMA_SEMS + 1)
            if n >= DMA_SEMS:
                deps.add(hist[n - DMA_SEMS])
            hist.append(o)
        o.deps = tuple(deps)
        for t in reads:
            if dma is not None:
                t.rdma.append(o)
            else:
                t.rs[eng] = o
        for t in writes:
            if dma is not None and t.ws and not t.rs and not t.rdma and all(w.dma == dma for w in t.ws):
                t.ws.append(o)
            else:
                t.ws = [o]
            t.rs = {}
            t.rdma = []
        self.ops[eng].append(o)
        return o

    def barrier(self, engines=("pe", "act", "dve", "sp", "pool")):
        lasts = []
        for e in ENGS:
            for o in reversed(self.ops[e]):
                if o.dma is None and o.fn is not None:
                    lasts.append(o)
                    break
            lasts += self.dma_hist[e][-DMA_SEMS:]
        for e in engines:
            o = Op(e, None, None)
            o.deps = tuple(lasts)
            self.ops[e].append(o)

    def emit(self, final_dma_keys=()):
        nc = self.nc
        for e in ENGS:
            for o in self.ops[e]:
                for d in o.deps:
                    if d.dma is None:
                        if d.eng == "pe" and o.eng == "pe" and o.dma is None and o.fn is not None:
                            continue
                        d.signal = True
        for e in ENGS:
            c = 0
            for o in self.ops[e]:
                if o.dma is None and o.signal:
                    c += 1
                    o.sigval = c
        with ExitStack() as es:
            esem = {e: es.enter_context(nc.semaphore("s_" + e)) for e in ENGS}
            dsem = {}
            for e in ENGS:
                for i in range(min(DMA_SEMS, len(self.dma_hist[e]))):
                    dsem[(e, i)] = es.enter_context(nc.semaphore("d_%s%d" % (e, i)))
            block = es.enter_context(nc.Block())

            def run(e, eng):
                waited = {}
                for o in self.ops[e]:
                    need = {}
                    for d in o.deps:
                        if d.dma is not None:
                            key, sem, val = ("d",) + d.dsem, dsem[d.dsem], d.dmaval
                        else:
                            if d.eng == "pe" and e == "pe" and o.dma is None and o.fn is not None:
                                continue
                            key, sem, val = ("e", d.eng), esem[d.eng], d.sigval
                        if val > need.get(key, (None, 0))[1]:
                            need[key] = (sem, val)
                    for key, (sem, val) in need.items():
                        if waited.get(key, 0) >= val:
                            continue
                        waited[key] = val
                        eng.wait_ge(sem, val)
                    if o.fn is None:
                        continue
                    ins = o.fn(eng)
                    if o.dma is not None:
                        ins.then_inc(dsem[o.dsem], 16)
                    elif o.signal:
                        ins.then_inc(esem[e], 1)
                if e == "sp":
                    fin = {}
                    for q in ENGS:
                        for d in self.dma_hist[q]:
                            if d.dma in final_dma_keys:
                                fin[d.dsem] = max(fin.get(d.dsem, 0), d.dmaval)
                    for k, v in fin.items():
                        if waited.get(("d",) + k, 0) < v:
                            eng.wait_ge(dsem[k], v)

            block.sync(lambda eng: run("sp", eng))
            block.scalar(lambda eng: run("act", eng))
            block.vector(lambda eng: run("dve", eng))
            block.gpsimd(lambda eng: run("pool", eng))
            block.tensor(lambda eng: run("pe", eng))


class StopBuild(Exception):
    pass


class TT:
    def __init__(self, ap):
        self.ap = ap
        self.toks = {}

    def t(self, *key):
        tk = self.toks.get(key)
        if tk is None:
            tk = self.toks[key] = Tok()
        return tk

    def all(self):
        return list(self.toks.values())


def build_program(stop=None):
    nc = bass.Bass("TRN2", target_bir_lowering=False)

    def din(name, shape):
        return nc.dram_tensor(name, list(shape), F32, kind="ExternalInput").ap()

    d_x = din("xT", [128, KC, T])
    d_c = din("cT", [128, KC])
    d_vecs = din("vecs", [128, NV, KC])
    d_consts = din("consts", [128, 256])
    d_w_ada = din("w_ada", [2, D, 6 * D])
    d_w_a_in = din("w_a_in", [1, D, 3 * D])
    d_w_a_out = din("w_a_out", [1, D, D])
    d_w_mlp1 = din("w_mlp1", [2, D, 4 * D])
    d_w_mlp2 = din("w_mlp2", [2, 4 * D, D])
    d_w_ada_kv = din("w_ada_kv", [D, 2 * D])
    d_w_kv = din("w_kv", [D, 1536])
    d_w_qg = din("w_qg", [1, D, 1072])
    d_w_o = din("w_o", [1, D, D])
    d_cmp_w1 = din("cmp_w1", [2, 2048, 256])
    d_cmp_w2 = din("cmp_w2", [2, 256, 64])
    d_hvecs = din("hvecs", [128, 4])
    d_peT = din("peT", [128, 64])
    d_amask = din("amask", [128, 256 + 2048])
    d_e128 = din("e128", [128, 2048])
    d_cmap = din("cmap", [128, 40])
    d_force = din("force", [128, 256])
    d_gsel = din("gsel", [48, 3072])
    d_bd = din("bdones", [128, 128])
    d_y = nc.dram_tensor("yT", [128, KC, T], F32, kind="ExternalOutput").ap()
    d_xs = nc.dram_tensor("xs_scratch", [128, KC, T], F32).ap()

    S = Sched(nc)
    es = ExitStack()
    with es:
        def sb(name, shape, dt):
            return es.enter_context(nc.sbuf_tensor(name, list(shape), dt))

        XR = sb("XR", [128, KC * T], F32)
        HR = sb("HR", [128, KC * T], BF16)
        AR = sb("AR", [128, KC * T], BF16)
        RING = [sb("RING%d" % i, [128, 8192], BF16) for i in range(2)]
        ring_t = [Tok(), Tok()]
        rstd_s = sb("rstd", [128, T], F32)
        TMPF = [sb("tmpf%d" % i, [128, TGW], F32) for i in range(3)]
        tmpf_t = [Tok() for _ in TMPF]
        TMPB = [sb("tmpb%d" % i, [128, TGW], BF16) for i in range(3)]
        tmpb_t = [Tok() for _ in TMPB]
        SCR = sb("SCR", [128, 11048], BF16)
        ones_b = sb("ones_b", [128, 128], BF16)
        vecs = sb("vecs_s", [128, NV, KC], F32)
        c_f = sb("c_f", [128, KC], F32)
        cact = sb("cact", [128, KC], BF16)
        modv = sb("modv", [128, 3, 48], F32)
        der = sb("der", [128, 8, KC], F32)
        hvecs = sb("hvecs_s", [128, 4], F32)
        gq = sb("gq", [128, 1], F32)
        ident_b = sb("ident_b", [128, 128], BF16)
        bd_ones = sb("bd_ones", [128, 128], BF16)
        cbias = sb("cbias", [128, 4], F32)
        kcT = sb("kcT", [128, 4, 128], BF16)
        vcA = sb("vcA", [128, 4, 128], BF16)
        rd4 = sb("rd4", [128, 4], F32)
        scA = sb("scA", [128, 32], F32)
        scB = sb("scB", [128, 32], F32)
        m8a = sb("m8a", [128, 8], F32)
        m8b = sb("m8b", [128, 8], F32)
        selm = sb("selm", [128, 32], BF16)
        ACC = [sb("acc%d" % i, [128, 256], F32) for i in range(2)]
        BIAS_EXTRA = sb("m2", [128, 1024], BF16)
        PS = [es.enter_context(nc.psum_tensor("ps%d" % i, [128, TGW], F32)) for i in range(8)]
        ps_t = [Tok(excl=True) for _ in PS]

        X = TT(XR[:].rearrange("p (k t) -> p k t", k=KC))
        H = TT(HR[:].rearrange("p (k t) -> p k t", k=KC))
        A = TT(AR[:].rearrange("p (k t) -> p k t", k=KC))
        t_const = Tok()
        eps_t = sb("eps_t", [128, 2], F32)
        t_eps = Tok()
        S.op("dve", lambda e: e.memset(eps_t[:, 0:1], EPS), writes=[t_eps])
        S.op("dve", lambda e: e.memset(eps_t[:, 1:2], 1e-30), writes=[t_eps])
        t_vecs = Tok()
        t_c = Tok()
        t_cact = Tok()
        t_modv = [Tok(), Tok(), Tok()]
        t_der = Tok()
        t_rstd = [Tok() for _ in range(NTG)]

        cnt = {"ps": 0, "ring": 0, "tf": 0, "tb": 0}

        POOLS = {"ALL": list(range(8)), "P6": [0, 1, 2, 3, 4, 5], "S": [0, 1, 2, 3], "O": [4, 5, 6], "M": [7], "OM": [4, 5, 6, 7]}
        pcnt = {k: 0 for k in POOLS}

        defpool = {"v": "ALL"}

        def next_ps(pool=None):
            pool = pool or defpool["v"]
            lst = POOLS[pool]
            i = lst[pcnt[pool] % len(lst)]
            pcnt[pool] += 1
            return PS[i], ps_t[i]

        def next_tf():
            i = cnt["tf"] % len(TMPF)
            cnt["tf"] += 1
            return TMPF[i], tmpf_t[i]

        def next_tb():
            i = cnt["tb"] % len(TMPB)
            cnt["tb"] += 1
            return TMPB[i], tmpb_t[i]

        def tsl(tg):
            return slice(tg * TGW, (tg + 1) * TGW)

        S.op("pool", lambda e: e.dma_start(out=ones_b[:], in_=d_consts[:, 128:256]), writes=[t_const], dma="cb")
        S.op("sp", lambda e: e.dma_start(out=vecs[:], in_=d_vecs), writes=[t_vecs], dma="c")
        S.op("sp", lambda e: e.dma_start(out=c_f[:], in_=d_c), writes=[t_c], dma="c")
        for tg in range(NTG):
            for kh in range(2):
                ks = slice(kh * 4, kh * 4 + 4)
                S.op("sp", (lambda e, tg=tg, ks=ks: e.dma_start(out=X.ap[:, ks, tsl(tg)], in_=d_x[:, ks, tsl(tg)])),
                     writes=[X.t(k, tg) for k in range(kh * 4, kh * 4 + 4)], dma="x")

        S.op("act", lambda e: e.activation(out=cact[:], in_=c_f[:], func=AF.Silu), reads=[t_c], writes=[t_cact])

        def load_w(src_aps, dst_view_fn):
            i = cnt["ring"] % 2
            cnt["ring"] += 1
            slot = RING[i]
            for dst_fn, src in src_aps:
                S.op("pool", (lambda e, dst_fn=dst_fn, src=src, slot=slot: e.dma_start(out=dst_fn(slot), in_=src)),
                     writes=[ring_t[i]], dma="r%d" % i)
            return dst_view_fn(slot), ring_t[i]

        def load_w_std(w2d, c0, ncols, k0=0):
            src = w2d.rearrange("(k p) n -> p k n", p=128)

            def view(slot):
                return slot[:, 0:8 * ncols].rearrange("p (k c) -> p k c", k=8)
            pieces = []
            for kh in range(2):
                ks = slice(kh * 4, kh * 4 + 4)
                pieces.append(((lambda slot, ks=ks: view(slot)[:, ks, :]), src[:, k0 + kh * 4:k0 + kh * 4 + 4, c0:c0 + ncols]))
            return load_w(pieces, view)

        def ada_matvec(w2d, ncb, bias_idx, mi):
            ps, pt = next_ps()
            for cb in range(ncb):
                wv, wt = load_w_std(w2d, cb * 1024, 1024)
                for oc in range(8):
                    col = cb * 8 + oc
                    for k in range(KC):
                        S.op("pe", (lambda e, ps=ps, wv=wv, oc=oc, k=k, col=col: e.matmul(
                            ps[:, col:col + 1], wv[:, k, oc * 128:(oc + 1) * 128], cact[:, k:k + 1],
                            start=(k == 0), stop=(k == KC - 1))), reads=[wt, t_cact], writes=[pt])
            n = ncb * 8
            S.op("dve", (lambda e, ps=ps, n=n: e.tensor_tensor(
                out=modv[:, mi, 0:n], in0=ps[:, 0:n],
                in1=vecs[:, bias_idx:bias_idx + ncb, :].rearrange("p a b -> p (a b)"), op=ALU.add)),
                reads=[pt, t_vecs], writes=[t_modv[mi]])

        def mod_part(mi, part):
            return modv[:, mi, part * 8:(part + 1) * 8]

        def derive(mi, part_sc, gain_idx, dst):
            S.op("dve", lambda e: e.tensor_scalar(out=der[:, dst, :], in0=mod_part(mi, part_sc), scalar1=1.0, scalar2=1.0,
                                                  op0=ALU.add, op1=ALU.mult), reads=[t_modv[mi]], writes=[t_der])
            S.op("dve", lambda e: e.tensor_tensor(out=der[:, dst, :], in0=der[:, dst, :], in1=vecs[:, gain_idx, :], op=ALU.mult),
                 reads=[t_der, t_vecs], writes=[t_der])

        def compute_rstd():
            for tg in range(NTG):
                ps, pt = next_ps()
                for k in range(KC):
                    tb, tbt = next_tb()
                    S.op("act", (lambda e, tb=tb, k=k, tg=tg: e.activation(out=tb[:], in_=X.ap[:, k, tsl(tg)], func=AF.Square)),
                         reads=[X.t(k, tg)], writes=[tbt])
                    S.op("pe", (lambda e, ps=ps, tb=tb, k=k: e.matmul(ps[:], ones_b[:], tb[:], start=(k == 0), stop=(k == KC - 1))),
                         reads=[tbt, t_const], writes=[pt])
                tf, tft = next_tf()
                S.op("act", (lambda e, ps=ps, tf=tf: e.activation(out=tf[:], in_=ps[:], func=AF.Ln, bias=eps_t[:, 0:1], scale=1.0 / D)),
                     reads=[pt, t_eps], writes=[tft])
                S.op("act", (lambda e, tf=tf, tg=tg: e.activation(out=rstd_s[:, tsl(tg)], in_=tf[:], func=AF.Exp, scale=-0.5)), reads=[tft], writes=[t_rstd[tg]])

        def norm_mod(dst, a_ap, b_ap, extra_reads):
            for tg in range(NTG):
                for k in range(KC):
                    tf, tft = next_tf()
                    S.op("dve", (lambda e, tf=tf, k=k, tg=tg: e.tensor_tensor(out=tf[:], in0=X.ap[:, k, tsl(tg)], in1=rstd_s[:, tsl(tg)], op=ALU.mult)),
                         reads=[X.t(k, tg), t_rstd[tg]], writes=[tft])
                    S.op("act", (lambda e, tf=tf, k=k, tg=tg: e.activation(out=dst.ap[:, k, tsl(tg)], in_=tf[:], func=AF.Identity,
                                                                            bias=b_ap[:, k:k + 1], scale=a_ap[:, k:k + 1])),
                         reads=[tft] + extra_reads, writes=[dst.t(k, tg)])

        def proj(wv, wt, src, n_oc, evac, tgs=range(NTG), oc_cols=None):
            for tg in tgs:
                for oc in range(n_oc):
                    ps, pt = next_ps()
                    for k in range(KC):
                        lhs = wv[:, k, oc * 128:(oc + 1) * 128] if oc_cols is None else oc_cols(wv, k, oc)
                        S.op("pe", (lambda e, ps=ps, lhs=lhs, k=k, tg=tg: e.matmul(ps[:], lhs, src.ap[:, k, tsl(tg)],
                                                                                   start=(k == 0), stop=(k == KC - 1))),
                             reads=[wt, src.t(k, tg)], writes=[pt])
                    evac(oc, tg, ps, pt)

        def resid_evac(g_ap, extra_reads):
            def ev(oc, tg, ps, pt):
                S.op("dve", (lambda e: e.scalar_tensor_tensor(out=X.ap[:, oc, tsl(tg)], in0=ps[:], scalar=g_ap[:, oc:oc + 1],
                                                              in1=X.ap[:, oc, tsl(tg)], op0=ALU.mult, op1=ALU.add)),
                     reads=[pt, X.t(oc, tg)] + extra_reads, writes=[X.t(oc, tg)])
            return ev

        def mlp(layer, mi):
            compute_rstd()
            derive(mi, 4, V_NG + 2 * layer + 1, 1)
            norm_mod(H, der[:, 1, :], mod_part(mi, 3), [t_der, t_modv[mi]])
            g2 = mod_part(mi, 5)
            for hb in range(4):
                wv, wt = load_w_std(d_w_mlp1[layer], hb * 1024, 1024)

                def ev1(oc, tg, ps, pt):
                    tf, tft = next_tf()
                    S.op("act", (lambda e: e.activation(out=tf[:], in_=ps[:], func=AF.Relu)), reads=[pt], writes=[tft])
                    S.op("dve", (lambda e: e.tensor_tensor(out=A.ap[:, oc, tsl(tg)], in0=tf[:], in1=tf[:], op=ALU.mult)),
                         reads=[tft], writes=[A.t(oc, tg)])
                proj(wv, wt, H, 8, ev1)
                wv2, wt2 = load_w_std(d_w_mlp2[layer], 0, 1024, k0=hb * 8)
                proj(wv2, wt2, A, 8, resid_evac(g2, [t_modv[mi]]))

        ada_matvec(d_w_ada[0], 6, V_BADA, 0)
        compute_rstd()
        derive(0, 1, V_NG + 0, 0)
        norm_mod(H, der[:, 0, :], mod_part(0, 0), [t_der, t_modv[0]])

        gbv = SCR[:, 0:4096].rearrange("p (j t) -> p j t", j=2)
        vv = SCR[:, 4096:4096 + 2 * 2056].rearrange("p (j t) -> p j t", j=2)
        t_gb = [[Tok() for _ in range(NTG)] for _ in range(2)]
        t_v = [[Tok() for _ in range(NTG)] for _ in range(2)]
        t_vhalo = Tok()
        S.op("dve", lambda e: e.memset(vv[:, :, 0:2], 0.0), writes=[t_vhalo])
        w_in_v = d_w_a_in[0].rearrange("(k p) (s c) -> p k s c", p=128, s=3)
        g1 = mod_part(0, 2)
        def mixer_j(wv, wt, jj, j):
            jb = j % 2
            for tg in range(NTG):
                pss = []
                for s in range(3):
                    ps, pt = next_ps()
                    for k in range(KC):
                        S.op("pe", (lambda e, ps=ps, s=s, k=k, tg=tg: e.matmul(ps[:], wv[:, k, s, jj * 128:(jj + 1) * 128], H.ap[:, k, tsl(tg)],
                                                                               start=(k == 0), stop=(k == KC - 1))),
                             reads=[wt, H.t(k, tg)], writes=[pt])
                    pss.append((ps, pt))
                (psb, ptb), (psc, ptc), (psu, ptu) = pss
                S.op("act", (lambda e, psb=psb, tg=tg: e.activation(out=gbv[:, jb, tsl(tg)], in_=psb[:], func=AF.Copy)),
                     reads=[ptb], writes=[t_gb[jb][tg]])
                tb, tbt = next_tb()
                S.op("act", (lambda e, psc=psc, tb=tb: e.activation(out=tb[:], in_=psc[:], func=AF.Copy)), reads=[ptc], writes=[tbt])
                S.op("dve", (lambda e, psu=psu, tb=tb, tg=tg: e.tensor_tensor(out=vv[:, jb, 2 + tg * TGW:2 + (tg + 1) * TGW], in0=psu[:], in1=tb[:], op=ALU.mult)),
                     reads=[ptu, tbt], writes=[t_v[jb][tg]])
            for tg in range(NTG):
                tf, tft = next_tf()
                rd = [t_v[jb][tg], t_vhalo, t_vecs] + ([t_v[jb][tg - 1]] if tg > 0 else [])
                b0 = tg * TGW
                S.op("dve", (lambda e, tf=tf, b0=b0: e.tensor_scalar(out=tf[:], in0=vv[:, jb, b0 + 2:b0 + 2 + TGW], scalar1=vecs[:, V_CONV + 2, j:j + 1], scalar2=None, op0=ALU.mult)),
                     reads=rd, writes=[tft])
                S.op("dve", (lambda e, tf=tf, b0=b0: e.scalar_tensor_tensor(out=tf[:], in0=vv[:, jb, b0 + 1:b0 + 1 + TGW], scalar=vecs[:, V_CONV + 1, j:j + 1], in1=tf[:], op0=ALU.mult, op1=ALU.add)),
                     reads=rd + [tft], writes=[tft])
                S.op("dve", (lambda e, tf=tf, b0=b0: e.scalar_tensor_tensor(out=tf[:], in0=vv[:, jb, b0:b0 + TGW], scalar=vecs[:, V_CONV + 0, j:j + 1], in1=tf[:], op0=ALU.mult, op1=ALU.add)),
                     reads=rd + [tft], writes=[tft])
                S.op("dve", (lambda e, tf=tf, tg=tg: e.tensor_tensor(out=A.ap[:, j, tsl(tg)], in0=tf[:], in1=gbv[:, jb, tsl(tg)], op=ALU.mult)),
                     reads=[tft, t_gb[jb][tg]], writes=[A.t(j, tg)])

        for jp in range(4):
            def view(slot):
                return slot[:, 0:6144].rearrange("p (k s c) -> p k s c", k=8, s=3)
            pieces = []
            for s3 in range(3):
                pieces.append(((lambda slot, s3=s3: view(slot)[:, :, s3, :]), w_in_v[:, :, s3, jp * 256:(jp + 1) * 256]))
            wv, wt = load_w(pieces, view)
            for jj in range(2):
                mixer_j(wv, wt, jj, 2 * jp + jj)
        wv, wt = load_w_std(d_w_a_out[0], 0, 1024)
        proj(wv, wt, A, 8, resid_evac(g1, [t_modv[0]]))
        if stop != "mix0":
            mlp(0, 0)
        def chk(name, dumps):
            if stop != name:
                return
            S.barrier()
            for ap, dst in dumps:
                S.op("pool", (lambda e, ap=ap, dst=dst: e.dma_start(out=dst, in_=ap)), dma="out")
            raise StopBuild()

        if stop not in ("mix0", "l0"):
          try:
            G4 = 4
            defpool["v"] = "P6"
            t_l1c = Tok()
            wqg_g = SCR[:, 8192:8576].rearrange("p (k c) -> p k c", k=8)
            cw2k = SCR[:, 8576:8832].rearrange("p (c d) -> p c d", c=2)
            cw2v = SCR[:, 8832:8960].rearrange("p (c d) -> p c d", c=2)
            peT_b = SCR[:, 8960:9024]
            S.op("sp", lambda e: e.dma_start(out=hvecs[:], in_=d_hvecs), writes=[t_l1c], dma="c")
            S.op("pool", lambda e: e.dma_start(out=ident_b[:], in_=d_consts[:, 0:128]), writes=[t_l1c], dma="cb")
            S.op("pool", lambda e: e.dma_start(out=bd_ones[:], in_=d_bd), writes=[t_l1c], dma="cb")
            S.op("pool", lambda e: e.dma_start(out=peT_b, in_=d_peT), writes=[t_l1c], dma="cb")
            S.op("pool", lambda e: e.dma_start(out=cw2k[:, :, 0:64], in_=d_cmp_w2[0].rearrange("(c p) d -> p c d", p=128)), writes=[t_l1c], dma="cb")
            S.op("pool", lambda e: e.dma_start(out=cw2k[:, :, 64:128], in_=d_cmp_w2[0].rearrange("(c p) d -> p c d", p=128)), writes=[t_l1c], dma="cb")
            S.op("pool", lambda e: e.dma_start(out=cw2v, in_=d_cmp_w2[1].rearrange("(c p) d -> p c d", p=128)), writes=[t_l1c], dma="cb")
            S.op("pool", lambda e: e.dma_start(out=wqg_g, in_=d_w_qg[0].rearrange("(k p) n -> p k n", p=128)[:, :, 1024:1072]), writes=[t_l1c], dma="cb")
            t_gq = Tok()
            S.op("dve", lambda e: e.tensor_scalar(out=gq[:], in0=hvecs[:, 0:1], scalar1=0.125, scalar2=None, op0=ALU.mult), reads=[t_l1c], writes=[t_gq])

            ada_matvec(d_w_ada[1], 6, V_BADA + 6, 1)
            ada_matvec(d_w_ada_kv, 2, V_BKV, 2)
            compute_rstd()
            derive(2, 1, V_KVG, 2)
            derive(1, 1, V_NG + 2, 3)
            norm_mod(H, der[:, 2, :], mod_part(2, 0), [t_der, t_modv[2]])
            norm_mod(A, der[:, 3, :], mod_part(1, 0), [t_der, t_modv[1]])
            xs_t = [Tok() for _ in range(NTG)]
            for tg in range(NTG):
                for kh in range(2):
                    ks = slice(kh * 4, kh * 4 + 4)
                    S.op("sp", (lambda e, tg=tg, ks=ks: e.dma_start(out=d_xs[:, ks, tsl(tg)], in_=X.ap[:, ks, tsl(tg)])),
                         reads=[X.t(k, tg) for k in range(kh * 4, kh * 4 + 4)], writes=[xs_t[tg]], dma="xs")

            def head_norm(ps, pt, ncol, gain_ap, gain_reads, dst_ap, dst_toks):
                tb, tbt = next_tb()
                S.op("act", (lambda e: e.activation(out=tb[:, 0:ncol], in_=ps[:, 0:ncol], func=AF.Square)), reads=[pt], writes=[tbt])
                ps2, pt2 = next_ps("M")
                S.op("pe", (lambda e: e.matmul(ps2[:, 0:ncol], bd_ones[:], tb[:, 0:ncol], start=True, stop=True)), reads=[tbt, t_l1c], writes=[pt2])
                tf, tft = next_tf()
                S.op("act", (lambda e: e.activation(out=tf[:, 0:ncol], in_=ps2[:, 0:ncol], func=AF.Ln, bias=eps_t[:, 0:1], scale=1.0 / 64)),
                     reads=[pt2, t_eps], writes=[tft])
                tf2, tft2 = next_tf()
                S.op("act", (lambda e: e.activation(out=tf2[:, 0:ncol], in_=tf[:, 0:ncol], func=AF.Exp, scale=-0.5)), reads=[tft], writes=[tft2])
                S.op("dve", (lambda e: e.scalar_tensor_tensor(out=dst_ap, in0=ps[:, 0:ncol], scalar=gain_ap, in1=tf2[:, 0:ncol], op0=ALU.mult, op1=ALU.mult)),
                     reads=[pt, tft2] + gain_reads, writes=dst_toks)

            RAW = TT(SCR[:, 0:8192].rearrange("p (c t) -> p c t", c=4))
            wv, wt = load_w_std(d_w_kv, 0, 512)

            def ev_raw(oc, tg, ps, pt):
                S.op("act", (lambda e: e.activation(out=RAW.ap[:, oc, tsl(tg)], in_=ps[:], func=AF.Copy)), reads=[pt], writes=[RAW.t(oc, tg)])
            proj(wv, wt, H, 4, ev_raw)

            chk("raw", [(RAW.ap, d_y[:, 0:4, :])])
            t_kc = [Tok() for _ in range(G4)]
            t_vc = [Tok() for _ in range(G4)]
            t_vc_ones = Tok()
            S.op("dve", lambda e: e.memset(vcA[:, :, 64:128], 1.0), writes=[t_vc_ones])
            t_cb = Tok()
            for kv in range(2):
                def view1(slot):
                    return slot[:, 0:8192].rearrange("p (l h) -> p l h", l=32)
                src1 = d_cmp_w1[kv].rearrange("(l d) h -> d l h", d=64)
                pieces = [((lambda slot: view1(slot)[0:64, :, :]), src1), ((lambda slot: view1(slot)[64:128, :, :]), src1)]
                cwv, cwt = load_w(pieces, view1)
                psb, ptb = next_ps()
                for hc in range(2):
                    for l in range(32):
                        S.op("pe", (lambda e, hc=hc, l=l, cwv=cwv, psb=psb, kv=kv: e.matmul(
                            psb[:, hc:hc + 1], cwv[0:64, l, hc * 128:(hc + 1) * 128], peT_b[0:64, kv * 32 + l:kv * 32 + l + 1],
                            start=(l == 0), stop=(l == 31))), reads=[cwt, t_l1c], writes=[ptb])
                S.op("dve", (lambda e, psb=psb, kv=kv: e.tensor_copy(out=cbias[:, 2 * kv:2 * kv + 2], in_=psb[:, 0:2])), reads=[ptb], writes=[t_cb])
                for g in range(G4):
                    base = (g % 2) * 64
                    c = kv * 2 + g // 2
                    hids = []
                    for hc in range(2):
                        ps, pt = next_ps()
                        for l in range(32):
                            S.op("pe", (lambda e, ps=ps, l=l, hc=hc, cwv=cwv, base=base, c=c: e.matmul(
                                ps[:, 0:127], cwv[base:base + 64, l, hc * 128:(hc + 1) * 128],
                                RAW.ap[base:base + 64, c, l:l + 16 * 126 + 1:16], start=(l == 0), stop=(l == 31))),
                                reads=[cwt] + [RAW.t(c, tg) for tg in range(NTG)], writes=[pt])
                        z, zt = next_tf()
                        S.op("act", (lambda e, ps=ps, z=z, hc=hc, kv=kv: e.activation(out=z[:, 0:127], in_=ps[:, 0:127], func=AF.Identity,
                                                                                      bias=cbias[:, 2 * kv + hc:2 * kv + hc + 1], scale=1.0)),
                             reads=[pt, t_cb], writes=[zt])
                        u, ut = next_tf()
                        S.op("dve", (lambda e, z=z, u=u: e.tensor_tensor(out=u[:, 0:127], in0=z[:, 0:127], in1=z[:, 0:127], op=ALU.mult)), reads=[zt], writes=[ut])
                        S.op("dve", (lambda e, u=u: e.tensor_scalar(out=u[:, 0:127], in0=u[:, 0:127], scalar1=0.044715, scalar2=1.0, op0=ALU.mult, op1=ALU.add)),
                             reads=[ut], writes=[ut])
                        S.op("dve", (lambda e, z=z, u=u: e.tensor_tensor(out=u[:, 0:127], in0=u[:, 0:127], in1=z[:, 0:127], op=ALU.mult)), reads=[ut, zt], writes=[ut])
                        S.op("act", (lambda e, u=u: e.activation(out=u[:, 0:127], in_=u[:, 0:127], func=AF.Sigmoid, scale=1.5957691216057308)), reads=[ut], writes=[ut])
                        hb_, hbt = next_tb()
                        S.op("dve", (lambda e, z=z, u=u, hb_=hb_: e.tensor_tensor(out=hb_[:, 0:127], in0=u[:, 0:127], in1=z[:, 0:127], op=ALU.mult)), reads=[ut, zt], writes=[hbt])
                        hids.append((hb_, hbt))
                    if kv == 0:
                        ps, pt = next_ps()
                        for hc in range(2):
                            S.op("pe", (lambda e, ps=ps, hc=hc, hb_=hids[hc][0]: e.matmul(ps[:, 0:127], cw2k[:, hc, :], hb_[:, 0:127], start=(hc == 0), stop=(hc == 1))),
                                 reads=[hids[hc][1], t_l1c], writes=[pt])
                        head_norm(ps, pt, 127, hvecs[:, 1:2], [t_l1c], kcT[:, g, 0:127], [t_kc[g]])
                    else:
                        ps, pt = next_ps()
                        for hc in range(2):
                            S.op("pe", (lambda e, ps=ps, hc=hc, hb_=hids[hc][0]: e.matmul(ps[0:127, 0:64], hb_[:, 0:127], cw2v[:, hc, :], start=(hc == 0), stop=(hc == 1))),
                                 reads=[hids[hc][1], t_l1c], writes=[pt])
                        S.op("act", (lambda e, ps=ps, g=g: e.activation(out=vcA[0:127, g, 0:64], in_=ps[0:127, 0:64], func=AF.Copy)), reads=[pt], writes=[t_vc[g]])

            chk("cmp", [(kcT[:].rearrange("p g n -> p (g n)"), d_y[:, 0, 0:512]), (vcA[:].rearrange("p g n -> p (g n)"), d_y[:, 1, 0:512])])
            S.barrier()
            XB = XR[:].bitcast(BF16)
            QT = TT(XB[:, 0:16384].rearrange("p (h t) -> p h t", h=8))
            KST = TT(XB[:, 16384:24576].rearrange("p (g t) -> p g t", g=4))
            KWT = TT(XB[:, 24576:32768].rearrange("p (g t) -> p g t", g=4))
            RB = rstd_s[:].bitcast(BF16)
            SIGG = TT(RB[0:48, 0:T])

            wv, wt = load_w_std(d_w_qg[0], 0, 1024)

            def ev_q(oc, tg, ps, pt):
                head_norm(ps, pt, TGW, gq[:, 0:1], [t_gq], QT.ap[:, oc, tsl(tg)], [QT.t(oc, tg)])
            proj(wv, wt, A, 8, ev_q)
            for tg in range(NTG):
                ps, pt = next_ps()
                for k in range(KC):
                    S.op("pe", (lambda e, ps=ps, k=k, tg=tg: e.matmul(ps[0:48, :], wqg_g[:, k, :], A.ap[:, k, tsl(tg)], start=(k == 0), stop=(k == KC - 1))),
                         reads=[t_l1c, A.t(k, tg)], writes=[pt])
                S.op("act", (lambda e, ps=ps, tg=tg: e.activation(out=SIGG.ap[:, tsl(tg)], in_=ps[0:48, :], func=AF.Sigmoid)), reads=[pt], writes=[SIGG.t(tg)])

            for typ, dst, gi in ((2, KST, 2), (4, KWT, 3)):
                def viewk(slot):
                    return slot[:, 0:4096].rearrange("p (k g r d) -> p k g r d", k=8, g=4, r=2)
                srck = d_w_kv.rearrange("(k p) n -> p k n", p=128)[:, :, typ * 256:(typ + 1) * 256].rearrange("p k (g d) -> p k g d", g=4)
                pieces = [((lambda slot, r=r, gg=gg: viewk(slot)[:, :, gg, r, :]), srck[:, :, gg, :]) for r in range(2) for gg in range(4)]
                kv_, kt_ = load_w(pieces, viewk)

                def ev_k(oc, tg, ps, pt, dst=dst, gi=gi):
                    head_norm(ps, pt, TGW, hvecs[:, gi:gi + 1], [t_l1c], dst.ap[:, oc, tsl(tg)], [dst.t(oc, tg)])
                proj(kv_, kt_, H, 4, ev_k, oc_cols=(lambda wv_, k, oc: wv_[:, k, oc, :, :].rearrange("p r d -> p (r d)")))
            chk("q", [(QT.ap, d_y)])
            chk("k", [(KST.ap, d_y[:, 0:4, :]), (KWT.ap, d_y[:, 4:8, :]), ])
            S.barrier()
            AB = AR[:]
            VS = TT(AB[:, 0:8192].rearrange("p (t g c) -> p t g c", t=16, g=4))
            VW = TT(AB[:, 8192:16384].rearrange("p (t g c) -> p t g c", t=16, g=4))
            t_vones = Tok()
            S.op("dve", lambda e: e.memset(VS.ap[:, :, :, 64:128], 1.0), writes=[t_vones])
            S.op("dve", lambda e: e.memset(VW.ap[:, :, :, 64:128], 1.0), writes=[t_vones])

            def viewv(slot):
                return slot[:, 0:4096].rearrange("p (k s c) -> p k s c", k=8, s=2)
            srcv = d_w_kv.rearrange("(k p) n -> p k n", p=128)
            pieces = [((lambda slot: viewv(slot)[:, :, 0, :]), srcv[:, :, 768:1024]), ((lambda slot: viewv(slot)[:, :, 1, :]), srcv[:, :, 1280:1536])]
            vv_, vt_ = load_w(pieces, viewv)
            for tt in range(16):
                ps, pt = next_ps()
                tg = tt // 4
                for k in range(KC):
                    S.op("pe", (lambda e, ps=ps, k=k, tt=tt: e.matmul(ps[:], H.ap[:, k, tt * 128:(tt + 1) * 128], vv_[:, k, :, :].rearrange("p s c -> p (s c)"),
                                                                    start=(k == 0), stop=(k == KC - 1))), reads=[vt_, H.t(k, tg)], writes=[pt])
                S.op("act", (lambda e, ps=ps, tt=tt: e.activation(out=VS.ap[:, tt, :, 0:64], in_=ps[:, 0:256].rearrange("p (g d) -> p g d", g=4), func=AF.Copy)),
                     reads=[pt, t_vones], writes=[VS.t(tt)])
                S.op("dve", (lambda e, ps=ps, tt=tt: e.tensor_copy(out=VW.ap[:, tt, :, 0:64], in_=ps[:, 256:512].rearrange("p (g d) -> p g d", g=4))),
                     reads=[pt, t_vones], writes=[VW.t(tt)])
            chk("v", [(AB[:, 0:8192], d_y[:, 0:4, :].rearrange("p a b -> p (a b)")), (AB[:, 8192:16384], d_y[:, 4:8, :].rearrange("p a b -> p (a b)"))])
            S.barrier()

            t_ac = Tok()
            tri_b = SCR[:, 0:128]
            anti_b = SCR[:, 128:256]
            cmpm = SCR[:, 256:2304]
            e128 = SCR[:, 2304:4352].rearrange("p (k j) -> p k j", k=16)
            cmap = SCR[:, 4352:4392]
            force = SCR[:, 4392:4904].bitcast(F32).rearrange("p (q j) -> p q j", q=8)
            gsel = SCR[0:48, 4904:7976].rearrange("p (h b m) -> p h b m", h=8, b=3)
            PTS = [SCR[:, 7976 + i * 1024:7976 + (i + 1) * 1024].rearrange("p (k r c) -> p k r c", k=2, r=2) for i in range(3)]
            pts_t = [Tok() for _ in PTS]
            S.op("pool", lambda e: e.dma_start(out=SCR[:, 0:2304], in_=d_amask), writes=[t_ac], dma="cb")
            S.op("pool", lambda e: e.dma_start(out=SCR[:, 2304:4352], in_=d_e128), writes=[t_ac], dma="cb")
            S.op("pool", lambda e: e.dma_start(out=cmap, in_=d_cmap), writes=[t_ac], dma="cb")
            S.op("sp", lambda e: e.dma_start(out=SCR[:, 4392:4904].bitcast(F32), in_=d_force), writes=[t_ac], dma="c")
            S.op("pool", lambda e: e.dma_start(out=SCR[0:48, 4904:7976], in_=d_gsel), writes=[t_ac], dma="cb")
            HB = HR[:]
            OTG = TT(HB[:, 0:4096].rearrange("p (h t) -> p h t", h=8))
            GBC = HB[:, 4096:7168].rearrange("p (h b t) -> p h b t", h=2, b=3)
            t_gbc = Tok()
            XST = TT(HB[:, 7168:15360].bitcast(F32).rearrange("p (k t) -> p k t", k=8))
            BIAS = [HB[:, 15360 + i * 256:15360 + (i + 1) * 256].rearrange("p (r q) -> p r q", r=2) for i in range(2)]
            bias_t = [Tok(), Tok()]
            for i in range(2):
                S.op("dve", (lambda e, i=i: e.memset(BIAS[i], 0.0)), writes=[bias_t[i]])
            wo_v, wo_t = load_w_std(d_w_o[0], 0, 1024)
            ptc = {"n": 0, "b": 0, "a": 0}

            M2 = BIAS_EXTRA
            tri2 = M2[:, 0:256]
            anti2 = M2[:, 256:512]
            t_m2 = Tok()
            S.op("dve", lambda e: e.tensor_copy(out=tri2.rearrange("p (r q) -> p r q", r=2), in_=tri_b.unsqueeze(1).to_broadcast([128, 2, 128])), reads=[t_ac], writes=[t_m2])
            S.op("dve", lambda e: e.tensor_copy(out=anti2.rearrange("p (r q) -> p r q", r=2), in_=anti_b.unsqueeze(1).to_broadcast([128, 2, 128])), reads=[t_ac], writes=[t_m2])
            CM2 = [M2[:, 512 + i * 256:512 + (i + 1) * 256] for i in range(2)]
            cm2_t = [Tok(), Tok()]

            def next_pt():
                i = ptc["n"] % 3
                ptc["n"] += 1
                return PTS[i], pts_t[i]

            def qk_tile(bank, bt, colbase, KT, g, kt0, nk, par, qt, mask, bias_i, phase):
                lhs_k = KT.ap[par * 64:(par + 1) * 64, g, kt0:kt0 + nk] if KT is not None else kcT[par * 64:(par + 1) * 64, g, 0:127]
                k_reads = [KT.t(g, kt0 // TGW)] if KT is not None else [t_kc[g]]
                qsl = slice(qt * 128, (qt + 1) * 128)
                q_reads = [QT.t(2 * g, qt // 4), QT.t(2 * g + 1, qt // 4)]
                if phase == 1:
                    S.op("pe", (lambda e: e.matmul(bank[0:nk, colbase:colbase + 256], lhs_k, QT.ap[par * 64:(par + 1) * 64, 2 * g:2 * g + 2, qsl],
                                                   start=(mask is None), stop=True)), reads=k_reads + q_reads, writes=[bt])
                elif mask is None:
                    pass
                elif mask == "bias":
                    kt = kt0 // 128
                    S.op("pe", (lambda e: e.matmul(bank[:, colbase:colbase + 256], e128[:, kt, :], BIAS[bias_i].rearrange("p r q -> p (r q)"), start=True, stop=False)),
                         reads=[t_ac, bias_t[bias_i]], writes=[bt])
                else:
                    mask_ap, mask_reads = mask
                    S.op("pe", (lambda e: e.matmul(bank[0:nk, colbase:colbase + 256], ident_b[:, 0:nk], mask_ap, start=True, stop=False)),
                         reads=[t_l1c] + mask_reads, writes=[bt])

            def branch_steps(g, qt, kind, bias_i, pos, br, acc, acct):
                ps_o, pt_o = next_ps("O")
                out = []
                if kind == "cmp":
                    bA, tA = next_ps("S")
                    bB, tB = next_ps("S")
                    PT, ptt = next_pt()

                    ci = ptc["a"] % 2
                    ptc["a"] += 1

                    def qk():
                        S.op("dve", (lambda e: e.tensor_copy(out=CM2[ci].rearrange("p (r q) -> p r q", r=2),
                                                             in_=cmpm[:, qt * 128:(qt + 1) * 128].unsqueeze(1).to_broadcast([128, 2, 128]))),
                             reads=[t_ac], writes=[cm2_t[ci]])
                        for phase in range(2):
                            for par, bank, bt in ((0, bA, tA), (1, bB, tB)):
                                qk_tile(bank, bt, 0, None, g, 0, 127, par, qt, (CM2[ci], [cm2_t[ci]]), None, phase)
                        for par, bank, bt in ((0, bA, tA), (1, bB, tB)):
                            S.op("act", (lambda e, par=par, bank=bank: e.activation(out=PT[0:127, 0, par, :], in_=bank[0:127, 0:256], func=AF.Exp)),
                                 reads=[bt], writes=[ptt])
                        if qt >= 8:
                            topk_a(g, qt, PT, ptt)

                    def pv():
                        S.op("pe", (lambda e: e.matmul(ps_o[:], vcA[0:127, g, :], PT[0:127, 0, :, :].rearrange("p r c -> p (r c)"), start=True, stop=True)),
                             reads=[ptt, t_vc[g], t_vc_ones], writes=[pt_o])
                        combine(g, qt, pos, br, ps_o, pt_o, acc, acct)
                    return [(qk, pv)]
                KT, V = (KST, VS) if kind == "sel" else (KWT, VW)
                kts = list(range(0, qt + 1)) if kind == "sel" else list(range(max(0, qt - 4), qt + 1))
                for pi in range(0, len(kts), 2):
                    pair = kts[pi:pi + 2]
                    banks = (next_ps("S"), next_ps("S"))
                    PTp = next_pt()

                    def qk(pair=pair, banks=banks, PTp=PTp):
                        PT, ptt = PTp
                        npair = len(pair)
                        if kind == "sel" and qt >= 8 and pair[0] == 0:
                            topk_b(bias_i)
                        for ktp, kt in enumerate(pair):
                            if kt == qt:
                                mask = (tri2, [t_m2])
                            elif kind == "win" and kt == qt - 4:
                                mask = (anti2, [t_m2])
                            elif kind == "sel" and qt >= 8:
                                mask = "bias"
                            else:
                                mask = None
                            for phase in range(2):
                                for par, (bank, bt) in enumerate(banks):
                                    qk_tile(bank, bt, ktp * 256, KT, g, kt * 128, 128, par, qt, mask, bias_i, phase)
                        for par, (bank, bt) in enumerate(banks):
                            S.op("act", (lambda e, par=par, bank=bank: e.activation(
                                out=PT[:, 0:npair, par, :], in_=bank[:, 0:npair * 256].rearrange("p (k c) -> p k c", k=npair), func=AF.Exp)),
                                reads=[bt], writes=[ptt])

                    def pv(pair=pair, PTp=PTp, last=(pi + 2 >= len(kts))):
                        PT, ptt = PTp
                        for ktp, kt in enumerate(pair):
                            S.op("pe", (lambda e, ktp=ktp, kt=kt: e.matmul(ps_o[:], V.ap[:, kt, g, :], PT[:, ktp, :, :].rearrange("p r c -> p (r c)"),
                                                                         start=(kt == kts[0]), stop=(kt == kts[-1]))),
                                 reads=[ptt, V.t(kt), t_vones], writes=[pt_o])
                        if last:
                            combine(g, qt, pos, br, ps_o, pt_o, acc, acct)
                    out.append((qk, pv))
                return out

            def combine(g, qt, pos, br, ps_o, pt_o, acc, acct):
                qi = qt % 4
                tf, tft = next_tf()
                S.op("act", (lambda e: e.activation(out=tf[64:128, :], in_=ps_o[64:128, :], func=AF.Ln, bias=eps_t[64:128, 1:2], scale=1.0)), reads=[pt_o, t_eps], writes=[tft])
                tf2, tft2 = next_tf()
                S.op("act", (lambda e: e.activation(out=tf2[64:128, :], in_=tf[64:128, :], func=AF.Exp, scale=-1.0)), reads=[tft], writes=[tft2])
                on, ont = next_tf()
                for par in range(2):
                    S.op("dve", (lambda e, par=par: e.tensor_tensor(out=on[par * 64:(par + 1) * 64, 0:256], in0=ps_o[0:64, par * 256:(par + 1) * 256],
                                                                     in1=tf2[64:128, par * 256:(par + 1) * 256], op=ALU.mult)), reads=[pt_o, tft2], writes=[ont])
                onv = on[:, 0:256].rearrange("p (h q) -> p h q", h=2)
                accv = acc[:].rearrange("p (h q) -> p h q", h=2)
                Gv = GBC[:, :, br, qi * 128:(qi + 1) * 128]
                if pos == 0:
                    S.op("dve", (lambda e: e.tensor_tensor(out=accv, in0=onv, in1=Gv, op=ALU.mult)), reads=[ont, t_gbc], writes=[acct])
                else:
                    S.op("dve", (lambda e: e.tensor_tensor(out=onv, in0=onv, in1=Gv, op=ALU.mult)), reads=[ont, t_gbc], writes=[ont])
                    if pos == 1:
                        S.op("dve", (lambda e: e.tensor_tensor(out=accv, in0=accv, in1=onv, op=ALU.add)), reads=[ont, acct], writes=[acct])
                    else:
                        S.op("dve", (lambda e: e.tensor_tensor(out=OTG.ap[:, 2 * g:2 * g + 2, qi * 128:(qi + 1) * 128], in0=accv, in1=onv, op=ALU.add)),
                             reads=[ont, acct], writes=[OTG.t(2 * g, 0), OTG.t(2 * g + 1, 0)])

            def topk_a(g, qt, PT, ptt):
                ps_i, pt_i = next_ps("M")
                for c in range(4):
                    S.op("pe", (lambda e, c=c: e.matmul(ps_i[:, c * 33:(c + 1) * 33], PT[0:127, 0, c // 2, (c % 2) * 128:(c % 2 + 1) * 128], cmap[0:127, 0:33],
                                                          start=True, stop=True)), reads=[ptt, t_ac], writes=[pt_i])
                psv = ps_i[:, 0:132].rearrange("p (c j) -> p c j", c=4)
                S.op("dve", (lambda e: e.reciprocal(out=rd4[:], in_=psv[:, :, 32])), reads=[pt_i], writes=[t_tk])
                S.op("dve", (lambda e: e.tensor_scalar(out=scA[:], in0=psv[:, 0, 0:32], scalar1=rd4[:, 0:1], scalar2=None, op0=ALU.mult)), reads=[pt_i, t_tk], writes=[t_tk])
                for c in range(1, 4):
                    S.op("dve", (lambda e, c=c: e.scalar_tensor_tensor(out=scA[:], in0=psv[:, c, 0:32], scalar=rd4[:, c:c + 1], in1=scA[:], op0=ALU.mult, op1=ALU.add)),
                         reads=[pt_i, t_tk], writes=[t_tk])
                S.op("dve", (lambda e: e.tensor_tensor(out=scA[:], in0=scA[:], in1=force[:, qt - 8, :], op=ALU.add)), reads=[t_tk, t_ac], writes=[t_tk])
                S.op("dve", (lambda e: e.max(out=m8a[:], in_=scA[:])), reads=[t_tk], writes=[t_tk])
                S.op("dve", (lambda e: e.match_replace(out=scB[:], in_to_replace=m8a[:], in_values=scA[:], imm_value=-1e30)), reads=[t_tk], writes=[t_tk])
                S.op("dve", (lambda e: e.max(out=m8b[:], in_=scB[:])), reads=[t_tk], writes=[t_tk])
                S.op("dve", (lambda e: e.tensor_scalar(out=selm[:], in0=scA[:], scalar1=m8b[:, 7:8], scalar2=1.0, op0=ALU.is_ge, op1=ALU.subtract)), reads=[t_tk], writes=[t_tk])

            def topk_b(bias_i):
                ps_m, pt_m = next_ps("M")
                S.op("pe", (lambda e: e.matmul(ps_m[0:32, 0:128], selm[:], ident_b[:], start=True, stop=True)), reads=[t_tk, t_l1c], writes=[pt_m])
                S.op("act", (lambda e: e.activation(out=BIAS[bias_i][0:32, :, :], in_=ps_m[0:32, 0:128].unsqueeze(1).to_broadcast([32, 2, 128]), func=AF.Copy, scale=30000.0)),
                     reads=[pt_m], writes=[bias_t[bias_i]])

            on_t = [Tok(), Tok()]
            acc_t = [Tok(), Tok()]
            t_tk = Tok()
            g1 = mod_part(1, 2)
            steps = []

            def emit_xreload(tg):
                for kh in range(2):
                    ks = slice(kh * 4, kh * 4 + 4)
                    S.op("sp", (lambda e, ks=ks: e.dma_start(out=XST.ap[:, ks, :], in_=d_xs[:, ks, tsl(tg)])),
                         reads=[xs_t[tg]], writes=[XST.t(k) for k in range(kh * 4, kh * 4 + 4)], dma="xr")

            def emit_gbc(tg, g):
                for hpl in range(2):
                    for br in range(3):
                        ps, pt = next_ps("OM")
                        S.op("pe", (lambda e, ps=ps, hpl=hpl, br=br: e.matmul(ps[:], gsel[:, 2 * g + hpl, br, :], SIGG.ap[:, tsl(tg)], start=True, stop=True)),
                             reads=[t_ac, SIGG.t(tg)], writes=[pt])
                        S.op("act", (lambda e, ps=ps, hpl=hpl, br=br: e.activation(out=GBC[:, hpl, br, :], in_=ps[:], func=AF.Copy)), reads=[pt], writes=[t_gbc])

            def emit_wo(tg):
                def ev_o(oc, tg_, ps, pt):
                    S.op("dve", (lambda e: e.scalar_tensor_tensor(out=XST.ap[:, oc, :], in0=ps[:], scalar=g1[:, oc:oc + 1], in1=XST.ap[:, oc, :], op0=ALU.mult, op1=ALU.add)),
                         reads=[pt, XST.t(oc), t_modv[1]], writes=[XST.t(oc)])
                defpool["v"] = "OM"
                proj(wo_v, wo_t, OTG, 8, ev_o, tgs=[0])
                defpool["v"] = "P6"
                for kh in range(2):
                    ks = slice(kh * 4, kh * 4 + 4)
                    S.op("sp", (lambda e, ks=ks: e.dma_start(out=d_xs[:, ks, tsl(tg)], in_=XST.ap[:, ks, :])),
                         reads=[XST.t(k) for k in range(kh * 4, kh * 4 + 4)], writes=[xs_t[tg]], dma="xw")

            for tg in range(NTG):
                steps.append((None, (lambda tg=tg: emit_xreload(tg))))
                for g in range(G4):
                    steps.append((None, (lambda tg=tg, g=g: emit_gbc(tg, g))))
                    for qi in range(4):
                        qt = tg * 4 + qi
                        ai = ptc["b"] % 2
                        ptc["b"] += 1
                        acc, acct = ACC[ai], acc_t[ai]
                        steps += branch_steps(g, qt, "cmp", ai, 0, 0, acc, acct)
                        steps += branch_steps(g, qt, "win", ai, 1, 2, acc, acct)
                        steps += branch_steps(g, qt, "sel", ai, 2, 1, acc, acct)
                steps.append((None, (lambda tg=tg: emit_wo(tg))))
            prev_pv = None
            for qk_fn, pv_fn in steps:
                if qk_fn is not None:
                    qk_fn()
                if prev_pv is not None:
                    prev_pv()
                prev_pv = pv_fn
            if prev_pv is not None:
                prev_pv()
            S.barrier()
            for tg in range(NTG):
                for kh in range(2):
                    ks = slice(kh * 4, kh * 4 + 4)
                    S.op("sp", (lambda e, tg=tg, ks=ks: e.dma_start(out=X.ap[:, ks, tsl(tg)], in_=d_xs[:, ks, tsl(tg)])),
                         reads=[xs_t[tg]], writes=[X.t(k, tg) for k in range(kh * 4, kh * 4 + 4)], dma="x2")
            defpool["v"] = "ALL"
            if stop != "mix1":
                mlp(1, 1)
          except StopBuild:
            S.emit(final_dma_keys=["out"])
            return nc

        for tg in range(NTG):
            for kh in range(2):
                ks = slice(kh * 4, kh * 4 + 4)
                S.op("sp", (lambda e, tg=tg, ks=ks: e.dma_start(out=d_y[:, ks, tsl(tg)], in_=X.ap[:, ks, tsl(tg)])),
                     reads=[X.t(k, tg) for k in range(kh * 4, kh * 4 + 4)], dma="out")
        S.emit(final_dma_keys=["out"])
    return nc


def _fm(v):
    return np.ascontiguousarray(v.reshape(KC, 128).T)


def _const_tables():
    f32 = np.float32
    NEGM = -30000.0
    j = np.arange(128)[:, None]
    t = np.arange(128)[None, :]
    tri = np.where(j <= t, 0.0, NEGM)
    anti = np.where(j > t, 0.0, NEGM)
    n = np.arange(128)[:, None]
    tt = np.arange(T)[None, :]
    cmpm = np.where((16 * n + 31 <= tt) & (n < 127), 0.0, NEGM)
    amask = np.concatenate([tri, anti, cmpm], axis=1).astype(f32)
    e128 = np.zeros((128, 16, 128), f32)
    for kt in range(16):
        for jj in range(128):
            e128[2 * kt + jj // 64, kt, jj] = 1.0
    cmap = np.zeros((128, 40), f32)
    c0 = np.arange(127)[:, None] * 16
    s0 = np.arange(32)[None, :] * 64
    ov = np.minimum(c0 + 32, s0 + 64) - np.maximum(c0, s0)
    cmap[:127, :32] = np.clip(ov, 0, None) / 32.0
    cmap[:127, 32] = 1.0
    force = np.zeros((128, 8, 32), f32)
    for q in range(8):
        tq = 128 * (q + 8) + np.arange(128)
        cur = tq // 64
        jb = np.arange(32)[None, :]
        forced = (jb == 0) | (jb == cur[:, None]) | (jb == cur[:, None] - 1)
        force[:, q, :] = np.where(forced, 1e4, np.where(jb > cur[:, None], -1e4, 0.0))
    gsel = np.zeros((48, 8, 3, 128), f32)
    for hp in range(8):
        for br in range(3):
            for m in range(128):
                gsel[(2 * hp + m // 64) * 3 + br, hp, br, m] = 1.0
    p = np.arange(128)
    bd = (p[:, None] // 64 == p[None, :] // 64).astype(f32)
    return {"amask": amask, "e128": e128.reshape(128, 2048), "cmap": cmap, "force": force.reshape(128, 256),
            "gsel": gsel.reshape(48, 3072), "bdones": bd}


def prep_inputs(inputs):
    f32 = np.float32
    g = {k: np.asarray(v, dtype=f32) for k, v in inputs.items()}
    vecs = np.zeros((128, NV, KC), f32)
    for i in range(2):
        for j in range(2):
            vecs[:, V_NG + 2 * i + j, :] = _fm(g["norm_gain"][i, j])
        for part in range(6):
            vecs[:, V_BADA + 6 * i + part, :] = _fm(g["b_ada"][i, part * D:(part + 1) * D])
    vecs[:, V_KVG, :] = _fm(g["kv_norm_gain"])
    for part in range(2):
        vecs[:, V_BKV + part, :] = _fm(g["b_ada_kv"][part * D:(part + 1) * D])
    for j in range(3):
        vecs[:, V_CONV + j, :] = _fm(g["conv_w"][0, j])
    consts = np.concatenate([np.eye(128, dtype=f32), np.ones((128, 128), f32)], axis=1)
    p64 = np.arange(128) % 64
    hvecs = np.zeros((128, 4), f32)
    hvecs[:, 0] = g["q_gain"][0, p64]
    for i in range(3):
        hvecs[:, 1 + i] = g["k_gain"][i, p64]
    peT = np.zeros((128, 64), f32)
    for kv in range(2):
        peT[:, kv * 32:(kv + 1) * 32] = g["cmp_pe"][kv][:, p64].T
    shared = {
        "vecs": vecs, "consts": consts, "hvecs": hvecs, "peT": peT,
        "w_ada": g["w_ada"], "w_a_in": g["w_a_in"], "w_a_out": g["w_a_out"],
        "w_mlp1": g["w_mlp1"], "w_mlp2": g["w_mlp2"],
        "w_ada_kv": g["w_ada_kv"], "w_kv": g["w_kv"], "w_qg": g["w_qg"], "w_o": g["w_o"],
        "cmp_w1": g["cmp_w1"], "cmp_w2": g["cmp_w2"],
    }
    shared.update(_const_tables())
    in_maps = []
    for b in range(N_CORES):
        m = dict(shared)
        xT = g["x"][b].T.reshape(KC, 128, T).transpose(1, 0, 2)
        m["xT"] = np.ascontiguousarray(xT)
        m["cT"] = _fm(g["c"][b])
        in_maps.append(m)
    return in_maps


def post_outputs(results):
    outs = []
    for r in results:
        yT = np.asarray(r["yT"])
        outs.append(yT.transpose(2, 1, 0).reshape(T, D))
    return np.stack(outs, axis=0).astype(np.float32)


def kernel(**inputs):
    in_maps = prep_inputs(inputs)
    nc = build_program(DEBUG_STOP)
    res = run_bass_kernel_spmd(nc, in_maps, core_ids=list(range(N_CORES)))
    return post_outputs(res.results)
```

```python
import numpy as np
from contextlib import ExitStack
import concourse.bass as bass
import concourse.mybir as mybir
from concourse.bass_utils import run_bass_kernel_spmd

F32 = mybir.dt.float32
BF16 = mybir.dt.bfloat16
AF = mybir.ActivationFunctionType
ALU = mybir.AluOpType

D = 1024
T = 2048
KC = 8
NTG = 4
TGW = 512
EPS = 1e-6
N_CORES = 8

V_NG = 0
V_KVG = 4
V_BADA = 5
V_BKV = 17
V_CONV = 19
NV = 22

DEBUG_STOP = None


class Tok:
    __slots__ = ("ws", "rs", "rdma", "excl")

    def __init__(self, excl=False):
        self.ws = []
        self.rs = {}
        self.rdma = []
        self.excl = excl


class Op:
    __slots__ = ("eng", "fn", "deps", "dma", "signal", "sigval", "dmaval", "dsem")

    def __init__(self, eng, fn, dma):
        self.eng = eng
        self.fn = fn
        self.dma = dma
        self.deps = ()
        self.signal = False
        self.sigval = 0
        self.dmaval = 0
        self.dsem = None


ENGS = ["pe", "act", "dve", "pool", "sp"]
DMA_SEMS = 16


class Sched:
    def __init__(self, nc):
        self.nc = nc
        self.ops = {e: [] for e in ENGS}
        self.dma_hist = {e: [] for e in ENGS}

    def op(self, eng, fn, reads=(), writes=(), dma=None):
        o = Op(eng, fn, dma)
        ex = [t for t in reads if t.excl]
        if ex:
            reads = [t for t in reads if not t.excl]
            writes = list(writes) + ex
        deps = set()
        for t in reads:
            deps.update(t.ws)
        for t in writes:
            deps.update(t.ws)
            deps.update(t.rs.values())
            deps.update(t.rdma)
        if dma is not None:
            deps = {d for d in deps if d.dma != dma}
            hist = self.dma_hist[eng]
            n = len(hist)
            o.dsem = (eng, n % DMA_SEMS)
            o.dmaval = 16 * (n // DMA_SEMS + 1)
            if n >= DMA_SEMS:
                deps.add(hist[n - DMA_SEMS])
            hist.append(o)
        o.deps = tuple(deps)
        for t in reads:
            if dma is not None:
                t.rdma.append(o)
            else:
                t.rs[eng] = o
        for t in writes:
            if dma is not None and t.ws and not t.rs and not t.rdma and all(w.dma == dma for w in t.ws):
                t.ws.append(o)
            else:
                t.ws = [o]
            t.rs = {}
            t.rdma = []
        self.ops[eng].append(o)
        return o

    def barrier(self, engines=("pe", "act", "dve", "sp", "pool")):
        lasts = []
        for e in ENGS:
            for o in reversed(self.ops[e]):
                if o.dma is None and o.fn is not None:
                    lasts.append(o)
                    break
            lasts += self.dma_hist[e][-DMA_SEMS:]
        for e in engines:
            o = Op(e, None, None)
            o.deps = tuple(lasts)
            self.ops[e].append(o)

    def emit(self, final_dma_keys=()):
        nc = self.nc
        for e in ENGS:
            for o in self.ops[e]:
                for d in o.deps:
                    if d.dma is None:
                        if d.eng == "pe" and o.eng == "pe" and o.dma is None and o.fn is not None:
                            continue
                        d.signal = True
        for e in ENGS:
            c = 0
            for o in self.ops[e]:
                if o.dma is None and o.signal:
                    c += 1
                    o.sigval = c
        with ExitStack() as es:
            esem = {e: es.enter_context(nc.semaphore("s_" + e)) for e in ENGS}
            dsem = {}
            for e in ENGS:
                for i in range(min(DMA_SEMS, len(self.dma_hist[e]))):
                    dsem[(e, i)] = es.enter_context(nc.semaphore("d_%s%d" % (e, i)))
            block = es.enter_context(nc.Block())

            def run(e, eng):
                waited = {}
                for o in self.ops[e]:
                    need = {}
                    for d in o.deps:
                        if d.dma is not None:
                            key, sem, val = ("d",) + d.dsem, dsem[d.dsem], d.dmaval
                        else:
                            if d.eng == "pe" and e == "pe" and o.dma is None and o.fn is not None:
                                continue
                            key, sem, val = ("e", d.eng), esem[d.eng], d.sigval
                        if val > need.get(key, (None, 0))[1]:
                            need[key] = (sem, val)
                    for key, (sem, val) in need.items():
                        if waited.get(key, 0) >= val:
                            continue
                        waited[key] = val
                        eng.wait_ge(sem, val)
                    if o.fn is None:
                        continue
                    ins = o.fn(eng)
                    if o.dma is not None:
                        ins.then_inc(dsem[o.dsem], 16)
                    elif o.signal:
                        ins.then_inc(esem[e], 1)
                if e == "sp":
                    fin = {}
                    for q in ENGS:
                        for d in self.dma_hist[q]:
                            if d.dma in final_dma_keys:
                                fin[d.dsem] = max(fin.get(d.dsem, 0), d.dmaval)
                    for k, v in fin.items():
                        if waited.get(("d",) + k, 0) < v:
                            eng.wait_ge(dsem[k], v)

            block.sync(lambda eng: run("sp", eng))
            block.scalar(lambda eng: run("act", eng))
            block.vector(lambda eng: run("dve", eng))
            block.gpsimd(lambda eng: run("pool", eng))
            block.tensor(lambda eng: run("pe", eng))


class StopBuild(Exception):
    pass


class TT:
    def __init__(self, ap):
        self.ap = ap
        self.toks = {}

    def t(self, *key):
        tk = self.toks.get(key)
        if tk is None:
            tk = self.toks[key] = Tok()
        return tk

    def all(self):
        return list(self.toks.values())


def build_program(stop=None):
    nc = bass.Bass("TRN2", target_bir_lowering=False)

    def din(name, shape):
        return nc.dram_tensor(name, list(shape), F32, kind="ExternalInput").ap()

    d_x = din("xT", [128, KC, T])
    d_c = din("cT", [128, KC])
    d_vecs = din("vecs", [128, NV, KC])
    d_consts = din("consts", [128, 256])
    d_w_ada = din("w_ada", [2, D, 6 * D])
    d_w_a_in = din("w_a_in", [1, D, 3 * D])
    d_w_a_out = din("w_a_out", [1, D, D])
    d_w_mlp1 = din("w_mlp1", [2, D, 4 * D])
    d_w_mlp2 = din("w_mlp2", [2, 4 * D, D])
    d_w_ada_kv = din("w_ada_kv", [D, 2 * D])
    d_w_kv = din("w_kv", [D, 1536])
    d_w_qg = din("w_qg", [1, D, 1072])
    d_w_o = din("w_o", [1, D, D])
    d_cmp_w1 = din("cmp_w1", [2, 2048, 256])
    d_cmp_w2 = din("cmp_w2", [2, 256, 64])
    d_hvecs = din("hvecs", [128, 4])
    d_peT = din("peT", [128, 64])
    d_amask = din("amask", [128, 256 + 2048])
    d_e128 = din("e128", [128, 2048])
    d_cmap = din("cmap", [128, 40])
    d_force = din("force", [128, 256])
    d_gsel = din("gsel", [48, 3072])
    d_bd = din("bdones", [128, 128])
    d_y = nc.dram_tensor("yT", [128, KC, T], F32, kind="ExternalOutput").ap()
    d_xs = nc.dram_tensor("xs_scratch", [128, KC, T], F32).ap()

    S = Sched(nc)
    es = ExitStack()
    with es:
        def sb(name, shape, dt):
            return es.enter_context(nc.sbuf_tensor(name, list(shape), dt))

        XR = sb("XR", [128, KC * T], F32)
        HR = sb("HR", [128, KC * T], BF16)
        AR = sb("AR", [128, KC * T], BF16)
        RING = [sb("RING%d" % i, [128, 8192], BF16) for i in range(2)]
        ring_t = [Tok(), Tok()]
        rstd_s = sb("rstd", [128, T], F32)
        TMPF = [sb("tmpf%d" % i, [128, TGW], F32) for i in range(3)]
        tmpf_t = [Tok() for _ in TMPF]
        TMPB = [sb("tmpb%d" % i, [128, TGW], BF16) for i in range(3)]
        tmpb_t = [Tok() for _ in TMPB]
        SCR = sb("SCR", [128, 11048], BF16)
        ones_b = sb("ones_b", [128, 128], BF16)
        vecs = sb("vecs_s", [128, NV, KC], F32)
        c_f = sb("c_f", [128, KC], F32)
        cact = sb("cact", [128, KC], BF16)
        modv = sb("modv", [128, 3, 48], F32)
        der = sb("der", [128, 8, KC], F32)
        hvecs = sb("hvecs_s", [128, 4], F32)
        gq = sb("gq", [128, 1], F32)
        ident_b = sb("ident_b", [128, 128], BF16)
        bd_ones = sb("bd_ones", [128, 128], BF16)
        cbias = sb("cbias", [128, 4], F32)
        kcT = sb("kcT", [128, 4, 128], BF16)
        vcA = sb("vcA", [128, 4, 128], BF16)
        rd4 = sb("rd4", [128, 4], F32)
        scA = sb("scA", [128, 32], F32)
        scB = sb("scB", [128, 32], F32)
        m8a = sb("m8a", [128, 8], F32)
        m8b = sb("m8b", [128, 8], F32)
        selm = sb("selm", [128, 32], BF16)
        ACC = [sb("acc%d" % i, [128, 256], F32) for i in range(2)]
        BIAS_EXTRA = sb("m2", [128, 1024], BF16)
        PS = [es.enter_context(nc.psum_tensor("ps%d" % i, [128, TGW], F32)) for i in range(8)]
        ps_t = [Tok(excl=True) for _ in PS]

        X = TT(XR[:].rearrange("p (k t) -> p k t", k=KC))
        H = TT(HR[:].rearrange("p (k t) -> p k t", k=KC))
        A = TT(AR[:].rearrange("p (k t) -> p k t", k=KC))
        t_const = Tok()
        eps_t = sb("eps_t", [128, 2], F32)
        t_eps = Tok()
        S.op("dve", lambda e: e.memset(eps_t[:, 0:1], EPS), writes=[t_eps])
        S.op("dve", lambda e: e.memset(eps_t[:, 1:2], 1e-30), writes=[t_eps])
        t_vecs = Tok()
        t_c = Tok()
        t_cact = Tok()
        t_modv = [Tok(), Tok(), Tok()]
        t_der = Tok()
        t_rstd = [Tok() for _ in range(NTG)]

        cnt = {"ps": 0, "ring": 0, "tf": 0, "tb": 0}

        POOLS = {"ALL": list(range(8)), "P6": [0, 1, 2, 3, 4, 5], "S": [0, 1, 2, 3], "O": [4, 5, 6], "M": [7], "OM": [4, 5, 6, 7]}
        pcnt = {k: 0 for k in POOLS}

        defpool = {"v": "ALL"}

        def next_ps(pool=None):
            pool = pool or defpool["v"]
            lst = POOLS[pool]
            i = lst[pcnt[pool] % len(lst)]
            pcnt[pool] += 1
            return PS[i], ps_t[i]

        def next_tf():
            i = cnt["tf"] % len(TMPF)
            cnt["tf"] += 1
            return TMPF[i], tmpf_t[i]

        def next_tb():
            i = cnt["tb"] % len(TMPB)
            cnt["tb"] += 1
            return TMPB[i], tmpb_t[i]

        def tsl(tg):
            return slice(tg * TGW, (tg + 1) * TGW)

        S.op("pool", lambda e: e.dma_start(out=ones_b[:], in_=d_consts[:, 128:256]), writes=[t_const], dma="cb")
        S.op("sp", lambda e: e.dma_start(out=vecs[:], in_=d_vecs), writes=[t_vecs], dma="c")
        S.op("sp", lambda e: e.dma_start(out=c_f[:], in_=d_c), writes=[t_c], dma="c")
        for tg in range(NTG):
            for kh in range(2):
                ks = slice(kh * 4, kh * 4 + 4)
                S.op("sp", (lambda e, tg=tg, ks=ks: e.dma_start(out=X.ap[:, ks, tsl(tg)], in_=d_x[:, ks, tsl(tg)])),
                     writes=[X.t(k, tg) for k in range(kh * 4, kh * 4 + 4)], dma="x")

        S.op("act", lambda e: e.activation(out=cact[:], in_=c_f[:], func=AF.Silu), reads=[t_c], writes=[t_cact])

        def load_w(src_aps, dst_view_fn):
            i = cnt["ring"] % 2
            cnt["ring"] += 1
            slot = RING[i]
            for dst_fn, src in src_aps:
                S.op("pool", (lambda e, dst_fn=dst_fn, src=src, slot=slot: e.dma_start(out=dst_fn(slot), in_=src)),
                     writes=[ring_t[i]], dma="r%d" % i)
            return dst_view_fn(slot), ring_t[i]

        def load_w_std(w2d, c0, ncols, k0=0):
            src = w2d.rearrange("(k p) n -> p k n", p=128)

            def view(slot):
                return slot[:, 0:8 * ncols].rearrange("p (k c) -> p k c", k=8)
            pieces = []
            for kh in range(2):
                ks = slice(kh * 4, kh * 4 + 4)
                pieces.append(((lambda slot, ks=ks: view(slot)[:, ks, :]), src[:, k0 + kh * 4:k0 + kh * 4 + 4, c0:c0 + ncols]))
            return load_w(pieces, view)

        def ada_matvec(w2d, ncb, bias_idx, mi):
            ps, pt = next_ps()
            for cb in range(ncb):
                wv, wt = load_w_std(w2d, cb * 1024, 1024)
                for oc in range(8):
                    col = cb * 8 + oc
                    for k in range(KC):
                        S.op("pe", (lambda e, ps=ps, wv=wv, oc=oc, k=k, col=col: e.matmul(
                            ps[:, col:col + 1], wv[:, k, oc * 128:(oc + 1) * 128], cact[:, k:k + 1],
                            start=(k == 0), stop=(k == KC - 1))), reads=[wt, t_cact], writes=[pt])
            n = ncb * 8
            S.op("dve", (lambda e, ps=ps, n=n: e.tensor_tensor(
                out=modv[:, mi, 0:n], in0=ps[:, 0:n],
                in1=vecs[:, bias_idx:bias_idx + ncb, :].rearrange("p a b -> p (a b)"), op=ALU.add)),
                reads=[pt, t_vecs], writes=[t_modv[mi]])

        def mod_part(mi, part):
            return modv[:, mi, part * 8:(part + 1) * 8]

        def derive(mi, part_sc, gain_idx, dst):
            S.op("dve", lambda e: e.tensor_scalar(out=der[:, dst, :], in0=mod_part(mi, part_sc), scalar1=1.0, scalar2=1.0,
                                                  op0=ALU.add, op1=ALU.mult), reads=[t_modv[mi]], writes=[t_der])
            S.op("dve", lambda e: e.tensor_tensor(out=der[:, dst, :], in0=der[:, dst, :], in1=vecs[:, gain_idx, :], op=ALU.mult),
                 reads=[t_der, t_vecs], writes=[t_der])

        def compute_rstd():
            for tg in range(NTG):
                ps, pt = next_ps()
                for k in range(KC):
                    tb, tbt = next_tb()
                    S.op("act", (lambda e, tb=tb, k=k, tg=tg: e.activation(out=tb[:], in_=X.ap[:, k, tsl(tg)], func=AF.Square)),
                         reads=[X.t(k, tg)], writes=[tbt])
                    S.op("pe", (lambda e, ps=ps, tb=tb, k=k: e.matmul(ps[:], ones_b[:], tb[:], start=(k == 0), stop=(k == KC - 1))),
                         reads=[tbt, t_const], writes=[pt])
                tf, tft = next_tf()
                S.op("act", (lambda e, ps=ps, tf=tf: e.activation(out=tf[:], in_=ps[:], func=AF.Ln, bias=eps_t[:, 0:1], scale=1.0 / D)),
                     reads=[pt, t_eps], writes=[tft])
                S.op("act", (lambda e, tf=tf, tg=tg: e.activation(out=rstd_s[:, tsl(tg)], in_=tf[:], func=AF.Exp, scale=-0.5)), reads=[tft], writes=[t_rstd[tg]])

        def norm_mod(dst, a_ap, b_ap, extra_reads):
            for tg in range(NTG):
                for k in range(KC):
                    tf, tft = next_tf()
                    S.op("dve", (lambda e, tf=tf, k=k, tg=tg: e.tensor_tensor(out=tf[:], in0=X.ap[:, k, tsl(tg)], in1=rstd_s[:, tsl(tg)], op=ALU.mult)),
                         reads=[X.t(k, tg), t_rstd[tg]], writes=[tft])
                    S.op("act", (lambda e, tf=tf, k=k, tg=tg: e.activation(out=dst.ap[:, k, tsl(tg)], in_=tf[:], func=AF.Identity,
                                                                            bias=b_ap[:, k:k + 1], scale=a_ap[:, k:k + 1])),
                         reads=[tft] + extra_reads, writes=[dst.t(k, tg)])

        def proj(wv, wt, src, n_oc, evac, tgs=range(NTG), oc_cols=None):
            for tg in tgs:
                for oc in range(n_oc):
                    ps, pt = next_ps()
                    for k in range(KC):
                        lhs = wv[:, k, oc * 128:(oc + 1) * 128] if oc_cols is None else oc_cols(wv, k, oc)
                        S.op("pe", (lambda e, ps=ps, lhs=lhs, k=k, tg=tg: e.matmul(ps[:], lhs, src.ap[:, k, tsl(tg)],
                                                                                   start=(k == 0), stop=(k == KC - 1))),
                             reads=[wt, src.t(k, tg)], writes=[pt])
                    evac(oc, tg, ps, pt)

        def resid_evac(g_ap, extra_reads):
            def ev(oc, tg, ps, pt):
                S.op("dve", (lambda e: e.scalar_tensor_tensor(out=X.ap[:, oc, tsl(tg)], in0=ps[:], scalar=g_ap[:, oc:oc + 1],
                                                              in1=X.ap[:, oc, tsl(tg)], op0=ALU.mult, op1=ALU.add)),
                     reads=[pt, X.t(oc, tg)] + extra_reads, writes=[X.t(oc, tg)])
            return ev

        def mlp(layer, mi):
            compute_rstd()
            derive(mi, 4, V_NG + 2 * layer + 1, 1)
            norm_mod(H, der[:, 1, :], mod_part(mi, 3), [t_der, t_modv[mi]])
            g2 = mod_part(mi, 5)
            for hb in range(4):
                wv, wt = load_w_std(d_w_mlp1[layer], hb * 1024, 1024)

                def ev1(oc, tg, ps, pt):
                    tf, tft = next_tf()
                    S.op("act", (lambda e: e.activation(out=tf[:], in_=ps[:], func=AF.Relu)), reads=[pt], writes=[tft])
                    S.op("dve", (lambda e: e.tensor_tensor(out=A.ap[:, oc, tsl(tg)], in0=tf[:], in1=tf[:], op=ALU.mult)),
                         reads=[tft], writes=[A.t(oc, tg)])
                proj(wv, wt, H, 8, ev1)
                wv2, wt2 = load_w_std(d_w_mlp2[layer], 0, 1024, k0=hb * 8)
                proj(wv2, wt2, A, 8, resid_evac(g2, [t_modv[mi]]))

        ada_matvec(d_w_ada[0], 6, V_BADA, 0)
        compute_rstd()
        derive(0, 1, V_NG + 0, 0)
        norm_mod(H, der[:, 0, :], mod_part(0, 0), [t_der, t_modv[0]])

        gbv = SCR[:, 0:4096].rearrange("p (j t) -> p j t", j=2)
        vv = SCR[:, 4096:4096 + 2 * 2056].rearrange("p (j t) -> p j t", j=2)
        t_gb = [[Tok() for _ in range(NTG)] for _ in range(2)]
        t_v = [[Tok() for _ in range(NTG)] for _ in range(2)]
        t_vhalo = Tok()
        S.op("dve", lambda e: e.memset(vv[:, :, 0:2], 0.0), writes=[t_vhalo])
        w_in_v = d_w_a_in[0].rearrange("(k p) (s c) -> p k s c", p=128, s=3)
        g1 = mod_part(0, 2)
        def mixer_j(wv, wt, jj, j):
            jb = j % 2
            for tg in range(NTG):
                pss = []
                for s in range(3):
                    ps, pt = next_ps()
                    for k in range(KC):
                        S.op("pe", (lambda e, ps=ps, s=s, k=k, tg=tg: e.matmul(ps[:], wv[:, k, s, jj * 128:(jj + 1) * 128], H.ap[:, k, tsl(tg)],
                                                                               start=(k == 0), stop=(k == KC - 1))),
                             reads=[wt, H.t(k, tg)], writes=[pt])
                    pss.append((ps, pt))
                (psb, ptb), (psc, ptc), (psu, ptu) = pss
                S.op("act", (lambda e, psb=psb, tg=tg: e.activation(out=gbv[:, jb, tsl(tg)], in_=psb[:], func=AF.Copy)),
                     reads=[ptb], writes=[t_gb[jb][tg]])
                tb, tbt = next_tb()
                S.op("act", (lambda e, psc=psc, tb=tb: e.activation(out=tb[:], in_=psc[:], func=AF.Copy)), reads=[ptc], writes=[tbt])
                S.op("dve", (lambda e, psu=psu, tb=tb, tg=tg: e.tensor_tensor(out=vv[:, jb, 2 + tg * TGW:2 + (tg + 1) * TGW], in0=psu[:], in1=tb[:], op=ALU.mult)),
                     reads=[ptu, tbt], writes=[t_v[jb][tg]])
            for tg in range(NTG):
                tf, tft = next_tf()
                rd = [t_v[jb][tg], t_vhalo, t_vecs] + ([t_v[jb][tg - 1]] if tg > 0 else [])
                b0 = tg * TGW
                S.op("dve", (lambda e, tf=tf, b0=b0: e.tensor_scalar(out=tf[:], in0=vv[:, jb, b0 + 2:b0 + 2 + TGW], scalar1=vecs[:, V_CONV + 2, j:j + 1], scalar2=None, op0=ALU.mult)),
                     reads=rd, writes=[tft])
                S.op("dve", (lambda e, tf=tf, b0=b0: e.scalar_tensor_tensor(out=tf[:], in0=vv[:, jb, b0 + 1:b0 + 1 + TGW], scalar=vecs[:, V_CONV + 1, j:j + 1], in1=tf[:], op0=ALU.mult, op1=ALU.add)),
                     reads=rd + [tft], writes=[tft])
                S.op("dve", (lambda e, tf=tf, b0=b0: e.scalar_tensor_tensor(out=tf[:], in0=vv[:, jb, b0:b0 + TGW], scalar=vecs[:, V_CONV + 0, j:j + 1], in1=tf[:], op0=ALU.mult, op1=ALU.add)),
                     reads=rd + [tft], writes=[tft])
                S.op("dve", (lambda e, tf=tf, tg=tg: e.tensor_tensor(out=A.ap[:, j, tsl(tg)], in0=tf[:], in1=gbv[:, jb, tsl(tg)], op=ALU.mult)),
                     reads=[tft, t_gb[jb][tg]], writes=[A.t(j, tg)])

        for jp in range(4):
            def view(slot):
                return slot[:, 0:6144].rearrange("p (k s c) -> p k s c", k=8, s=3)
            pieces = []
            for s3 in range(3):
                pieces.append(((lambda slot, s3=s3: view(slot)[:, :, s3, :]), w_in_v[:, :, s3, jp * 256:(jp + 1) * 256]))
            wv, wt = load_w(pieces, view)
            for jj in range(2):
                mixer_j(wv, wt, jj, 2 * jp + jj)
        wv, wt = load_w_std(d_w_a_out[0], 0, 1024)
        proj(wv, wt, A, 8, resid_evac(g1, [t_modv[0]]))
        if stop != "mix0":
            mlp(0, 0)
        def chk(name, dumps):
            if stop != name:
                return
            S.barrier()
            for ap, dst in dumps:
                S.op("pool", (lambda e, ap=ap, dst=dst: e.dma_start(out=dst, in_=ap)), dma="out")
            raise StopBuild()

        if stop not in ("mix0", "l0"):
          try:
            G4 = 4
            defpool["v"] = "P6"
            t_l1c = Tok()
            wqg_g = SCR[:, 8192:8576].rearrange("p (k c) -> p k c", k=8)
            cw2k = SCR[:, 8576:8832].rearrange("p (c d) -> p c d", c=2)
            cw2v = SCR[:, 8832:8960].rearrange("p (c d) -> p c d", c=2)
            peT_b = SCR[:, 8960:9024]
            S.op("sp", lambda e: e.dma_start(out=hvecs[:], in_=d_hvecs), writes=[t_l1c], dma="c")
            S.op("pool", lambda e: e.dma_start(out=ident_b[:], in_=d_consts[:, 0:128]), writes=[t_l1c], dma="cb")
            S.op("pool", lambda e: e.dma_start(out=bd_ones[:], in_=d_bd), writes=[t_l1c], dma="cb")
            S.op("pool", lambda e: e.dma_start(out=peT_b, in_=d_peT), writes=[t_l1c], dma="cb")
            S.op("pool", lambda e: e.dma_start(out=cw2k[:, :, 0:64], in_=d_cmp_w2[0].rearrange("(c p) d -> p c d", p=128)), writes=[t_l1c], dma="cb")
            S.op("pool", lambda e: e.dma_start(out=cw2k[:, :, 64:128], in_=d_cmp_w2[0].rearrange("(c p) d -> p c d", p=128)), writes=[t_l1c], dma="cb")
            S.op("pool", lambda e: e.dma_start(out=cw2v, in_=d_cmp_w2[1].rearrange("(c p) d -> p c d", p=128)), writes=[t_l1c], dma="cb")
            S.op("pool", lambda e: e.dma_start(out=wqg_g, in_=d_w_qg[0].rearrange("(k p) n -> p k n", p=128)[:, :, 1024:1072]), writes=[t_l1c], dma="cb")
            t_gq = Tok()
            S.op("dve", lambda e: e.tensor_scalar(out=gq[:], in0=hvecs[:, 0:1], scalar1=0.125, scalar2=None, op0=ALU.mult), reads=[t_l1c], writes=[t_gq])

            ada_matvec(d_w_ada[1], 6, V_BADA + 6, 1)
            ada_matvec(d_w_ada_kv, 2, V_BKV, 2)
            compute_rstd()
            derive(2, 1, V_KVG, 2)
            derive(1, 1, V_NG + 2, 3)
            norm_mod(H, der[:, 2, :], mod_part(2, 0), [t_der, t_modv[2]])
            norm_mod(A, der[:, 3, :], mod_part(1, 0), [t_der, t_modv[1]])
            xs_t = [Tok() for _ in range(NTG)]
            for tg in range(NTG):
                for kh in range(2):
                    ks = slice(kh * 4, kh * 4 + 4)
                    S.op("sp", (lambda e, tg=tg, ks=ks: e.dma_start(out=d_xs[:, ks, tsl(tg)], in_=X.ap[:, ks, tsl(tg)])),
                         reads=[X.t(k, tg) for k in range(kh * 4, kh * 4 + 4)], writes=[xs_t[tg]], dma="xs")

            def head_norm(ps, pt, ncol, gain_ap, gain_reads, dst_ap, dst_toks):
                tb, tbt = next_tb()
                S.op("act", (lambda e: e.activation(out=tb[:, 0:ncol], in_=ps[:, 0:ncol], func=AF.Square)), reads=[pt], writes=[tbt])
                ps2, pt2 = next_ps("M")
                S.op("pe", (lambda e: e.matmul(ps2[:, 0:ncol], bd_ones[:], tb[:, 0:ncol], start=True, stop=True)), reads=[tbt, t_l1c], writes=[pt2])
                tf, tft = next_tf()
                S.op("act", (lambda e: e.activation(out=tf[:, 0:ncol], in_=ps2[:, 0:ncol], func=AF.Ln, bias=eps_t[:, 0:1], scale=1.0 / 64)),
                     reads=[pt2, t_eps], writes=[tft])
                tf2, tft2 = next_tf()
                S.op("act", (lambda e: e.activation(out=tf2[:, 0:ncol], in_=tf[:, 0:ncol], func=AF.Exp, scale=-0.5)), reads=[tft], writes=[tft2])
                S.op("dve", (lambda e: e.scalar_tensor_tensor(out=dst_ap, in0=ps[:, 0:ncol], scalar=gain_ap, in1=tf2[:, 0:ncol], op0=ALU.mult, op1=ALU.mult)),
                     reads=[pt, tft2] + gain_reads, writes=dst_toks)

            RAW = TT(SCR[:, 0:8192].rearrange("p (c t) -> p c t", c=4))
            wv, wt = load_w_std(d_w_kv, 0, 512)

            def ev_raw(oc, tg, ps, pt):
                S.op("act", (lambda e: e.activation(out=RAW.ap[:, oc, tsl(tg)], in_=ps[:], func=AF.Copy)), reads=[pt], writes=[RAW.t(oc, tg)])
            proj(wv, wt, H, 4, ev_raw)

            chk("raw", [(RAW.ap, d_y[:, 0:4, :])])
            t_kc = [Tok() for _ in range(G4)]
            t_vc = [Tok() for _ in range(G4)]
            t_vc_ones = Tok()
            S.op("dve", lambda e: e.memset(vcA[:, :, 64:128], 1.0), writes=[t_vc_ones])
            t_cb = Tok()
            for kv in range(2):
                def view1(slot):
                    return slot[:, 0:8192].rearrange("p (l h) -> p l h", l=32)
                src1 = d_cmp_w1[kv].rearrange("(l d) h -> d l h", d=64)
                pieces = [((lambda slot: view1(slot)[0:64, :, :]), src1), ((lambda slot: view1(slot)[64:128, :, :]), src1)]
                cwv, cwt = load_w(pieces, view1)
                psb, ptb = next_ps()
                for hc in range(2):
                    for l in range(32):
                        S.op("pe", (lambda e, hc=hc, l=l, cwv=cwv, psb=psb, kv=kv: e.matmul(
                            psb[:, hc:hc + 1], cwv[0:64, l, hc * 128:(hc + 1) * 128], peT_b[0:64, kv * 32 + l:kv * 32 + l + 1],
                            start=(l == 0), stop=(l == 31))), reads=[cwt, t_l1c], writes=[ptb])
                S.op("dve", (lambda e, psb=psb, kv=kv: e.tensor_copy(out=cbias[:, 2 * kv:2 * kv + 2], in_=psb[:, 0:2])), reads=[ptb], writes=[t_cb])
                for g in range(G4):
                    base = (g % 2) * 64
                    c = kv * 2 + g // 2
                    hids = []
                    for hc in range(2):
                        ps, pt = next_ps()
                        for l in range(32):
                            S.op("pe", (lambda e, ps=ps, l=l, hc=hc, cwv=cwv, base=base, c=c: e.matmul(
                                ps[:, 0:127], cwv[base:base + 64, l, hc * 128:(hc + 1) * 128],
                                RAW.ap[base:base + 64, c, l:l + 16 * 126 + 1:16], start=(l == 0), stop=(l == 31))),
                                reads=[cwt] + [RAW.t(c, tg) for tg in range(NTG)], writes=[pt])
                        z, zt = next_tf()
                        S.op("act", (lambda e, ps=ps, z=z, hc=hc, kv=kv: e.activation(out=z[:, 0:127], in_=ps[:, 0:127], func=AF.Identity,
                                                                                      bias=cbias[:, 2 * kv + hc:2 * kv + hc + 1], scale=1.0)),
                             reads=[pt, t_cb], writes=[zt])
                        u, ut = next_tf()
                        S.op("dve", (lambda e, z=z, u=u: e.tensor_tensor(out=u[:, 0:127], in0=z[:, 0:127], in1=z[:, 0:127], op=ALU.mult)), reads=[zt], writes=[ut])
                        S.op("dve", (lambda e, u=u: e.tensor_scalar(out=u[:, 0:127], in0=u[:, 0:127], scalar1=0.044715, scalar2=1.0, op0=ALU.mult, op1=ALU.add)),
                             reads=[ut], writes=[ut])
                        S.op("dve", (lambda e, z=z, u=u: e.tensor_tensor(out=u[:, 0:127], in0=u[:, 0:127], in1=z[:, 0:127], op=ALU.mult)), reads=[ut, zt], writes=[ut])
                        S.op("act", (lambda e, u=u: e.activation(out=u[:, 0:127], in_=u[:, 0:127], func=AF.Sigmoid, scale=1.5957691216057308)), reads=[ut], writes=[ut])
                        hb_, hbt = next_tb()
                        S.op("dve", (lambda e, z=z, u=u, hb_=hb_: e.tensor_tensor(out=hb_[:, 0:127], in0=u[:, 0:127], in1=z[:, 0:127], op=ALU.mult)), reads=[ut, zt], writes=[hbt])
                        hids.append((hb_, hbt))
                    if kv == 0:
                        ps, pt = next_ps()
                        for hc in range(2):
                            S.op("pe", (lambda e, ps=ps, hc=hc, hb_=hids[hc][0]: e.matmul(ps[:, 0:127], cw2k[:, hc, :], hb_[:, 0:127], start=(hc == 0), stop=(hc == 1))),
                                 reads=[hids[hc][1], t_l1c], writes=[pt])
                        head_norm(ps, pt, 127, hvecs[:, 1:2], [t_l1c], kcT[:, g, 0:127], [t_kc[g]])
                    else:
                        ps, pt = next_ps()
                        for hc in range(2):
                            S.op("pe", (lambda e, ps=ps, hc=hc, hb_=hids[hc][0]: e.matmul(ps[0:127, 0:64], hb_[:, 0:127], cw2v[:, hc, :], start=(hc == 0), stop=(hc == 1))),
                                 reads=[hids[hc][1], t_l1c], writes=[pt])
                        S.op("act", (lambda e, ps=ps, g=g: e.activation(out=vcA[0:127, g, 0:64], in_=ps[0:127, 0:64], func=AF.Copy)), reads=[pt], writes=[t_vc[g]])

            chk("cmp", [(kcT[:].rearrange("p g n -> p (g n)"), d_y[:, 0, 0:512]), (vcA[:].rearrange("p g n -> p (g n)"), d_y[:, 1, 0:512])])
            wv_q, wt_q = load_w_std(d_w_qg[0], 0, 1024)
            S.barrier()
            XB = XR[:].bitcast(BF16)
            QT = TT(XB[:, 0:16384].rearrange("p (h t) -> p h t", h=8))
            KST = TT(XB[:, 16384:24576].rearrange("p (g t) -> p g t", g=4))
            KWT = TT(XB[:, 24576:32768].rearrange("p (g t) -> p g t", g=4))
            RB = rstd_s[:].bitcast(BF16)
            SIGG = TT(RB[0:48, 0:T])

            def ev_q(oc, tg, ps, pt):
                head_norm(ps, pt, TGW, gq[:, 0:1], [t_gq], QT.ap[:, oc, tsl(tg)], [QT.t(oc, tg)])
            proj(wv_q, wt_q, A, 8, ev_q)
            for tg in range(NTG):
                ps, pt = next_ps()
                for k in range(KC):
                    S.op("pe", (lambda e, ps=ps, k=k, tg=tg: e.matmul(ps[0:48, :], wqg_g[:, k, :], A.ap[:, k, tsl(tg)], start=(k == 0), stop=(k == KC - 1))),
                         reads=[t_l1c, A.t(k, tg)], writes=[pt])
                S.op("act", (lambda e, ps=ps, tg=tg: e.activation(out=SIGG.ap[:, tsl(tg)], in_=ps[0:48, :], func=AF.Sigmoid)), reads=[pt], writes=[SIGG.t(tg)])

            for typ, dst, gi in ((2, KST, 2), (4, KWT, 3)):
                def viewk(slot):
                    return slot[:, 0:4096].rearrange("p (k g r d) -> p k g r d", k=8, g=4, r=2)
                srck = d_w_kv.rearrange("(k p) n -> p k n", p=128)[:, :, typ * 256:(typ + 1) * 256].rearrange("p k (g d) -> p k g d", g=4)
                pieces = [((lambda slot, r=r, gg=gg: viewk(slot)[:, :, gg, r, :]), srck[:, :, gg, :]) for r in range(2) for gg in range(4)]
                kv_, kt_ = load_w(pieces, viewk)

                def ev_k(oc, tg, ps, pt, dst=dst, gi=gi):
                    head_norm(ps, pt, TGW, hvecs[:, gi:gi + 1], [t_l1c], dst.ap[:, oc, tsl(tg)], [dst.t(oc, tg)])
                proj(kv_, kt_, H, 4, ev_k, oc_cols=(lambda wv_, k, oc: wv_[:, k, oc, :, :].rearrange("p r d -> p (r d)")))
            chk("q", [(QT.ap, d_y)])
            chk("k", [(KST.ap, d_y[:, 0:4, :]), (KWT.ap, d_y[:, 4:8, :]), ])

            def viewv(slot):
                return slot[:, 0:4096].rearrange("p (k s c) -> p k s c", k=8, s=2)
            srcv = d_w_kv.rearrange("(k p) n -> p k n", p=128)
            pieces = [((lambda slot: viewv(slot)[:, :, 0, :]), srcv[:, :, 768:1024]), ((lambda slot: viewv(slot)[:, :, 1, :]), srcv[:, :, 1280:1536])]
            vv_, vt_ = load_w(pieces, viewv)
            S.barrier()
            AB = AR[:]
            VS = TT(AB[:, 0:8192].rearrange("p (t g c) -> p t g c", t=16, g=4))
            VW = TT(AB[:, 8192:16384].rearrange("p (t g c) -> p t g c", t=16, g=4))
            t_vones = Tok()
            S.op("dve", lambda e: e.memset(VS.ap[:, :, :, 64:128], 1.0), writes=[t_vones])
            S.op("dve", lambda e: e.memset(VW.ap[:, :, :, 64:128], 1.0), writes=[t_vones])

            for tt in range(16):
                ps, pt = next_ps()
                tg = tt // 4
                for k in range(KC):
                    S.op("pe", (lambda e, ps=ps, k=k, tt=tt: e.matmul(ps[:], H.ap[:, k, tt * 128:(tt + 1) * 128], vv_[:, k, :, :].rearrange("p s c -> p (s c)"),
                                                                    start=(k == 0), stop=(k == KC - 1))), reads=[vt_, H.t(k, tg)], writes=[pt])
                S.op("act", (lambda e, ps=ps, tt=tt: e.activation(out=VS.ap[:, tt, :, 0:64], in_=ps[:, 0:256].rearrange("p (g d) -> p g d", g=4), func=AF.Copy)),
                     reads=[pt, t_vones], writes=[VS.t(tt)])
                S.op("dve", (lambda e, ps=ps, tt=tt: e.tensor_copy(out=VW.ap[:, tt, :, 0:64], in_=ps[:, 256:512].rearrange("p (g d) -> p g d", g=4))),
                     reads=[pt, t_vones], writes=[VW.t(tt)])
            chk("v", [(AB[:, 0:8192], d_y[:, 0:4, :].rearrange("p a b -> p (a b)")), (AB[:, 8192:16384], d_y[:, 4:8, :].rearrange("p a b -> p (a b)"))])
            wo_v, wo_t = load_w_std(d_w_o[0], 0, 1024)
            S.barrier()

            t_ac = Tok()
            tri_b = SCR[:, 0:128]
            anti_b = SCR[:, 128:256]
            cmpm = SCR[:, 256:2304]
            e128 = SCR[:, 2304:4352].rearrange("p (k j) -> p k j", k=16)
            cmap = SCR[:, 4352:4392]
            force = SCR[:, 4392:4904].bitcast(F32).rearrange("p (q j) -> p q j", q=8)
            gsel = SCR[0:48, 4904:7976].rearrange("p (h b m) -> p h b m", h=8, b=3)
            PTS = [SCR[:, 7976 + i * 1024:7976 + (i + 1) * 1024].rearrange("p (k r c) -> p k r c", k=2, r=2) for i in range(3)]
            pts_t = [Tok() for _ in PTS]
            S.op("pool", lambda e: e.dma_start(out=SCR[:, 0:2304], in_=d_amask), writes=[t_ac], dma="cb")
            S.op("pool", lambda e: e.dma_start(out=SCR[:, 2304:4352], in_=d_e128), writes=[t_ac], dma="cb")
            S.op("pool", lambda e: e.dma_start(out=cmap, in_=d_cmap), writes=[t_ac], dma="cb")
            S.op("sp", lambda e: e.dma_start(out=SCR[:, 4392:4904].bitcast(F32), in_=d_force), writes=[t_ac], dma="c")
            S.op("pool", lambda e: e.dma_start(out=SCR[0:48, 4904:7976], in_=d_gsel), writes=[t_ac], dma="cb")
            HB = HR[:]
            OTG = TT(HB[:, 0:4096].rearrange("p (h t) -> p h t", h=8))
            GBC = HB[:, 4096:7168].rearrange("p (h b t) -> p h b t", h=2, b=3)
            t_gbc = Tok()
            XST = TT(HB[:, 7168:15360].bitcast(F32).rearrange("p (k t) -> p k t", k=8))
            BIAS = [HB[:, 15360 + i * 256:15360 + (i + 1) * 256].rearrange("p (r q) -> p r q", r=2) for i in range(2)]
            bias_t = [Tok(), Tok()]
            for i in range(2):
                S.op("dve", (lambda e, i=i: e.memset(BIAS[i], 0.0)), writes=[bias_t[i]])
            ptc = {"n": 0, "b": 0, "a": 0, "c": 0}
            cmb_lo = [Tok(), Tok()]
            cmb_hi = [Tok(), Tok()]
            cmb_on = [Tok(), Tok()]

            M2 = BIAS_EXTRA
            tri2 = M2[:, 0:256]
            anti2 = M2[:, 256:512]
            t_m2 = Tok()
            S.op("dve", lambda e: e.tensor_copy(out=tri2.rearrange("p (r q) -> p r q", r=2), in_=tri_b.unsqueeze(1).to_broadcast([128, 2, 128])), reads=[t_ac], writes=[t_m2])
            S.op("dve", lambda e: e.tensor_copy(out=anti2.rearrange("p (r q) -> p r q", r=2), in_=anti_b.unsqueeze(1).to_broadcast([128, 2, 128])), reads=[t_ac], writes=[t_m2])
            CM2 = [M2[:, 512 + i * 256:512 + (i + 1) * 256] for i in range(2)]
            cm2_t = [Tok(), Tok()]

            def emit_cm2(ci, qt):
                S.op("dve", (lambda e: e.tensor_copy(out=CM2[ci].rearrange("p (r q) -> p r q", r=2),
                                                     in_=cmpm[:, qt * 128:(qt + 1) * 128].unsqueeze(1).to_broadcast([128, 2, 128]))),
                     reads=[t_ac], writes=[cm2_t[ci]])

            def next_pt():
                i = ptc["n"] % 3
                ptc["n"] += 1
                return PTS[i], pts_t[i]

            def qk_tile(bank, bt, colbase, KT, g, kt0, nk, par, qt, mask, bias_i, phase):
                lhs_k = KT.ap[par * 64:(par + 1) * 64, g, kt0:kt0 + nk] if KT is not None else kcT[par * 64:(par + 1) * 64, g, 0:127]
                k_reads = [KT.t(g, kt0 // TGW)] if KT is not None else [t_kc[g]]
                qsl = slice(qt * 128, (qt + 1) * 128)
                q_reads = [QT.t(2 * g, qt // 4), QT.t(2 * g + 1, qt // 4)]
                if phase == 1:
                    S.op("pe", (lambda e: e.matmul(bank[0:nk, colbase:colbase + 256], lhs_k, QT.ap[par * 64:(par + 1) * 64, 2 * g:2 * g + 2, qsl],
                                                   start=(mask is None), stop=True)), reads=k_reads + q_reads, writes=[bt])
                elif mask is None:
                    pass
                elif mask == "bias":
                    kt = kt0 // 128
                    S.op("pe", (lambda e: e.matmul(bank[:, colbase:colbase + 256], e128[:, kt, :], BIAS[bias_i].rearrange("p r q -> p (r q)"), start=True, stop=False)),
                         reads=[t_ac, bias_t[bias_i]], writes=[bt])
                else:
                    mask_ap, mask_reads = mask
                    S.op("pe", (lambda e: e.matmul(bank[0:nk, colbase:colbase + 256], ident_b[:, 0:nk], mask_ap, start=True, stop=False)),
                         reads=[t_l1c] + mask_reads, writes=[bt])

            def branch_steps(g, qt, kind, bias_i, pos, br, acc, acct):
                ps_o, pt_o = next_ps("O")
                out = []
                if kind == "cmp":
                    bA, tA = next_ps("S")
                    bB, tB = next_ps("S")
                    PT, ptt = next_pt()

                    ci = ptc["a"] % 2
                    ptc["a"] += 1

                    def qk():
                        if g == 0 and qt == 0:
                            emit_cm2(ci, qt)
                        for phase in range(2):
                            for par, bank, bt in ((0, bA, tA), (1, bB, tB)):
                                qk_tile(bank, bt, 0, None, g, 0, 127, par, qt, (CM2[ci], [cm2_t[ci]]), None, phase)
                        for par, bank, bt in ((0, bA, tA), (1, bB, tB)):
                            S.op("act", (lambda e, par=par, bank=bank: e.activation(out=PT[0:127, 0, par, :], in_=bank[0:127, 0:256], func=AF.Exp)),
                                 reads=[bt], writes=[ptt])
                        nqt = qt + 1 if qt % 4 != 3 else (qt - 3 if g < 3 else qt + 1)
                        if nqt < 16:
                            emit_cm2(1 - ci, nqt)
                        if qt >= 8:
                            topk_a(g, qt, PT, ptt)

                    def pv():
                        S.op("pe", (lambda e: e.matmul(ps_o[:], vcA[0:127, g, :], PT[0:127, 0, :, :].rearrange("p r c -> p (r c)"), start=True, stop=True)),
                             reads=[ptt, t_vc[g], t_vc_ones], writes=[pt_o])
                        combine(g, qt, pos, br, ps_o, pt_o, acc, acct)
                    return [(qk, pv)]
                KT, V = (KST, VS) if kind == "sel" else (KWT, VW)
                kts = list(range(0, qt + 1)) if kind == "sel" else list(range(max(0, qt - 4), qt + 1))
                for pi in range(0, len(kts), 2):
                    pair = kts[pi:pi + 2]
                    banks = (next_ps("S"), next_ps("S"))
                    PTp = next_pt()

                    def qk(pair=pair, banks=banks, PTp=PTp):
                        PT, ptt = PTp
                        npair = len(pair)
                        if kind == "win" and qt >= 8 and pair[-1] == qt:
                            topk_b(bias_i)
                        for ktp, kt in enumerate(pair):
                            if kt == qt:
                                mask = (tri2, [t_m2])
                            elif kind == "win" and kt == qt - 4:
                                mask = (anti2, [t_m2])
                            elif kind == "sel" and qt >= 8:
                                mask = "bias"
                            else:
                                mask = None
                            for phase in range(2):
                                for par, (bank, bt) in enumerate(banks):
                                    qk_tile(bank, bt, ktp * 256, KT, g, kt * 128, 128, par, qt, mask, bias_i, phase)
                        for par, (bank, bt) in enumerate(banks):
                            S.op("act", (lambda e, par=par, bank=bank: e.activation(
                                out=PT[:, 0:npair, par, :], in_=bank[:, 0:npair * 256].rearrange("p (k c) -> p k c", k=npair), func=AF.Exp)),
                                reads=[bt], writes=[ptt])

                    def pv(pair=pair, PTp=PTp, last=(pi + 2 >= len(kts))):
                        PT, ptt = PTp
                        for ktp, kt in enumerate(pair):
                            S.op("pe", (lambda e, ktp=ktp, kt=kt: e.matmul(ps_o[:], V.ap[:, kt, g, :], PT[:, ktp, :, :].rearrange("p r c -> p (r c)"),
                                                                         start=(kt == kts[0]), stop=(kt == kts[-1]))),
                                 reads=[ptt, V.t(kt), t_vones], writes=[pt_o])
                        if last:
                            combine(g, qt, pos, br, ps_o, pt_o, acc, acct)
                    out.append((qk, pv))
                return out

            def combine(g, qt, pos, br, ps_o, pt_o, acc, acct):
                qi = qt % 4
                ci = ptc["c"] % 2
                ptc["c"] += 1
                T, tlo, thi = TMPF[ci], cmb_lo[ci], cmb_hi[ci]
                S.op("act", (lambda e: e.activation(out=T[0:64, :], in_=ps_o[64:128, :], func=AF.Ln, bias=eps_t[64:128, 1:2], scale=1.0)), reads=[pt_o, t_eps], writes=[tlo])
                S.op("act", (lambda e: e.activation(out=T[64:128, :], in_=T[0:64, :], func=AF.Exp, scale=-1.0)), reads=[tlo], writes=[thi])
                on, ont = TMPF[2][:, ci * 256:(ci + 1) * 256], cmb_on[ci]
                for par in range(2):
                    S.op("dve", (lambda e, par=par: e.tensor_tensor(out=on[par * 64:(par + 1) * 64, :], in0=ps_o[0:64, par * 256:(par + 1) * 256],
                                                                     in1=T[64:128, par * 256:(par + 1) * 256], op=ALU.mult)), reads=[pt_o, thi], writes=[ont])
                onv = on.rearrange("p (h q) -> p h q", h=2)
                accv = acc[:].rearrange("p (h q) -> p h q", h=2)
                Gv = GBC[:, :, br, qi * 128:(qi + 1) * 128]
                if pos == 0:
                    S.op("dve", (lambda e: e.tensor_tensor(out=accv, in0=onv, in1=Gv, op=ALU.mult)), reads=[ont, t_gbc], writes=[acct])
                else:
                    S.op("dve", (lambda e: e.tensor_tensor(out=onv, in0=onv, in1=Gv, op=ALU.mult)), reads=[ont, t_gbc], writes=[ont])
                    if pos == 1:
                        S.op("dve", (lambda e: e.tensor_tensor(out=accv, in0=accv, in1=onv, op=ALU.add)), reads=[ont, acct], writes=[acct])
                    else:
                        S.op("dve", (lambda e: e.tensor_tensor(out=OTG.ap[:, 2 * g:2 * g + 2, qi * 128:(qi + 1) * 128], in0=accv, in1=onv, op=ALU.add)),
                             reads=[ont, acct], writes=[OTG.t(2 * g, 0), OTG.t(2 * g + 1, 0)])

            def topk_a(g, qt, PT, ptt):
                ps_i, pt_i = next_ps("M")
                for c in range(4):
                    S.op("pe", (lambda e, c=c: e.matmul(ps_i[:, c * 33:(c + 1) * 33], PT[0:127, 0, c // 2, (c % 2) * 128:(c % 2 + 1) * 128], cmap[0:127, 0:33],
                                                          start=True, stop=True)), reads=[ptt, t_ac], writes=[pt_i])
                psv = ps_i[:, 0:132].rearrange("p (c j) -> p c j", c=4)
                S.op("dve", (lambda e: e.reciprocal(out=rd4[:], in_=psv[:, :, 32])), reads=[pt_i], writes=[t_tk])
                S.op("dve", (lambda e: e.tensor_scalar(out=scA[:], in0=psv[:, 0, 0:32], scalar1=rd4[:, 0:1], scalar2=None, op0=ALU.mult)), reads=[pt_i, t_tk], writes=[t_tk])
                for c in range(1, 4):
                    S.op("dve", (lambda e, c=c: e.scalar_tensor_tensor(out=scA[:], in0=psv[:, c, 0:32], scalar=rd4[:, c:c + 1], in1=scA[:], op0=ALU.mult, op1=ALU.add)),
                         reads=[pt_i, t_tk], writes=[t_tk])
                S.op("dve", (lambda e: e.tensor_tensor(out=scA[:], in0=scA[:], in1=force[:, qt - 8, :], op=ALU.add)), reads=[t_tk, t_ac], writes=[t_tk])
                S.op("dve", (lambda e: e.max(out=m8a[:], in_=scA[:])), reads=[t_tk], writes=[t_tk])
                S.op("dve", (lambda e: e.match_replace(out=scB[:], in_to_replace=m8a[:], in_values=scA[:], imm_value=-1e30)), reads=[t_tk], writes=[t_tk])
                S.op("dve", (lambda e: e.max(out=m8b[:], in_=scB[:])), reads=[t_tk], writes=[t_tk])
                S.op("dve", (lambda e: e.tensor_scalar(out=selm[:], in0=scA[:], scalar1=m8b[:, 7:8], scalar2=1.0, op0=ALU.is_ge, op1=ALU.subtract)), reads=[t_tk], writes=[t_tk])

            def topk_b(bias_i):
                ps_m, pt_m = next_ps("M")
                S.op("pe", (lambda e: e.matmul(ps_m[0:32, 0:128], selm[:], ident_b[:], start=True, stop=True)), reads=[t_tk, t_l1c], writes=[pt_m])
                S.op("act", (lambda e: e.activation(out=BIAS[bias_i][0:32, :, :], in_=ps_m[0:32, 0:128].unsqueeze(1).to_broadcast([32, 2, 128]), func=AF.Copy, scale=30000.0)),
                     reads=[pt_m], writes=[bias_t[bias_i]])

            on_t = [Tok(), Tok()]
            acc_t = [Tok(), Tok()]
            t_tk = Tok()
            g1 = mod_part(1, 2)
            steps = []

            def emit_xreload(tg):
                for kh in range(2):
                    ks = slice(kh * 4, kh * 4 + 4)
                    S.op("sp", (lambda e, ks=ks: e.dma_start(out=XST.ap[:, ks, :], in_=d_xs[:, ks, tsl(tg)])),
                         reads=[xs_t[tg]], writes=[XST.t(k) for k in range(kh * 4, kh * 4 + 4)], dma="xr")

            def emit_gbc(tg, g):
                for hpl in range(2):
                    for br in range(3):
                        ps, pt = next_ps("OM")
                        S.op("pe", (lambda e, ps=ps, hpl=hpl, br=br: e.matmul(ps[:], gsel[:, 2 * g + hpl, br, :], SIGG.ap[:, tsl(tg)], start=True, stop=True)),
                             reads=[t_ac, SIGG.t(tg)], writes=[pt])
                        if (hpl * 3 + br) % 2 == 0:
                            S.op("act", (lambda e, ps=ps, hpl=hpl, br=br: e.activation(out=GBC[:, hpl, br, :], in_=ps[:], func=AF.Copy)), reads=[pt], writes=[t_gbc])
                        else:
                            S.op("dve", (lambda e, ps=ps, hpl=hpl, br=br: e.tensor_copy(out=GBC[:, hpl, br, :], in_=ps[:])), reads=[pt], writes=[t_gbc])

            def emit_wo(tg):
                def ev_o(oc, tg_, ps, pt):
                    S.op("dve", (lambda e: e.scalar_tensor_tensor(out=XST.ap[:, oc, :], in0=ps[:], scalar=g1[:, oc:oc + 1], in1=XST.ap[:, oc, :], op0=ALU.mult, op1=ALU.add)),
                         reads=[pt, XST.t(oc), t_modv[1]], writes=[XST.t(oc)])
                defpool["v"] = "OM"
                proj(wo_v, wo_t, OTG, 8, ev_o, tgs=[0])
                defpool["v"] = "P6"
                for kh in range(2):
                    ks = slice(kh * 4, kh * 4 + 4)
                    S.op("sp", (lambda e, ks=ks: e.dma_start(out=d_xs[:, ks, tsl(tg)], in_=XST.ap[:, ks, :])),
                         reads=[XST.t(k) for k in range(kh * 4, kh * 4 + 4)], writes=[xs_t[tg]], dma="xw")

            for tg in range(NTG):
                steps.append((None, (lambda tg=tg: emit_xreload(tg))))
                for g in range(G4):
                    steps.append((None, (lambda tg=tg, g=g: emit_gbc(tg, g))))
                    for qi in range(4):
                        qt = tg * 4 + qi
                        ai = ptc["b"] % 2
                        ptc["b"] += 1
                        acc, acct = ACC[ai], acc_t[ai]
                        steps += branch_steps(g, qt, "cmp", ai, 0, 0, acc, acct)
                        steps += branch_steps(g, qt, "win", ai, 1, 2, acc, acct)
                        steps += branch_steps(g, qt, "sel", ai, 2, 1, acc, acct)
                steps.append((None, (lambda tg=tg: emit_wo(tg))))
            prev_pv = None
            for qk_fn, pv_fn in steps:
                if qk_fn is not None:
                    qk_fn()
                if prev_pv is not None:
                    prev_pv()
                prev_pv = pv_fn
            if prev_pv is not None:
                prev_pv()
            S.barrier()
            for tg in range(NTG):
                for kh in range(2):
                    ks = slice(kh * 4, kh * 4 + 4)
                    S.op("sp", (lambda e, tg=tg, ks=ks: e.dma_start(out=X.ap[:, ks, tsl(tg)], in_=d_xs[:, ks, tsl(tg)])),
                         reads=[xs_t[tg]], writes=[X.t(k, tg) for k in range(kh * 4, kh * 4 + 4)], dma="x2")
            defpool["v"] = "ALL"
            if stop != "mix1":
                mlp(1, 1)
          except StopBuild:
            S.emit(final_dma_keys=["out"])
            return nc

        for tg in range(NTG):
            for kh in range(2):
                ks = slice(kh * 4, kh * 4 + 4)
                S.op("sp", (lambda e, tg=tg, ks=ks: e.dma_start(out=d_y[:, ks, tsl(tg)], in_=X.ap[:, ks, tsl(tg)])),
                     reads=[X.t(k, tg) for k in range(kh * 4, kh * 4 + 4)], dma="out")
        S.emit(final_dma_keys=["out"])
    return nc


def _fm(v):
    return np.ascontiguousarray(v.reshape(KC, 128).T)


def _const_tables():
    f32 = np.float32
    NEGM = -30000.0
    j = np.arange(128)[:, None]
    t = np.arange(128)[None, :]
    tri = np.where(j <= t, 0.0, NEGM)
    anti = np.where(j > t, 0.0, NEGM)
    n = np.arange(128)[:, None]
    tt = np.arange(T)[None, :]
    cmpm = np.where((16 * n + 31 <= tt) & (n < 127), 0.0, NEGM)
    amask = np.concatenate([tri, anti, cmpm], axis=1).astype(f32)
    e128 = np.zeros((128, 16, 128), f32)
    for kt in range(16):
        for jj in range(128):
            e128[2 * kt + jj // 64, kt, jj] = 1.0
    cmap = np.zeros((128, 40), f32)
    c0 = np.arange(127)[:, None] * 16
    s0 = np.arange(32)[None, :] * 64
    ov = np.minimum(c0 + 32, s0 + 64) - np.maximum(c0, s0)
    cmap[:127, :32] = np.clip(ov, 0, None) / 32.0
    cmap[:127, 32] = 1.0
    force = np.zeros((128, 8, 32), f32)
    for q in range(8):
        tq = 128 * (q + 8) + np.arange(128)
        cur = tq // 64
        jb = np.arange(32)[None, :]
        forced = (jb == 0) | (jb == cur[:, None]) | (jb == cur[:, None] - 1)
        force[:, q, :] = np.where(forced, 1e4, np.where(jb > cur[:, None], -1e4, 0.0))
    gsel = np.zeros((48, 8, 3, 128), f32)
    for hp in range(8):
        for br in range(3):
            for m in range(128):
                gsel[(2 * hp + m // 64) * 3 + br, hp, br, m] = 1.0
    p = np.arange(128)
    bd = (p[:, None] // 64 == p[None, :] // 64).astype(f32)
    return {"amask": amask, "e128": e128.reshape(128, 2048), "cmap": cmap, "force": force.reshape(128, 256),
            "gsel": gsel.reshape(48, 3072), "bdones": bd}


def prep_inputs(inputs):
    f32 = np.float32
    g = {k: np.asarray(v, dtype=f32) for k, v in inputs.items()}
    vecs = np.zeros((128, NV, KC), f32)
    for i in range(2):
        for j in range(2):
            vecs[:, V_NG + 2 * i + j, :] = _fm(g["norm_gain"][i, j])
        for part in range(6):
            vecs[:, V_BADA + 6 * i + part, :] = _fm(g["b_ada"][i, part * D:(part + 1) * D])
    vecs[:, V_KVG, :] = _fm(g["kv_norm_gain"])
    for part in range(2):
        vecs[:, V_BKV + part, :] = _fm(g["b_ada_kv"][part * D:(part + 1) * D])
    for j in range(3):
        vecs[:, V_CONV + j, :] = _fm(g["conv_w"][0, j])
    consts = np.concatenate([np.eye(128, dtype=f32), np.ones((128, 128), f32)], axis=1)
    p64 = np.arange(128) % 64
    hvecs = np.zeros((128, 4), f32)
    hvecs[:, 0] = g["q_gain"][0, p64]
    for i in range(3):
        hvecs[:, 1 + i] = g["k_gain"][i, p64]
    peT = np.zeros((128, 64), f32)
    for kv in range(2):
        peT[:, kv * 32:(kv + 1) * 32] = g["cmp_pe"][kv][:, p64].T
    shared = {
        "vecs": vecs, "consts": consts, "hvecs": hvecs, "peT": peT,
        "w_ada": g["w_ada"], "w_a_in": g["w_a_in"], "w_a_out": g["w_a_out"],
        "w_mlp1": g["w_mlp1"], "w_mlp2": g["w_mlp2"],
        "w_ada_kv": g["w_ada_kv"], "w_kv": g["w_kv"], "w_qg": g["w_qg"], "w_o": g["w_o"],
        "cmp_w1": g["cmp_w1"], "cmp_w2": g["cmp_w2"],
    }
    shared.update(_const_tables())
    in_maps = []
    for b in range(N_CORES):
        m = dict(shared)
        xT = g["x"][b].T.reshape(KC, 128, T).transpose(1, 0, 2)
        m["xT"] = np.ascontiguousarray(xT)
        m["cT"] = _fm(g["c"][b])
        in_maps.append(m)
    return in_maps


def post_outputs(results):
    outs = []
    for r in results:
        yT = np.asarray(r["yT"])
        outs.append(yT.transpose(2, 1, 0).reshape(T, D))
    return np.stack(outs, axis=0).astype(np.float32)


def kernel(**inputs):
    in_maps = prep_inputs(inputs)
    nc = build_program(DEBUG_STOP)
    res = run_bass_kernel_spmd(nc, in_maps, core_ids=list(range(N_CORES)))
    return post_outputs(res.results)
```

```python
import numpy as np
from contextlib import ExitStack
import concourse.bass as bass
import concourse.mybir as mybir
from concourse.bass_utils import run_bass_kernel_spmd

F32 = mybir.dt.float32
BF16 = mybir.dt.bfloat16
AF = mybir.ActivationFunctionType
ALU = mybir.AluOpType

D = 1024
T = 2048
KC = 8
NTG = 4
TGW = 512
EPS = 1e-6
N_CORES = 8

V_NG = 0
V_KVG = 4
V_BADA = 5
V_BKV = 17
V_CONV = 19
NV = 22

DEBUG_STOP = None


class Tok:
    __slots__ = ("ws", "rs", "rdma", "excl")

    def __init__(self, excl=False):
        self.ws = []
        self.rs = {}
        self.rdma = []
        self.excl = excl


class Op:
    __slots__ = ("eng", "fn", "deps", "dma", "signal", "sigval", "dmaval", "dsem")

    def __init__(self, eng, fn, dma):
        self.eng = eng
        self.fn = fn
        self.dma = dma
        self.deps = ()
        self.signal = False
        self.sigval = 0
        self.dmaval = 0
        self.dsem = None


ENGS = ["pe", "act", "dve", "pool", "sp"]
DMA_SEMS = 16


class Sched:
    def __init__(self, nc):
        self.nc = nc
        self.ops = {e: [] for e in ENGS}
        self.dma_hist = {e: [] for e in ENGS}

    def op(self, eng, fn, reads=(), writes=(), dma=None):
        o = Op(eng, fn, dma)
        ex = [t for t in reads if t.excl]
        if ex:
            reads = [t for t in reads if not t.excl]
            writes = list(writes) + ex
        deps = set()
        for t in reads:
            deps.update(t.ws)
        for t in writes:
            deps.update(t.ws)
            deps.update(t.rs.values())
            deps.update(t.rdma)
        if dma is not None:
            deps = {d for d in deps if d.dma != dma}
            hist = self.dma_hist[eng]
            n = len(hist)
            o.dsem = (eng, n % DMA_SEMS)
            o.dmaval = 16 * (n // DMA_SEMS + 1)
            if n >= DMA_SEMS:
                deps.add(hist[n - DMA_SEMS])
            hist.append(o)
        o.deps = tuple(deps)
        for t in reads:
            if dma is not None:
                t.rdma.append(o)
            else:
                t.rs[eng] = o
        for t in writes:
            if dma is not None and t.ws and not t.rs and not t.rdma and all(w.dma == dma for w in t.ws):
                t.ws.append(o)
            else:
                t.ws = [o]
            t.rs = {}
            t.rdma = []
        self.ops[eng].append(o)
        return o

    def barrier(self, engines=("pe", "act", "dve", "sp", "pool")):
        lasts = []
        for e in ENGS:
            for o in reversed(self.ops[e]):
                if o.dma is None and o.fn is not None:
                    lasts.append(o)
                    break
            lasts += self.dma_hist[e][-DMA_SEMS:]
        for e in engines:
            o = Op(e, None, None)
            o.deps = tuple(lasts)
            self.ops[e].append(o)

    def emit(self, final_dma_keys=()):
        nc = self.nc
        for e in ENGS:
            for o in self.ops[e]:
                for d in o.deps:
                    if d.dma is None:
                        if d.eng == "pe" and o.eng == "pe" and o.dma is None and o.fn is not None:
                            continue
                        d.signal = True
        for e in ENGS:
            c = 0
            for o in self.ops[e]:
                if o.dma is None and o.signal:
                    c += 1
                    o.sigval = c
        with ExitStack() as es:
            esem = {e: es.enter_context(nc.semaphore("s_" + e)) for e in ENGS}
            dsem = {}
            for e in ENGS:
                for i in range(min(DMA_SEMS, len(self.dma_hist[e]))):
                    dsem[(e, i)] = es.enter_context(nc.semaphore("d_%s%d" % (e, i)))
            block = es.enter_context(nc.Block())

            def run(e, eng):
                waited = {}
                for o in self.ops[e]:
                    need = {}
                    for d in o.deps:
                        if d.dma is not None:
                            key, sem, val = ("d",) + d.dsem, dsem[d.dsem], d.dmaval
                        else:
                            if d.eng == "pe" and e == "pe" and o.dma is None and o.fn is not None:
                                continue
                            key, sem, val = ("e", d.eng), esem[d.eng], d.sigval
                        if val > need.get(key, (None, 0))[1]:
                            need[key] = (sem, val)
                    for key, (sem, val) in need.items():
                        if waited.get(key, 0) >= val:
                            continue
                        waited[key] = val
                        eng.wait_ge(sem, val)
                    if o.fn is None:
                        continue
                    ins = o.fn(eng)
                    if o.dma is not None:
                        ins.then_inc(dsem[o.dsem], 16)
                    elif o.signal:
                        ins.then_inc(esem[e], 1)
                if e == "sp":
                    fin = {}
                    for q in ENGS:
                        for d in self.dma_hist[q]:
                            if d.dma in final_dma_keys:
                                fin[d.dsem] = max(fin.get(d.dsem, 0), d.dmaval)
                    for k, v in fin.items():
                        if waited.get(("d",) + k, 0) < v:
                            eng.wait_ge(dsem[k], v)

            block.sync(lambda eng: run("sp", eng))
            block.scalar(lambda eng: run("act", eng))
            block.vector(lambda eng: run("dve", eng))
            block.gpsimd(lambda eng: run("pool", eng))
            block.tensor(lambda eng: run("pe", eng))


class StopBuild(Exception):
    pass


class TT:
    def __init__(self, ap):
        self.ap = ap
        self.toks = {}

    def t(self, *key):
        tk = self.toks.get(key)
        if tk is None:
            tk = self.toks[key] = Tok()
        return tk

    def all(self):
        return list(self.toks.values())


def build_program(stop=None):
    nc = bass.Bass("TRN2", target_bir_lowering=False)

    def din(name, shape):
        return nc.dram_tensor(name, list(shape), F32, kind="ExternalInput").ap()

    d_x = din("xT", [128, KC, T])
    d_c = din("cT", [128, KC])
    d_vecs = din("vecs", [128, NV, KC])
    d_consts = din("consts", [128, 256])
    d_w_ada = din("w_ada", [2, D, 6 * D])
    d_w_a_in = din("w_a_in", [1, D, 3 * D])
    d_w_a_out = din("w_a_out", [1, D, D])
    d_w_mlp1 = din("w_mlp1", [2, D, 4 * D])
    d_w_mlp2 = din("w_mlp2", [2, 4 * D, D])
    d_w_ada_kv = din("w_ada_kv", [D, 2 * D])
    d_w_kv = din("w_kv", [D, 1536])
    d_w_qg = din("w_qg", [1, D, 1072])
    d_w_o = din("w_o", [1, D, D])
    d_cmp_w1 = din("cmp_w1", [2, 2048, 256])
    d_cmp_w2 = din("cmp_w2", [2, 256, 64])
    d_hvecs = din("hvecs", [128, 4])
    d_peT = din("peT", [128, 64])
    d_amask = din("amask", [128, 256 + 2048])
    d_e128 = din("e128", [128, 2048])
    d_cmap = din("cmap", [128, 40])
    d_force = din("force", [128, 256])
    d_gsel = din("gsel", [48, 3072])
    d_bd = din("bdones", [128, 128])
    d_y = nc.dram_tensor("yT", [128, KC, T], F32, kind="ExternalOutput").ap()
    d_xs = nc.dram_tensor("xs_scratch", [128, KC, T], F32).ap()

    S = Sched(nc)
    es = ExitStack()
    with es:
        def sb(name, shape, dt):
            return es.enter_context(nc.sbuf_tensor(name, list(shape), dt))

        XR = sb("XR", [128, KC * T], F32)
        HR = sb("HR", [128, KC * T], BF16)
        AR = sb("AR", [128, KC * T], BF16)
        RING = [sb("RING%d" % i, [128, 8192], BF16) for i in range(2)]
        ring_t = [Tok(), Tok()]
        rstd_s = sb("rstd", [128, T], F32)
        TMPF = [sb("tmpf%d" % i, [128, TGW], F32) for i in range(3)]
        tmpf_t = [Tok() for _ in TMPF]
        TMPB = [sb("tmpb%d" % i, [128, TGW], BF16) for i in range(3)]
        tmpb_t = [Tok() for _ in TMPB]
        SCR = sb("SCR", [128, 11048], BF16)
        ones_b = sb("ones_b", [128, 128], BF16)
        vecs = sb("vecs_s", [128, NV, KC], F32)
        c_f = sb("c_f", [128, KC], F32)
        cact = sb("cact", [128, KC], BF16)
        modv = sb("modv", [128, 3, 48], F32)
        der = sb("der", [128, 8, KC], F32)
        hvecs = sb("hvecs_s", [128, 4], F32)
        gq = sb("gq", [128, 1], F32)
        ident_b = sb("ident_b", [128, 128], BF16)
        bd_ones = sb("bd_ones", [128, 128], BF16)
        cbias = sb("cbias", [128, 4], F32)
        kcT = sb("kcT", [128, 4, 128], BF16)
        vcA = sb("vcA", [128, 4, 128], BF16)
        rd4 = sb("rd4", [128, 4], F32)
        scA = sb("scA", [128, 32], F32)
        scB = sb("scB", [128, 32], F32)
        m8a = sb("m8a", [128, 8], F32)
        m8b = sb("m8b", [128, 8], F32)
        selm = sb("selm", [128, 32], BF16)
        ACC = [sb("acc%d" % i, [128, 256], F32) for i in range(2)]
        BIAS_EXTRA = sb("m2", [128, 1024], BF16)
        PS = [es.enter_context(nc.psum_tensor("ps%d" % i, [128, TGW], F32)) for i in range(8)]
        ps_t = [Tok(excl=True) for _ in PS]

        X = TT(XR[:].rearrange("p (k t) -> p k t", k=KC))
        H = TT(HR[:].rearrange("p (k t) -> p k t", k=KC))
        A = TT(AR[:].rearrange("p (k t) -> p k t", k=KC))
        t_const = Tok()
        eps_t = sb("eps_t", [128, 2], F32)
        t_eps = Tok()
        S.op("dve", lambda e: e.memset(eps_t[:, 0:1], EPS), writes=[t_eps])
        S.op("dve", lambda e: e.memset(eps_t[:, 1:2], 1e-30), writes=[t_eps])
        t_vecs = Tok()
        t_c = Tok()
        t_cact = Tok()
        t_modv = [Tok(), Tok(), Tok()]
        t_der = Tok()
        t_rstd = [Tok() for _ in range(NTG)]

        cnt = {"ps": 0, "ring": 0, "tf": 0, "tb": 0}

        POOLS = {"ALL": list(range(8)), "P6": [0, 1, 2, 3, 4, 5], "S": [0, 1, 2, 3], "O": [4, 5, 6], "M": [7], "OM": [4, 5, 6, 7]}
        pcnt = {k: 0 for k in POOLS}

        defpool = {"v": "ALL"}

        def next_ps(pool=None):
            pool = pool or defpool["v"]
            lst = POOLS[pool]
            i = lst[pcnt[pool] % len(lst)]
            pcnt[pool] += 1
            return PS[i], ps_t[i]

        def next_tf():
            i = cnt["tf"] % len(TMPF)
            cnt["tf"] += 1
            return TMPF[i], tmpf_t[i]

        def next_tb():
            i = cnt["tb"] % len(TMPB)
            cnt["tb"] += 1
            return TMPB[i], tmpb_t[i]

        def tsl(tg):
            return slice(tg * TGW, (tg + 1) * TGW)

        S.op("pool", lambda e: e.dma_start(out=ones_b[:], in_=d_consts[:, 128:256]), writes=[t_const], dma="cb")
        S.op("sp", lambda e: e.dma_start(out=vecs[:], in_=d_vecs), writes=[t_vecs], dma="c")
        S.op("sp", lambda e: e.dma_start(out=c_f[:], in_=d_c), writes=[t_c], dma="c")
        for tg in range(NTG):
            for kh in range(2):
                ks = slice(kh * 4, kh * 4 + 4)
                S.op("sp", (lambda e, tg=tg, ks=ks: e.dma_start(out=X.ap[:, ks, tsl(tg)], in_=d_x[:, ks, tsl(tg)])),
                     writes=[X.t(k, tg) for k in range(kh * 4, kh * 4 + 4)], dma="x")

        S.op("act", lambda e: e.activation(out=cact[:], in_=c_f[:], func=AF.Silu), reads=[t_c], writes=[t_cact])

        def load_w(src_aps, dst_view_fn):
            i = cnt["ring"] % 2
            cnt["ring"] += 1
            slot = RING[i]
            for dst_fn, src in src_aps:
                S.op("pool", (lambda e, dst_fn=dst_fn, src=src, slot=slot: e.dma_start(out=dst_fn(slot), in_=src)),
                     writes=[ring_t[i]], dma="r%d" % i)
            return dst_view_fn(slot), ring_t[i]

        def load_w_std(w2d, c0, ncols, k0=0):
            src = w2d.rearrange("(k p) n -> p k n", p=128)

            def view(slot):
                return slot[:, 0:8 * ncols].rearrange("p (k c) -> p k c", k=8)
            pieces = []
            for kh in range(2):
                ks = slice(kh * 4, kh * 4 + 4)
                pieces.append(((lambda slot, ks=ks: view(slot)[:, ks, :]), src[:, k0 + kh * 4:k0 + kh * 4 + 4, c0:c0 + ncols]))
            return load_w(pieces, view)

        def ada_matvec(w2d, ncb, bias_idx, mi):
            ps, pt = next_ps()
            for cb in range(ncb):
                wv, wt = load_w_std(w2d, cb * 1024, 1024)
                for oc in range(8):
                    col = cb * 8 + oc
                    for k in range(KC):
                        S.op("pe", (lambda e, ps=ps, wv=wv, oc=oc, k=k, col=col: e.matmul(
                            ps[:, col:col + 1], wv[:, k, oc * 128:(oc + 1) * 128], cact[:, k:k + 1],
                            start=(k == 0), stop=(k == KC - 1))), reads=[wt, t_cact], writes=[pt])
            n = ncb * 8
            S.op("dve", (lambda e, ps=ps, n=n: e.tensor_tensor(
                out=modv[:, mi, 0:n], in0=ps[:, 0:n],
                in1=vecs[:, bias_idx:bias_idx + ncb, :].rearrange("p a b -> p (a b)"), op=ALU.add)),
                reads=[pt, t_vecs], writes=[t_modv[mi]])

        def mod_part(mi, part):
            return modv[:, mi, part * 8:(part + 1) * 8]

        def derive(mi, part_sc, gain_idx, dst):
            S.op("dve", lambda e: e.tensor_scalar(out=der[:, dst, :], in0=mod_part(mi, part_sc), scalar1=1.0, scalar2=1.0,
                                                  op0=ALU.add, op1=ALU.mult), reads=[t_modv[mi]], writes=[t_der])
            S.op("dve", lambda e: e.tensor_tensor(out=der[:, dst, :], in0=der[:, dst, :], in1=vecs[:, gain_idx, :], op=ALU.mult),
                 reads=[t_der, t_vecs], writes=[t_der])

        def compute_rstd():
            for tg in range(NTG):
                ps, pt = next_ps()
                for k in range(KC):
                    tb, tbt = next_tb()
                    S.op("act", (lambda e, tb=tb, k=k, tg=tg: e.activation(out=tb[:], in_=X.ap[:, k, tsl(tg)], func=AF.Square)),
                         reads=[X.t(k, tg)], writes=[tbt])
                    S.op("pe", (lambda e, ps=ps, tb=tb, k=k: e.matmul(ps[:], ones_b[:], tb[:], start=(k == 0), stop=(k == KC - 1))),
                         reads=[tbt, t_const], writes=[pt])
                tf, tft = next_tf()
                S.op("act", (lambda e, ps=ps, tf=tf: e.activation(out=tf[:], in_=ps[:], func=AF.Ln, bias=eps_t[:, 0:1], scale=1.0 / D)),
                     reads=[pt, t_eps], writes=[tft])
                S.op("act", (lambda e, tf=tf, tg=tg: e.activation(out=rstd_s[:, tsl(tg)], in_=tf[:], func=AF.Exp, scale=-0.5)), reads=[tft], writes=[t_rstd[tg]])

        def norm_mod(dst, a_ap, b_ap, extra_reads):
            for tg in range(NTG):
                for k in range(KC):
                    tf, tft = next_tf()
                    S.op("dve", (lambda e, tf=tf, k=k, tg=tg: e.tensor_tensor(out=tf[:], in0=X.ap[:, k, tsl(tg)], in1=rstd_s[:, tsl(tg)], op=ALU.mult)),
                         reads=[X.t(k, tg), t_rstd[tg]], writes=[tft])
                    S.op("act", (lambda e, tf=tf, k=k, tg=tg: e.activation(out=dst.ap[:, k, tsl(tg)], in_=tf[:], func=AF.Identity,
                                                                            bias=b_ap[:, k:k + 1], scale=a_ap[:, k:k + 1])),
                         reads=[tft] + extra_reads, writes=[dst.t(k, tg)])

        def proj(wv, wt, src, n_oc, evac, tgs=range(NTG), oc_cols=None):
            for tg in tgs:
                for oc in range(n_oc):
                    ps, pt = next_ps()
                    for k in range(KC):
                        lhs = wv[:, k, oc * 128:(oc + 1) * 128] if oc_cols is None else oc_cols(wv, k, oc)
                        S.op("pe", (lambda e, ps=ps, lhs=lhs, k=k, tg=tg: e.matmul(ps[:], lhs, src.ap[:, k, tsl(tg)],
                                                                                   start=(k == 0), stop=(k == KC - 1))),
                             reads=[wt, src.t(k, tg)], writes=[pt])
                    evac(oc, tg, ps, pt)

        def resid_evac(g_ap, extra_reads):
            def ev(oc, tg, ps, pt):
                S.op("dve", (lambda e: e.scalar_tensor_tensor(out=X.ap[:, oc, tsl(tg)], in0=ps[:], scalar=g_ap[:, oc:oc + 1],
                                                              in1=X.ap[:, oc, tsl(tg)], op0=ALU.mult, op1=ALU.add)),
                     reads=[pt, X.t(oc, tg)] + extra_reads, writes=[X.t(oc, tg)])
            return ev

        mv_tasks = []
        mv_t = [Tok(), Tok()]
        mv_state = {"n": 0}

        def make_mv_tasks(w2d, n512, bias_idx, mi):
            src = w2d.rearrange("(k p) n -> p k n", p=128)
            nparts = (n512 * 4) // 8
            bias_flat = vecs[:, bias_idx:bias_idx + nparts, :].rearrange("p a b -> p (a b)")
            for cb in range(n512):
                def task(cb=cb):
                    n = mv_state["n"]
                    mv_state["n"] += 1
                    i = n % 2
                    slot = SCR[:, i * 4096:(i + 1) * 4096].rearrange("p (k c) -> p k c", k=8)
                    extra = ([t for row in t_gb for t in row] + [t for row in t_v for t in row] + [t_vhalo]) if n < 2 else []
                    S.op("pool", (lambda e: e.dma_start(out=slot, in_=src[:, :, cb * 512:(cb + 1) * 512])), writes=[mv_t[i]] + extra, dma="mv%d" % i)
                    ps, pt = next_ps()
                    for oc in range(4):
                        for k in range(KC):
                            S.op("pe", (lambda e, oc=oc, k=k: e.matmul(ps[:, oc:oc + 1], slot[:, k, oc * 128:(oc + 1) * 128], cact[:, k:k + 1],
                                                                      start=(k == 0), stop=(k == KC - 1))), reads=[mv_t[i], t_cact], writes=[pt])
                    c0 = cb * 4
                    S.op("dve", (lambda e: e.tensor_tensor(out=modv[:, mi, c0:c0 + 4], in0=ps[:, 0:4], in1=bias_flat[:, c0:c0 + 4], op=ALU.add)),
                         reads=[pt, t_vecs], writes=[t_modv[mi]])
                mv_tasks.append(task)

        def run_mv_tasks(n):
            for _ in range(n):
                if mv_tasks:
                    mv_tasks.pop(0)()

        def mlp(layer, mi):
            compute_rstd()
            derive(mi, 4, V_NG + 2 * layer + 1, 1)
            norm_mod(H, der[:, 1, :], mod_part(mi, 3), [t_der, t_modv[mi]])
            g2 = mod_part(mi, 5)
            for hb in range(4):
                wv, wt = load_w_std(d_w_mlp1[layer], hb * 1024, 1024)

                def ev1(oc, tg, ps, pt):
                    tf, tft = next_tf()
                    S.op("act", (lambda e: e.activation(out=tf[:], in_=ps[:], func=AF.Relu)), reads=[pt], writes=[tft])
                    S.op("dve", (lambda e: e.tensor_tensor(out=A.ap[:, oc, tsl(tg)], in0=tf[:], in1=tf[:], op=ALU.mult)),
                         reads=[tft], writes=[A.t(oc, tg)])
                proj(wv, wt, H, 8, ev1)
                wv2, wt2 = load_w_std(d_w_mlp2[layer], 0, 1024, k0=hb * 8)
                run_mv_tasks(2)
                proj(wv2, wt2, A, 8, resid_evac(g2, [t_modv[mi]]))
                run_mv_tasks(2)

        ada_matvec(d_w_ada[0], 6, V_BADA, 0)
        compute_rstd()
        derive(0, 1, V_NG + 0, 0)
        norm_mod(H, der[:, 0, :], mod_part(0, 0), [t_der, t_modv[0]])

        gbv = SCR[:, 0:4096].rearrange("p (j t) -> p j t", j=2)
        vv = SCR[:, 4096:4096 + 2 * 2056].rearrange("p (j t) -> p j t", j=2)
        t_gb = [[Tok() for _ in range(NTG)] for _ in range(2)]
        t_v = [[Tok() for _ in range(NTG)] for _ in range(2)]
        t_vhalo = Tok()
        S.op("dve", lambda e: e.memset(vv[:, :, 0:2], 0.0), writes=[t_vhalo])
        w_in_v = d_w_a_in[0].rearrange("(k p) (s c) -> p k s c", p=128, s=3)
        g1 = mod_part(0, 2)
        def mixer_j(wv, wt, jj, j):
            jb = j % 2
            for tg in range(NTG):
                pss = []
                for s in range(3):
                    ps, pt = next_ps()
                    for k in range(KC):
                        S.op("pe", (lambda e, ps=ps, s=s, k=k, tg=tg: e.matmul(ps[:], wv[:, k, s, jj * 128:(jj + 1) * 128], H.ap[:, k, tsl(tg)],
                                                                               start=(k == 0), stop=(k == KC - 1))),
                             reads=[wt, H.t(k, tg)], writes=[pt])
                    pss.append((ps, pt))
                (psb, ptb), (psc, ptc), (psu, ptu) = pss
                S.op("act", (lambda e, psb=psb, tg=tg: e.activation(out=gbv[:, jb, tsl(tg)], in_=psb[:], func=AF.Copy)),
                     reads=[ptb], writes=[t_gb[jb][tg]])
                tb, tbt = next_tb()
                S.op("act", (lambda e, psc=psc, tb=tb: e.activation(out=tb[:], in_=psc[:], func=AF.Copy)), reads=[ptc], writes=[tbt])
                S.op("dve", (lambda e, psu=psu, tb=tb, tg=tg: e.tensor_tensor(out=vv[:, jb, 2 + tg * TGW:2 + (tg + 1) * TGW], in0=psu[:], in1=tb[:], op=ALU.mult)),
                     reads=[ptu, tbt], writes=[t_v[jb][tg]])
            for tg in range(NTG):
                tf, tft = next_tf()
                rd = [t_v[jb][tg], t_vhalo, t_vecs] + ([t_v[jb][tg - 1]] if tg > 0 else [])
                b0 = tg * TGW
                S.op("dve", (lambda e, tf=tf, b0=b0: e.tensor_scalar(out=tf[:], in0=vv[:, jb, b0 + 2:b0 + 2 + TGW], scalar1=vecs[:, V_CONV + 2, j:j + 1], scalar2=None, op0=ALU.mult)),
                     reads=rd, writes=[tft])
                S.op("dve", (lambda e, tf=tf, b0=b0: e.scalar_tensor_tensor(out=tf[:], in0=vv[:, jb, b0 + 1:b0 + 1 + TGW], scalar=vecs[:, V_CONV + 1, j:j + 1], in1=tf[:], op0=ALU.mult, op1=ALU.add)),
                     reads=rd + [tft], writes=[tft])
                S.op("dve", (lambda e, tf=tf, b0=b0: e.scalar_tensor_tensor(out=tf[:], in0=vv[:, jb, b0:b0 + TGW], scalar=vecs[:, V_CONV + 0, j:j + 1], in1=tf[:], op0=ALU.mult, op1=ALU.add)),
                     reads=rd + [tft], writes=[tft])
                S.op("dve", (lambda e, tf=tf, tg=tg: e.tensor_tensor(out=A.ap[:, j, tsl(tg)], in0=tf[:], in1=gbv[:, jb, tsl(tg)], op=ALU.mult)),
                     reads=[tft, t_gb[jb][tg]], writes=[A.t(j, tg)])

        for jp in range(4):
            def view(slot):
                return slot[:, 0:6144].rearrange("p (k s c) -> p k s c", k=8, s=3)
            pieces = []
            for s3 in range(3):
                pieces.append(((lambda slot, s3=s3: view(slot)[:, :, s3, :]), w_in_v[:, :, s3, jp * 256:(jp + 1) * 256]))
            wv, wt = load_w(pieces, view)
            for jj in range(2):
                mixer_j(wv, wt, jj, 2 * jp + jj)
        wv, wt = load_w_std(d_w_a_out[0], 0, 1024)
        proj(wv, wt, A, 8, resid_evac(g1, [t_modv[0]]))
        if stop != "mix0":
            if stop not in ("mix0", "l0"):
                make_mv_tasks(d_w_ada[1], 12, V_BADA + 6, 1)
                make_mv_tasks(d_w_ada_kv, 4, V_BKV, 2)
            mlp(0, 0)
        def chk(name, dumps):
            if stop != name:
                return
            S.barrier()
            for ap, dst in dumps:
                S.op("pool", (lambda e, ap=ap, dst=dst: e.dma_start(out=dst, in_=ap)), dma="out")
            raise StopBuild()

        if stop not in ("mix0", "l0"):
          try:
            G4 = 4
            defpool["v"] = "P6"
            POOLS["P6"] = [0, 1, 2, 3, 4]
            POOLS["M"] = [5, 6, 7]
            t_l1c = Tok()
            wqg_g = SCR[:, 8192:8576].rearrange("p (k c) -> p k c", k=8)
            cw2k = SCR[:, 8576:8832].rearrange("p (c d) -> p c d", c=2)
            cw2v = SCR[:, 8832:8960].rearrange("p (c d) -> p c d", c=2)
            peT_b = SCR[:, 8960:9024]
            S.op("sp", lambda e: e.dma_start(out=hvecs[:], in_=d_hvecs), writes=[t_l1c], dma="c")
            S.op("pool", lambda e: e.dma_start(out=ident_b[:], in_=d_consts[:, 0:128]), writes=[t_l1c], dma="cb")
            S.op("pool", lambda e: e.dma_start(out=bd_ones[:], in_=d_bd), writes=[t_l1c], dma="cb")
            S.op("pool", lambda e: e.dma_start(out=peT_b, in_=d_peT), writes=[t_l1c], dma="cb")
            S.op("pool", lambda e: e.dma_start(out=cw2k[:, :, 0:64], in_=d_cmp_w2[0].rearrange("(c p) d -> p c d", p=128)), writes=[t_l1c], dma="cb")
            S.op("pool", lambda e: e.dma_start(out=cw2k[:, :, 64:128], in_=d_cmp_w2[0].rearrange("(c p) d -> p c d", p=128)), writes=[t_l1c], dma="cb")
            S.op("pool", lambda e: e.dma_start(out=cw2v, in_=d_cmp_w2[1].rearrange("(c p) d -> p c d", p=128)), writes=[t_l1c], dma="cb")
            S.op("pool", lambda e: e.dma_start(out=wqg_g, in_=d_w_qg[0].rearrange("(k p) n -> p k n", p=128)[:, :, 1024:1072]), writes=[t_l1c], dma="cb")
            t_gq = Tok()
            S.op("dve", lambda e: e.tensor_scalar(out=gq[:], in0=hvecs[:, 0:1], scalar1=0.125, scalar2=None, op0=ALU.mult), reads=[t_l1c], writes=[t_gq])

            run_mv_tasks(99)
            compute_rstd()
            derive(2, 1, V_KVG, 2)
            derive(1, 1, V_NG + 2, 3)
            norm_mod(H, der[:, 2, :], mod_part(2, 0), [t_der, t_modv[2]])
            norm_mod(A, der[:, 3, :], mod_part(1, 0), [t_der, t_modv[1]])
            xs_t = [Tok() for _ in range(NTG)]
            for tg in range(NTG):
                for kh in range(2):
                    ks = slice(kh * 4, kh * 4 + 4)
                    S.op("sp", (lambda e, tg=tg, ks=ks: e.dma_start(out=d_xs[:, ks, tsl(tg)], in_=X.ap[:, ks, tsl(tg)])),
                         reads=[X.t(k, tg) for k in range(kh * 4, kh * 4 + 4)], writes=[xs_t[tg]], dma="xs")

            def head_norm(ps, pt, ncol, gain_ap, gain_reads, dst_ap, dst_toks):
                tb, tbt = next_tb()
                S.op("act", (lambda e: e.activation(out=tb[:, 0:ncol], in_=ps[:, 0:ncol], func=AF.Square)), reads=[pt], writes=[tbt])
                ps2, pt2 = next_ps("M")
                S.op("pe", (lambda e: e.matmul(ps2[:, 0:ncol], bd_ones[:], tb[:, 0:ncol], start=True, stop=True)), reads=[tbt, t_l1c], writes=[pt2])
                tf, tft = next_tf()
                S.op("act", (lambda e: e.activation(out=tf[:, 0:ncol], in_=ps2[:, 0:ncol], func=AF.Ln, bias=eps_t[:, 0:1], scale=1.0 / 64)),
                     reads=[pt2, t_eps], writes=[tft])
                tf2, tft2 = next_tf()
                S.op("act", (lambda e: e.activation(out=tf2[:, 0:ncol], in_=tf[:, 0:ncol], func=AF.Exp, scale=-0.5)), reads=[tft], writes=[tft2])
                S.op("dve", (lambda e: e.scalar_tensor_tensor(out=dst_ap, in0=ps[:, 0:ncol], scalar=gain_ap, in1=tf2[:, 0:ncol], op0=ALU.mult, op1=ALU.mult)),
                     reads=[pt, tft2] + gain_reads, writes=dst_toks)

            RAW = TT(SCR[:, 0:8192].rearrange("p (c t) -> p c t", c=4))
            wv, wt = load_w_std(d_w_kv, 0, 512)

            def ev_raw(oc, tg, ps, pt):
                S.op("act", (lambda e: e.activation(out=RAW.ap[:, oc, tsl(tg)], in_=ps[:], func=AF.Copy)), reads=[pt], writes=[RAW.t(oc, tg)])
            proj(wv, wt, H, 4, ev_raw)

            chk("raw", [(RAW.ap, d_y[:, 0:4, :])])
            t_kc = [Tok() for _ in range(G4)]
            t_vc = [Tok() for _ in range(G4)]
            t_vc_ones = Tok()
            S.op("dve", lambda e: e.memset(vcA[:, :, 64:128], 1.0), writes=[t_vc_ones])
            t_cb = Tok()
            for kv in range(2):
                def view1(slot):
                    return slot[:, 0:8192].rearrange("p (l h) -> p l h", l=32)
                src1 = d_cmp_w1[kv].rearrange("(l d) h -> d l h", d=64)
                pieces = [((lambda slot: view1(slot)[0:64, :, :]), src1), ((lambda slot: view1(slot)[64:128, :, :]), src1)]
                cwv, cwt = load_w(pieces, view1)
                psb, ptb = next_ps()
                for hc in range(2):
                    for l in range(32):
                        S.op("pe", (lambda e, hc=hc, l=l, cwv=cwv, psb=psb, kv=kv: e.matmul(
                            psb[:, hc:hc + 1], cwv[0:64, l, hc * 128:(hc + 1) * 128], peT_b[0:64, kv * 32 + l:kv * 32 + l + 1],
                            start=(l == 0), stop=(l == 31))), reads=[cwt, t_l1c], writes=[ptb])
                S.op("dve", (lambda e, psb=psb, kv=kv: e.tensor_copy(out=cbias[:, 2 * kv:2 * kv + 2], in_=psb[:, 0:2])), reads=[ptb], writes=[t_cb])
                for g in range(G4):
                    base = (g % 2) * 64
                    c = kv * 2 + g // 2
                    hids = []
                    for hc in range(2):
                        ps, pt = next_ps()
                        for l in range(32):
                            S.op("pe", (lambda e, ps=ps, l=l, hc=hc, cwv=cwv, base=base, c=c: e.matmul(
                                ps[:, 0:127], cwv[base:base + 64, l, hc * 128:(hc + 1) * 128],
                                RAW.ap[base:base + 64, c, l:l + 16 * 126 + 1:16], start=(l == 0), stop=(l == 31))),
                                reads=[cwt] + [RAW.t(c, tg) for tg in range(NTG)], writes=[pt])
                        z, zt = next_tf()
                        S.op("act", (lambda e, ps=ps, z=z, hc=hc, kv=kv: e.activation(out=z[:, 0:127], in_=ps[:, 0:127], func=AF.Identity,
                                                                                      bias=cbias[:, 2 * kv + hc:2 * kv + hc + 1], scale=1.0)),
                             reads=[pt, t_cb], writes=[zt])
                        u, ut = next_tf()
                        S.op("dve", (lambda e, z=z, u=u: e.tensor_tensor(out=u[:, 0:127], in0=z[:, 0:127], in1=z[:, 0:127], op=ALU.mult)), reads=[zt], writes=[ut])
                        S.op("dve", (lambda e, u=u: e.tensor_scalar(out=u[:, 0:127], in0=u[:, 0:127], scalar1=0.044715, scalar2=1.0, op0=ALU.mult, op1=ALU.add)),
                             reads=[ut], writes=[ut])
                        S.op("dve", (lambda e, z=z, u=u: e.tensor_tensor(out=u[:, 0:127], in0=u[:, 0:127], in1=z[:, 0:127], op=ALU.mult)), reads=[ut, zt], writes=[ut])
                        S.op("act", (lambda e, u=u: e.activation(out=u[:, 0:127], in_=u[:, 0:127], func=AF.Sigmoid, scale=1.5957691216057308)), reads=[ut], writes=[ut])
                        hb_, hbt = next_tb()
                        S.op("dve", (lambda e, z=z, u=u, hb_=hb_: e.tensor_tensor(out=hb_[:, 0:127], in0=u[:, 0:127], in1=z[:, 0:127], op=ALU.mult)), reads=[ut, zt], writes=[hbt])
                        hids.append((hb_, hbt))
                    if kv == 0:
                        ps, pt = next_ps()
                        for hc in range(2):
                            S.op("pe", (lambda e, ps=ps, hc=hc, hb_=hids[hc][0]: e.matmul(ps[:, 0:127], cw2k[:, hc, :], hb_[:, 0:127], start=(hc == 0), stop=(hc == 1))),
                                 reads=[hids[hc][1], t_l1c], writes=[pt])
                        head_norm(ps, pt, 127, hvecs[:, 1:2], [t_l1c], kcT[:, g, 0:127], [t_kc[g]])
                    else:
                        ps, pt = next_ps()
                        for hc in range(2):
                            S.op("pe", (lambda e, ps=ps, hc=hc, hb_=hids[hc][0]: e.matmul(ps[0:127, 0:64], hb_[:, 0:127], cw2v[:, hc, :], start=(hc == 0), stop=(hc == 1))),
                                 reads=[hids[hc][1], t_l1c], writes=[pt])
                        S.op("act", (lambda e, ps=ps, g=g: e.activation(out=vcA[0:127, g, 0:64], in_=ps[0:127, 0:64], func=AF.Copy)), reads=[pt], writes=[t_vc[g]])

            chk("cmp", [(kcT[:].rearrange("p g n -> p (g n)"), d_y[:, 0, 0:512]), (vcA[:].rearrange("p g n -> p (g n)"), d_y[:, 1, 0:512])])
            wv_q, wt_q = load_w_std(d_w_qg[0], 0, 1024)
            S.barrier()
            XB = XR[:].bitcast(BF16)
            QT = TT(XB[:, 0:16384].rearrange("p (h t) -> p h t", h=8))
            KST = TT(XB[:, 16384:24576].rearrange("p (g t) -> p g t", g=4))
            KWT = TT(XB[:, 24576:32768].rearrange("p (g t) -> p g t", g=4))
            RB = rstd_s[:].bitcast(BF16)
            SIGG = TT(RB[0:48, 0:T])

            def ev_q(oc, tg, ps, pt):
                head_norm(ps, pt, TGW, gq[:, 0:1], [t_gq], QT.ap[:, oc, tsl(tg)], [QT.t(oc, tg)])
            proj(wv_q, wt_q, A, 8, ev_q)
            for tg in range(NTG):
                ps, pt = next_ps()
                for k in range(KC):
                    S.op("pe", (lambda e, ps=ps, k=k, tg=tg: e.matmul(ps[0:48, :], wqg_g[:, k, :], A.ap[:, k, tsl(tg)], start=(k == 0), stop=(k == KC - 1))),
                         reads=[t_l1c, A.t(k, tg)], writes=[pt])
                S.op("act", (lambda e, ps=ps, tg=tg: e.activation(out=SIGG.ap[:, tsl(tg)], in_=ps[0:48, :], func=AF.Sigmoid)), reads=[pt], writes=[SIGG.t(tg)])

            for typ, dst, gi in ((2, KST, 2), (4, KWT, 3)):
                def viewk(slot):
                    return slot[:, 0:4096].rearrange("p (k g r d) -> p k g r d", k=8, g=4, r=2)
                srck = d_w_kv.rearrange("(k p) n -> p k n", p=128)[:, :, typ * 256:(typ + 1) * 256].rearrange("p k (g d) -> p k g d", g=4)
                pieces = [((lambda slot, r=r, gg=gg: viewk(slot)[:, :, gg, r, :]), srck[:, :, gg, :]) for r in range(2) for gg in range(4)]
                kv_, kt_ = load_w(pieces, viewk)

                def ev_k(oc, tg, ps, pt, dst=dst, gi=gi):
                    head_norm(ps, pt, TGW, hvecs[:, gi:gi + 1], [t_l1c], dst.ap[:, oc, tsl(tg)], [dst.t(oc, tg)])
                proj(kv_, kt_, H, 4, ev_k, oc_cols=(lambda wv_, k, oc: wv_[:, k, oc, :, :].rearrange("p r d -> p (r d)")))
            chk("q", [(QT.ap, d_y)])
            chk("k", [(KST.ap, d_y[:, 0:4, :]), (KWT.ap, d_y[:, 4:8, :]), ])

            def viewv(slot):
                return slot[:, 0:4096].rearrange("p (k s c) -> p k s c", k=8, s=2)
            srcv = d_w_kv.rearrange("(k p) n -> p k n", p=128)
            pieces = [((lambda slot: viewv(slot)[:, :, 0, :]), srcv[:, :, 768:1024]), ((lambda slot: viewv(slot)[:, :, 1, :]), srcv[:, :, 1280:1536])]
            vv_, vt_ = load_w(pieces, viewv)
            S.barrier()
            AB = AR[:]
            VS = TT(AB[:, 0:8192].rearrange("p (t g c) -> p t g c", t=16, g=4))
            VW = TT(AB[:, 8192:16384].rearrange("p (t g c) -> p t g c", t=16, g=4))
            t_vones = Tok()
            S.op("dve", lambda e: e.memset(VS.ap[:, :, :, 64:128], 1.0), writes=[t_vones])
            S.op("dve", lambda e: e.memset(VW.ap[:, :, :, 64:128], 1.0), writes=[t_vones])

            for tt in range(16):
                ps, pt = next_ps()
                tg = tt // 4
                for k in range(KC):
                    S.op("pe", (lambda e, ps=ps, k=k, tt=tt: e.matmul(ps[:], H.ap[:, k, tt * 128:(tt + 1) * 128], vv_[:, k, :, :].rearrange("p s c -> p (s c)"),
                                                                    start=(k == 0), stop=(k == KC - 1))), reads=[vt_, H.t(k, tg)], writes=[pt])
                S.op("act", (lambda e, ps=ps, tt=tt: e.activation(out=VS.ap[:, tt, :, 0:64], in_=ps[:, 0:256].rearrange("p (g d) -> p g d", g=4), func=AF.Copy)),
                     reads=[pt, t_vones], writes=[VS.t(tt)])
                S.op("dve", (lambda e, ps=ps, tt=tt: e.tensor_copy(out=VW.ap[:, tt, :, 0:64], in_=ps[:, 256:512].rearrange("p (g d) -> p g d", g=4))),
                     reads=[pt, t_vones], writes=[VW.t(tt)])
            chk("v", [(AB[:, 0:8192], d_y[:, 0:4, :].rearrange("p a b -> p (a b)")), (AB[:, 8192:16384], d_y[:, 4:8, :].rearrange("p a b -> p (a b)"))])
            wo_v, wo_t = load_w_std(d_w_o[0], 0, 1024)
            S.barrier()

            POOLS["M"] = [7]
            t_ac = Tok()
            tri_b = SCR[:, 0:128]
            anti_b = SCR[:, 128:256]
            cmpm = SCR[:, 256:2304]
            e128 = SCR[:, 2304:4352].rearrange("p (k j) -> p k j", k=16)
            cmap = SCR[:, 4352:4392]
            force = SCR[:, 4392:4904].bitcast(F32).rearrange("p (q j) -> p q j", q=8)
            gsel = SCR[0:48, 4904:7976].rearrange("p (h b m) -> p h b m", h=8, b=3)
            PTS = [SCR[:, 7976 + i * 1024:7976 + (i + 1) * 1024].rearrange("p (k r c) -> p k r c", k=2, r=2) for i in range(3)]
            pts_t = [Tok() for _ in PTS]
            S.op("pool", lambda e: e.dma_start(out=SCR[:, 0:2304], in_=d_amask), writes=[t_ac], dma="cb")
            S.op("pool", lambda e: e.dma_start(out=SCR[:, 2304:4352], in_=d_e128), writes=[t_ac], dma="cb")
            S.op("pool", lambda e: e.dma_start(out=cmap, in_=d_cmap), writes=[t_ac], dma="cb")
            S.op("sp", lambda e: e.dma_start(out=SCR[:, 4392:4904].bitcast(F32), in_=d_force), writes=[t_ac], dma="c")
            S.op("pool", lambda e: e.dma_start(out=SCR[0:48, 4904:7976], in_=d_gsel), writes=[t_ac], dma="cb")
            HB = HR[:]
            OTG = TT(HB[:, 0:4096].rearrange("p (h t) -> p h t", h=8))
            GBC = HB[:, 4096:7168].rearrange("p (h b t) -> p h b t", h=2, b=3)
            t_gbc = Tok()
            XST = TT(HB[:, 7168:15360].bitcast(F32).rearrange("p (k t) -> p k t", k=8))
            BIAS = [HB[:, 15360 + i * 256:15360 + (i + 1) * 256].rearrange("p (r q) -> p r q", r=2) for i in range(2)]
            bias_t = [Tok(), Tok()]
            for i in range(2):
                S.op("dve", (lambda e, i=i: e.memset(BIAS[i], 0.0)), writes=[bias_t[i]])
            ptc = {"n": 0, "b": 0, "a": 0, "c": 0}
            cmb_lo = [Tok(), Tok()]
            cmb_hi = [Tok(), Tok()]
            cmb_on = [Tok(), Tok()]

            M2 = BIAS_EXTRA
            tri2 = M2[:, 0:256]
            anti2 = M2[:, 256:512]
            t_m2 = Tok()
            S.op("dve", lambda e: e.tensor_copy(out=tri2.rearrange("p (r q) -> p r q", r=2), in_=tri_b.unsqueeze(1).to_broadcast([128, 2, 128])), reads=[t_ac], writes=[t_m2])
            S.op("dve", lambda e: e.tensor_copy(out=anti2.rearrange("p (r q) -> p r q", r=2), in_=anti_b.unsqueeze(1).to_broadcast([128, 2, 128])), reads=[t_ac], writes=[t_m2])
            CM2 = [M2[:, 512 + i * 256:512 + (i + 1) * 256] for i in range(2)]
            cm2_t = [Tok(), Tok()]

            def emit_cm2(ci, qt):
                S.op("dve", (lambda e: e.tensor_copy(out=CM2[ci].rearrange("p (r q) -> p r q", r=2),
                                                     in_=cmpm[:, qt * 128:(qt + 1) * 128].unsqueeze(1).to_broadcast([128, 2, 128]))),
                     reads=[t_ac], writes=[cm2_t[ci]])

            def next_pt():
                i = ptc["n"] % 3
                ptc["n"] += 1
                return PTS[i], pts_t[i]

            def qk_tile(bank, bt, colbase, KT, g, kt0, nk, par, qt, mask, bias_i, phase):
                lhs_k = KT.ap[par * 64:(par + 1) * 64, g, kt0:kt0 + nk] if KT is not None else kcT[par * 64:(par + 1) * 64, g, 0:127]
                k_reads = [KT.t(g, kt0 // TGW)] if KT is not None else [t_kc[g]]
                qsl = slice(qt * 128, (qt + 1) * 128)
                q_reads = [QT.t(2 * g, qt // 4), QT.t(2 * g + 1, qt // 4)]
                if phase == 1:
                    S.op("pe", (lambda e: e.matmul(bank[0:nk, colbase:colbase + 256], lhs_k, QT.ap[par * 64:(par + 1) * 64, 2 * g:2 * g + 2, qsl],
                                                   start=(mask is None), stop=True)), reads=k_reads + q_reads, writes=[bt])
                elif mask is None:
                    pass
                elif mask == "bias":
                    kt = kt0 // 128
                    S.op("pe", (lambda e: e.matmul(bank[:, colbase:colbase + 256], e128[:, kt, :], BIAS[bias_i].rearrange("p r q -> p (r q)"), start=True, stop=False)),
                         reads=[t_ac, bias_t[bias_i]], writes=[bt])
                else:
                    mask_ap, mask_reads = mask
                    S.op("pe", (lambda e: e.matmul(bank[0:nk, colbase:colbase + 256], ident_b[:, 0:nk], mask_ap, start=True, stop=False)),
                         reads=[t_l1c] + mask_reads, writes=[bt])

            def branch_steps(g, qt, kind, bias_i, pos, br, acc, acct):
                ps_o, pt_o = next_ps("O")
                out = []
                if kind == "cmp":
                    bA, tA = next_ps("S")
                    bB, tB = next_ps("S")
                    PT, ptt = next_pt()

                    ci = ptc["a"] % 2
                    ptc["a"] += 1

                    def qk():
                        if g == 0 and qt == 0:
                            emit_cm2(ci, qt)
                        for phase in range(2):
                            for par, bank, bt in ((0, bA, tA), (1, bB, tB)):
                                qk_tile(bank, bt, 0, None, g, 0, 127, par, qt, (CM2[ci], [cm2_t[ci]]), None, phase)
                        for par, bank, bt in ((0, bA, tA), (1, bB, tB)):
                            S.op("act", (lambda e, par=par, bank=bank: e.activation(out=PT[0:127, 0, par, :], in_=bank[0:127, 0:256], func=AF.Exp)),
                                 reads=[bt], writes=[ptt])
                        nqt = qt + 1 if qt % 4 != 3 else (qt - 3 if g < 3 else qt + 1)
                        if nqt < 16:
                            emit_cm2(1 - ci, nqt)
                        if qt >= 8:
                            topk_a(g, qt, PT, ptt)

                    def pv():
                        S.op("pe", (lambda e: e.matmul(ps_o[:], vcA[0:127, g, :], PT[0:127, 0, :, :].rearrange("p r c -> p (r c)"), start=True, stop=True)),
                             reads=[ptt, t_vc[g], t_vc_ones], writes=[pt_o])
                        combine(g, qt, pos, br, ps_o, pt_o, acc, acct)
                    return [(qk, pv)]
                KT, V = (KST, VS) if kind == "sel" else (KWT, VW)
                kts = list(range(0, qt + 1)) if kind == "sel" else list(range(max(0, qt - 4), qt + 1))
                for pi in range(0, len(kts), 2):
                    pair = kts[pi:pi + 2]
                    banks = (next_ps("S"), next_ps("S"))
                    PTp = next_pt()

                    def qk(pair=pair, banks=banks, PTp=PTp):
                        PT, ptt = PTp
                        npair = len(pair)
                        if kind == "win" and qt >= 8 and pair[-1] == qt:
                            topk_b(bias_i)
                        for ktp, kt in enumerate(pair):
                            if kt == qt:
                                mask = (tri2, [t_m2])
                            elif kind == "win" and kt == qt - 4:
                                mask = (anti2, [t_m2])
                            elif kind == "sel" and qt >= 8:
                                mask = "bias"
                            else:
                                mask = None
                            for phase in range(2):
                                for par, (bank, bt) in enumerate(banks):
                                    qk_tile(bank, bt, ktp * 256, KT, g, kt * 128, 128, par, qt, mask, bias_i, phase)
                        for par, (bank, bt) in enumerate(banks):
                            S.op("act", (lambda e, par=par, bank=bank: e.activation(
                                out=PT[:, 0:npair, par, :], in_=bank[:, 0:npair * 256].rearrange("p (k c) -> p k c", k=npair), func=AF.Exp)),
                                reads=[bt], writes=[ptt])

                    def pv(pair=pair, PTp=PTp, last=(pi + 2 >= len(kts))):
                        PT, ptt = PTp
                        for ktp, kt in enumerate(pair):
                            S.op("pe", (lambda e, ktp=ktp, kt=kt: e.matmul(ps_o[:], V.ap[:, kt, g, :], PT[:, ktp, :, :].rearrange("p r c -> p (r c)"),
                                                                         start=(kt == kts[0]), stop=(kt == kts[-1]))),
                                 reads=[ptt, V.t(kt), t_vones], writes=[pt_o])
                        if last:
                            combine(g, qt, pos, br, ps_o, pt_o, acc, acct)
                    out.append((qk, pv))
                return out

            def combine(g, qt, pos, br, ps_o, pt_o, acc, acct):
                qi = qt % 4
                ci = ptc["c"] % 2
                ptc["c"] += 1
                T, tlo, thi = TMPF[ci], cmb_lo[ci], cmb_hi[ci]
                S.op("act", (lambda e: e.activation(out=T[0:64, :], in_=ps_o[64:128, :], func=AF.Ln, bias=eps_t[64:128, 1:2], scale=1.0)), reads=[pt_o, t_eps], writes=[tlo])
                S.op("act", (lambda e: e.activation(out=T[64:128, :], in_=T[0:64, :], func=AF.Exp, scale=-1.0)), reads=[tlo], writes=[thi])
                on, ont = TMPF[2][:, ci * 256:(ci + 1) * 256], cmb_on[ci]
                for par in range(2):
                    S.op("dve", (lambda e, par=par: e.tensor_tensor(out=on[par * 64:(par + 1) * 64, :], in0=ps_o[0:64, par * 256:(par + 1) * 256],
                                                                     in1=T[64:128, par * 256:(par + 1) * 256], op=ALU.mult)), reads=[pt_o, thi], writes=[ont])
                onv = on.rearrange("p (h q) -> p h q", h=2)
                accv = acc[:].rearrange("p (h q) -> p h q", h=2)
                Gv = GBC[:, :, br, qi * 128:(qi + 1) * 128]
                if pos == 0:
                    S.op("dve", (lambda e: e.tensor_tensor(out=accv, in0=onv, in1=Gv, op=ALU.mult)), reads=[ont, t_gbc], writes=[acct])
                else:
                    S.op("dve", (lambda e: e.tensor_tensor(out=onv, in0=onv, in1=Gv, op=ALU.mult)), reads=[ont, t_gbc], writes=[ont])
                    if pos == 1:
                        S.op("dve", (lambda e: e.tensor_tensor(out=accv, in0=accv, in1=onv, op=ALU.add)), reads=[ont, acct], writes=[acct])
                    else:
                        S.op("dve", (lambda e: e.tensor_tensor(out=OTG.ap[:, 2 * g:2 * g + 2, qi * 128:(qi + 1) * 128], in0=accv, in1=onv, op=ALU.add)),
                             reads=[ont, acct], writes=[OTG.t(2 * g, 0), OTG.t(2 * g + 1, 0)])

            def topk_a(g, qt, PT, ptt):
                ps_i, pt_i = next_ps("M")
                for c in range(4):
                    S.op("pe", (lambda e, c=c: e.matmul(ps_i[:, c * 33:(c + 1) * 33], PT[0:127, 0, c // 2, (c % 2) * 128:(c % 2 + 1) * 128], cmap[0:127, 0:33],
                                                          start=True, stop=True)), reads=[ptt, t_ac], writes=[pt_i])
                psv = ps_i[:, 0:132].rearrange("p (c j) -> p c j", c=4)
                S.op("dve", (lambda e: e.reciprocal(out=rd4[:], in_=psv[:, :, 32])), reads=[pt_i], writes=[t_tk])
                S.op("dve", (lambda e: e.tensor_scalar(out=scA[:], in0=psv[:, 0, 0:32], scalar1=rd4[:, 0:1], scalar2=None, op0=ALU.mult)), reads=[pt_i, t_tk], writes=[t_tk])
                for c in range(1, 4):
                    S.op("dve", (lambda e, c=c: e.scalar_tensor_tensor(out=scA[:], in0=psv[:, c, 0:32], scalar=rd4[:, c:c + 1], in1=scA[:], op0=ALU.mult, op1=ALU.add)),
                         reads=[pt_i, t_tk], writes=[t_tk])
                S.op("dve", (lambda e: e.tensor_tensor(out=scA[:], in0=scA[:], in1=force[:, qt - 8, :], op=ALU.add)), reads=[t_tk, t_ac], writes=[t_tk])
                S.op("dve", (lambda e: e.max(out=m8a[:], in_=scA[:])), reads=[t_tk], writes=[t_tk])
                S.op("dve", (lambda e: e.match_replace(out=scB[:], in_to_replace=m8a[:], in_values=scA[:], imm_value=-1e30)), reads=[t_tk], writes=[t_tk])
                S.op("dve", (lambda e: e.max(out=m8b[:], in_=scB[:])), reads=[t_tk], writes=[t_tk])
                S.op("dve", (lambda e: e.tensor_scalar(out=selm[:], in0=scA[:], scalar1=m8b[:, 7:8], scalar2=1.0, op0=ALU.is_ge, op1=ALU.subtract)), reads=[t_tk], writes=[t_tk])

            def topk_b(bias_i):
                ps_m, pt_m = next_ps("M")
                S.op("pe", (lambda e: e.matmul(ps_m[0:32, 0:128], selm[:], ident_b[:], start=True, stop=True)), reads=[t_tk, t_l1c], writes=[pt_m])
                S.op("act", (lambda e: e.activation(out=BIAS[bias_i][0:32, :, :], in_=ps_m[0:32, 0:128].unsqueeze(1).to_broadcast([32, 2, 128]), func=AF.Copy, scale=30000.0)),
                     reads=[pt_m], writes=[bias_t[bias_i]])

            on_t = [Tok(), Tok()]
            acc_t = [Tok(), Tok()]
            t_tk = Tok()
            g1 = mod_part(1, 2)
            steps = []

            def emit_xreload(tg):
                for kh in range(2):
                    ks = slice(kh * 4, kh * 4 + 4)
                    S.op("sp", (lambda e, ks=ks: e.dma_start(out=XST.ap[:, ks, :], in_=d_xs[:, ks, tsl(tg)])),
                         reads=[xs_t[tg]], writes=[XST.t(k) for k in range(kh * 4, kh * 4 + 4)], dma="xr")

            def emit_gbc(tg, g):
                for hpl in range(2):
                    for br in range(3):
                        ps, pt = next_ps("OM")
                        S.op("pe", (lambda e, ps=ps, hpl=hpl, br=br: e.matmul(ps[:], gsel[:, 2 * g + hpl, br, :], SIGG.ap[:, tsl(tg)], start=True, stop=True)),
                             reads=[t_ac, SIGG.t(tg)], writes=[pt])
                        if (hpl * 3 + br) % 2 == 0:
                            S.op("act", (lambda e, ps=ps, hpl=hpl, br=br: e.activation(out=GBC[:, hpl, br, :], in_=ps[:], func=AF.Copy)), reads=[pt], writes=[t_gbc])
                        else:
                            S.op("dve", (lambda e, ps=ps, hpl=hpl, br=br: e.tensor_copy(out=GBC[:, hpl, br, :], in_=ps[:])), reads=[pt], writes=[t_gbc])

            def emit_wo(tg):
                def ev_o(oc, tg_, ps, pt):
                    S.op("dve", (lambda e: e.scalar_tensor_tensor(out=XST.ap[:, oc, :], in0=ps[:], scalar=g1[:, oc:oc + 1], in1=XST.ap[:, oc, :], op0=ALU.mult, op1=ALU.add)),
                         reads=[pt, XST.t(oc), t_modv[1]], writes=[XST.t(oc)])
                defpool["v"] = "OM"
                proj(wo_v, wo_t, OTG, 8, ev_o, tgs=[0])
                defpool["v"] = "P6"
                for kh in range(2):
                    ks = slice(kh * 4, kh * 4 + 4)
                    S.op("sp", (lambda e, ks=ks: e.dma_start(out=d_xs[:, ks, tsl(tg)], in_=XST.ap[:, ks, :])),
                         reads=[XST.t(k) for k in range(kh * 4, kh * 4 + 4)], writes=[xs_t[tg]], dma="xw")

            for tg in range(NTG):
                steps.append((None, (lambda tg=tg: emit_xreload(tg))))
                for g in range(G4):
                    steps.append((None, (lambda tg=tg, g=g: emit_gbc(tg, g))))
                    for qi in range(4):
                        qt = tg * 4 + qi
                        ai = ptc["b"] % 2
                        ptc["b"] += 1
                        acc, acct = ACC[ai], acc_t[ai]
                        steps += branch_steps(g, qt, "cmp", ai, 0, 0, acc, acct)
                        steps += branch_steps(g, qt, "win", ai, 1, 2, acc, acct)
                        steps += branch_steps(g, qt, "sel", ai, 2, 1, acc, acct)
                steps.append((None, (lambda tg=tg: emit_wo(tg))))
            prev_pv = None
            for qk_fn, pv_fn in steps:
                if qk_fn is not None:
                    qk_fn()
                if prev_pv is not None:
                    prev_pv()
                prev_pv = pv_fn
            if prev_pv is not None:
                prev_pv()
            S.barrier()
            for tg in range(NTG):
                for kh in range(2):
                    ks = slice(kh * 4, kh * 4 + 4)
                    S.op("sp", (lambda e, tg=tg, ks=ks: e.dma_start(out=X.ap[:, ks, tsl(tg)], in_=d_xs[:, ks, tsl(tg)])),
                         reads=[xs_t[tg]], writes=[X.t(k, tg) for k in range(kh * 4, kh * 4 + 4)], dma="x2")
            defpool["v"] = "ALL"
            if stop != "mix1":
                mlp(1, 1)
          except StopBuild:
            S.emit(final_dma_keys=["out"])
            return nc

        for tg in range(NTG):
            for kh in range(2):
                ks = slice(kh * 4, kh * 4 + 4)
                S.op("sp", (lambda e, tg=tg, ks=ks: e.dma_start(out=d_y[:, ks, tsl(tg)], in_=X.ap[:, ks, tsl(tg)])),
                     reads=[X.t(k, tg) for k in range(kh * 4, kh * 4 + 4)], dma="out")
        S.emit(final_dma_keys=["out"])
    return nc


def _fm(v):
    return np.ascontiguousarray(v.reshape(KC, 128).T)


def _const_tables():
    f32 = np.float32
    NEGM = -30000.0
    j = np.arange(128)[:, None]
    t = np.arange(128)[None, :]
    tri = np.where(j <= t, 0.0, NEGM)
    anti = np.where(j > t, 0.0, NEGM)
    n = np.arange(128)[:, None]
    tt = np.arange(T)[None, :]
    cmpm = np.where((16 * n + 31 <= tt) & (n < 127), 0.0, NEGM)
    amask = np.concatenate([tri, anti, cmpm], axis=1).astype(f32)
    e128 = np.zeros((128, 16, 128), f32)
    for kt in range(16):
        for jj in range(128):
            e128[2 * kt + jj // 64, kt, jj] = 1.0
    cmap = np.zeros((128, 40), f32)
    c0 = np.arange(127)[:, None] * 16
    s0 = np.arange(32)[None, :] * 64
    ov = np.minimum(c0 + 32, s0 + 64) - np.maximum(c0, s0)
    cmap[:127, :32] = np.clip(ov, 0, None) / 32.0
    cmap[:127, 32] = 1.0
    force = np.zeros((128, 8, 32), f32)
    for q in range(8):
        tq = 128 * (q + 8) + np.arange(128)
        cur = tq // 64
        jb = np.arange(32)[None, :]
        forced = (jb == 0) | (jb == cur[:, None]) | (jb == cur[:, None] - 1)
        force[:, q, :] = np.where(forced, 1e4, np.where(jb > cur[:, None], -1e4, 0.0))
    gsel = np.zeros((48, 8, 3, 128), f32)
    for hp in range(8):
        for br in range(3):
            for m in range(128):
                gsel[(2 * hp + m // 64) * 3 + br, hp, br, m] = 1.0
    p = np.arange(128)
    bd = (p[:, None] // 64 == p[None, :] // 64).astype(f32)
    return {"amask": amask, "e128": e128.reshape(128, 2048), "cmap": cmap, "force": force.reshape(128, 256),
            "gsel": gsel.reshape(48, 3072), "bdones": bd}


def prep_inputs(inputs):
    f32 = np.float32
    g = {k: np.asarray(v, dtype=f32) for k, v in inputs.items()}
    vecs = np.zeros((128, NV, KC), f32)
    for i in range(2):
        for j in range(2):
            vecs[:, V_NG + 2 * i + j, :] = _fm(g["norm_gain"][i, j])
        for part in range(6):
            vecs[:, V_BADA + 6 * i + part, :] = _fm(g["b_ada"][i, part * D:(part + 1) * D])
    vecs[:, V_KVG, :] = _fm(g["kv_norm_gain"])
    for part in range(2):
        vecs[:, V_BKV + part, :] = _fm(g["b_ada_kv"][part * D:(part + 1) * D])
    for j in range(3):
        vecs[:, V_CONV + j, :] = _fm(g["conv_w"][0, j])
    consts = np.concatenate([np.eye(128, dtype=f32), np.ones((128, 128), f32)], axis=1)
    p64 = np.arange(128) % 64
    hvecs = np.zeros((128, 4), f32)
    hvecs[:, 0] = g["q_gain"][0, p64]
    for i in range(3):
        hvecs[:, 1 + i] = g["k_gain"][i, p64]
    peT = np.zeros((128, 64), f32)
    for kv in range(2):
        peT[:, kv * 32:(kv + 1) * 32] = g["cmp_pe"][kv][:, p64].T
    shared = {
        "vecs": vecs, "consts": consts, "hvecs": hvecs, "peT": peT,
        "w_ada": g["w_ada"], "w_a_in": g["w_a_in"], "w_a_out": g["w_a_out"],
        "w_mlp1": g["w_mlp1"], "w_mlp2": g["w_mlp2"],
        "w_ada_kv": g["w_ada_kv"], "w_kv": g["w_kv"], "w_qg": g["w_qg"], "w_o": g["w_o"],
        "cmp_w1": g["cmp_w1"], "cmp_w2": g["cmp_w2"],
    }
    shared.update(_const_tables())
    in_maps = []
    for b in range(N_CORES):
        m = dict(shared)
        xT = g["x"][b].T.reshape(KC, 128, T).transpose(1, 0, 2)
        m["xT"] = np.ascontiguousarray(xT)
        m["cT"] = _fm(g["c"][b])
        in_maps.append(m)
    return in_maps


def post_outputs(results):
    outs = []
    for r in results:
        yT = np.asarray(r["yT"])
        outs.append(yT.transpose(2, 1, 0).reshape(T, D))
    return np.stack(outs, axis=0).astype(np.float32)


def kernel(**inputs):
    in_maps = prep_inputs(inputs)
    nc = build_program(DEBUG_STOP)
    res = run_bass_kernel_spmd(nc, in_maps, core_ids=list(range(N_CORES)))
    return post_outputs(res.results)
```

```python
import numpy as np
from contextlib import ExitStack
import concourse.bass as bass
import concourse.mybir as mybir
from concourse.bass_utils import run_bass_kernel_spmd

F32 = mybir.dt.float32
BF16 = mybir.dt.bfloat16
AF = mybir.ActivationFunctionType
ALU = mybir.AluOpType

D = 1024
T = 2048
KC = 8
NTG = 4
TGW = 512
EPS = 1e-6
N_CORES = 8

V_NG = 0
V_KVG = 4
V_BADA = 5
V_BKV = 17
V_CONV = 19
NV = 22

DEBUG_STOP = None


class Tok:
    __slots__ = ("ws", "rs", "rdma", "excl")

    def __init__(self, excl=False):
        self.ws = []
        self.rs = {}
        self.rdma = []
        self.excl = excl


class Op:
    __slots__ = ("eng", "fn", "deps", "dma", "signal", "sigval", "dmaval", "dsem")

    def __init__(self, eng, fn, dma):
        self.eng = eng
        self.fn = fn
        self.dma = dma
        self.deps = ()
        self.signal = False
        self.sigval = 0
        self.dmaval = 0
        self.dsem = None


ENGS = ["pe", "act", "dve", "pool", "sp"]
DMA_SEMS = 16


class Sched:
    def __init__(self, nc):
        self.nc = nc
        self.ops = {e: [] for e in ENGS}
        self.dma_hist = {e: [] for e in ENGS}

    def op(self, eng, fn, reads=(), writes=(), dma=None):
        o = Op(eng, fn, dma)
        ex = [t for t in reads if t.excl]
        if ex:
            reads = [t for t in reads if not t.excl]
            writes = list(writes) + ex
        deps = set()
        for t in reads:
            deps.update(t.ws)
        for t in writes:
            deps.update(t.ws)
            deps.update(t.rs.values())
            deps.update(t.rdma)
        if dma is not None:
            deps = {d for d in deps if d.dma != dma}
            hist = self.dma_hist[eng]
            n = len(hist)
            o.dsem = (eng, n % DMA_SEMS)
            o.dmaval = 16 * (n // DMA_SEMS + 1)
            if n >= DMA_SEMS:
                deps.add(hist[n - DMA_SEMS])
            hist.append(o)
        o.deps = tuple(deps)
        for t in reads:
            if dma is not None:
                t.rdma.append(o)
            else:
                t.rs[eng] = o
        for t in writes:
            if dma is not None and t.ws and not t.rs and not t.rdma and all(w.dma == dma for w in t.ws):
                t.ws.append(o)
            else:
                t.ws = [o]
            t.rs = {}
            t.rdma = []
        self.ops[eng].append(o)
        return o

    def barrier(self, engines=("pe", "act", "dve", "sp", "pool")):
        lasts = []
        for e in ENGS:
            for o in reversed(self.ops[e]):
                if o.dma is None and o.fn is not None:
                    lasts.append(o)
                    break
            lasts += self.dma_hist[e][-DMA_SEMS:]
        for e in engines:
            o = Op(e, None, None)
            o.deps = tuple(lasts)
            self.ops[e].append(o)

    def emit(self, final_dma_keys=()):
        nc = self.nc
        for e in ENGS:
            for o in self.ops[e]:
                for d in o.deps:
                    if d.dma is None:
                        if d.eng == "pe" and o.eng == "pe" and o.dma is None and o.fn is not None:
                            continue
                        d.signal = True
        for e in ENGS:
            c = 0
            for o in self.ops[e]:
                if o.dma is None and o.signal:
                    c += 1
                    o.sigval = c
        with ExitStack() as es:
            esem = {e: es.enter_context(nc.semaphore("s_" + e)) for e in ENGS}
            dsem = {}
            for e in ENGS:
                for i in range(min(DMA_SEMS, len(self.dma_hist[e]))):
                    dsem[(e, i)] = es.enter_context(nc.semaphore("d_%s%d" % (e, i)))
            block = es.enter_context(nc.Block())

            def run(e, eng):
                waited = {}
                for o in self.ops[e]:
                    need = {}
                    for d in o.deps:
                        if d.dma is not None:
                            key, sem, val = ("d",) + d.dsem, dsem[d.dsem], d.dmaval
                        else:
                            if d.eng == "pe" and e == "pe" and o.dma is None and o.fn is not None:
                                continue
                            key, sem, val = ("e", d.eng), esem[d.eng], d.sigval
                        if val > need.get(key, (None, 0))[1]:
                            need[key] = (sem, val)
                    for key, (sem, val) in need.items():
                        if waited.get(key, 0) >= val:
                            continue
                        waited[key] = val
                        eng.wait_ge(sem, val)
                    if o.fn is None:
                        continue
                    ins = o.fn(eng)
                    if o.dma is not None:
                        ins.then_inc(dsem[o.dsem], 16)
                    elif o.signal:
                        ins.then_inc(esem[e], 1)
                if e == "sp":
                    fin = {}
                    for q in ENGS:
                        for d in self.dma_hist[q]:
                            if d.dma in final_dma_keys:
                                fin[d.dsem] = max(fin.get(d.dsem, 0), d.dmaval)
                    for k, v in fin.items():
                        if waited.get(("d",) + k, 0) < v:
                            eng.wait_ge(dsem[k], v)

            block.sync(lambda eng: run("sp", eng))
            block.scalar(lambda eng: run("act", eng))
            block.vector(lambda eng: run("dve", eng))
            block.gpsimd(lambda eng: run("pool", eng))
            block.tensor(lambda eng: run("pe", eng))


class StopBuild(Exception):
    pass


class TT:
    def __init__(self, ap):
        self.ap = ap
        self.toks = {}

    def t(self, *key):
        tk = self.toks.get(key)
        if tk is None:
            tk = self.toks[key] = Tok()
        return tk

    def all(self):
        return list(self.toks.values())


def build_program(stop=None):
    nc = bass.Bass("TRN2", target_bir_lowering=False)

    def din(name, shape):
        return nc.dram_tensor(name, list(shape), F32, kind="ExternalInput").ap()

    d_x = din("xT", [128, KC, T])
    d_c = din("cT", [128, KC])
    d_vecs = din("vecs", [128, NV, KC])
    d_consts = din("consts", [128, 256])
    d_w_ada = din("w_ada", [2, D, 6 * D])
    d_w_a_in = din("w_a_in", [1, D, 3 * D])
    d_w_a_out = din("w_a_out", [1, D, D])
    d_w_mlp1 = din("w_mlp1", [2, D, 4 * D])
    d_w_mlp2 = din("w_mlp2", [2, 4 * D, D])
    d_w_ada_kv = din("w_ada_kv", [D, 2 * D])
    d_w_kv = din("w_kv", [D, 1536])
    d_w_qg = din("w_qg", [1, D, 1072])
    d_w_o = din("w_o", [1, D, D])
    d_cmp_w1 = din("cmp_w1", [2, 2048, 256])
    d_cmp_w2 = din("cmp_w2", [2, 256, 64])
    d_hvecs = din("hvecs", [128, 4])
    d_peT = din("peT", [128, 64])
    d_amask = din("amask", [128, 256 + 2048])
    d_e128 = din("e128", [128, 2048])
    d_cmap = din("cmap", [128, 40])
    d_force = din("force", [128, 256])
    d_gsel = din("gsel", [48, 3072])
    d_bd = din("bdones", [128, 128])
    d_y = nc.dram_tensor("yT", [128, KC, T], F32, kind="ExternalOutput").ap()
    d_xs = nc.dram_tensor("xs_scratch", [128, KC, T], F32).ap()

    S = Sched(nc)
    es = ExitStack()
    with es:
        def sb(name, shape, dt):
            return es.enter_context(nc.sbuf_tensor(name, list(shape), dt))

        XR = sb("XR", [128, KC * T], F32)
        HR = sb("HR", [128, KC * T], BF16)
        AR = sb("AR", [128, KC * T], BF16)
        RING = [sb("RING%d" % i, [128, 8192], BF16) for i in range(2)]
        ring_t = [Tok(), Tok()]
        rstd_s = sb("rstd", [128, T], F32)
        TMPF = [sb("tmpf%d" % i, [128, TGW], F32) for i in range(3)]
        tmpf_t = [Tok() for _ in TMPF]
        TMPB = [sb("tmpb%d" % i, [128, TGW], BF16) for i in range(3)]
        tmpb_t = [Tok() for _ in TMPB]
        SCR = sb("SCR", [128, 11048], BF16)
        ones_b = sb("ones_b", [128, 128], BF16)
        vecs = sb("vecs_s", [128, NV, KC], F32)
        c_f = sb("c_f", [128, KC], F32)
        cact = sb("cact", [128, KC], BF16)
        modv = sb("modv", [128, 3, 48], F32)
        der = sb("der", [128, 8, KC], F32)
        hvecs = sb("hvecs_s", [128, 4], F32)
        gq = sb("gq", [128, 1], F32)
        ident_b = sb("ident_b", [128, 128], BF16)
        bd_ones = sb("bd_ones", [128, 128], BF16)
        cbias = sb("cbias", [128, 4], F32)
        kcT = sb("kcT", [128, 4, 128], BF16)
        vcA = sb("vcA", [128, 4, 128], BF16)
        rd4 = sb("rd4", [128, 4], F32)
        scA = sb("scA", [128, 32], F32)
        scB = sb("scB", [128, 32], F32)
        m8a = sb("m8a", [128, 8], F32)
        m8b = sb("m8b", [128, 8], F32)
        selm = sb("selm", [128, 32], BF16)
        ACC = [sb("acc%d" % i, [128, 256], F32) for i in range(2)]
        BIAS_EXTRA = sb("m2", [128, 1024], BF16)
        PS = [es.enter_context(nc.psum_tensor("ps%d" % i, [128, TGW], F32)) for i in range(8)]
        ps_t = [Tok(excl=True) for _ in PS]

        X = TT(XR[:].rearrange("p (k t) -> p k t", k=KC))
        H = TT(HR[:].rearrange("p (k t) -> p k t", k=KC))
        A = TT(AR[:].rearrange("p (k t) -> p k t", k=KC))
        t_const = Tok()
        eps_t = sb("eps_t", [128, 2], F32)
        t_eps = Tok()
        S.op("dve", lambda e: e.memset(eps_t[:, 0:1], EPS), writes=[t_eps])
        S.op("dve", lambda e: e.memset(eps_t[:, 1:2], 1e-30), writes=[t_eps])
        t_vecs = Tok()
        t_c = Tok()
        t_cact = Tok()
        t_modv = [Tok(), Tok(), Tok()]
        t_der = Tok()
        t_rstd = [Tok() for _ in range(NTG)]

        cnt = {"ps": 0, "ring": 0, "tf": 0, "tb": 0}

        POOLS = {"ALL": list(range(8)), "P6": [0, 1, 2, 3, 4, 5], "S": [0, 1, 2, 3], "O": [4, 5, 6], "M": [7], "OM": [4, 5, 6, 7]}
        pcnt = {k: 0 for k in POOLS}

        defpool = {"v": "ALL"}

        def next_ps(pool=None):
            pool = pool or defpool["v"]
            lst = POOLS[pool]
            i = lst[pcnt[pool] % len(lst)]
            pcnt[pool] += 1
            return PS[i], ps_t[i]

        def next_tf():
            i = cnt["tf"] % len(TMPF)
            cnt["tf"] += 1
            return TMPF[i], tmpf_t[i]

        def next_tb():
            i = cnt["tb"] % len(TMPB)
            cnt["tb"] += 1
            return TMPB[i], tmpb_t[i]

        def tsl(tg):
            return slice(tg * TGW, (tg + 1) * TGW)

        S.op("pool", lambda e: e.dma_start(out=ones_b[:], in_=d_consts[:, 128:256]), writes=[t_const], dma="cb")
        S.op("sp", lambda e: e.dma_start(out=vecs[:], in_=d_vecs), writes=[t_vecs], dma="c")
        S.op("sp", lambda e: e.dma_start(out=c_f[:], in_=d_c), writes=[t_c], dma="c")
        for tg in range(NTG):
            for kh in range(2):
                ks = slice(kh * 4, kh * 4 + 4)
                S.op("sp", (lambda e, tg=tg, ks=ks: e.dma_start(out=X.ap[:, ks, tsl(tg)], in_=d_x[:, ks, tsl(tg)])),
                     writes=[X.t(k, tg) for k in range(kh * 4, kh * 4 + 4)], dma="x")

        S.op("act", lambda e: e.activation(out=cact[:], in_=c_f[:], func=AF.Silu), reads=[t_c], writes=[t_cact])

        def load_w(src_aps, dst_view_fn):
            i = cnt["ring"] % 2
            cnt["ring"] += 1
            slot = RING[i]
            for dst_fn, src in src_aps:
                S.op("pool", (lambda e, dst_fn=dst_fn, src=src, slot=slot: e.dma_start(out=dst_fn(slot), in_=src)),
                     writes=[ring_t[i]], dma="r%d" % i)
            return dst_view_fn(slot), ring_t[i]

        def load_w_std(w2d, c0, ncols, k0=0):
            src = w2d.rearrange("(k p) n -> p k n", p=128)

            def view(slot):
                return slot[:, 0:8 * ncols].rearrange("p (k c) -> p k c", k=8)
            pieces = []
            for kh in range(2):
                ks = slice(kh * 4, kh * 4 + 4)
                pieces.append(((lambda slot, ks=ks: view(slot)[:, ks, :]), src[:, k0 + kh * 4:k0 + kh * 4 + 4, c0:c0 + ncols]))
            return load_w(pieces, view)

        def ada_matvec(w2d, ncb, bias_idx, mi):
            ps, pt = next_ps()
            for cb in range(ncb):
                wv, wt = load_w_std(w2d, cb * 1024, 1024)
                for oc in range(8):
                    col = cb * 8 + oc
                    for k in range(KC):
                        S.op("pe", (lambda e, ps=ps, wv=wv, oc=oc, k=k, col=col: e.matmul(
                            ps[:, col:col + 1], wv[:, k, oc * 128:(oc + 1) * 128], cact[:, k:k + 1],
                            start=(k == 0), stop=(k == KC - 1))), reads=[wt, t_cact], writes=[pt])
            n = ncb * 8
            S.op("dve", (lambda e, ps=ps, n=n: e.tensor_tensor(
                out=modv[:, mi, 0:n], in0=ps[:, 0:n],
                in1=vecs[:, bias_idx:bias_idx + ncb, :].rearrange("p a b -> p (a b)"), op=ALU.add)),
                reads=[pt, t_vecs], writes=[t_modv[mi]])

        def mod_part(mi, part):
            return modv[:, mi, part * 8:(part + 1) * 8]

        def derive(mi, part_sc, gain_idx, dst):
            S.op("dve", lambda e: e.tensor_scalar(out=der[:, dst, :], in0=mod_part(mi, part_sc), scalar1=1.0, scalar2=1.0,
                                                  op0=ALU.add, op1=ALU.mult), reads=[t_modv[mi]], writes=[t_der])
            S.op("dve", lambda e: e.tensor_tensor(out=der[:, dst, :], in0=der[:, dst, :], in1=vecs[:, gain_idx, :], op=ALU.mult),
                 reads=[t_der, t_vecs], writes=[t_der])

        def compute_rstd():
            for tg in range(NTG):
                ps, pt = next_ps()
                for k in range(KC):
                    tb, tbt = next_tb()
                    S.op("act", (lambda e, tb=tb, k=k, tg=tg: e.activation(out=tb[:], in_=X.ap[:, k, tsl(tg)], func=AF.Square)),
                         reads=[X.t(k, tg)], writes=[tbt])
                    S.op("pe", (lambda e, ps=ps, tb=tb, k=k: e.matmul(ps[:], ones_b[:], tb[:], start=(k == 0), stop=(k == KC - 1))),
                         reads=[tbt, t_const], writes=[pt])
                tf, tft = next_tf()
                S.op("act", (lambda e, ps=ps, tf=tf: e.activation(out=tf[:], in_=ps[:], func=AF.Ln, bias=eps_t[:, 0:1], scale=1.0 / D)),
                     reads=[pt, t_eps], writes=[tft])
                S.op("act", (lambda e, tf=tf, tg=tg: e.activation(out=rstd_s[:, tsl(tg)], in_=tf[:], func=AF.Exp, scale=-0.5)), reads=[tft], writes=[t_rstd[tg]])

        def norm_mod(dst, a_ap, b_ap, extra_reads):
            for tg in range(NTG):
                for k in range(KC):
                    tf, tft = next_tf()
                    S.op("dve", (lambda e, tf=tf, k=k, tg=tg: e.tensor_tensor(out=tf[:], in0=X.ap[:, k, tsl(tg)], in1=rstd_s[:, tsl(tg)], op=ALU.mult)),
                         reads=[X.t(k, tg), t_rstd[tg]], writes=[tft])
                    S.op("act", (lambda e, tf=tf, k=k, tg=tg: e.activation(out=dst.ap[:, k, tsl(tg)], in_=tf[:], func=AF.Identity,
                                                                            bias=b_ap[:, k:k + 1], scale=a_ap[:, k:k + 1])),
                         reads=[tft] + extra_reads, writes=[dst.t(k, tg)])

        def proj(wv, wt, src, n_oc, evac, tgs=range(NTG), oc_cols=None):
            for tg in tgs:
                for oc in range(n_oc):
                    ps, pt = next_ps()
                    for k in range(KC):
                        lhs = wv[:, k, oc * 128:(oc + 1) * 128] if oc_cols is None else oc_cols(wv, k, oc)
                        S.op("pe", (lambda e, ps=ps, lhs=lhs, k=k, tg=tg: e.matmul(ps[:], lhs, src.ap[:, k, tsl(tg)],
                                                                                   start=(k == 0), stop=(k == KC - 1))),
                             reads=[wt, src.t(k, tg)], writes=[pt])
                    evac(oc, tg, ps, pt)

        def resid_evac(g_ap, extra_reads):
            def ev(oc, tg, ps, pt):
                S.op("dve", (lambda e: e.scalar_tensor_tensor(out=X.ap[:, oc, tsl(tg)], in0=ps[:], scalar=g_ap[:, oc:oc + 1],
                                                              in1=X.ap[:, oc, tsl(tg)], op0=ALU.mult, op1=ALU.add)),
                     reads=[pt, X.t(oc, tg)] + extra_reads, writes=[X.t(oc, tg)])
            return ev

        mv_tasks = []
        mv_t = [Tok(), Tok()]
        mv_state = {"n": 0}

        def make_mv_tasks(w2d, n512, bias_idx, mi):
            src = w2d.rearrange("(k p) n -> p k n", p=128)
            nparts = (n512 * 4) // 8
            bias_flat = vecs[:, bias_idx:bias_idx + nparts, :].rearrange("p a b -> p (a b)")
            for cb in range(n512):
                def task(cb=cb):
                    n = mv_state["n"]
                    mv_state["n"] += 1
                    i = n % 2
                    slot = SCR[:, i * 4096:(i + 1) * 4096].rearrange("p (k c) -> p k c", k=8)
                    extra = ([t for row in t_gb for t in row] + [t for row in t_v for t in row] + [t_vhalo]) if n < 2 else []
                    S.op("pool", (lambda e: e.dma_start(out=slot, in_=src[:, :, cb * 512:(cb + 1) * 512])), writes=[mv_t[i]] + extra, dma="mv%d" % i)
                    ps, pt = next_ps()
                    for oc in range(4):
                        for k in range(KC):
                            S.op("pe", (lambda e, oc=oc, k=k: e.matmul(ps[:, oc:oc + 1], slot[:, k, oc * 128:(oc + 1) * 128], cact[:, k:k + 1],
                                                                      start=(k == 0), stop=(k == KC - 1))), reads=[mv_t[i], t_cact], writes=[pt])
                    c0 = cb * 4
                    S.op("dve", (lambda e: e.tensor_tensor(out=modv[:, mi, c0:c0 + 4], in0=ps[:, 0:4], in1=bias_flat[:, c0:c0 + 4], op=ALU.add)),
                         reads=[pt, t_vecs], writes=[t_modv[mi]])
                mv_tasks.append(task)

        def run_mv_tasks(n):
            for _ in range(n):
                if mv_tasks:
                    mv_tasks.pop(0)()

        def mlp(layer, mi):
            compute_rstd()
            derive(mi, 4, V_NG + 2 * layer + 1, 1)
            norm_mod(H, der[:, 1, :], mod_part(mi, 3), [t_der, t_modv[mi]])
            g2 = mod_part(mi, 5)
            for hb in range(4):
                wv, wt = load_w_std(d_w_mlp1[layer], hb * 1024, 1024)

                def ev1(oc, tg, ps, pt):
                    tf, tft = next_tf()
                    S.op("act", (lambda e: e.activation(out=tf[:], in_=ps[:], func=AF.Relu)), reads=[pt], writes=[tft])
                    S.op("dve", (lambda e: e.tensor_tensor(out=A.ap[:, oc, tsl(tg)], in0=tf[:], in1=tf[:], op=ALU.mult)),
                         reads=[tft], writes=[A.t(oc, tg)])
                proj(wv, wt, H, 8, ev1)
                wv2, wt2 = load_w_std(d_w_mlp2[layer], 0, 1024, k0=hb * 8)
                run_mv_tasks(2)
                proj(wv2, wt2, A, 8, resid_evac(g2, [t_modv[mi]]))
                run_mv_tasks(2)

        compute_rstd()
        ada_matvec(d_w_ada[0], 6, V_BADA, 0)
        derive(0, 1, V_NG + 0, 0)
        norm_mod(H, der[:, 0, :], mod_part(0, 0), [t_der, t_modv[0]])

        gbv = SCR[:, 0:4096].rearrange("p (j t) -> p j t", j=2)
        vv = SCR[:, 4096:4096 + 2 * 2056].rearrange("p (j t) -> p j t", j=2)
        t_gb = [[Tok() for _ in range(NTG)] for _ in range(2)]
        t_v = [[Tok() for _ in range(NTG)] for _ in range(2)]
        t_vhalo = Tok()
        S.op("dve", lambda e: e.memset(vv[:, :, 0:2], 0.0), writes=[t_vhalo])
        w_in_v = d_w_a_in[0].rearrange("(k p) (s c) -> p k s c", p=128, s=3)
        g1 = mod_part(0, 2)
        def mixer_j(wv, wt, jj, j):
            jb = j % 2
            for tg in range(NTG):
                pss = []
                for s in range(3):
                    ps, pt = next_ps()
                    for k in range(KC):
                        S.op("pe", (lambda e, ps=ps, s=s, k=k, tg=tg: e.matmul(ps[:], wv[:, k, s, jj * 128:(jj + 1) * 128], H.ap[:, k, tsl(tg)],
                                                                               start=(k == 0), stop=(k == KC - 1))),
                             reads=[wt, H.t(k, tg)], writes=[pt])
                    pss.append((ps, pt))
                (psb, ptb), (psc, ptc), (psu, ptu) = pss
                S.op("act", (lambda e, psb=psb, tg=tg: e.activation(out=gbv[:, jb, tsl(tg)], in_=psb[:], func=AF.Copy)),
                     reads=[ptb], writes=[t_gb[jb][tg]])
                tb, tbt = next_tb()
                S.op("act", (lambda e, psc=psc, tb=tb: e.activation(out=tb[:], in_=psc[:], func=AF.Copy)), reads=[ptc], writes=[tbt])
                S.op("dve", (lambda e, psu=psu, tb=tb, tg=tg: e.tensor_tensor(out=vv[:, jb, 2 + tg * TGW:2 + (tg + 1) * TGW], in0=psu[:], in1=tb[:], op=ALU.mult)),
                     reads=[ptu, tbt], writes=[t_v[jb][tg]])
            for tg in range(NTG):
                tf, tft = next_tf()
                rd = [t_v[jb][tg], t_vhalo, t_vecs] + ([t_v[jb][tg - 1]] if tg > 0 else [])
                b0 = tg * TGW
                S.op("dve", (lambda e, tf=tf, b0=b0: e.tensor_scalar(out=tf[:], in0=vv[:, jb, b0 + 2:b0 + 2 + TGW], scalar1=vecs[:, V_CONV + 2, j:j + 1], scalar2=None, op0=ALU.mult)),
                     reads=rd, writes=[tft])
                S.op("dve", (lambda e, tf=tf, b0=b0: e.scalar_tensor_tensor(out=tf[:], in0=vv[:, jb, b0 + 1:b0 + 1 + TGW], scalar=vecs[:, V_CONV + 1, j:j + 1], in1=tf[:], op0=ALU.mult, op1=ALU.add)),
                     reads=rd + [tft], writes=[tft])
                S.op("dve", (lambda e, tf=tf, b0=b0: e.scalar_tensor_tensor(out=tf[:], in0=vv[:, jb, b0:b0 + TGW], scalar=vecs[:, V_CONV + 0, j:j + 1], in1=tf[:], op0=ALU.mult, op1=ALU.add)),
                     reads=rd + [tft], writes=[tft])
                S.op("dve", (lambda e, tf=tf, tg=tg: e.tensor_tensor(out=A.ap[:, j, tsl(tg)], in0=tf[:], in1=gbv[:, jb, tsl(tg)], op=ALU.mult)),
                     reads=[tft, t_gb[jb][tg]], writes=[A.t(j, tg)])

        for jp in range(4):
            def view(slot):
                return slot[:, 0:6144].rearrange("p (k s c) -> p k s c", k=8, s=3)
            pieces = []
            for s3 in range(3):
                pieces.append(((lambda slot, s3=s3: view(slot)[:, :, s3, :]), w_in_v[:, :, s3, jp * 256:(jp + 1) * 256]))
            wv, wt = load_w(pieces, view)
            for jj in range(2):
                mixer_j(wv, wt, jj, 2 * jp + jj)
        wv, wt = load_w_std(d_w_a_out[0], 0, 1024)
        proj(wv, wt, A, 8, resid_evac(g1, [t_modv[0]]))
        if stop != "mix0":
            if stop not in ("mix0", "l0"):
                make_mv_tasks(d_w_ada[1], 12, V_BADA + 6, 1)
                make_mv_tasks(d_w_ada_kv, 4, V_BKV, 2)
            mlp(0, 0)
        def chk(name, dumps):
            if stop != name:
                return
            S.barrier()
            for ap, dst in dumps:
                S.op("pool", (lambda e, ap=ap, dst=dst: e.dma_start(out=dst, in_=ap)), dma="out")
            raise StopBuild()

        if stop not in ("mix0", "l0"):
          try:
            G4 = 4
            defpool["v"] = "P6"
            POOLS["P6"] = [0, 1, 2, 3, 4]
            POOLS["M"] = [5, 6, 7]
            t_l1c = Tok()
            wqg_g = SCR[:, 8192:8576].rearrange("p (k c) -> p k c", k=8)
            cw2k = SCR[:, 8576:8832].rearrange("p (c d) -> p c d", c=2)
            cw2v = SCR[:, 8832:8960].rearrange("p (c d) -> p c d", c=2)
            peT_b = SCR[:, 8960:9024]
            S.op("sp", lambda e: e.dma_start(out=hvecs[:], in_=d_hvecs), writes=[t_l1c], dma="c")
            S.op("pool", lambda e: e.dma_start(out=ident_b[:], in_=d_consts[:, 0:128]), writes=[t_l1c], dma="cb")
            S.op("pool", lambda e: e.dma_start(out=bd_ones[:], in_=d_bd), writes=[t_l1c], dma="cb")
            S.op("pool", lambda e: e.dma_start(out=peT_b, in_=d_peT), writes=[t_l1c], dma="cb")
            S.op("pool", lambda e: e.dma_start(out=cw2k[:, :, 0:64], in_=d_cmp_w2[0].rearrange("(c p) d -> p c d", p=128)), writes=[t_l1c], dma="cb")
            S.op("pool", lambda e: e.dma_start(out=cw2k[:, :, 64:128], in_=d_cmp_w2[0].rearrange("(c p) d -> p c d", p=128)), writes=[t_l1c], dma="cb")
            S.op("pool", lambda e: e.dma_start(out=cw2v, in_=d_cmp_w2[1].rearrange("(c p) d -> p c d", p=128)), writes=[t_l1c], dma="cb")
            S.op("pool", lambda e: e.dma_start(out=wqg_g, in_=d_w_qg[0].rearrange("(k p) n -> p k n", p=128)[:, :, 1024:1072]), writes=[t_l1c], dma="cb")
            t_gq = Tok()
            S.op("dve", lambda e: e.tensor_scalar(out=gq[:], in0=hvecs[:, 0:1], scalar1=0.125, scalar2=None, op0=ALU.mult), reads=[t_l1c], writes=[t_gq])

            run_mv_tasks(99)
            compute_rstd()
            derive(2, 1, V_KVG, 2)
            derive(1, 1, V_NG + 2, 3)
            norm_mod(H, der[:, 2, :], mod_part(2, 0), [t_der, t_modv[2]])
            norm_mod(A, der[:, 3, :], mod_part(1, 0), [t_der, t_modv[1]])
            xs_t = [Tok() for _ in range(NTG)]
            for tg in range(NTG):
                for kh in range(2):
                    ks = slice(kh * 4, kh * 4 + 4)
                    S.op("sp", (lambda e, tg=tg, ks=ks: e.dma_start(out=d_xs[:, ks, tsl(tg)], in_=X.ap[:, ks, tsl(tg)])),
                         reads=[X.t(k, tg) for k in range(kh * 4, kh * 4 + 4)], writes=[xs_t[tg]], dma="xs")

            def head_norm(ps, pt, ncol, gain_ap, gain_reads, dst_ap, dst_toks):
                tb, tbt = next_tb()
                S.op("act", (lambda e: e.activation(out=tb[:, 0:ncol], in_=ps[:, 0:ncol], func=AF.Square)), reads=[pt], writes=[tbt])
                ps2, pt2 = next_ps("M")
                S.op("pe", (lambda e: e.matmul(ps2[:, 0:ncol], bd_ones[:], tb[:, 0:ncol], start=True, stop=True)), reads=[tbt, t_l1c], writes=[pt2])
                tf, tft = next_tf()
                S.op("act", (lambda e: e.activation(out=tf[:, 0:ncol], in_=ps2[:, 0:ncol], func=AF.Ln, bias=eps_t[:, 0:1], scale=1.0 / 64)),
                     reads=[pt2, t_eps], writes=[tft])
                tf2, tft2 = next_tf()
                S.op("act", (lambda e: e.activation(out=tf2[:, 0:ncol], in_=tf[:, 0:ncol], func=AF.Exp, scale=-0.5)), reads=[tft], writes=[tft2])
                S.op("dve", (lambda e: e.scalar_tensor_tensor(out=dst_ap, in0=ps[:, 0:ncol], scalar=gain_ap, in1=tf2[:, 0:ncol], op0=ALU.mult, op1=ALU.mult)),
                     reads=[pt, tft2] + gain_reads, writes=dst_toks)

            RAW = TT(SCR[:, 0:8192].rearrange("p (c t) -> p c t", c=4))
            wv, wt = load_w_std(d_w_kv, 0, 512)

            def ev_raw(oc, tg, ps, pt):
                S.op("act", (lambda e: e.activation(out=RAW.ap[:, oc, tsl(tg)], in_=ps[:], func=AF.Copy)), reads=[pt], writes=[RAW.t(oc, tg)])
            proj(wv, wt, H, 4, ev_raw)

            chk("raw", [(RAW.ap, d_y[:, 0:4, :])])
            t_kc = [Tok() for _ in range(G4)]
            t_vc = [Tok() for _ in range(G4)]
            t_vc_ones = Tok()
            S.op("dve", lambda e: e.memset(vcA[:, :, 64:128], 1.0), writes=[t_vc_ones])
            t_cb = Tok()
            for kv in range(2):
                def view1(slot):
                    return slot[:, 0:8192].rearrange("p (l h) -> p l h", l=32)
                src1 = d_cmp_w1[kv].rearrange("(l d) h -> d l h", d=64)
                pieces = [((lambda slot: view1(slot)[0:64, :, :]), src1), ((lambda slot: view1(slot)[64:128, :, :]), src1)]
                cwv, cwt = load_w(pieces, view1)
                psb, ptb = next_ps()
                for hc in range(2):
                    for l in range(32):
                        S.op("pe", (lambda e, hc=hc, l=l, cwv=cwv, psb=psb, kv=kv: e.matmul(
                            psb[:, hc:hc + 1], cwv[0:64, l, hc * 128:(hc + 1) * 128], peT_b[0:64, kv * 32 + l:kv * 32 + l + 1],
                            start=(l == 0), stop=(l == 31))), reads=[cwt, t_l1c], writes=[ptb])
                S.op("dve", (lambda e, psb=psb, kv=kv: e.tensor_copy(out=cbias[:, 2 * kv:2 * kv + 2], in_=psb[:, 0:2])), reads=[ptb], writes=[t_cb])
                for g in range(G4):
                    base = (g % 2) * 64
                    c = kv * 2 + g // 2
                    hids = []
                    for hc in range(2):
                        ps, pt = next_ps()
                        for l in range(32):
                            S.op("pe", (lambda e, ps=ps, l=l, hc=hc, cwv=cwv, base=base, c=c: e.matmul(
                                ps[:, 0:127], cwv[base:base + 64, l, hc * 128:(hc + 1) * 128],
                                RAW.ap[base:base + 64, c, l:l + 16 * 126 + 1:16], start=(l == 0), stop=(l == 31))),
                                reads=[cwt] + [RAW.t(c, tg) for tg in range(NTG)], writes=[pt])
                        z, zt = next_tf()
                        S.op("act", (lambda e, ps=ps, z=z, hc=hc, kv=kv: e.activation(out=z[:, 0:127], in_=ps[:, 0:127], func=AF.Identity,
                                                                                      bias=cbias[:, 2 * kv + hc:2 * kv + hc + 1], scale=1.0)),
                             reads=[pt, t_cb], writes=[zt])
                        u, ut = next_tf()
                        S.op("dve", (lambda e, z=z, u=u: e.tensor_tensor(out=u[:, 0:127], in0=z[:, 0:127], in1=z[:, 0:127], op=ALU.mult)), reads=[zt], writes=[ut])
                        S.op("dve", (lambda e, u=u: e.tensor_scalar(out=u[:, 0:127], in0=u[:, 0:127], scalar1=0.044715, scalar2=1.0, op0=ALU.mult, op1=ALU.add)),
                             reads=[ut], writes=[ut])
                        S.op("dve", (lambda e, z=z, u=u: e.tensor_tensor(out=u[:, 0:127], in0=u[:, 0:127], in1=z[:, 0:127], op=ALU.mult)), reads=[ut, zt], writes=[ut])
                        S.op("act", (lambda e, u=u: e.activation(out=u[:, 0:127], in_=u[:, 0:127], func=AF.Sigmoid, scale=1.5957691216057308)), reads=[ut], writes=[ut])
                        hb_, hbt = next_tb()
                        S.op("dve", (lambda e, z=z, u=u, hb_=hb_: e.tensor_tensor(out=hb_[:, 0:127], in0=u[:, 0:127], in1=z[:, 0:127], op=ALU.mult)), reads=[ut, zt], writes=[hbt])
                        hids.append((hb_, hbt))
                    if kv == 0:
                        ps, pt = next_ps()
                        for hc in range(2):
                            S.op("pe", (lambda e, ps=ps, hc=hc, hb_=hids[hc][0]: e.matmul(ps[:, 0:127], cw2k[:, hc, :], hb_[:, 0:127], start=(hc == 0), stop=(hc == 1))),
                                 reads=[hids[hc][1], t_l1c], writes=[pt])
                        head_norm(ps, pt, 127, hvecs[:, 1:2], [t_l1c], kcT[:, g, 0:127], [t_kc[g]])
                    else:
                        ps, pt = next_ps()
                        for hc in range(2):
                            S.op("pe", (lambda e, ps=ps, hc=hc, hb_=hids[hc][0]: e.matmul(ps[0:127, 0:64], hb_[:, 0:127], cw2v[:, hc, :], start=(hc == 0), stop=(hc == 1))),
                                 reads=[hids[hc][1], t_l1c], writes=[pt])
                        S.op("act", (lambda e, ps=ps, g=g: e.activation(out=vcA[0:127, g, 0:64], in_=ps[0:127, 0:64], func=AF.Copy)), reads=[pt], writes=[t_vc[g]])

            chk("cmp", [(kcT[:].rearrange("p g n -> p (g n)"), d_y[:, 0, 0:512]), (vcA[:].rearrange("p g n -> p (g n)"), d_y[:, 1, 0:512])])
            wv_q, wt_q = load_w_std(d_w_qg[0], 0, 1024)
            S.barrier()
            XB = XR[:].bitcast(BF16)
            QT = TT(XB[:, 0:16384].rearrange("p (h t) -> p h t", h=8))
            KST = TT(XB[:, 16384:24576].rearrange("p (g t) -> p g t", g=4))
            KWT = TT(XB[:, 24576:32768].rearrange("p (g t) -> p g t", g=4))
            RB = rstd_s[:].bitcast(BF16)
            SIGG = TT(RB[0:48, 0:T])

            def ev_q(oc, tg, ps, pt):
                head_norm(ps, pt, TGW, gq[:, 0:1], [t_gq], QT.ap[:, oc, tsl(tg)], [QT.t(oc, tg)])
            proj(wv_q, wt_q, A, 8, ev_q)
            for tg in range(NTG):
                ps, pt = next_ps()
                for k in range(KC):
                    S.op("pe", (lambda e, ps=ps, k=k, tg=tg: e.matmul(ps[0:48, :], wqg_g[:, k, :], A.ap[:, k, tsl(tg)], start=(k == 0), stop=(k == KC - 1))),
                         reads=[t_l1c, A.t(k, tg)], writes=[pt])
                S.op("act", (lambda e, ps=ps, tg=tg: e.activation(out=SIGG.ap[:, tsl(tg)], in_=ps[0:48, :], func=AF.Sigmoid)), reads=[pt], writes=[SIGG.t(tg)])

            for typ, dst, gi in ((2, KST, 2), (4, KWT, 3)):
                def viewk(slot):
                    return slot[:, 0:4096].rearrange("p (k g r d) -> p k g r d", k=8, g=4, r=2)
                srck = d_w_kv.rearrange("(k p) n -> p k n", p=128)[:, :, typ * 256:(typ + 1) * 256].rearrange("p k (g d) -> p k g d", g=4)
                pieces = [((lambda slot, r=r, gg=gg: viewk(slot)[:, :, gg, r, :]), srck[:, :, gg, :]) for r in range(2) for gg in range(4)]
                kv_, kt_ = load_w(pieces, viewk)

                def ev_k(oc, tg, ps, pt, dst=dst, gi=gi):
                    head_norm(ps, pt, TGW, hvecs[:, gi:gi + 1], [t_l1c], dst.ap[:, oc, tsl(tg)], [dst.t(oc, tg)])
                proj(kv_, kt_, H, 4, ev_k, oc_cols=(lambda wv_, k, oc: wv_[:, k, oc, :, :].rearrange("p r d -> p (r d)")))
            chk("q", [(QT.ap, d_y)])
            chk("k", [(KST.ap, d_y[:, 0:4, :]), (KWT.ap, d_y[:, 4:8, :]), ])

            def viewv(slot):
                return slot[:, 0:4096].rearrange("p (k s c) -> p k s c", k=8, s=2)
            srcv = d_w_kv.rearrange("(k p) n -> p k n", p=128)
            pieces = [((lambda slot: viewv(slot)[:, :, 0, :]), srcv[:, :, 768:1024]), ((lambda slot: viewv(slot)[:, :, 1, :]), srcv[:, :, 1280:1536])]
            vv_, vt_ = load_w(pieces, viewv)
            S.barrier()
            AB = AR[:]
            VS = TT(AB[:, 0:8192].rearrange("p (t g c) -> p t g c", t=16, g=4))
            VW = TT(AB[:, 8192:16384].rearrange("p (t g c) -> p t g c", t=16, g=4))
            t_vones = Tok()
            S.op("dve", lambda e: e.memset(VS.ap[:, :, :, 64:128], 1.0), writes=[t_vones])
            S.op("dve", lambda e: e.memset(VW.ap[:, :, :, 64:128], 1.0), writes=[t_vones])

            for tt in range(16):
                ps, pt = next_ps()
                tg = tt // 4
                for k in range(KC):
                    S.op("pe", (lambda e, ps=ps, k=k, tt=tt: e.matmul(ps[:], H.ap[:, k, tt * 128:(tt + 1) * 128], vv_[:, k, :, :].rearrange("p s c -> p (s c)"),
                                                                    start=(k == 0), stop=(k == KC - 1))), reads=[vt_, H.t(k, tg)], writes=[pt])
                S.op("act", (lambda e, ps=ps, tt=tt: e.activation(out=VS.ap[:, tt, :, 0:64], in_=ps[:, 0:256].rearrange("p (g d) -> p g d", g=4), func=AF.Copy)),
                     reads=[pt, t_vones], writes=[VS.t(tt)])
                S.op("dve", (lambda e, ps=ps, tt=tt: e.tensor_copy(out=VW.ap[:, tt, :, 0:64], in_=ps[:, 256:512].rearrange("p (g d) -> p g d", g=4))),
                     reads=[pt, t_vones], writes=[VW.t(tt)])
            chk("v", [(AB[:, 0:8192], d_y[:, 0:4, :].rearrange("p a b -> p (a b)")), (AB[:, 8192:16384], d_y[:, 4:8, :].rearrange("p a b -> p (a b)"))])
            wo_v, wo_t = load_w_std(d_w_o[0], 0, 1024)
            S.barrier()

            POOLS["M"] = [7]
            t_ac = Tok()
            tri_b = SCR[:, 0:128]
            anti_b = SCR[:, 128:256]
            cmpm = SCR[:, 256:2304]
            e128 = SCR[:, 2304:4352].rearrange("p (k j) -> p k j", k=16)
            cmap = SCR[:, 4352:4392]
            force = SCR[:, 4392:4904].bitcast(F32).rearrange("p (q j) -> p q j", q=8)
            gsel = SCR[0:48, 4904:7976].rearrange("p (h b m) -> p h b m", h=8, b=3)
            PTS = [SCR[:, 7976 + i * 1024:7976 + (i + 1) * 1024].rearrange("p (k r c) -> p k r c", k=2, r=2) for i in range(3)]
            pts_t = [Tok() for _ in PTS]
            S.op("pool", lambda e: e.dma_start(out=SCR[:, 0:2304], in_=d_amask), writes=[t_ac], dma="cb")
            S.op("pool", lambda e: e.dma_start(out=SCR[:, 2304:4352], in_=d_e128), writes=[t_ac], dma="cb")
            S.op("pool", lambda e: e.dma_start(out=cmap, in_=d_cmap), writes=[t_ac], dma="cb")
            S.op("sp", lambda e: e.dma_start(out=SCR[:, 4392:4904].bitcast(F32), in_=d_force), writes=[t_ac], dma="c")
            S.op("pool", lambda e: e.dma_start(out=SCR[0:48, 4904:7976], in_=d_gsel), writes=[t_ac], dma="cb")
            HB = HR[:]
            OTG = TT(HB[:, 0:4096].rearrange("p (h t) -> p h t", h=8))
            GBC = HB[:, 4096:7168].rearrange("p (h b t) -> p h b t", h=2, b=3)
            t_gbc = Tok()
            XST = TT(HB[:, 7168:15360].bitcast(F32).rearrange("p (k t) -> p k t", k=8))
            BIAS = [HB[:, 15360 + i * 256:15360 + (i + 1) * 256].rearrange("p (r q) -> p r q", r=2) for i in range(2)]
            bias_t = [Tok(), Tok()]
            for i in range(2):
                S.op("dve", (lambda e, i=i: e.memset(BIAS[i], 0.0)), writes=[bias_t[i]])
            pend = {"tk": None}
            ptc = {"n": 0, "b": 0, "a": 0, "c": 0}
            cmb_lo = [Tok(), Tok()]
            cmb_hi = [Tok(), Tok()]
            cmb_on = [Tok(), Tok()]

            M2 = BIAS_EXTRA
            tri2 = M2[:, 0:256]
            anti2 = M2[:, 256:512]
            t_m2 = Tok()
            S.op("dve", lambda e: e.tensor_copy(out=tri2.rearrange("p (r q) -> p r q", r=2), in_=tri_b.unsqueeze(1).to_broadcast([128, 2, 128])), reads=[t_ac], writes=[t_m2])
            S.op("dve", lambda e: e.tensor_copy(out=anti2.rearrange("p (r q) -> p r q", r=2), in_=anti_b.unsqueeze(1).to_broadcast([128, 2, 128])), reads=[t_ac], writes=[t_m2])
            CM2 = [M2[:, 512 + i * 256:512 + (i + 1) * 256] for i in range(2)]
            cm2_t = [Tok(), Tok()]

            def emit_cm2(ci, qt):
                S.op("dve", (lambda e: e.tensor_copy(out=CM2[ci].rearrange("p (r q) -> p r q", r=2),
                                                     in_=cmpm[:, qt * 128:(qt + 1) * 128].unsqueeze(1).to_broadcast([128, 2, 128]))),
                     reads=[t_ac], writes=[cm2_t[ci]])

            def next_pt():
                i = ptc["n"] % 3
                ptc["n"] += 1
                return PTS[i], pts_t[i]

            def qk_tile(bank, bt, colbase, KT, g, kt0, nk, par, qt, mask, bias_i, phase):
                lhs_k = KT.ap[par * 64:(par + 1) * 64, g, kt0:kt0 + nk] if KT is not None else kcT[par * 64:(par + 1) * 64, g, 0:127]
                k_reads = [KT.t(g, kt0 // TGW)] if KT is not None else [t_kc[g]]
                qsl = slice(qt * 128, (qt + 1) * 128)
                q_reads = [QT.t(2 * g, qt // 4), QT.t(2 * g + 1, qt // 4)]
                if phase == 1:
                    S.op("pe", (lambda e: e.matmul(bank[0:nk, colbase:colbase + 256], lhs_k, QT.ap[par * 64:(par + 1) * 64, 2 * g:2 * g + 2, qsl],
                                                   start=(mask is None), stop=True)), reads=k_reads + q_reads, writes=[bt])
                elif mask is None:
                    pass
                elif mask == "bias":
                    kt = kt0 // 128
                    S.op("pe", (lambda e: e.matmul(bank[:, colbase:colbase + 256], e128[:, kt, :], BIAS[bias_i].rearrange("p r q -> p (r q)"), start=True, stop=False)),
                         reads=[t_ac, bias_t[bias_i]], writes=[bt])
                else:
                    mask_ap, mask_reads = mask
                    S.op("pe", (lambda e: e.matmul(bank[0:nk, colbase:colbase + 256], ident_b[:, 0:nk], mask_ap, start=True, stop=False)),
                         reads=[t_l1c] + mask_reads, writes=[bt])

            def branch_steps(g, qt, kind, bias_i, pos, br, acc, acct):
                ps_o, pt_o = next_ps("O")
                out = []
                if kind == "cmp":
                    bA, tA = next_ps("S")
                    bB, tB = next_ps("S")
                    PT, ptt = next_pt()

                    ci = ptc["a"] % 2
                    ptc["a"] += 1

                    def qk():
                        if g == 0 and qt == 0:
                            emit_cm2(ci, qt)
                        for phase in range(2):
                            for par, bank, bt in ((0, bA, tA), (1, bB, tB)):
                                qk_tile(bank, bt, 0, None, g, 0, 127, par, qt, (CM2[ci], [cm2_t[ci]]), None, phase)
                        for par, bank, bt in ((0, bA, tA), (1, bB, tB)):
                            S.op("act", (lambda e, par=par, bank=bank: e.activation(out=PT[0:127, 0, par, :], in_=bank[0:127, 0:256], func=AF.Exp)),
                                 reads=[bt], writes=[ptt])
                        nqt = qt + 1 if qt % 4 != 3 else (qt - 3 if g < 3 else qt + 1)
                        if nqt < 16:
                            emit_cm2(1 - ci, nqt)
                        if qt >= 8:
                            pend["tk"] = (lambda: topk_a(g, qt, PT, ptt))

                    def pv():
                        S.op("pe", (lambda e: e.matmul(ps_o[:], vcA[0:127, g, :], PT[0:127, 0, :, :].rearrange("p r c -> p (r c)"), start=True, stop=True)),
                             reads=[ptt, t_vc[g], t_vc_ones], writes=[pt_o])
                        combine(g, qt, pos, br, ps_o, pt_o, acc, acct)
                    return [(qk, pv)]
                KT, V = (KST, VS) if kind == "sel" else (KWT, VW)
                kts = list(range(0, qt + 1)) if kind == "sel" else list(range(max(0, qt - 4), qt + 1))
                for pi in range(0, len(kts), 2):
                    pair = kts[pi:pi + 2]
                    banks = (next_ps("S"), next_ps("S"))
                    PTp = next_pt()

                    def qk(pair=pair, banks=banks, PTp=PTp):
                        PT, ptt = PTp
                        npair = len(pair)
                        if kind == "sel" and qt >= 8 and pair[0] == 0:
                            topk_b(bias_i)
                        for ktp, kt in enumerate(pair):
                            if kt == qt:
                                mask = (tri2, [t_m2])
                            elif kind == "win" and kt == qt - 4:
                                mask = (anti2, [t_m2])
                            elif kind == "sel" and qt >= 8:
                                mask = "bias"
                            else:
                                mask = None
                            for phase in range(2):
                                for par, (bank, bt) in enumerate(banks):
                                    qk_tile(bank, bt, ktp * 256, KT, g, kt * 128, 128, par, qt, mask, bias_i, phase)
                        for par, (bank, bt) in enumerate(banks):
                            S.op("act", (lambda e, par=par, bank=bank: e.activation(
                                out=PT[:, 0:npair, par, :], in_=bank[:, 0:npair * 256].rearrange("p (k c) -> p k c", k=npair), func=AF.Exp)),
                                reads=[bt, banks[1][1]], writes=[ptt])
                        if pend["tk"] is not None:
                            pend["tk"]()
                            pend["tk"] = None

                    def pv(pair=pair, PTp=PTp, last=(pi + 2 >= len(kts))):
                        PT, ptt = PTp
                        for ktp, kt in enumerate(pair):
                            S.op("pe", (lambda e, ktp=ktp, kt=kt: e.matmul(ps_o[:], V.ap[:, kt, g, :], PT[:, ktp, :, :].rearrange("p r c -> p (r c)"),
                                                                         start=(kt == kts[0]), stop=(kt == kts[-1]))),
                                 reads=[ptt, V.t(kt), t_vones], writes=[pt_o])
                        if last:
                            combine(g, qt, pos, br, ps_o, pt_o, acc, acct)
                    out.append((qk, pv))
                return out

            def combine(g, qt, pos, br, ps_o, pt_o, acc, acct):
                qi = qt % 4
                ci = ptc["c"] % 2
                ptc["c"] += 1
                T, tlo, thi = TMPF[ci], cmb_lo[ci], cmb_hi[ci]
                S.op("act", (lambda e: e.activation(out=T[0:64, :], in_=ps_o[64:128, :], func=AF.Ln, bias=eps_t[64:128, 1:2], scale=1.0)), reads=[pt_o, t_eps], writes=[tlo])
                S.op("act", (lambda e: e.activation(out=T[64:128, :], in_=T[0:64, :], func=AF.Exp, scale=-1.0)), reads=[tlo], writes=[thi])
                on, ont = TMPF[2][:, ci * 256:(ci + 1) * 256], cmb_on[ci]
                for par in range(2):
                    S.op("dve", (lambda e, par=par: e.tensor_tensor(out=on[par * 64:(par + 1) * 64, :], in0=ps_o[0:64, par * 256:(par + 1) * 256],
                                                                     in1=T[64:128, par * 256:(par + 1) * 256], op=ALU.mult)), reads=[pt_o, thi], writes=[ont])
                onv = on.rearrange("p (h q) -> p h q", h=2)
                accv = acc[:].rearrange("p (h q) -> p h q", h=2)
                Gv = GBC[:, :, br, qi * 128:(qi + 1) * 128]
                ce = "pool"
                if pos == 0:
                    S.op(ce, (lambda e: e.tensor_tensor(out=accv, in0=onv, in1=Gv, op=ALU.mult)), reads=[ont, t_gbc], writes=[acct])
                else:
                    S.op(ce, (lambda e: e.tensor_tensor(out=onv, in0=onv, in1=Gv, op=ALU.mult)), reads=[ont, t_gbc], writes=[ont])
                    if pos == 1:
                        S.op(ce, (lambda e: e.tensor_tensor(out=accv, in0=accv, in1=onv, op=ALU.add)), reads=[ont, acct], writes=[acct])
                    else:
                        S.op(ce, (lambda e: e.tensor_tensor(out=OTG.ap[:, 2 * g:2 * g + 2, qi * 128:(qi + 1) * 128], in0=accv, in1=onv, op=ALU.add)),
                             reads=[ont, acct], writes=[OTG.t(2 * g, 0), OTG.t(2 * g + 1, 0)])

            def topk_a(g, qt, PT, ptt):
                ps_i, pt_i = next_ps("M")
                for c in range(4):
                    S.op("pe", (lambda e, c=c: e.matmul(ps_i[:, c * 33:(c + 1) * 33], PT[0:127, 0, c // 2, (c % 2) * 128:(c % 2 + 1) * 128], cmap[0:127, 0:33],
                                                          start=True, stop=True)), reads=[ptt, t_ac], writes=[pt_i])
                psv = ps_i[:, 0:132].rearrange("p (c j) -> p c j", c=4)
                S.op("dve", (lambda e: e.reciprocal(out=rd4[:], in_=psv[:, :, 32])), reads=[pt_i], writes=[t_tk])
                S.op("dve", (lambda e: e.tensor_scalar(out=scA[:], in0=psv[:, 0, 0:32], scalar1=rd4[:, 0:1], scalar2=None, op0=ALU.mult)), reads=[pt_i, t_tk], writes=[t_tk])
                for c in range(1, 4):
                    S.op("dve", (lambda e, c=c: e.scalar_tensor_tensor(out=scA[:], in0=psv[:, c, 0:32], scalar=rd4[:, c:c + 1], in1=scA[:], op0=ALU.mult, op1=ALU.add)),
                         reads=[pt_i, t_tk], writes=[t_tk])
                S.op("dve", (lambda e: e.tensor_tensor(out=scA[:], in0=scA[:], in1=force[:, qt - 8, :], op=ALU.add)), reads=[t_tk, t_ac], writes=[t_tk])
                S.op("dve", (lambda e: e.max(out=m8a[:], in_=scA[:])), reads=[t_tk], writes=[t_tk])
                S.op("dve", (lambda e: e.match_replace(out=scB[:], in_to_replace=m8a[:], in_values=scA[:], imm_value=-1e30)), reads=[t_tk], writes=[t_tk])
                S.op("dve", (lambda e: e.max(out=m8b[:], in_=scB[:])), reads=[t_tk], writes=[t_tk])
                S.op("dve", (lambda e: e.tensor_scalar(out=selm[:], in0=scA[:], scalar1=m8b[:, 7:8], scalar2=1.0, op0=ALU.is_ge, op1=ALU.subtract)), reads=[t_tk], writes=[t_tk])

            def topk_b(bias_i):
                ps_m, pt_m = next_ps("M")
                S.op("pe", (lambda e: e.matmul(ps_m[0:32, 0:128], selm[:], ident_b[:], start=True, stop=True)), reads=[t_tk, t_l1c], writes=[pt_m])
                S.op("act", (lambda e: e.activation(out=BIAS[bias_i][0:32, :, :], in_=ps_m[0:32, 0:128].unsqueeze(1).to_broadcast([32, 2, 128]), func=AF.Copy, scale=30000.0)),
                     reads=[pt_m], writes=[bias_t[bias_i]])

            on_t = [Tok(), Tok()]
            acc_t = [Tok(), Tok()]
            t_tk = Tok()
            g1 = mod_part(1, 2)
            steps = []

            def emit_xreload(tg):
                for kh in range(2):
                    ks = slice(kh * 4, kh * 4 + 4)
                    S.op("sp", (lambda e, ks=ks: e.dma_start(out=XST.ap[:, ks, :], in_=d_xs[:, ks, tsl(tg)])),
                         reads=[xs_t[tg]], writes=[XST.t(k) for k in range(kh * 4, kh * 4 + 4)], dma="xr")

            def emit_gbc(tg, g):
                for hpl in range(2):
                    for br in range(3):
                        ps, pt = next_ps("OM")
                        S.op("pe", (lambda e, ps=ps, hpl=hpl, br=br: e.matmul(ps[:], gsel[:, 2 * g + hpl, br, :], SIGG.ap[:, tsl(tg)], start=True, stop=True)),
                             reads=[t_ac, SIGG.t(tg)], writes=[pt])
                        S.op("dve", (lambda e, ps=ps, hpl=hpl, br=br: e.tensor_copy(out=GBC[:, hpl, br, :], in_=ps[:])), reads=[pt], writes=[t_gbc])

            def emit_wo(tg):
                def ev_o(oc, tg_, ps, pt):
                    S.op("dve", (lambda e: e.scalar_tensor_tensor(out=XST.ap[:, oc, :], in0=ps[:], scalar=g1[:, oc:oc + 1], in1=XST.ap[:, oc, :], op0=ALU.mult, op1=ALU.add)),
                         reads=[pt, XST.t(oc), t_modv[1]], writes=[XST.t(oc)])
                defpool["v"] = "OM"
                proj(wo_v, wo_t, OTG, 8, ev_o, tgs=[0])
                defpool["v"] = "P6"
                for kh in range(2):
                    ks = slice(kh * 4, kh * 4 + 4)
                    S.op("sp", (lambda e, ks=ks: e.dma_start(out=d_xs[:, ks, tsl(tg)], in_=XST.ap[:, ks, :])),
                         reads=[XST.t(k) for k in range(kh * 4, kh * 4 + 4)], writes=[xs_t[tg]], dma="xw")

            for tg in range(NTG):
                steps.append((None, (lambda tg=tg: emit_xreload(tg))))
                for g in range(G4):
                    steps.append((None, (lambda tg=tg, g=g: emit_gbc(tg, g))))
                    for qi in range(4):
                        qt = tg * 4 + qi
                        ai = ptc["b"] % 2
                        ptc["b"] += 1
                        acc, acct = ACC[ai], acc_t[ai]
                        steps += branch_steps(g, qt, "cmp", ai, 0, 0, acc, acct)
                        steps += branch_steps(g, qt, "win", ai, 1, 2, acc, acct)
                        steps += branch_steps(g, qt, "sel", ai, 2, 1, acc, acct)
                steps.append((None, (lambda tg=tg: emit_wo(tg))))
            prev_pv = None
            for qk_fn, pv_fn in steps:
                if qk_fn is not None:
                    qk_fn()
                if prev_pv is not None:
                    prev_pv()
                prev_pv = pv_fn
            if prev_pv is not None:
                prev_pv()
            S.barrier()
            for tg in range(NTG):
                for kh in range(2):
                    ks = slice(kh * 4, kh * 4 + 4)
                    S.op("sp", (lambda e, tg=tg, ks=ks: e.dma_start(out=X.ap[:, ks, tsl(tg)], in_=d_xs[:, ks, tsl(tg)])),
                         reads=[xs_t[tg]], writes=[X.t(k, tg) for k in range(kh * 4, kh * 4 + 4)], dma="x2")
            defpool["v"] = "ALL"
            if stop != "mix1":
                mlp(1, 1)
          except StopBuild:
            S.emit(final_dma_keys=["out"])
            return nc

        for tg in range(NTG):
            for kh in range(2):
                ks = slice(kh * 4, kh * 4 + 4)
                S.op("sp", (lambda e, tg=tg, ks=ks: e.dma_start(out=d_y[:, ks, tsl(tg)], in_=X.ap[:, ks, tsl(tg)])),
                     reads=[X.t(k, tg) for k in range(kh * 4, kh * 4 + 4)], dma="out")
        S.emit(final_dma_keys=["out"])
    return nc


def _fm(v):
    return np.ascontiguousarray(v.reshape(KC, 128).T)


def _const_tables():
    f32 = np.float32
    NEGM = -30000.0
    j = np.arange(128)[:, None]
    t = np.arange(128)[None, :]
    tri = np.where(j <= t, 0.0, NEGM)
    anti = np.where(j > t, 0.0, NEGM)
    n = np.arange(128)[:, None]
    tt = np.arange(T)[None, :]
    cmpm = np.where((16 * n + 31 <= tt) & (n < 127), 0.0, NEGM)
    amask = np.concatenate([tri, anti, cmpm], axis=1).astype(f32)
    e128 = np.zeros((128, 16, 128), f32)
    for kt in range(16):
        for jj in range(128):
            e128[2 * kt + jj // 64, kt, jj] = 1.0
    cmap = np.zeros((128, 40), f32)
    c0 = np.arange(127)[:, None] * 16
    s0 = np.arange(32)[None, :] * 64
    ov = np.minimum(c0 + 32, s0 + 64) - np.maximum(c0, s0)
    cmap[:127, :32] = np.clip(ov, 0, None) / 32.0
    cmap[:127, 32] = 1.0
    force = np.zeros((128, 8, 32), f32)
    for q in range(8):
        tq = 128 * (q + 8) + np.arange(128)
        cur = tq // 64
        jb = np.arange(32)[None, :]
        forced = (jb == 0) | (jb == cur[:, None]) | (jb == cur[:, None] - 1)
        force[:, q, :] = np.where(forced, 1e4, np.where(jb > cur[:, None], -1e4, 0.0))
    gsel = np.zeros((48, 8, 3, 128), f32)
    for hp in range(8):
        for br in range(3):
            for m in range(128):
                gsel[(2 * hp + m // 64) * 3 + br, hp, br, m] = 1.0
    p = np.arange(128)
    bd = (p[:, None] // 64 == p[None, :] // 64).astype(f32)
    return {"amask": amask, "e128": e128.reshape(128, 2048), "cmap": cmap, "force": force.reshape(128, 256),
            "gsel": gsel.reshape(48, 3072), "bdones": bd}


def prep_inputs(inputs):
    f32 = np.float32
    g = {k: np.asarray(v, dtype=f32) for k, v in inputs.items()}
    vecs = np.zeros((128, NV, KC), f32)
    for i in range(2):
        for j in range(2):
            vecs[:, V_NG + 2 * i + j, :] = _fm(g["norm_gain"][i, j])
        for part in range(6):
            vecs[:, V_BADA + 6 * i + part, :] = _fm(g["b_ada"][i, part * D:(part + 1) * D])
    vecs[:, V_KVG, :] = _fm(g["kv_norm_gain"])
    for part in range(2):
        vecs[:, V_BKV + part, :] = _fm(g["b_ada_kv"][part * D:(part + 1) * D])
    for j in range(3):
        vecs[:, V_CONV + j, :] = _fm(g["conv_w"][0, j])
    consts = np.concatenate([np.eye(128, dtype=f32), np.ones((128, 128), f32)], axis=1)
    p64 = np.arange(128) % 64
    hvecs = np.zeros((128, 4), f32)
    hvecs[:, 0] = g["q_gain"][0, p64]
    for i in range(3):
        hvecs[:, 1 + i] = g["k_gain"][i, p64]
    peT = np.zeros((128, 64), f32)
    for kv in range(2):
        peT[:, kv * 32:(kv + 1) * 32] = g["cmp_pe"][kv][:, p64].T
    shared = {
        "vecs": vecs, "consts": consts, "hvecs": hvecs, "peT": peT,
        "w_ada": g["w_ada"], "w_a_in": g["w_a_in"], "w_a_out": g["w_a_out"],
        "w_mlp1": g["w_mlp1"], "w_mlp2": g["w_mlp2"],
        "w_ada_kv": g["w_ada_kv"], "w_kv": g["w_kv"], "w_qg": g["w_qg"], "w_o": g["w_o"],
        "cmp_w1": g["cmp_w1"], "cmp_w2": g["cmp_w2"],
    }
    shared.update(_const_tables())
    in_maps = []
    for b in range(N_CORES):
        m = dict(shared)
        xT = g["x"][b].T.reshape(KC, 128, T).transpose(1, 0, 2)
        m["xT"] = np.ascontiguousarray(xT)
        m["cT"] = _fm(g["c"][b])
        in_maps.append(m)
    return in_maps


def post_outputs(results):
    outs = []
    for r in results:
        yT = np.asarray(r["yT"])
        outs.append(yT.transpose(2, 1, 0).reshape(T, D))
    return np.stack(outs, axis=0).astype(np.float32)


def kernel(**inputs):
    in_maps = prep_inputs(inputs)
    nc = build_program(DEBUG_STOP)
    res = run_bass_kernel_spmd(nc, in_maps, core_ids=list(range(N_CORES)))
    return post_outputs(res.results)
```

```python
import numpy as np
from contextlib import ExitStack
import concourse.bass as bass
import concourse.mybir as mybir
from concourse.bass_utils import run_bass_kernel_spmd

F32 = mybir.dt.float32
BF16 = mybir.dt.bfloat16
AF = mybir.ActivationFunctionType
ALU = mybir.AluOpType

D = 1024
T = 2048
KC = 8
NTG = 4
TGW = 512
EPS = 1e-6
N_CORES = 8

V_NG = 0
V_KVG = 4
V_BADA = 5
V_BKV = 17
V_CONV = 19
NV = 22

DEBUG_STOP = None


class Tok:
    __slots__ = ("ws", "rs", "rdma", "excl")

    def __init__(self, excl=False):
        self.ws = []
        self.rs = {}
        self.rdma = []
        self.excl = excl


class Op:
    __slots__ = ("eng", "fn", "deps", "dma", "signal", "sigval", "dmaval", "dsem")

    def __init__(self, eng, fn, dma):
        self.eng = eng
        self.fn = fn
        self.dma = dma
        self.deps = ()
        self.signal = False
        self.sigval = 0
        self.dmaval = 0
        self.dsem = None


ENGS = ["pe", "act", "dve", "pool", "sp"]
DMA_SEMS = 16


class Sched:
    def __init__(self, nc):
        self.nc = nc
        self.ops = {e: [] for e in ENGS}
        self.dma_hist = {e: [] for e in ENGS}

    def op(self, eng, fn, reads=(), writes=(), dma=None):
        o = Op(eng, fn, dma)
        ex = [t for t in reads if t.excl]
        if ex:
            reads = [t for t in reads if not t.excl]
            writes = list(writes) + ex
        deps = set()
        for t in reads:
            deps.update(t.ws)
        for t in writes:
            deps.update(t.ws)
            deps.update(t.rs.values())
            deps.update(t.rdma)
        if dma is not None:
            deps = {d for d in deps if d.dma != dma}
            hist = self.dma_hist[eng]
            n = len(hist)
            o.dsem = (eng, n % DMA_SEMS)
            o.dmaval = 16 * (n // DMA_SEMS + 1)
            if n >= DMA_SEMS:
                deps.add(hist[n - DMA_SEMS])
            hist.append(o)
        o.deps = tuple(deps)
        for t in reads:
            if dma is not None:
                t.rdma.append(o)
            else:
                t.rs[eng] = o
        for t in writes:
            if dma is not None and t.ws and not t.rs and not t.rdma and all(w.dma == dma for w in t.ws):
                t.ws.append(o)
            else:
                t.ws = [o]
            t.rs = {}
            t.rdma = []
        self.ops[eng].append(o)
        return o

    def barrier(self, engines=("pe", "act", "dve", "sp", "pool")):
        lasts = []
        for e in ENGS:
            for o in reversed(self.ops[e]):
                if o.dma is None and o.fn is not None:
                    lasts.append(o)
                    break
            lasts += self.dma_hist[e][-DMA_SEMS:]
        for e in engines:
            o = Op(e, None, None)
            o.deps = tuple(lasts)
            self.ops[e].append(o)

    def emit(self, final_dma_keys=()):
        nc = self.nc
        for e in ENGS:
            for o in self.ops[e]:
                for d in o.deps:
                    if d.dma is None:
                        if d.eng == "pe" and o.eng == "pe" and o.dma is None and o.fn is not None:
                            continue
                        d.signal = True
        for e in ENGS:
            c = 0
            for o in self.ops[e]:
                if o.dma is None and o.signal:
                    c += 1
                    o.sigval = c
        with ExitStack() as es:
            esem = {e: es.enter_context(nc.semaphore("s_" + e)) for e in ENGS}
            dsem = {}
            for e in ENGS:
                for i in range(min(DMA_SEMS, len(self.dma_hist[e]))):
                    dsem[(e, i)] = es.enter_context(nc.semaphore("d_%s%d" % (e, i)))
            block = es.enter_context(nc.Block())

            def run(e, eng):
                waited = {}
                for o in self.ops[e]:
                    need = {}
                    for d in o.deps:
                        if d.dma is not None:
                            key, sem, val = ("d",) + d.dsem, dsem[d.dsem], d.dmaval
                        else:
                            if d.eng == "pe" and e == "pe" and o.dma is None and o.fn is not None:
                                continue
                            key, sem, val = ("e", d.eng), esem[d.eng], d.sigval
                        if val > need.get(key, (None, 0))[1]:
                            need[key] = (sem, val)
                    for key, (sem, val) in need.items():
                        if waited.get(key, 0) >= val:
                            continue
                        waited[key] = val
                        eng.wait_ge(sem, val)
                    if o.fn is None:
                        continue
                    ins = o.fn(eng)
                    if o.dma is not None:
                        ins.then_inc(dsem[o.dsem], 16)
                    elif o.signal:
                        ins.then_inc(esem[e], 1)
                if e == "sp":
                    fin = {}
                    for q in ENGS:
                        for d in self.dma_hist[q]:
                            if d.dma in final_dma_keys:
                                fin[d.dsem] = max(fin.get(d.dsem, 0), d.dmaval)
                    for k, v in fin.items():
                        if waited.get(("d",) + k, 0) < v:
                            eng.wait_ge(dsem[k], v)

            block.sync(lambda eng: run("sp", eng))
            block.scalar(lambda eng: run("act", eng))
            block.vector(lambda eng: run("dve", eng))
            block.gpsimd(lambda eng: run("pool", eng))
            block.tensor(lambda eng: run("pe", eng))


class StopBuild(Exception):
    pass


class TT:
    def __init__(self, ap):
        self.ap = ap
        self.toks = {}

    def t(self, *key):
        tk = self.toks.get(key)
        if tk is None:
            tk = self.toks[key] = Tok()
        return tk

    def all(self):
        return list(self.toks.values())


def build_program(stop=None):
    nc = bass.Bass("TRN2", target_bir_lowering=False)

    def din(name, shape):
        return nc.dram_tensor(name, list(shape), F32, kind="ExternalInput").ap()

    d_x = din("xT", [128, KC, T])
    d_c = din("cT", [128, KC])
    d_vecs = din("vecs", [128, NV, KC])
    d_consts = din("consts", [128, 256])
    d_w_ada = din("w_ada", [2, D, 6 * D])
    d_w_a_in = din("w_a_in", [1, D, 3 * D])
    d_w_a_out = din("w_a_out", [1, D, D])
    d_w_mlp1 = din("w_mlp1", [2, D, 4 * D])
    d_w_mlp2 = din("w_mlp2", [2, 4 * D, D])
    d_w_ada_kv = din("w_ada_kv", [D, 2 * D])
    d_w_kv = din("w_kv", [D, 1536])
    d_w_qg = din("w_qg", [1, D, 1072])
    d_w_o = din("w_o", [1, D, D])
    d_cmp_w1 = din("cmp_w1", [2, 2048, 256])
    d_cmp_w2 = din("cmp_w2", [2, 256, 64])
    d_hvecs = din("hvecs", [128, 4])
    d_peT = din("peT", [128, 64])
    d_amask = din("amask", [128, 256 + 2048])
    d_e128 = din("e128", [128, 2048])
    d_cmap = din("cmap", [128, 40])
    d_force = din("force", [128, 256])
    d_gsel = din("gsel", [48, 3072])
    d_bd = din("bdones", [128, 128])
    d_y = nc.dram_tensor("yT", [128, KC, T], F32, kind="ExternalOutput").ap()
    d_xs = nc.dram_tensor("xs_scratch", [128, KC, T], F32).ap()

    S = Sched(nc)
    es = ExitStack()
    with es:
        def sb(name, shape, dt):
            return es.enter_context(nc.sbuf_tensor(name, list(shape), dt))

        XR = sb("XR", [128, KC * T], F32)
        HR = sb("HR", [128, KC * T], BF16)
        AR = sb("AR", [128, KC * T], BF16)
        RING = [sb("RING%d" % i, [128, 8192], BF16) for i in range(2)]
        ring_t = [Tok(), Tok()]
        rstd_s = sb("rstd", [128, T], F32)
        TMPF = [sb("tmpf%d" % i, [128, TGW], F32) for i in range(3)]
        tmpf_t = [Tok() for _ in TMPF]
        TMPB = [sb("tmpb%d" % i, [128, TGW], BF16) for i in range(3)]
        tmpb_t = [Tok() for _ in TMPB]
        SCR = sb("SCR", [128, 11048], BF16)
        ones_b = sb("ones_b", [128, 128], BF16)
        vecs = sb("vecs_s", [128, NV, KC], F32)
        c_f = sb("c_f", [128, KC], F32)
        cact = sb("cact", [128, KC], BF16)
        modv = sb("modv", [128, 3, 48], F32)
        der = sb("der", [128, 8, KC], F32)
        hvecs = sb("hvecs_s", [128, 4], F32)
        gq = sb("gq", [128, 1], F32)
        ident_b = sb("ident_b", [128, 128], BF16)
        bd_ones = sb("bd_ones", [128, 128], BF16)
        cbias = sb("cbias", [128, 4], F32)
        kcT = sb("kcT", [128, 4, 128], BF16)
        vcA = sb("vcA", [128, 4, 128], BF16)
        rd4 = sb("rd4", [128, 4], F32)
        scA = sb("scA", [128, 32], F32)
        scB = sb("scB", [128, 32], F32)
        m8a = sb("m8a", [128, 8], F32)
        m8b = sb("m8b", [128, 8], F32)
        selm = sb("selm", [128, 32], BF16)
        ACC = [sb("acc%d" % i, [128, 256], F32) for i in range(2)]
        BIAS_EXTRA = sb("m2", [128, 1024], BF16)
        PS = [es.enter_context(nc.psum_tensor("ps%d" % i, [128, TGW], F32)) for i in range(8)]
        ps_t = [Tok(excl=True) for _ in PS]

        X = TT(XR[:].rearrange("p (k t) -> p k t", k=KC))
        H = TT(HR[:].rearrange("p (k t) -> p k t", k=KC))
        A = TT(AR[:].rearrange("p (k t) -> p k t", k=KC))
        t_const = Tok()
        eps_t = sb("eps_t", [128, 2], F32)
        t_eps = Tok()
        S.op("dve", lambda e: e.memset(eps_t[:, 0:1], EPS), writes=[t_eps])
        S.op("dve", lambda e: e.memset(eps_t[:, 1:2], 1e-30), writes=[t_eps])
        t_vecs = Tok()
        t_c = Tok()
        t_cact = Tok()
        t_modv = [Tok(), Tok(), Tok()]
        t_der = Tok()
        t_rstd = [Tok() for _ in range(NTG)]

        cnt = {"ps": 0, "ring": 0, "tf": 0, "tb": 0}

        POOLS = {"ALL": list(range(8)), "P6": [0, 1, 2, 3, 4, 5], "S": [0, 1, 2, 3], "O": [4, 5, 6], "M": [7], "OM": [4, 5, 6, 7]}
        pcnt = {k: 0 for k in POOLS}

        defpool = {"v": "ALL"}

        def next_ps(pool=None):
            pool = pool or defpool["v"]
            lst = POOLS[pool]
            i = lst[pcnt[pool] % len(lst)]
            pcnt[pool] += 1
            return PS[i], ps_t[i]

        def next_tf():
            i = cnt["tf"] % len(TMPF)
            cnt["tf"] += 1
            return TMPF[i], tmpf_t[i]

        def next_tb():
            i = cnt["tb"] % len(TMPB)
            cnt["tb"] += 1
            return TMPB[i], tmpb_t[i]

        def tsl(tg):
            return slice(tg * TGW, (tg + 1) * TGW)

        S.op("pool", lambda e: e.dma_start(out=ones_b[:], in_=d_consts[:, 128:256]), writes=[t_const], dma="cb")
        S.op("sp", lambda e: e.dma_start(out=vecs[:], in_=d_vecs), writes=[t_vecs], dma="c")
        S.op("sp", lambda e: e.dma_start(out=c_f[:], in_=d_c), writes=[t_c], dma="c")
        for tg in range(NTG):
            for kh in range(2):
                ks = slice(kh * 4, kh * 4 + 4)
                S.op("sp", (lambda e, tg=tg, ks=ks: e.dma_start(out=X.ap[:, ks, tsl(tg)], in_=d_x[:, ks, tsl(tg)])),
                     writes=[X.t(k, tg) for k in range(kh * 4, kh * 4 + 4)], dma="x")

        S.op("act", lambda e: e.activation(out=cact[:], in_=c_f[:], func=AF.Silu), reads=[t_c], writes=[t_cact])

        def load_w(src_aps, dst_view_fn):
            i = cnt["ring"] % 2
            cnt["ring"] += 1
            slot = RING[i]
            for dst_fn, src in src_aps:
                S.op("pool", (lambda e, dst_fn=dst_fn, src=src, slot=slot: e.dma_start(out=dst_fn(slot), in_=src)),
                     writes=[ring_t[i]], dma="r%d" % i)
            return dst_view_fn(slot), ring_t[i]

        def load_w_std(w2d, c0, ncols, k0=0):
            src = w2d.rearrange("(k p) n -> p k n", p=128)

            def view(slot):
                return slot[:, 0:8 * ncols].rearrange("p (k c) -> p k c", k=8)
            pieces = []
            for kh in range(2):
                ks = slice(kh * 4, kh * 4 + 4)
                pieces.append(((lambda slot, ks=ks: view(slot)[:, ks, :]), src[:, k0 + kh * 4:k0 + kh * 4 + 4, c0:c0 + ncols]))
            return load_w(pieces, view)

        def ada_matvec(w2d, ncb, bias_idx, mi):
            ps, pt = next_ps()
            for cb in range(ncb):
                wv, wt = load_w_std(w2d, cb * 1024, 1024)
                for oc in range(8):
                    col = cb * 8 + oc
                    for k in range(KC):
                        S.op("pe", (lambda e, ps=ps, wv=wv, oc=oc, k=k, col=col: e.matmul(
                            ps[:, col:col + 1], wv[:, k, oc * 128:(oc + 1) * 128], cact[:, k:k + 1],
                            start=(k == 0), stop=(k == KC - 1))), reads=[wt, t_cact], writes=[pt])
            n = ncb * 8
            S.op("dve", (lambda e, ps=ps, n=n: e.tensor_tensor(
                out=modv[:, mi, 0:n], in0=ps[:, 0:n],
                in1=vecs[:, bias_idx:bias_idx + ncb, :].rearrange("p a b -> p (a b)"), op=ALU.add)),
                reads=[pt, t_vecs], writes=[t_modv[mi]])

        def mod_part(mi, part):
            return modv[:, mi, part * 8:(part + 1) * 8]

        def derive(mi, part_sc, gain_idx, dst):
            S.op("dve", lambda e: e.tensor_scalar(out=der[:, dst, :], in0=mod_part(mi, part_sc), scalar1=1.0, scalar2=1.0,
                                                  op0=ALU.add, op1=ALU.mult), reads=[t_modv[mi]], writes=[t_der])
            S.op("dve", lambda e: e.tensor_tensor(out=der[:, dst, :], in0=der[:, dst, :], in1=vecs[:, gain_idx, :], op=ALU.mult),
                 reads=[t_der, t_vecs], writes=[t_der])

        def compute_rstd():
            for tg in range(NTG):
                ps, pt = next_ps()
                for k in range(KC):
                    tb, tbt = next_tb()
                    S.op("act", (lambda e, tb=tb, k=k, tg=tg: e.activation(out=tb[:], in_=X.ap[:, k, tsl(tg)], func=AF.Square)),
                         reads=[X.t(k, tg)], writes=[tbt])
                    S.op("pe", (lambda e, ps=ps, tb=tb, k=k: e.matmul(ps[:], ones_b[:], tb[:], start=(k == 0), stop=(k == KC - 1))),
                         reads=[tbt, t_const], writes=[pt])
                tf, tft = next_tf()
                S.op("act", (lambda e, ps=ps, tf=tf: e.activation(out=tf[:], in_=ps[:], func=AF.Ln, bias=eps_t[:, 0:1], scale=1.0 / D)),
                     reads=[pt, t_eps], writes=[tft])
                S.op("act", (lambda e, tf=tf, tg=tg: e.activation(out=rstd_s[:, tsl(tg)], in_=tf[:], func=AF.Exp, scale=-0.5)), reads=[tft], writes=[t_rstd[tg]])

        def norm_mod(dst, a_ap, b_ap, extra_reads):
            for tg in range(NTG):
                for k in range(KC):
                    tf, tft = next_tf()
                    S.op("dve", (lambda e, tf=tf, k=k, tg=tg: e.tensor_tensor(out=tf[:], in0=X.ap[:, k, tsl(tg)], in1=rstd_s[:, tsl(tg)], op=ALU.mult)),
                         reads=[X.t(k, tg), t_rstd[tg]], writes=[tft])
                    S.op("act", (lambda e, tf=tf, k=k, tg=tg: e.activation(out=dst.ap[:, k, tsl(tg)], in_=tf[:], func=AF.Identity,
                                                                            bias=b_ap[:, k:k + 1], scale=a_ap[:, k:k + 1])),
                         reads=[tft] + extra_reads, writes=[dst.t(k, tg)])

        def proj(wv, wt, src, n_oc, evac, tgs=range(NTG), oc_cols=None):
            for tg in tgs:
                for oc in range(n_oc):
                    ps, pt = next_ps()
                    for k in range(KC):
                        lhs = wv[:, k, oc * 128:(oc + 1) * 128] if oc_cols is None else oc_cols(wv, k, oc)
                        S.op("pe", (lambda e, ps=ps, lhs=lhs, k=k, tg=tg: e.matmul(ps[:], lhs, src.ap[:, k, tsl(tg)],
                                                                                   start=(k == 0), stop=(k == KC - 1))),
                             reads=[wt, src.t(k, tg)], writes=[pt])
                    evac(oc, tg, ps, pt)

        def resid_evac(g_ap, extra_reads):
            def ev(oc, tg, ps, pt):
                S.op("dve", (lambda e: e.scalar_tensor_tensor(out=X.ap[:, oc, tsl(tg)], in0=ps[:], scalar=g_ap[:, oc:oc + 1],
                                                              in1=X.ap[:, oc, tsl(tg)], op0=ALU.mult, op1=ALU.add)),
                     reads=[pt, X.t(oc, tg)] + extra_reads, writes=[X.t(oc, tg)])
            return ev

        mv_tasks = []
        mv_t = [Tok(), Tok()]
        mv_state = {"n": 0}

        def make_mv_tasks(w2d, n512, bias_idx, mi):
            src = w2d.rearrange("(k p) n -> p k n", p=128)
            nparts = (n512 * 4) // 8
            bias_flat = vecs[:, bias_idx:bias_idx + nparts, :].rearrange("p a b -> p (a b)")
            for cb in range(n512):
                def task(cb=cb):
                    n = mv_state["n"]
                    mv_state["n"] += 1
                    i = n % 2
                    slot = SCR[:, i * 4096:(i + 1) * 4096].rearrange("p (k c) -> p k c", k=8)
                    extra = ([t for row in t_gb for t in row] + [t for row in t_v for t in row] + [t_vhalo]) if n < 2 else []
                    S.op("pool", (lambda e: e.dma_start(out=slot, in_=src[:, :, cb * 512:(cb + 1) * 512])), writes=[mv_t[i]] + extra, dma="mv%d" % i)
                    ps, pt = next_ps()
                    for oc in range(4):
                        for k in range(KC):
                            S.op("pe", (lambda e, oc=oc, k=k: e.matmul(ps[:, oc:oc + 1], slot[:, k, oc * 128:(oc + 1) * 128], cact[:, k:k + 1],
                                                                      start=(k == 0), stop=(k == KC - 1))), reads=[mv_t[i], t_cact], writes=[pt])
                    c0 = cb * 4
                    S.op("dve", (lambda e: e.tensor_tensor(out=modv[:, mi, c0:c0 + 4], in0=ps[:, 0:4], in1=bias_flat[:, c0:c0 + 4], op=ALU.add)),
                         reads=[pt, t_vecs], writes=[t_modv[mi]])
                mv_tasks.append(task)

        def run_mv_tasks(n):
            for _ in range(n):
                if mv_tasks:
                    mv_tasks.pop(0)()

        def mlp(layer, mi):
            compute_rstd()
            derive(mi, 4, V_NG + 2 * layer + 1, 1)
            norm_mod(H, der[:, 1, :], mod_part(mi, 3), [t_der, t_modv[mi]])
            g2 = mod_part(mi, 5)
            for hb in range(4):
                wv, wt = load_w_std(d_w_mlp1[layer], hb * 1024, 1024)

                def ev1(oc, tg, ps, pt):
                    tf, tft = next_tf()
                    S.op("act", (lambda e: e.activation(out=tf[:], in_=ps[:], func=AF.Relu)), reads=[pt], writes=[tft])
                    S.op("dve", (lambda e: e.tensor_tensor(out=A.ap[:, oc, tsl(tg)], in0=tf[:], in1=tf[:], op=ALU.mult)),
                         reads=[tft], writes=[A.t(oc, tg)])
                proj(wv, wt, H, 8, ev1)
                wv2, wt2 = load_w_std(d_w_mlp2[layer], 0, 1024, k0=hb * 8)
                run_mv_tasks(2)
                proj(wv2, wt2, A, 8, resid_evac(g2, [t_modv[mi]]))
                run_mv_tasks(2)

        compute_rstd()
        ada_matvec(d_w_ada[0], 6, V_BADA, 0)
        derive(0, 1, V_NG + 0, 0)
        norm_mod(H, der[:, 0, :], mod_part(0, 0), [t_der, t_modv[0]])

        gbv = SCR[:, 0:4096].rearrange("p (j t) -> p j t", j=2)
        vv = SCR[:, 4096:4096 + 2 * 2056].rearrange("p (j t) -> p j t", j=2)
        t_gb = [[Tok() for _ in range(NTG)] for _ in range(2)]
        t_v = [[Tok() for _ in range(NTG)] for _ in range(2)]
        t_vhalo = Tok()
        S.op("dve", lambda e: e.memset(vv[:, :, 0:2], 0.0), writes=[t_vhalo])
        w_in_v = d_w_a_in[0].rearrange("(k p) (s c) -> p k s c", p=128, s=3)
        g1 = mod_part(0, 2)
        def mixer_j(wv, wt, jj, j):
            jb = j % 2
            for tg in range(NTG):
                pss = []
                for s in range(3):
                    ps, pt = next_ps()
                    for k in range(KC):
                        S.op("pe", (lambda e, ps=ps, s=s, k=k, tg=tg: e.matmul(ps[:], wv[:, k, s, jj * 128:(jj + 1) * 128], H.ap[:, k, tsl(tg)],
                                                                               start=(k == 0), stop=(k == KC - 1))),
                             reads=[wt, H.t(k, tg)], writes=[pt])
                    pss.append((ps, pt))
                (psb, ptb), (psc, ptc), (psu, ptu) = pss
                S.op("act", (lambda e, psb=psb, tg=tg: e.activation(out=gbv[:, jb, tsl(tg)], in_=psb[:], func=AF.Copy)),
                     reads=[ptb], writes=[t_gb[jb][tg]])
                tb, tbt = next_tb()
                S.op("act", (lambda e, psc=psc, tb=tb: e.activation(out=tb[:], in_=psc[:], func=AF.Copy)), reads=[ptc], writes=[tbt])
                S.op("dve", (lambda e, psu=psu, tb=tb, tg=tg: e.tensor_tensor(out=vv[:, jb, 2 + tg * TGW:2 + (tg + 1) * TGW], in0=psu[:], in1=tb[:], op=ALU.mult)),
                     reads=[ptu, tbt], writes=[t_v[jb][tg]])
            for tg in range(NTG):
                tf, tft = next_tf()
                rd = [t_v[jb][tg], t_vhalo, t_vecs] + ([t_v[jb][tg - 1]] if tg > 0 else [])
                b0 = tg * TGW
                S.op("dve", (lambda e, tf=tf, b0=b0: e.tensor_scalar(out=tf[:], in0=vv[:, jb, b0 + 2:b0 + 2 + TGW], scalar1=vecs[:, V_CONV + 2, j:j + 1], scalar2=None, op0=ALU.mult)),
                     reads=rd, writes=[tft])
                S.op("dve", (lambda e, tf=tf, b0=b0: e.scalar_tensor_tensor(out=tf[:], in0=vv[:, jb, b0 + 1:b0 + 1 + TGW], scalar=vecs[:, V_CONV + 1, j:j + 1], in1=tf[:], op0=ALU.mult, op1=ALU.add)),
                     reads=rd + [tft], writes=[tft])
                S.op("dve", (lambda e, tf=tf, b0=b0: e.scalar_tensor_tensor(out=tf[:], in0=vv[:, jb, b0:b0 + TGW], scalar=vecs[:, V_CONV + 0, j:j + 1], in1=tf[:], op0=ALU.mult, op1=ALU.add)),
                     reads=rd + [tft], writes=[tft])
                S.op("dve", (lambda e, tf=tf, tg=tg: e.tensor_tensor(out=A.ap[:, j, tsl(tg)], in0=tf[:], in1=gbv[:, jb, tsl(tg)], op=ALU.mult)),
                     reads=[tft, t_gb[jb][tg]], writes=[A.t(j, tg)])

        for jp in range(4):
            def view(slot):
                return slot[:, 0:6144].rearrange("p (k s c) -> p k s c", k=8, s=3)
            pieces = []
            for s3 in range(3):
                pieces.append(((lambda slot, s3=s3: view(slot)[:, :, s3, :]), w_in_v[:, :, s3, jp * 256:(jp + 1) * 256]))
            wv, wt = load_w(pieces, view)
            for jj in range(2):
                mixer_j(wv, wt, jj, 2 * jp + jj)
        wv, wt = load_w_std(d_w_a_out[0], 0, 1024)
        proj(wv, wt, A, 8, resid_evac(g1, [t_modv[0]]))
        if stop != "mix0":
            if stop not in ("mix0", "l0"):
                make_mv_tasks(d_w_ada[1], 12, V_BADA + 6, 1)
                make_mv_tasks(d_w_ada_kv, 4, V_BKV, 2)
            mlp(0, 0)
        def chk(name, dumps):
            if stop != name:
                return
            S.barrier()
            for ap, dst in dumps:
                S.op("pool", (lambda e, ap=ap, dst=dst: e.dma_start(out=dst, in_=ap)), dma="out")
            raise StopBuild()

        if stop not in ("mix0", "l0"):
          try:
            G4 = 4
            defpool["v"] = "P6"
            POOLS["P6"] = [0, 1, 2, 3, 4]
            POOLS["M"] = [5, 6, 7]
            t_l1c = Tok()
            wqg_g = SCR[:, 8192:8576].rearrange("p (k c) -> p k c", k=8)
            cw2k = SCR[:, 8576:8832].rearrange("p (c d) -> p c d", c=2)
            cw2v = SCR[:, 8832:8960].rearrange("p (c d) -> p c d", c=2)
            peT_b = SCR[:, 8960:9024]
            S.op("sp", lambda e: e.dma_start(out=hvecs[:], in_=d_hvecs), writes=[t_l1c], dma="c")
            S.op("pool", lambda e: e.dma_start(out=ident_b[:], in_=d_consts[:, 0:128]), writes=[t_l1c], dma="cb")
            S.op("pool", lambda e: e.dma_start(out=bd_ones[:], in_=d_bd), writes=[t_l1c], dma="cb")
            S.op("pool", lambda e: e.dma_start(out=peT_b, in_=d_peT), writes=[t_l1c], dma="cb")
            S.op("pool", lambda e: e.dma_start(out=cw2k[:, :, 0:64], in_=d_cmp_w2[0].rearrange("(c p) d -> p c d", p=128)), writes=[t_l1c], dma="cb")
            S.op("pool", lambda e: e.dma_start(out=cw2k[:, :, 64:128], in_=d_cmp_w2[0].rearrange("(c p) d -> p c d", p=128)), writes=[t_l1c], dma="cb")
            S.op("pool", lambda e: e.dma_start(out=cw2v, in_=d_cmp_w2[1].rearrange("(c p) d -> p c d", p=128)), writes=[t_l1c], dma="cb")
            S.op("pool", lambda e: e.dma_start(out=wqg_g, in_=d_w_qg[0].rearrange("(k p) n -> p k n", p=128)[:, :, 1024:1072]), writes=[t_l1c], dma="cb")
            t_gq = Tok()
            S.op("dve", lambda e: e.tensor_scalar(out=gq[:], in0=hvecs[:, 0:1], scalar1=0.125, scalar2=None, op0=ALU.mult), reads=[t_l1c], writes=[t_gq])

            run_mv_tasks(99)
            compute_rstd()
            derive(2, 1, V_KVG, 2)
            derive(1, 1, V_NG + 2, 3)
            norm_mod(H, der[:, 2, :], mod_part(2, 0), [t_der, t_modv[2]])
            norm_mod(A, der[:, 3, :], mod_part(1, 0), [t_der, t_modv[1]])
            xs_t = [Tok() for _ in range(NTG)]
            for tg in range(NTG):
                for kh in range(2):
                    ks = slice(kh * 4, kh * 4 + 4)
                    S.op("sp", (lambda e, tg=tg, ks=ks: e.dma_start(out=d_xs[:, ks, tsl(tg)], in_=X.ap[:, ks, tsl(tg)])),
                         reads=[X.t(k, tg) for k in range(kh * 4, kh * 4 + 4)], writes=[xs_t[tg]], dma="xs")

            def head_norm(ps, pt, ncol, gain_ap, gain_reads, dst_ap, dst_toks):
                tb, tbt = next_tb()
                S.op("act", (lambda e: e.activation(out=tb[:, 0:ncol], in_=ps[:, 0:ncol], func=AF.Square)), reads=[pt], writes=[tbt])
                ps2, pt2 = next_ps("M")
                S.op("pe", (lambda e: e.matmul(ps2[:, 0:ncol], bd_ones[:], tb[:, 0:ncol], start=True, stop=True)), reads=[tbt, t_l1c], writes=[pt2])
                tf, tft = next_tf()
                S.op("act", (lambda e: e.activation(out=tf[:, 0:ncol], in_=ps2[:, 0:ncol], func=AF.Ln, bias=eps_t[:, 0:1], scale=1.0 / 64)),
                     reads=[pt2, t_eps], writes=[tft])
                tf2, tft2 = next_tf()
                S.op("act", (lambda e: e.activation(out=tf2[:, 0:ncol], in_=tf[:, 0:ncol], func=AF.Exp, scale=-0.5)), reads=[tft], writes=[tft2])
                S.op("dve", (lambda e: e.scalar_tensor_tensor(out=dst_ap, in0=ps[:, 0:ncol], scalar=gain_ap, in1=tf2[:, 0:ncol], op0=ALU.mult, op1=ALU.mult)),
                     reads=[pt, tft2] + gain_reads, writes=dst_toks)

            RAW = TT(SCR[:, 0:8192].rearrange("p (c t) -> p c t", c=4))
            wv, wt = load_w_std(d_w_kv, 0, 512)

            def ev_raw(oc, tg, ps, pt):
                S.op("act", (lambda e: e.activation(out=RAW.ap[:, oc, tsl(tg)], in_=ps[:], func=AF.Copy)), reads=[pt], writes=[RAW.t(oc, tg)])
            proj(wv, wt, H, 4, ev_raw)

            chk("raw", [(RAW.ap, d_y[:, 0:4, :])])
            t_kc = [Tok() for _ in range(G4)]
            t_vc = [Tok() for _ in range(G4)]
            t_vc_ones = Tok()
            S.op("dve", lambda e: e.memset(vcA[:, :, 64:128], 1.0), writes=[t_vc_ones])
            t_cb = Tok()
            for kv in range(2):
                def view1(slot):
                    return slot[:, 0:8192].rearrange("p (l h) -> p l h", l=32)
                src1 = d_cmp_w1[kv].rearrange("(l d) h -> d l h", d=64)
                pieces = [((lambda slot: view1(slot)[0:64, :, :]), src1), ((lambda slot: view1(slot)[64:128, :, :]), src1)]
                cwv, cwt = load_w(pieces, view1)
                psb, ptb = next_ps()
                for hc in range(2):
                    for l in range(32):
                        S.op("pe", (lambda e, hc=hc, l=l, cwv=cwv, psb=psb, kv=kv: e.matmul(
                            psb[:, hc:hc + 1], cwv[0:64, l, hc * 128:(hc + 1) * 128], peT_b[0:64, kv * 32 + l:kv * 32 + l + 1],
                            start=(l == 0), stop=(l == 31))), reads=[cwt, t_l1c], writes=[ptb])
                S.op("dve", (lambda e, psb=psb, kv=kv: e.tensor_copy(out=cbias[:, 2 * kv:2 * kv + 2], in_=psb[:, 0:2])), reads=[ptb], writes=[t_cb])
                for g in range(G4):
                    base = (g % 2) * 64
                    c = kv * 2 + g // 2
                    hids = []
                    for hc in range(2):
                        ps, pt = next_ps()
                        for l in range(32):
                            S.op("pe", (lambda e, ps=ps, l=l, hc=hc, cwv=cwv, base=base, c=c: e.matmul(
                                ps[:, 0:127], cwv[base:base + 64, l, hc * 128:(hc + 1) * 128],
                                RAW.ap[base:base + 64, c, l:l + 16 * 126 + 1:16], start=(l == 0), stop=(l == 31))),
                                reads=[cwt] + [RAW.t(c, tg) for tg in range(NTG)], writes=[pt])
                        z, zt = next_tf()
                        S.op("act", (lambda e, ps=ps, z=z, hc=hc, kv=kv: e.activation(out=z[:, 0:127], in_=ps[:, 0:127], func=AF.Identity,
                                                                                      bias=cbias[:, 2 * kv + hc:2 * kv + hc + 1], scale=1.0)),
                             reads=[pt, t_cb], writes=[zt])
                        u, ut = next_tf()
                        S.op("dve", (lambda e, z=z, u=u: e.tensor_tensor(out=u[:, 0:127], in0=z[:, 0:127], in1=z[:, 0:127], op=ALU.mult)), reads=[zt], writes=[ut])
                        S.op("dve", (lambda e, u=u: e.tensor_scalar(out=u[:, 0:127], in0=u[:, 0:127], scalar1=0.044715, scalar2=1.0, op0=ALU.mult, op1=ALU.add)),
                             reads=[ut], writes=[ut])
                        S.op("dve", (lambda e, z=z, u=u: e.tensor_tensor(out=u[:, 0:127], in0=u[:, 0:127], in1=z[:, 0:127], op=ALU.mult)), reads=[ut, zt], writes=[ut])
                        S.op("act", (lambda e, u=u: e.activation(out=u[:, 0:127], in_=u[:, 0:127], func=AF.Sigmoid, scale=1.5957691216057308)), reads=[ut], writes=[ut])
                        hb_, hbt = next_tb()
                        S.op("dve", (lambda e, z=z, u=u, hb_=hb_: e.tensor_tensor(out=hb_[:, 0:127], in0=u[:, 0:127], in1=z[:, 0:127], op=ALU.mult)), reads=[ut, zt], writes=[hbt])
                        hids.append((hb_, hbt))
                    if kv == 0:
                        ps, pt = next_ps()
                        for hc in range(2):
                            S.op("pe", (lambda e, ps=ps, hc=hc, hb_=hids[hc][0]: e.matmul(ps[:, 0:127], cw2k[:, hc, :], hb_[:, 0:127], start=(hc == 0), stop=(hc == 1))),
                                 reads=[hids[hc][1], t_l1c], writes=[pt])
                        head_norm(ps, pt, 127, hvecs[:, 1:2], [t_l1c], kcT[:, g, 0:127], [t_kc[g]])
                    else:
                        ps, pt = next_ps()
                        for hc in range(2):
                            S.op("pe", (lambda e, ps=ps, hc=hc, hb_=hids[hc][0]: e.matmul(ps[0:127, 0:64], hb_[:, 0:127], cw2v[:, hc, :], start=(hc == 0), stop=(hc == 1))),
                                 reads=[hids[hc][1], t_l1c], writes=[pt])
                        S.op("act", (lambda e, ps=ps, g=g: e.activation(out=vcA[0:127, g, 0:64], in_=ps[0:127, 0:64], func=AF.Copy)), reads=[pt], writes=[t_vc[g]])

            chk("cmp", [(kcT[:].rearrange("p g n -> p (g n)"), d_y[:, 0, 0:512]), (vcA[:].rearrange("p g n -> p (g n)"), d_y[:, 1, 0:512])])
            wv_q, wt_q = load_w_std(d_w_qg[0], 0, 1024)
            S.barrier()
            XB = XR[:].bitcast(BF16)
            QT = TT(XB[:, 0:16384].rearrange("p (h t) -> p h t", h=8))
            KST = TT(XB[:, 16384:24576].rearrange("p (g t) -> p g t", g=4))
            KWT = TT(XB[:, 24576:32768].rearrange("p (g t) -> p g t", g=4))
            RB = rstd_s[:].bitcast(BF16)
            SIGG = TT(RB[0:48, 0:T])

            def ev_q(oc, tg, ps, pt):
                head_norm(ps, pt, TGW, gq[:, 0:1], [t_gq], QT.ap[:, oc, tsl(tg)], [QT.t(oc, tg)])
            proj(wv_q, wt_q, A, 8, ev_q)
            for tg in range(NTG):
                ps, pt = next_ps()
                for k in range(KC):
                    S.op("pe", (lambda e, ps=ps, k=k, tg=tg: e.matmul(ps[0:48, :], wqg_g[:, k, :], A.ap[:, k, tsl(tg)], start=(k == 0), stop=(k == KC - 1))),
                         reads=[t_l1c, A.t(k, tg)], writes=[pt])
                S.op("act", (lambda e, ps=ps, tg=tg: e.activation(out=SIGG.ap[:, tsl(tg)], in_=ps[0:48, :], func=AF.Sigmoid)), reads=[pt], writes=[SIGG.t(tg)])

            for typ, dst, gi in ((2, KST, 2), (4, KWT, 3)):
                def viewk(slot):
                    return slot[:, 0:4096].rearrange("p (k g r d) -> p k g r d", k=8, g=4, r=2)
                srck = d_w_kv.rearrange("(k p) n -> p k n", p=128)[:, :, typ * 256:(typ + 1) * 256].rearrange("p k (g d) -> p k g d", g=4)
                pieces = [((lambda slot, r=r, gg=gg: viewk(slot)[:, :, gg, r, :]), srck[:, :, gg, :]) for r in range(2) for gg in range(4)]
                kv_, kt_ = load_w(pieces, viewk)

                def ev_k(oc, tg, ps, pt, dst=dst, gi=gi):
                    head_norm(ps, pt, TGW, hvecs[:, gi:gi + 1], [t_l1c], dst.ap[:, oc, tsl(tg)], [dst.t(oc, tg)])
                proj(kv_, kt_, H, 4, ev_k, oc_cols=(lambda wv_, k, oc: wv_[:, k, oc, :, :].rearrange("p r d -> p (r d)")))
            chk("q", [(QT.ap, d_y)])
            chk("k", [(KST.ap, d_y[:, 0:4, :]), (KWT.ap, d_y[:, 4:8, :]), ])

            def viewv(slot):
                return slot[:, 0:4096].rearrange("p (k s c) -> p k s c", k=8, s=2)
            srcv = d_w_kv.rearrange("(k p) n -> p k n", p=128)
            pieces = [((lambda slot: viewv(slot)[:, :, 0, :]), srcv[:, :, 768:1024]), ((lambda slot: viewv(slot)[:, :, 1, :]), srcv[:, :, 1280:1536])]
            vv_, vt_ = load_w(pieces, viewv)
            S.barrier()
            AB = AR[:]
            VS = TT(AB[:, 0:8192].rearrange("p (t g c) -> p t g c", t=16, g=4))
            VW = TT(AB[:, 8192:16384].rearrange("p (t g c) -> p t g c", t=16, g=4))
            t_vones = Tok()
            S.op("dve", lambda e: e.memset(VS.ap[:, :, :, 64:128], 1.0), writes=[t_vones])
            S.op("dve", lambda e: e.memset(VW.ap[:, :, :, 64:128], 1.0), writes=[t_vones])

            for tt in range(16):
                ps, pt = next_ps()
                tg = tt // 4
                for k in range(KC):
                    S.op("pe", (lambda e, ps=ps, k=k, tt=tt: e.matmul(ps[:], H.ap[:, k, tt * 128:(tt + 1) * 128], vv_[:, k, :, :].rearrange("p s c -> p (s c)"),
                                                                    start=(k == 0), stop=(k == KC - 1))), reads=[vt_, H.t(k, tg)], writes=[pt])
                S.op("act", (lambda e, ps=ps, tt=tt: e.activation(out=VS.ap[:, tt, :, 0:64], in_=ps[:, 0:256].rearrange("p (g d) -> p g d", g=4), func=AF.Copy)),
                     reads=[pt, t_vones], writes=[VS.t(tt)])
                S.op("dve", (lambda e, ps=ps, tt=tt: e.tensor_copy(out=VW.ap[:, tt, :, 0:64], in_=ps[:, 256:512].rearrange("p (g d) -> p g d", g=4))),
                     reads=[pt, t_vones], writes=[VW.t(tt)])
            chk("v", [(AB[:, 0:8192], d_y[:, 0:4, :].rearrange("p a b -> p (a b)")), (AB[:, 8192:16384], d_y[:, 4:8, :].rearrange("p a b -> p (a b)"))])
            wo_v, wo_t = load_w_std(d_w_o[0], 0, 1024)
            S.barrier()

            POOLS["M"] = [7]
            t_ac = Tok()
            tri_b = SCR[:, 0:128]
            anti_b = SCR[:, 128:256]
            cmpm = SCR[:, 256:2304]
            e128 = SCR[:, 2304:4352].rearrange("p (k j) -> p k j", k=16)
            cmap = SCR[:, 4352:4392]
            force = SCR[:, 4392:4904].bitcast(F32).rearrange("p (q j) -> p q j", q=8)
            gsel = SCR[0:48, 4904:7976].rearrange("p (h b m) -> p h b m", h=8, b=3)
            PTS = [SCR[:, 7976 + i * 1024:7976 + (i + 1) * 1024].rearrange("p (k r c) -> p k r c", k=2, r=2) for i in range(3)]
            pts_t = [Tok() for _ in PTS]
            S.op("pool", lambda e: e.dma_start(out=SCR[:, 0:2304], in_=d_amask), writes=[t_ac], dma="cb")
            S.op("pool", lambda e: e.dma_start(out=SCR[:, 2304:4352], in_=d_e128), writes=[t_ac], dma="cb")
            S.op("pool", lambda e: e.dma_start(out=cmap, in_=d_cmap), writes=[t_ac], dma="cb")
            S.op("sp", lambda e: e.dma_start(out=SCR[:, 4392:4904].bitcast(F32), in_=d_force), writes=[t_ac], dma="c")
            S.op("pool", lambda e: e.dma_start(out=SCR[0:48, 4904:7976], in_=d_gsel), writes=[t_ac], dma="cb")
            HB = HR[:]
            OTG = TT(HB[:, 0:4096].rearrange("p (h t) -> p h t", h=8))
            GBC = HB[:, 4096:7168].rearrange("p (h b t) -> p h b t", h=2, b=3)
            t_gbc = Tok()
            XST = TT(HB[:, 7168:15360].bitcast(F32).rearrange("p (k t) -> p k t", k=8))
            BIAS = [HB[:, 15360 + i * 256:15360 + (i + 1) * 256].rearrange("p (r q) -> p r q", r=2) for i in range(2)]
            bias_t = [Tok(), Tok()]
            for i in range(2):
                S.op("dve", (lambda e, i=i: e.memset(BIAS[i], 0.0)), writes=[bias_t[i]])
            pend = {"tk": None}
            ptc = {"n": 0, "b": 0, "a": 0, "c": 0}
            cmb_lo = [Tok(), Tok()]
            cmb_hi = [Tok(), Tok()]
            cmb_on = [Tok(), Tok()]

            M2 = BIAS_EXTRA
            tri2 = M2[:, 0:256]
            anti2 = M2[:, 256:512]
            t_m2 = Tok()
            S.op("dve", lambda e: e.tensor_copy(out=tri2.rearrange("p (r q) -> p r q", r=2), in_=tri_b.unsqueeze(1).to_broadcast([128, 2, 128])), reads=[t_ac], writes=[t_m2])
            S.op("dve", lambda e: e.tensor_copy(out=anti2.rearrange("p (r q) -> p r q", r=2), in_=anti_b.unsqueeze(1).to_broadcast([128, 2, 128])), reads=[t_ac], writes=[t_m2])
            CM2 = [M2[:, 512 + i * 256:512 + (i + 1) * 256] for i in range(2)]
            cm2_t = [Tok(), Tok()]

            def emit_cm2(ci, qt):
                S.op("dve", (lambda e: e.tensor_copy(out=CM2[ci].rearrange("p (r q) -> p r q", r=2),
                                                     in_=cmpm[:, qt * 128:(qt + 1) * 128].unsqueeze(1).to_broadcast([128, 2, 128]))),
                     reads=[t_ac], writes=[cm2_t[ci]])

            def next_pt():
                i = ptc["n"] % 3
                ptc["n"] += 1
                return PTS[i], pts_t[i]

            def qk_tile(bank, bt, colbase, KT, g, kt0, nk, par, qt, mask, bias_i, phase):
                lhs_k = KT.ap[par * 64:(par + 1) * 64, g, kt0:kt0 + nk] if KT is not None else kcT[par * 64:(par + 1) * 64, g, 0:127]
                k_reads = [KT.t(g, kt0 // TGW)] if KT is not None else [t_kc[g]]
                qsl = slice(qt * 128, (qt + 1) * 128)
                q_reads = [QT.t(2 * g, qt // 4), QT.t(2 * g + 1, qt // 4)]
                if phase == 1:
                    S.op("pe", (lambda e: e.matmul(bank[0:nk, colbase:colbase + 256], lhs_k, QT.ap[par * 64:(par + 1) * 64, 2 * g:2 * g + 2, qsl],
                                                   start=(mask is None), stop=True)), reads=k_reads + q_reads, writes=[bt])
                elif mask is None:
                    pass
                elif mask == "bias":
                    kt = kt0 // 128
                    S.op("pe", (lambda e: e.matmul(bank[:, colbase:colbase + 256], e128[:, kt, :], BIAS[bias_i].rearrange("p r q -> p (r q)"), start=True, stop=False)),
                         reads=[t_ac, bias_t[bias_i]], writes=[bt])
                else:
                    mask_ap, mask_reads = mask
                    S.op("pe", (lambda e: e.matmul(bank[0:nk, colbase:colbase + 256], ident_b[:, 0:nk], mask_ap, start=True, stop=False)),
                         reads=[t_l1c] + mask_reads, writes=[bt])

            def branch_steps(g, qt, kind, bias_i, pos, br, acc, acct):
                ps_o, pt_o = next_ps("O")
                out = []
                if kind == "cmp":
                    bA, tA = next_ps("S")
                    bB, tB = next_ps("S")
                    PT, ptt = next_pt()

                    ci = ptc["a"] % 2
                    ptc["a"] += 1

                    def qk():
                        if g == 0 and qt == 0:
                            emit_cm2(ci, qt)
                        for phase in range(2):
                            for par, bank, bt in ((0, bA, tA), (1, bB, tB)):
                                qk_tile(bank, bt, 0, None, g, 0, 127, par, qt, (CM2[ci], [cm2_t[ci]]), None, phase)
                        for par, bank, bt in ((0, bA, tA), (1, bB, tB)):
                            S.op("act", (lambda e, par=par, bank=bank: e.activation(out=PT[0:127, 0, par, :], in_=bank[0:127, 0:256], func=AF.Exp)),
                                 reads=[bt], writes=[ptt])
                        nqt = qt + 1 if qt % 4 != 3 else (qt - 3 if g < 3 else qt + 1)
                        if nqt < 16:
                            emit_cm2(1 - ci, nqt)
                        if qt >= 8:
                            pend["tk"] = (lambda: topk_a(g, qt, PT, ptt))

                    def pv():
                        S.op("pe", (lambda e: e.matmul(ps_o[:], vcA[0:127, g, :], PT[0:127, 0, :, :].rearrange("p r c -> p (r c)"), start=True, stop=True)),
                             reads=[ptt, t_vc[g], t_vc_ones], writes=[pt_o])
                        combine(g, qt, pos, br, ps_o, pt_o, acc, acct)
                    return [(qk, pv)]
                KT, V = (KST, VS) if kind == "sel" else (KWT, VW)
                kts = list(range(0, qt + 1)) if kind == "sel" else list(range(max(0, qt - 4), qt + 1))
                for pi in range(0, len(kts), 2):
                    pair = kts[pi:pi + 2]
                    banks = (next_ps("S"), next_ps("S"))
                    PTp = next_pt()

                    def qk(pair=pair, banks=banks, PTp=PTp):
                        PT, ptt = PTp
                        npair = len(pair)
                        if kind == "sel" and qt >= 8 and pair[0] == 0:
                            topk_b(bias_i)
                        for ktp, kt in enumerate(pair):
                            if kt == qt:
                                mask = (tri2, [t_m2])
                            elif kind == "win" and kt == qt - 4:
                                mask = (anti2, [t_m2])
                            elif kind == "sel" and qt >= 8:
                                mask = "bias"
                            else:
                                mask = None
                            for phase in range(2):
                                for par, (bank, bt) in enumerate(banks):
                                    qk_tile(bank, bt, ktp * 256, KT, g, kt * 128, 128, par, qt, mask, bias_i, phase)
                        for par, (bank, bt) in enumerate(banks):
                            S.op("act", (lambda e, par=par, bank=bank: e.activation(
                                out=PT[:, 0:npair, par, :], in_=bank[:, 0:npair * 256].rearrange("p (k c) -> p k c", k=npair), func=AF.Exp)),
                                reads=[bt, banks[1][1]], writes=[ptt])
                        if pend["tk"] is not None:
                            pend["tk"]()
                            pend["tk"] = None

                    def pv(pair=pair, PTp=PTp, last=(pi + 2 >= len(kts))):
                        PT, ptt = PTp
                        for ktp, kt in enumerate(pair):
                            S.op("pe", (lambda e, ktp=ktp, kt=kt: e.matmul(ps_o[:], V.ap[:, kt, g, :], PT[:, ktp, :, :].rearrange("p r c -> p (r c)"),
                                                                         start=(kt == kts[0]), stop=(kt == kts[-1]))),
                                 reads=[ptt, V.t(kt), t_vones], writes=[pt_o])
                        if last:
                            combine(g, qt, pos, br, ps_o, pt_o, acc, acct)
                    out.append((qk, pv))
                return out

            def combine(g, qt, pos, br, ps_o, pt_o, acc, acct):
                qi = qt % 4
                ci = ptc["c"] % 2
                ptc["c"] += 1
                T, tlo, thi = TMPF[ci], cmb_lo[ci], cmb_hi[ci]
                if pos == 1:
                    S.op("dve", (lambda e: e.reciprocal(out=T[64:128, :], in_=ps_o[64:128, :])), reads=[pt_o], writes=[thi, tlo])
                else:
                    S.op("act", (lambda e: e.activation(out=T[0:64, :], in_=ps_o[64:128, :], func=AF.Ln, bias=eps_t[64:128, 1:2], scale=1.0)), reads=[pt_o, t_eps], writes=[tlo])
                    S.op("act", (lambda e: e.activation(out=T[64:128, :], in_=T[0:64, :], func=AF.Exp, scale=-1.0)), reads=[tlo], writes=[thi])
                on, ont = TMPF[2][:, ci * 256:(ci + 1) * 256], cmb_on[ci]
                for par in range(2):
                    S.op("dve", (lambda e, par=par: e.tensor_tensor(out=on[par * 64:(par + 1) * 64, :], in0=ps_o[0:64, par * 256:(par + 1) * 256],
                                                                     in1=T[64:128, par * 256:(par + 1) * 256], op=ALU.mult)), reads=[pt_o, thi], writes=[ont])
                onv = on.rearrange("p (h q) -> p h q", h=2)
                accv = acc[:].rearrange("p (h q) -> p h q", h=2)
                Gv = GBC[:, :, br, qi * 128:(qi + 1) * 128]
                ce = "pool"
                if pos == 0:
                    S.op(ce, (lambda e: e.tensor_tensor(out=accv, in0=onv, in1=Gv, op=ALU.mult)), reads=[ont, t_gbc], writes=[acct])
                else:
                    S.op(ce, (lambda e: e.tensor_tensor(out=onv, in0=onv, in1=Gv, op=ALU.mult)), reads=[ont, t_gbc], writes=[ont])
                    if pos == 1:
                        S.op(ce, (lambda e: e.tensor_tensor(out=accv, in0=accv, in1=onv, op=ALU.add)), reads=[ont, acct], writes=[acct])
                    else:
                        S.op(ce, (lambda e: e.tensor_tensor(out=OTG.ap[:, 2 * g:2 * g + 2, qi * 128:(qi + 1) * 128], in0=accv, in1=onv, op=ALU.add)),
                             reads=[ont, acct], writes=[OTG.t(2 * g, 0), OTG.t(2 * g + 1, 0)])

            def topk_a(g, qt, PT, ptt):
                ps_i, pt_i = next_ps("M")
                for c in range(4):
                    S.op("pe", (lambda e, c=c: e.matmul(ps_i[:, c * 33:(c + 1) * 33], PT[0:127, 0, c // 2, (c % 2) * 128:(c % 2 + 1) * 128], cmap[0:127, 0:33],
                                                          start=True, stop=True)), reads=[ptt, t_ac], writes=[pt_i])
                psv = ps_i[:, 0:132].rearrange("p (c j) -> p c j", c=4)
                S.op("dve", (lambda e: e.reciprocal(out=rd4[:], in_=psv[:, :, 32])), reads=[pt_i], writes=[t_tk])
                S.op("dve", (lambda e: e.tensor_scalar(out=scA[:], in0=psv[:, 0, 0:32], scalar1=rd4[:, 0:1], scalar2=None, op0=ALU.mult)), reads=[pt_i, t_tk], writes=[t_tk])
                for c in range(1, 4):
                    S.op("dve", (lambda e, c=c: e.scalar_tensor_tensor(out=scA[:], in0=psv[:, c, 0:32], scalar=rd4[:, c:c + 1], in1=scA[:], op0=ALU.mult, op1=ALU.add)),
                         reads=[pt_i, t_tk], writes=[t_tk])
                S.op("dve", (lambda e: e.tensor_tensor(out=scA[:], in0=scA[:], in1=force[:, qt - 8, :], op=ALU.add)), reads=[t_tk, t_ac], writes=[t_tk])
                S.op("dve", (lambda e: e.max(out=m8a[:], in_=scA[:])), reads=[t_tk], writes=[t_tk])
                S.op("dve", (lambda e: e.match_replace(out=scB[:], in_to_replace=m8a[:], in_values=scA[:], imm_value=-1e30)), reads=[t_tk], writes=[t_tk])
                S.op("dve", (lambda e: e.max(out=m8b[:], in_=scB[:])), reads=[t_tk], writes=[t_tk])
                S.op("dve", (lambda e: e.tensor_scalar(out=selm[:], in0=scA[:], scalar1=m8b[:, 7:8], scalar2=1.0, op0=ALU.is_ge, op1=ALU.subtract)), reads=[t_tk], writes=[t_tk])

            def topk_b(bias_i):
                ps_m, pt_m = next_ps("M")
                S.op("pe", (lambda e: e.matmul(ps_m[0:32, 0:128], selm[:], ident_b[:], start=True, stop=True)), reads=[t_tk, t_l1c], writes=[pt_m])
                S.op("act", (lambda e: e.activation(out=BIAS[bias_i][0:32, :, :], in_=ps_m[0:32, 0:128].unsqueeze(1).to_broadcast([32, 2, 128]), func=AF.Copy, scale=30000.0)),
                     reads=[pt_m], writes=[bias_t[bias_i]])

            on_t = [Tok(), Tok()]
            acc_t = [Tok(), Tok()]
            t_tk = Tok()
            g1 = mod_part(1, 2)
            steps = []

            def emit_xreload(tg):
                for kh in range(2):
                    ks = slice(kh * 4, kh * 4 + 4)
                    S.op("sp", (lambda e, ks=ks: e.dma_start(out=XST.ap[:, ks, :], in_=d_xs[:, ks, tsl(tg)])),
                         reads=[xs_t[tg]], writes=[XST.t(k) for k in range(kh * 4, kh * 4 + 4)], dma="xr")

            def emit_gbc(tg, g):
                for hpl in range(2):
                    for br in range(3):
                        ps, pt = next_ps("OM")
                        S.op("pe", (lambda e, ps=ps, hpl=hpl, br=br: e.matmul(ps[:], gsel[:, 2 * g + hpl, br, :], SIGG.ap[:, tsl(tg)], start=True, stop=True)),
                             reads=[t_ac, SIGG.t(tg)], writes=[pt])
                        S.op("dve", (lambda e, ps=ps, hpl=hpl, br=br: e.tensor_copy(out=GBC[:, hpl, br, :], in_=ps[:])), reads=[pt], writes=[t_gbc])

            def emit_wo(tg):
                def ev_o(oc, tg_, ps, pt):
                    S.op("dve", (lambda e: e.scalar_tensor_tensor(out=XST.ap[:, oc, :], in0=ps[:], scalar=g1[:, oc:oc + 1], in1=XST.ap[:, oc, :], op0=ALU.mult, op1=ALU.add)),
                         reads=[pt, XST.t(oc), t_modv[1]], writes=[XST.t(oc)])
                defpool["v"] = "OM"
                proj(wo_v, wo_t, OTG, 8, ev_o, tgs=[0])
                defpool["v"] = "P6"
                for kh in range(2):
                    ks = slice(kh * 4, kh * 4 + 4)
                    S.op("sp", (lambda e, ks=ks: e.dma_start(out=d_xs[:, ks, tsl(tg)], in_=XST.ap[:, ks, :])),
                         reads=[XST.t(k) for k in range(kh * 4, kh * 4 + 4)], writes=[xs_t[tg]], dma="xw")

            for tg in range(NTG):
                steps.append((None, (lambda tg=tg: emit_xreload(tg))))
                for g in range(G4):
                    steps.append((None, (lambda tg=tg, g=g: emit_gbc(tg, g))))
                    for qi in range(4):
                        qt = tg * 4 + qi
                        ai = ptc["b"] % 2
                        ptc["b"] += 1
                        acc, acct = ACC[ai], acc_t[ai]
                        steps += branch_steps(g, qt, "cmp", ai, 0, 0, acc, acct)
                        steps += branch_steps(g, qt, "win", ai, 1, 2, acc, acct)
                        steps += branch_steps(g, qt, "sel", ai, 2, 1, acc, acct)
                steps.append((None, (lambda tg=tg: emit_wo(tg))))
            prev_pv = None
            for qk_fn, pv_fn in steps:
                if qk_fn is not None:
                    qk_fn()
                if prev_pv is not None:
                    prev_pv()
                prev_pv = pv_fn
            if prev_pv is not None:
                prev_pv()
            S.barrier()
            for tg in range(NTG):
                for kh in range(2):
                    ks = slice(kh * 4, kh * 4 + 4)
                    S.op("sp", (lambda e, tg=tg, ks=ks: e.dma_start(out=X.ap[:, ks, tsl(tg)], in_=d_xs[:, ks, tsl(tg)])),
                         reads=[xs_t[tg]], writes=[X.t(k, tg) for k in range(kh * 4, kh * 4 + 4)], dma="x2")
            defpool["v"] = "ALL"
            if stop != "mix1":
                mlp(1, 1)
          except StopBuild:
            S.emit(final_dma_keys=["out"])
            return nc

        for tg in range(NTG):
            for kh in range(2):
                ks = slice(kh * 4, kh * 4 + 4)
                S.op("sp", (lambda e, tg=tg, ks=ks: e.dma_start(out=d_y[:, ks, tsl(tg)], in_=X.ap[:, ks, tsl(tg)])),
                     reads=[X.t(k, tg) for k in range(kh * 4, kh * 4 + 4)], dma="out")
        S.emit(final_dma_keys=["out"])
    return nc


def _fm(v):
    return np.ascontiguousarray(v.reshape(KC, 128).T)


def _const_tables():
    f32 = np.float32
    NEGM = -30000.0
    j = np.arange(128)[:, None]
    t = np.arange(128)[None, :]
    tri = np.where(j <= t, 0.0, NEGM)
    anti = np.where(j > t, 0.0, NEGM)
    n = np.arange(128)[:, None]
    tt = np.arange(T)[None, :]
    cmpm = np.where((16 * n + 31 <= tt) & (n < 127), 0.0, NEGM)
    amask = np.concatenate([tri, anti, cmpm], axis=1).astype(f32)
    e128 = np.zeros((128, 16, 128), f32)
    for kt in range(16):
        for jj in range(128):
            e128[2 * kt + jj // 64, kt, jj] = 1.0
    cmap = np.zeros((128, 40), f32)
    c0 = np.arange(127)[:, None] * 16
    s0 = np.arange(32)[None, :] * 64
    ov = np.minimum(c0 + 32, s0 + 64) - np.maximum(c0, s0)
    cmap[:127, :32] = np.clip(ov, 0, None) / 32.0
    cmap[:127, 32] = 1.0
    force = np.zeros((128, 8, 32), f32)
    for q in range(8):
        tq = 128 * (q + 8) + np.arange(128)
        cur = tq // 64
        jb = np.arange(32)[None, :]
        forced = (jb == 0) | (jb == cur[:, None]) | (jb == cur[:, None] - 1)
        force[:, q, :] = np.where(forced, 1e4, np.where(jb > cur[:, None], -1e4, 0.0))
    gsel = np.zeros((48, 8, 3, 128), f32)
    for hp in range(8):
        for br in range(3):
            for m in range(128):
                gsel[(2 * hp + m // 64) * 3 + br, hp, br, m] = 1.0
    p = np.arange(128)
    bd = (p[:, None] // 64 == p[None, :] // 64).astype(f32)
    return {"amask": amask, "e128": e128.reshape(128, 2048), "cmap": cmap, "force": force.reshape(128, 256),
            "gsel": gsel.reshape(48, 3072), "bdones": bd}


def prep_inputs(inputs):
    f32 = np.float32
    g = {k: np.asarray(v, dtype=f32) for k, v in inputs.items()}
    vecs = np.zeros((128, NV, KC), f32)
    for i in range(2):
        for j in range(2):
            vecs[:, V_NG + 2 * i + j, :] = _fm(g["norm_gain"][i, j])
        for part in range(6):
            vecs[:, V_BADA + 6 * i + part, :] = _fm(g["b_ada"][i, part * D:(part + 1) * D])
    vecs[:, V_KVG, :] = _fm(g["kv_norm_gain"])
    for part in range(2):
        vecs[:, V_BKV + part, :] = _fm(g["b_ada_kv"][part * D:(part + 1) * D])
    for j in range(3):
        vecs[:, V_CONV + j, :] = _fm(g["conv_w"][0, j])
    consts = np.concatenate([np.eye(128, dtype=f32), np.ones((128, 128), f32)], axis=1)
    p64 = np.arange(128) % 64
    hvecs = np.zeros((128, 4), f32)
    hvecs[:, 0] = g["q_gain"][0, p64]
    for i in range(3):
        hvecs[:, 1 + i] = g["k_gain"][i, p64]
    peT = np.zeros((128, 64), f32)
    for kv in range(2):
        peT[:, kv * 32:(kv + 1) * 32] = g["cmp_pe"][kv][:, p64].T
    shared = {
        "vecs": vecs, "consts": consts, "hvecs": hvecs, "peT": peT,
        "w_ada": g["w_ada"], "w_a_in": g["w_a_in"], "w_a_out": g["w_a_out"],
        "w_mlp1": g["w_mlp1"], "w_mlp2": g["w_mlp2"],
        "w_ada_kv": g["w_ada_kv"], "w_kv": g["w_kv"], "w_qg": g["w_qg"], "w_o": g["w_o"],
        "cmp_w1": g["cmp_w1"], "cmp_w2": g["cmp_w2"],
    }
    shared.update(_const_tables())
    in_maps = []
    for b in range(N_CORES):
        m = dict(shared)
        xT = g["x"][b].T.reshape(KC, 128, T).transpose(1, 0, 2)
        m["xT"] = np.ascontiguousarray(xT)
        m["cT"] = _fm(g["c"][b])
        in_maps.append(m)
    return in_maps


def post_outputs(results):
    outs = []
    for r in results:
        yT = np.asarray(r["yT"])
        outs.append(yT.transpose(2, 1, 0).reshape(T, D))
    return np.stack(outs, axis=0).astype(np.float32)


def kernel(**inputs):
    in_maps = prep_inputs(inputs)
    nc = build_program(DEBUG_STOP)
    res = run_bass_kernel_spmd(nc, in_maps, core_ids=list(range(N_CORES)))
    return post_outputs(res.results)
```

```python
import numpy as np
from contextlib import ExitStack
import concourse.bass as bass
import concourse.mybir as mybir
from concourse.bass_utils import run_bass_kernel_spmd

F32 = mybir.dt.float32
BF16 = mybir.dt.bfloat16
AF = mybir.ActivationFunctionType
ALU = mybir.AluOpType

D = 1024
T = 2048
KC = 8
NTG = 4
TGW = 512
EPS = 1e-6
N_CORES = 8

V_NG = 0
V_KVG = 4
V_BADA = 5
V_BKV = 17
V_CONV = 19
NV = 22

DEBUG_STOP = None
FILLER = False


class Tok:
    __slots__ = ("ws", "rs", "rdma", "excl")

    def __init__(self, excl=False):
        self.ws = []
        self.rs = {}
        self.rdma = []
        self.excl = excl


class Op:
    __slots__ = ("eng", "fn", "deps", "dma", "signal", "sigval", "dmaval", "dsem")

    def __init__(self, eng, fn, dma):
        self.eng = eng
        self.fn = fn
        self.dma = dma
        self.deps = ()
        self.signal = False
        self.sigval = 0
        self.dmaval = 0
        self.dsem = None


ENGS = ["pe", "act", "dve", "pool", "sp"]
DMA_SEMS = 16


class Sched:
    def __init__(self, nc):
        self.nc = nc
        self.ops = {e: [] for e in ENGS}
        self.dma_hist = {e: [] for e in ENGS}

    def op(self, eng, fn, reads=(), writes=(), dma=None):
        o = Op(eng, fn, dma)
        ex = [t for t in reads if t.excl]
        if ex:
            reads = [t for t in reads if not t.excl]
            writes = list(writes) + ex
        deps = set()
        for t in reads:
            deps.update(t.ws)
        for t in writes:
            deps.update(t.ws)
            deps.update(t.rs.values())
            deps.update(t.rdma)
        if dma is not None:
            deps = {d for d in deps if d.dma != dma}
            hist = self.dma_hist[eng]
            n = len(hist)
            o.dsem = (eng, n % DMA_SEMS)
            o.dmaval = 16 * (n // DMA_SEMS + 1)
            if n >= DMA_SEMS:
                deps.add(hist[n - DMA_SEMS])
            hist.append(o)
        o.deps = tuple(deps)
        for t in reads:
            if dma is not None:
                t.rdma.append(o)
            else:
                t.rs[eng] = o
        for t in writes:
            if dma is not None and t.ws and not t.rs and not t.rdma and all(w.dma == dma for w in t.ws):
                t.ws.append(o)
            else:
                t.ws = [o]
            t.rs = {}
            t.rdma = []
        self.ops[eng].append(o)
        return o

    def barrier(self, engines=("pe", "act", "dve", "sp", "pool")):
        lasts = []
        for e in ENGS:
            for o in reversed(self.ops[e]):
                if o.dma is None and o.fn is not None:
                    lasts.append(o)
                    break
            lasts += self.dma_hist[e][-DMA_SEMS:]
        for e in engines:
            o = Op(e, None, None)
            o.deps = tuple(lasts)
            self.ops[e].append(o)

    def emit(self, final_dma_keys=()):
        nc = self.nc
        for e in ENGS:
            for o in self.ops[e]:
                for d in o.deps:
                    if d.dma is None:
                        if d.eng == "pe" and o.eng == "pe" and o.dma is None and o.fn is not None:
                            continue
                        d.signal = True
        for e in ENGS:
            c = 0
            for o in self.ops[e]:
                if o.dma is None and o.signal:
                    c += 1
                    o.sigval = c
        with ExitStack() as es:
            esem = {e: es.enter_context(nc.semaphore("s_" + e)) for e in ENGS}
            dsem = {}
            for e in ENGS:
                for i in range(min(DMA_SEMS, len(self.dma_hist[e]))):
                    dsem[(e, i)] = es.enter_context(nc.semaphore("d_%s%d" % (e, i)))
            block = es.enter_context(nc.Block())

            def run(e, eng):
                waited = {}
                for o in self.ops[e]:
                    need = {}
                    for d in o.deps:
                        if d.dma is not None:
                            key, sem, val = ("d",) + d.dsem, dsem[d.dsem], d.dmaval
                        else:
                            if d.eng == "pe" and e == "pe" and o.dma is None and o.fn is not None:
                                continue
                            key, sem, val = ("e", d.eng), esem[d.eng], d.sigval
                        if val > need.get(key, (None, 0))[1]:
                            need[key] = (sem, val)
                    for key, (sem, val) in need.items():
                        if waited.get(key, 0) >= val:
                            continue
                        waited[key] = val
                        eng.wait_ge(sem, val)
                    if o.fn is None:
                        continue
                    ins = o.fn(eng)
                    if o.dma is not None:
                        ins.then_inc(dsem[o.dsem], 16)
                    elif o.signal:
                        ins.then_inc(esem[e], 1)
                if e == "sp":
                    fin = {}
                    for q in ENGS:
                        for d in self.dma_hist[q]:
                            if d.dma in final_dma_keys:
                                fin[d.dsem] = max(fin.get(d.dsem, 0), d.dmaval)
                    for k, v in fin.items():
                        if waited.get(("d",) + k, 0) < v:
                            eng.wait_ge(dsem[k], v)

            block.sync(lambda eng: run("sp", eng))
            block.scalar(lambda eng: run("act", eng))
            block.vector(lambda eng: run("dve", eng))
            block.gpsimd(lambda eng: run("pool", eng))
            block.tensor(lambda eng: run("pe", eng))


class StopBuild(Exception):
    pass


class TT:
    def __init__(self, ap):
        self.ap = ap
        self.toks = {}

    def t(self, *key):
        tk = self.toks.get(key)
        if tk is None:
            tk = self.toks[key] = Tok()
        return tk

    def all(self):
        return list(self.toks.values())


def build_program(stop=None):
    nc = bass.Bass("TRN2", target_bir_lowering=False)

    def din(name, shape):
        return nc.dram_tensor(name, list(shape), F32, kind="ExternalInput").ap()

    d_x = din("xT", [128, KC, T])
    d_c = din("cT", [128, KC])
    d_vecs = din("vecs", [128, NV, KC])
    d_consts = din("consts", [128, 256])
    d_w_ada = din("w_ada", [2, D, 6 * D])
    d_w_a_in = din("w_a_in", [1, D, 3 * D])
    d_w_a_out = din("w_a_out", [1, D, D])
    d_w_mlp1 = din("w_mlp1", [2, D, 4 * D])
    d_w_mlp2 = din("w_mlp2", [2, 4 * D, D])
    d_w_ada_kv = din("w_ada_kv", [D, 2 * D])
    d_w_kv = din("w_kv", [D, 1536])
    d_w_qg = din("w_qg", [1, D, 1072])
    d_w_o = din("w_o", [1, D, D])
    d_cmp_w1 = din("cmp_w1", [2, 2048, 256])
    d_cmp_w2 = din("cmp_w2", [2, 256, 64])
    d_hvecs = din("hvecs", [128, 4])
    d_peT = din("peT", [128, 64])
    d_amask = din("amask", [128, 256 + 2048])
    d_e128 = din("e128", [128, 2048])
    d_cmap = din("cmap", [128, 40])
    d_force = din("force", [128, 256])
    d_gsel = din("gsel", [48, 3072])
    d_bd = din("bdones", [128, 128])
    d_y = nc.dram_tensor("yT", [128, KC, T], F32, kind="ExternalOutput").ap()
    d_xs = nc.dram_tensor("xs_scratch", [128, KC, T], F32).ap()

    S = Sched(nc)
    es = ExitStack()
    with es:
        def sb(name, shape, dt):
            return es.enter_context(nc.sbuf_tensor(name, list(shape), dt))

        XR = sb("XR", [128, KC * T], F32)
        HR = sb("HR", [128, KC * T], BF16)
        AR = sb("AR", [128, KC * T], BF16)
        RING = [sb("RING%d" % i, [128, 8192], BF16) for i in range(2)]
        ring_t = [Tok(), Tok()]
        rstd_s = sb("rstd", [128, T], F32)
        TMPF = [sb("tmpf%d" % i, [128, TGW], F32) for i in range(3)]
        tmpf_t = [Tok() for _ in TMPF]
        TMPB = [sb("tmpb%d" % i, [128, TGW], BF16) for i in range(3)]
        tmpb_t = [Tok() for _ in TMPB]
        SCR = sb("SCR", [128, 11048], BF16)
        ones_b = sb("ones_b", [128, 128], BF16)
        vecs = sb("vecs_s", [128, NV, KC], F32)
        c_f = sb("c_f", [128, KC], F32)
        cact = sb("cact", [128, KC], BF16)
        modv = sb("modv", [128, 3, 48], F32)
        der = sb("der", [128, 8, KC], F32)
        hvecs = sb("hvecs_s", [128, 4], F32)
        gq = sb("gq", [128, 1], F32)
        ident_b = sb("ident_b", [128, 128], BF16)
        bd_ones = sb("bd_ones", [128, 128], BF16)
        cbias = sb("cbias", [128, 4], F32)
        kcT = sb("kcT", [128, 4, 128], BF16)
        vcA = sb("vcA", [128, 4, 128], BF16)
        rd4 = sb("rd4", [128, 4], F32)
        scA = sb("scA", [128, 32], F32)
        scB = sb("scB", [128, 32], F32)
        m8a = sb("m8a", [128, 8], F32)
        m8b = sb("m8b", [128, 8], F32)
        selm = sb("selm", [128, 32], BF16)
        ACC = [sb("acc%d" % i, [128, 256], F32) for i in range(2)]
        BIAS_EXTRA = sb("m2", [128, 1024], BF16)
        zeros_b = sb("zeros_b", [128, 128], BF16)
        t_zero = Tok()
        S.op("dve", lambda e: e.memset(zeros_b[:], 0.0), writes=[t_zero])
        PS = [es.enter_context(nc.psum_tensor("ps%d" % i, [128, TGW], F32)) for i in range(8)]
        ps_t = [Tok(excl=True) for _ in PS]

        X = TT(XR[:].rearrange("p (k t) -> p k t", k=KC))
        H = TT(HR[:].rearrange("p (k t) -> p k t", k=KC))
        A = TT(AR[:].rearrange("p (k t) -> p k t", k=KC))
        t_const = Tok()
        eps_t = sb("eps_t", [128, 2], F32)
        t_eps = Tok()
        S.op("dve", lambda e: e.memset(eps_t[:, 0:1], EPS), writes=[t_eps])
        S.op("dve", lambda e: e.memset(eps_t[:, 1:2], 1e-30), writes=[t_eps])
        t_vecs = Tok()
        t_c = Tok()
        t_cact = Tok()
        t_modv = [Tok(), Tok(), Tok()]
        t_der = Tok()
        t_rstd = [Tok() for _ in range(NTG)]

        cnt = {"ps": 0, "ring": 0, "tf": 0, "tb": 0}

        POOLS = {"ALL": list(range(8)), "P6": [0, 1, 2, 3, 4, 5], "S": [0, 1, 2, 3], "O": [4, 5, 6], "M": [7], "OM": [4, 5, 6, 7]}
        pcnt = {k: 0 for k in POOLS}

        defpool = {"v": "ALL"}

        def next_ps(pool=None):
            pool = pool or defpool["v"]
            lst = POOLS[pool]
            i = lst[pcnt[pool] % len(lst)]
            pcnt[pool] += 1
            return PS[i], ps_t[i]

        def next_tf():
            i = cnt["tf"] % len(TMPF)
            cnt["tf"] += 1
            return TMPF[i], tmpf_t[i]

        def next_tb():
            i = cnt["tb"] % len(TMPB)
            cnt["tb"] += 1
            return TMPB[i], tmpb_t[i]

        def tsl(tg):
            return slice(tg * TGW, (tg + 1) * TGW)

        S.op("pool", lambda e: e.dma_start(out=ones_b[:], in_=d_consts[:, 128:256]), writes=[t_const], dma="cb")
        S.op("sp", lambda e: e.dma_start(out=vecs[:], in_=d_vecs), writes=[t_vecs], dma="c")
        S.op("sp", lambda e: e.dma_start(out=c_f[:], in_=d_c), writes=[t_c], dma="c")
        for tg in range(NTG):
            for kh in range(2):
                ks = slice(kh * 4, kh * 4 + 4)
                S.op("sp", (lambda e, tg=tg, ks=ks: e.dma_start(out=X.ap[:, ks, tsl(tg)], in_=d_x[:, ks, tsl(tg)])),
                     writes=[X.t(k, tg) for k in range(kh * 4, kh * 4 + 4)], dma="x")

        S.op("act", lambda e: e.activation(out=cact[:], in_=c_f[:], func=AF.Silu), reads=[t_c], writes=[t_cact])

        def load_w(src_aps, dst_view_fn):
            i = cnt["ring"] % 2
            cnt["ring"] += 1
            slot = RING[i]
            for dst_fn, src in src_aps:
                S.op("pool", (lambda e, dst_fn=dst_fn, src=src, slot=slot: e.dma_start(out=dst_fn(slot), in_=src)),
                     writes=[ring_t[i]], dma="r%d" % i)
            return dst_view_fn(slot), ring_t[i]

        def load_w_std(w2d, c0, ncols, k0=0):
            src = w2d.rearrange("(k p) n -> p k n", p=128)

            def view(slot):
                return slot[:, 0:8 * ncols].rearrange("p (k c) -> p k c", k=8)
            pieces = []
            for kh in range(2):
                ks = slice(kh * 4, kh * 4 + 4)
                pieces.append(((lambda slot, ks=ks: view(slot)[:, ks, :]), src[:, k0 + kh * 4:k0 + kh * 4 + 4, c0:c0 + ncols]))
            return load_w(pieces, view)

        def ada_matvec(w2d, ncb, bias_idx, mi):
            ps, pt = next_ps()
            for cb in range(ncb):
                wv, wt = load_w_std(w2d, cb * 1024, 1024)
                for oc in range(8):
                    col = cb * 8 + oc
                    for k in range(KC):
                        S.op("pe", (lambda e, ps=ps, wv=wv, oc=oc, k=k, col=col: e.matmul(
                            ps[:, col:col + 1], wv[:, k, oc * 128:(oc + 1) * 128], cact[:, k:k + 1],
                            start=(k == 0), stop=(k == KC - 1))), reads=[wt, t_cact], writes=[pt])
            n = ncb * 8
            S.op("dve", (lambda e, ps=ps, n=n: e.tensor_tensor(
                out=modv[:, mi, 0:n], in0=ps[:, 0:n],
                in1=vecs[:, bias_idx:bias_idx + ncb, :].rearrange("p a b -> p (a b)"), op=ALU.add)),
                reads=[pt, t_vecs], writes=[t_modv[mi]])

        def mod_part(mi, part):
            return modv[:, mi, part * 8:(part + 1) * 8]

        def derive(mi, part_sc, gain_idx, dst):
            S.op("dve", lambda e: e.tensor_scalar(out=der[:, dst, :], in0=mod_part(mi, part_sc), scalar1=1.0, scalar2=1.0,
                                                  op0=ALU.add, op1=ALU.mult), reads=[t_modv[mi]], writes=[t_der])
            S.op("dve", lambda e: e.tensor_tensor(out=der[:, dst, :], in0=der[:, dst, :], in1=vecs[:, gain_idx, :], op=ALU.mult),
                 reads=[t_der, t_vecs], writes=[t_der])

        def compute_rstd():
            for tg in range(NTG):
                ps, pt = next_ps()
                for k in range(KC):
                    tb, tbt = next_tb()
                    S.op("act", (lambda e, tb=tb, k=k, tg=tg: e.activation(out=tb[:], in_=X.ap[:, k, tsl(tg)], func=AF.Square)),
                         reads=[X.t(k, tg)], writes=[tbt])
                    S.op("pe", (lambda e, ps=ps, tb=tb, k=k: e.matmul(ps[:], ones_b[:], tb[:], start=(k == 0), stop=(k == KC - 1))),
                         reads=[tbt, t_const], writes=[pt])
                tf, tft = next_tf()
                S.op("act", (lambda e, ps=ps, tf=tf: e.activation(out=tf[:], in_=ps[:], func=AF.Ln, bias=eps_t[:, 0:1], scale=1.0 / D)),
                     reads=[pt, t_eps], writes=[tft])
                S.op("act", (lambda e, tf=tf, tg=tg: e.activation(out=rstd_s[:, tsl(tg)], in_=tf[:], func=AF.Exp, scale=-0.5)), reads=[tft], writes=[t_rstd[tg]])

        def norm_mod(dst, a_ap, b_ap, extra_reads):
            for tg in range(NTG):
                for k in range(KC):
                    tf, tft = next_tf()
                    S.op("dve", (lambda e, tf=tf, k=k, tg=tg: e.tensor_tensor(out=tf[:], in0=X.ap[:, k, tsl(tg)], in1=rstd_s[:, tsl(tg)], op=ALU.mult)),
                         reads=[X.t(k, tg), t_rstd[tg]], writes=[tft])
                    S.op("act", (lambda e, tf=tf, k=k, tg=tg: e.activation(out=dst.ap[:, k, tsl(tg)], in_=tf[:], func=AF.Identity,
                                                                            bias=b_ap[:, k:k + 1], scale=a_ap[:, k:k + 1])),
                         reads=[tft] + extra_reads, writes=[dst.t(k, tg)])

        def proj(wv, wt, src, n_oc, evac, tgs=range(NTG), oc_cols=None):
            for tg in tgs:
                for oc in range(n_oc):
                    ps, pt = next_ps()
                    for k in range(KC):
                        lhs = wv[:, k, oc * 128:(oc + 1) * 128] if oc_cols is None else oc_cols(wv, k, oc)
                        S.op("pe", (lambda e, ps=ps, lhs=lhs, k=k, tg=tg: e.matmul(ps[:], lhs, src.ap[:, k, tsl(tg)],
                                                                                   start=(k == 0), stop=(k == KC - 1))),
                             reads=[wt, src.t(k, tg)], writes=[pt])
                    evac(oc, tg, ps, pt)

        def resid_evac(g_ap, extra_reads):
            def ev(oc, tg, ps, pt):
                S.op("dve", (lambda e: e.scalar_tensor_tensor(out=X.ap[:, oc, tsl(tg)], in0=ps[:], scalar=g_ap[:, oc:oc + 1],
                                                              in1=X.ap[:, oc, tsl(tg)], op0=ALU.mult, op1=ALU.add)),
                     reads=[pt, X.t(oc, tg)] + extra_reads, writes=[X.t(oc, tg)])
            return ev

        mv_tasks = []
        mv_t = [Tok(), Tok()]
        mv_state = {"n": 0}

        def make_mv_tasks(w2d, n512, bias_idx, mi):
            src = w2d.rearrange("(k p) n -> p k n", p=128)
            nparts = (n512 * 4) // 8
            bias_flat = vecs[:, bias_idx:bias_idx + nparts, :].rearrange("p a b -> p (a b)")
            for cb in range(n512):
                def task(cb=cb):
                    n = mv_state["n"]
                    mv_state["n"] += 1
                    i = n % 2
                    slot = SCR[:, i * 4096:(i + 1) * 4096].rearrange("p (k c) -> p k c", k=8)
                    extra = ([t for row in t_gb for t in row] + [t for row in t_v for t in row] + [t_vhalo]) if n < 2 else []
                    S.op("pool", (lambda e: e.dma_start(out=slot, in_=src[:, :, cb * 512:(cb + 1) * 512])), writes=[mv_t[i]] + extra, dma="mv%d" % i)
                    ps, pt = next_ps()
                    for oc in range(4):
                        for k in range(KC):
                            S.op("pe", (lambda e, oc=oc, k=k: e.matmul(ps[:, oc:oc + 1], slot[:, k, oc * 128:(oc + 1) * 128], cact[:, k:k + 1],
                                                                      start=(k == 0), stop=(k == KC - 1))), reads=[mv_t[i], t_cact], writes=[pt])
                    c0 = cb * 4
                    S.op("dve", (lambda e: e.tensor_tensor(out=modv[:, mi, c0:c0 + 4], in0=ps[:, 0:4], in1=bias_flat[:, c0:c0 + 4], op=ALU.add)),
                         reads=[pt, t_vecs], writes=[t_modv[mi]])
                mv_tasks.append(task)

        def run_mv_tasks(n):
            for _ in range(n):
                if mv_tasks:
                    mv_tasks.pop(0)()

        def mlp(layer, mi):
            compute_rstd()
            derive(mi, 4, V_NG + 2 * layer + 1, 1)
            norm_mod(H, der[:, 1, :], mod_part(mi, 3), [t_der, t_modv[mi]])
            g2 = mod_part(mi, 5)
            for hb in range(4):
                wv, wt = load_w_std(d_w_mlp1[layer], hb * 1024, 1024)

                def ev1(oc, tg, ps, pt):
                    tf, tft = next_tf()
                    S.op("act", (lambda e: e.activation(out=tf[:], in_=ps[:], func=AF.Relu)), reads=[pt], writes=[tft])
                    S.op("dve", (lambda e: e.tensor_tensor(out=A.ap[:, oc, tsl(tg)], in0=tf[:], in1=tf[:], op=ALU.mult)),
                         reads=[tft], writes=[A.t(oc, tg)])
                proj(wv, wt, H, 8, ev1)
                wv2, wt2 = load_w_std(d_w_mlp2[layer], 0, 1024, k0=hb * 8)
                run_mv_tasks(2)
                proj(wv2, wt2, A, 8, resid_evac(g2, [t_modv[mi]]))
                run_mv_tasks(2)

        compute_rstd()
        ada_matvec(d_w_ada[0], 6, V_BADA, 0)
        derive(0, 1, V_NG + 0, 0)
        norm_mod(H, der[:, 0, :], mod_part(0, 0), [t_der, t_modv[0]])

        gbv = SCR[:, 0:4096].rearrange("p (j t) -> p j t", j=2)
        vv = SCR[:, 4096:4096 + 2 * 2056].rearrange("p (j t) -> p j t", j=2)
        t_gb = [[Tok() for _ in range(NTG)] for _ in range(2)]
        t_v = [[Tok() for _ in range(NTG)] for _ in range(2)]
        t_vhalo = Tok()
        S.op("dve", lambda e: e.memset(vv[:, :, 0:2], 0.0), writes=[t_vhalo])
        w_in_v = d_w_a_in[0].rearrange("(k p) (s c) -> p k s c", p=128, s=3)
        g1 = mod_part(0, 2)
        def mixer_j(wv, wt, jj, j):
            jb = j % 2
            for tg in range(NTG):
                pss = []
                for s in range(3):
                    ps, pt = next_ps()
                    for k in range(KC):
                        S.op("pe", (lambda e, ps=ps, s=s, k=k, tg=tg: e.matmul(ps[:], wv[:, k, s, jj * 128:(jj + 1) * 128], H.ap[:, k, tsl(tg)],
                                                                               start=(k == 0), stop=(k == KC - 1))),
                             reads=[wt, H.t(k, tg)], writes=[pt])
                    pss.append((ps, pt))
                (psb, ptb), (psc, ptc), (psu, ptu) = pss
                S.op("act", (lambda e, psb=psb, tg=tg: e.activation(out=gbv[:, jb, tsl(tg)], in_=psb[:], func=AF.Copy)),
                     reads=[ptb], writes=[t_gb[jb][tg]])
                tb, tbt = next_tb()
                S.op("act", (lambda e, psc=psc, tb=tb: e.activation(out=tb[:], in_=psc[:], func=AF.Copy)), reads=[ptc], writes=[tbt])
                S.op("dve", (lambda e, psu=psu, tb=tb, tg=tg: e.tensor_tensor(out=vv[:, jb, 2 + tg * TGW:2 + (tg + 1) * TGW], in0=psu[:], in1=tb[:], op=ALU.mult)),
                     reads=[ptu, tbt], writes=[t_v[jb][tg]])
            for tg in range(NTG):
                tf, tft = next_tf()
                rd = [t_v[jb][tg], t_vhalo, t_vecs] + ([t_v[jb][tg - 1]] if tg > 0 else [])
                b0 = tg * TGW
                S.op("dve", (lambda e, tf=tf, b0=b0: e.tensor_scalar(out=tf[:], in0=vv[:, jb, b0 + 2:b0 + 2 + TGW], scalar1=vecs[:, V_CONV + 2, j:j + 1], scalar2=None, op0=ALU.mult)),
                     reads=rd, writes=[tft])
                S.op("dve", (lambda e, tf=tf, b0=b0: e.scalar_tensor_tensor(out=tf[:], in0=vv[:, jb, b0 + 1:b0 + 1 + TGW], scalar=vecs[:, V_CONV + 1, j:j + 1], in1=tf[:], op0=ALU.mult, op1=ALU.add)),
                     reads=rd + [tft], writes=[tft])
                S.op("dve", (lambda e, tf=tf, b0=b0: e.scalar_tensor_tensor(out=tf[:], in0=vv[:, jb, b0:b0 + TGW], scalar=vecs[:, V_CONV + 0, j:j + 1], in1=tf[:], op0=ALU.mult, op1=ALU.add)),
                     reads=rd + [tft], writes=[tft])
                S.op("dve", (lambda e, tf=tf, tg=tg: e.tensor_tensor(out=A.ap[:, j, tsl(tg)], in0=tf[:], in1=gbv[:, jb, tsl(tg)], op=ALU.mult)),
                     reads=[tft, t_gb[jb][tg]], writes=[A.t(j, tg)])

        for jp in range(4):
            def view(slot):
                return slot[:, 0:6144].rearrange("p (k s c) -> p k s c", k=8, s=3)
            pieces = []
            for s3 in range(3):
                pieces.append(((lambda slot, s3=s3: view(slot)[:, :, s3, :]), w_in_v[:, :, s3, jp * 256:(jp + 1) * 256]))
            wv, wt = load_w(pieces, view)
            for jj in range(2):
                mixer_j(wv, wt, jj, 2 * jp + jj)
        wv, wt = load_w_std(d_w_a_out[0], 0, 1024)
        proj(wv, wt, A, 8, resid_evac(g1, [t_modv[0]]))
        if stop != "mix0":
            if stop not in ("mix0", "l0"):
                make_mv_tasks(d_w_ada[1], 12, V_BADA + 6, 1)
                make_mv_tasks(d_w_ada_kv, 4, V_BKV, 2)
            mlp(0, 0)
        def chk(name, dumps):
            if stop != name:
                return
            S.barrier()
            for ap, dst in dumps:
                S.op("pool", (lambda e, ap=ap, dst=dst: e.dma_start(out=dst, in_=ap)), dma="out")
            raise StopBuild()

        if stop not in ("mix0", "l0"):
          try:
            G4 = 4
            defpool["v"] = "P6"
            POOLS["P6"] = [0, 1, 2, 3, 4]
            POOLS["M"] = [5, 6, 7]
            t_l1c = Tok()
            wqg_g = SCR[:, 8192:8576].rearrange("p (k c) -> p k c", k=8)
            cw2k = SCR[:, 8576:8832].rearrange("p (c d) -> p c d", c=2)
            cw2v = SCR[:, 8832:8960].rearrange("p (c d) -> p c d", c=2)
            peT_b = SCR[:, 8960:9024]
            S.op("sp", lambda e: e.dma_start(out=hvecs[:], in_=d_hvecs), writes=[t_l1c], dma="c")
            S.op("pool", lambda e: e.dma_start(out=ident_b[:], in_=d_consts[:, 0:128]), writes=[t_l1c], dma="cb")
            S.op("pool", lambda e: e.dma_start(out=bd_ones[:], in_=d_bd), writes=[t_l1c], dma="cb")
            S.op("pool", lambda e: e.dma_start(out=peT_b, in_=d_peT), writes=[t_l1c], dma="cb")
            S.op("pool", lambda e: e.dma_start(out=cw2k[:, :, 0:64], in_=d_cmp_w2[0].rearrange("(c p) d -> p c d", p=128)), writes=[t_l1c], dma="cb")
            S.op("pool", lambda e: e.dma_start(out=cw2k[:, :, 64:128], in_=d_cmp_w2[0].rearrange("(c p) d -> p c d", p=128)), writes=[t_l1c], dma="cb")
            S.op("pool", lambda e: e.dma_start(out=cw2v, in_=d_cmp_w2[1].rearrange("(c p) d -> p c d", p=128)), writes=[t_l1c], dma="cb")
            S.op("pool", lambda e: e.dma_start(out=wqg_g, in_=d_w_qg[0].rearrange("(k p) n -> p k n", p=128)[:, :, 1024:1072]), writes=[t_l1c], dma="cb")
            t_gq = Tok()
            S.op("dve", lambda e: e.tensor_scalar(out=gq[:], in0=hvecs[:, 0:1], scalar1=0.125, scalar2=None, op0=ALU.mult), reads=[t_l1c], writes=[t_gq])

            run_mv_tasks(99)
            compute_rstd()
            derive(2, 1, V_KVG, 2)
            derive(1, 1, V_NG + 2, 3)
            norm_mod(H, der[:, 2, :], mod_part(2, 0), [t_der, t_modv[2]])
            norm_mod(A, der[:, 3, :], mod_part(1, 0), [t_der, t_modv[1]])
            xs_t = [Tok() for _ in range(NTG)]
            for tg in range(NTG):
                for kh in range(2):
                    ks = slice(kh * 4, kh * 4 + 4)
                    S.op("sp", (lambda e, tg=tg, ks=ks: e.dma_start(out=d_xs[:, ks, tsl(tg)], in_=X.ap[:, ks, tsl(tg)])),
                         reads=[X.t(k, tg) for k in range(kh * 4, kh * 4 + 4)], writes=[xs_t[tg]], dma="xs")

            def head_norm(ps, pt, ncol, gain_ap, gain_reads, dst_ap, dst_toks):
                tb, tbt = next_tb()
                S.op("act", (lambda e: e.activation(out=tb[:, 0:ncol], in_=ps[:, 0:ncol], func=AF.Square)), reads=[pt], writes=[tbt])
                ps2, pt2 = next_ps("M")
                S.op("pe", (lambda e: e.matmul(ps2[:, 0:ncol], bd_ones[:], tb[:, 0:ncol], start=True, stop=True)), reads=[tbt, t_l1c], writes=[pt2])
                tf, tft = next_tf()
                S.op("act", (lambda e: e.activation(out=tf[:, 0:ncol], in_=ps2[:, 0:ncol], func=AF.Ln, bias=eps_t[:, 0:1], scale=1.0 / 64)),
                     reads=[pt2, t_eps], writes=[tft])
                tf2, tft2 = next_tf()
                S.op("act", (lambda e: e.activation(out=tf2[:, 0:ncol], in_=tf[:, 0:ncol], func=AF.Exp, scale=-0.5)), reads=[tft], writes=[tft2])
                S.op("dve", (lambda e: e.scalar_tensor_tensor(out=dst_ap, in0=ps[:, 0:ncol], scalar=gain_ap, in1=tf2[:, 0:ncol], op0=ALU.mult, op1=ALU.mult)),
                     reads=[pt, tft2] + gain_reads, writes=dst_toks)

            RAW = TT(SCR[:, 0:8192].rearrange("p (c t) -> p c t", c=4))
            wv, wt = load_w_std(d_w_kv, 0, 512)

            def ev_raw(oc, tg, ps, pt):
                S.op("act", (lambda e: e.activation(out=RAW.ap[:, oc, tsl(tg)], in_=ps[:], func=AF.Copy)), reads=[pt], writes=[RAW.t(oc, tg)])
            proj(wv, wt, H, 4, ev_raw)

            chk("raw", [(RAW.ap, d_y[:, 0:4, :])])
            t_kc = [Tok() for _ in range(G4)]
            t_vc = [Tok() for _ in range(G4)]
            t_vc_ones = Tok()
            S.op("dve", lambda e: e.memset(vcA[:, :, 64:128], 1.0), writes=[t_vc_ones])
            t_cb = Tok()
            for kv in range(2):
                def view1(slot):
                    return slot[:, 0:8192].rearrange("p (l h) -> p l h", l=32)
                src1 = d_cmp_w1[kv].rearrange("(l d) h -> d l h", d=64)
                pieces = [((lambda slot: view1(slot)[0:64, :, :]), src1), ((lambda slot: view1(slot)[64:128, :, :]), src1)]
                cwv, cwt = load_w(pieces, view1)
                psb, ptb = next_ps()
                for hc in range(2):
                    for l in range(32):
                        S.op("pe", (lambda e, hc=hc, l=l, cwv=cwv, psb=psb, kv=kv: e.matmul(
                            psb[:, hc:hc + 1], cwv[0:64, l, hc * 128:(hc + 1) * 128], peT_b[0:64, kv * 32 + l:kv * 32 + l + 1],
                            start=(l == 0), stop=(l == 31))), reads=[cwt, t_l1c], writes=[ptb])
                S.op("dve", (lambda e, psb=psb, kv=kv: e.tensor_copy(out=cbias[:, 2 * kv:2 * kv + 2], in_=psb[:, 0:2])), reads=[ptb], writes=[t_cb])
                for g in range(G4):
                    base = (g % 2) * 64
                    c = kv * 2 + g // 2
                    hids = []
                    for hc in range(2):
                        ps, pt = next_ps()
                        for l in range(32):
                            S.op("pe", (lambda e, ps=ps, l=l, hc=hc, cwv=cwv, base=base, c=c: e.matmul(
                                ps[:, 0:127], cwv[base:base + 64, l, hc * 128:(hc + 1) * 128],
                                RAW.ap[base:base + 64, c, l:l + 16 * 126 + 1:16], start=(l == 0), stop=(l == 31))),
                                reads=[cwt] + [RAW.t(c, tg) for tg in range(NTG)], writes=[pt])
                        z, zt = next_tf()
                        S.op("act", (lambda e, ps=ps, z=z, hc=hc, kv=kv: e.activation(out=z[:, 0:127], in_=ps[:, 0:127], func=AF.Identity,
                                                                                      bias=cbias[:, 2 * kv + hc:2 * kv + hc + 1], scale=1.0)),
                             reads=[pt, t_cb], writes=[zt])
                        u, ut = next_tf()
                        S.op("dve", (lambda e, z=z, u=u: e.tensor_tensor(out=u[:, 0:127], in0=z[:, 0:127], in1=z[:, 0:127], op=ALU.mult)), reads=[zt], writes=[ut])
                        S.op("dve", (lambda e, u=u: e.tensor_scalar(out=u[:, 0:127], in0=u[:, 0:127], scalar1=0.044715, scalar2=1.0, op0=ALU.mult, op1=ALU.add)),
                             reads=[ut], writes=[ut])
                        S.op("dve", (lambda e, z=z, u=u: e.tensor_tensor(out=u[:, 0:127], in0=u[:, 0:127], in1=z[:, 0:127], op=ALU.mult)), reads=[ut, zt], writes=[ut])
                        S.op("act", (lambda e, u=u: e.activation(out=u[:, 0:127], in_=u[:, 0:127], func=AF.Sigmoid, scale=1.5957691216057308)), reads=[ut], writes=[ut])
                        hb_, hbt = next_tb()
                        S.op("dve", (lambda e, z=z, u=u, hb_=hb_: e.tensor_tensor(out=hb_[:, 0:127], in0=u[:, 0:127], in1=z[:, 0:127], op=ALU.mult)), reads=[ut, zt], writes=[hbt])
                        hids.append((hb_, hbt))
                    if kv == 0:
                        ps, pt = next_ps()
                        for hc in range(2):
                            S.op("pe", (lambda e, ps=ps, hc=hc, hb_=hids[hc][0]: e.matmul(ps[:, 0:127], cw2k[:, hc, :], hb_[:, 0:127], start=(hc == 0), stop=(hc == 1))),
                                 reads=[hids[hc][1], t_l1c], writes=[pt])
                        head_norm(ps, pt, 127, hvecs[:, 1:2], [t_l1c], kcT[:, g, 0:127], [t_kc[g]])
                    else:
                        ps, pt = next_ps()
                        for hc in range(2):
                            S.op("pe", (lambda e, ps=ps, hc=hc, hb_=hids[hc][0]: e.matmul(ps[0:127, 0:64], hb_[:, 0:127], cw2v[:, hc, :], start=(hc == 0), stop=(hc == 1))),
                                 reads=[hids[hc][1], t_l1c], writes=[pt])
                        S.op("act", (lambda e, ps=ps, g=g: e.activation(out=vcA[0:127, g, 0:64], in_=ps[0:127, 0:64], func=AF.Copy)), reads=[pt], writes=[t_vc[g]])

            chk("cmp", [(kcT[:].rearrange("p g n -> p (g n)"), d_y[:, 0, 0:512]), (vcA[:].rearrange("p g n -> p (g n)"), d_y[:, 1, 0:512])])
            wv_q, wt_q = load_w_std(d_w_qg[0], 0, 1024)
            S.barrier()
            t_ac = Tok()
            tri_b = SCR[:, 0:128]
            anti_b = SCR[:, 128:256]
            cmpm = SCR[:, 256:2304]
            e128 = SCR[:, 2304:4352].rearrange("p (k j) -> p k j", k=16)
            cmap = SCR[:, 4352:4392]
            force = SCR[:, 4392:4904].bitcast(F32).rearrange("p (q j) -> p q j", q=8)
            gsel = SCR[0:48, 4904:7976].rearrange("p (h b m) -> p h b m", h=8, b=3)
            PTS = [SCR[:, 7976 + i * 1024:7976 + (i + 1) * 1024].rearrange("p (k r c) -> p k r c", k=2, r=2) for i in range(3)]
            pts_t = [Tok() for _ in PTS]
            S.op("pool", lambda e: e.dma_start(out=SCR[:, 0:2304], in_=d_amask), writes=[t_ac], dma="cb")
            S.op("pool", lambda e: e.dma_start(out=SCR[:, 2304:4352], in_=d_e128), writes=[t_ac], dma="cb")
            S.op("pool", lambda e: e.dma_start(out=cmap, in_=d_cmap), writes=[t_ac], dma="cb")
            S.op("sp", lambda e: e.dma_start(out=SCR[:, 4392:4904].bitcast(F32), in_=d_force), writes=[t_ac], dma="c")
            S.op("pool", lambda e: e.dma_start(out=SCR[0:48, 4904:7976], in_=d_gsel), writes=[t_ac], dma="cb")
            XB = XR[:].bitcast(BF16)
            QT = TT(XB[:, 0:16384].rearrange("p (h t) -> p h t", h=8))
            KST = TT(XB[:, 16384:24576].rearrange("p (g t) -> p g t", g=4))
            KWT = TT(XB[:, 24576:32768].rearrange("p (g t) -> p g t", g=4))
            RB = rstd_s[:].bitcast(BF16)
            SIGG = TT(RB[0:48, 0:T])

            def ev_q(oc, tg, ps, pt):
                head_norm(ps, pt, TGW, gq[:, 0:1], [t_gq], QT.ap[:, oc, tsl(tg)], [QT.t(oc, tg)])
            proj(wv_q, wt_q, A, 8, ev_q)
            for tg in range(NTG):
                ps, pt = next_ps()
                for k in range(KC):
                    S.op("pe", (lambda e, ps=ps, k=k, tg=tg: e.matmul(ps[0:48, :], wqg_g[:, k, :], A.ap[:, k, tsl(tg)], start=(k == 0), stop=(k == KC - 1))),
                         reads=[t_l1c, A.t(k, tg)], writes=[pt])
                S.op("act", (lambda e, ps=ps, tg=tg: e.activation(out=SIGG.ap[:, tsl(tg)], in_=ps[0:48, :], func=AF.Sigmoid)), reads=[pt], writes=[SIGG.t(tg)])

            for typ, dst, gi in ((2, KST, 2), (4, KWT, 3)):
                def viewk(slot):
                    return slot[:, 0:4096].rearrange("p (k g r d) -> p k g r d", k=8, g=4, r=2)
                srck = d_w_kv.rearrange("(k p) n -> p k n", p=128)[:, :, typ * 256:(typ + 1) * 256].rearrange("p k (g d) -> p k g d", g=4)
                pieces = [((lambda slot, r=r, gg=gg: viewk(slot)[:, :, gg, r, :]), srck[:, :, gg, :]) for r in range(2) for gg in range(4)]
                kv_, kt_ = load_w(pieces, viewk)

                def ev_k(oc, tg, ps, pt, dst=dst, gi=gi):
                    head_norm(ps, pt, TGW, hvecs[:, gi:gi + 1], [t_l1c], dst.ap[:, oc, tsl(tg)], [dst.t(oc, tg)])
                proj(kv_, kt_, H, 4, ev_k, oc_cols=(lambda wv_, k, oc: wv_[:, k, oc, :, :].rearrange("p r d -> p (r d)")))
            chk("q", [(QT.ap, d_y)])
            chk("k", [(KST.ap, d_y[:, 0:4, :]), (KWT.ap, d_y[:, 4:8, :]), ])

            def viewv(slot):
                return slot[:, 0:4096].rearrange("p (k s c) -> p k s c", k=8, s=2)
            srcv = d_w_kv.rearrange("(k p) n -> p k n", p=128)
            pieces = [((lambda slot: viewv(slot)[:, :, 0, :]), srcv[:, :, 768:1024]), ((lambda slot: viewv(slot)[:, :, 1, :]), srcv[:, :, 1280:1536])]
            vv_, vt_ = load_w(pieces, viewv)
            S.barrier()
            AB = AR[:]
            VS = TT(AB[:, 0:8192].rearrange("p (t g c) -> p t g c", t=16, g=4))
            VW = TT(AB[:, 8192:16384].rearrange("p (t g c) -> p t g c", t=16, g=4))
            t_vones = Tok()
            S.op("dve", lambda e: e.memset(VS.ap[:, :, :, 64:128], 1.0), writes=[t_vones])
            S.op("dve", lambda e: e.memset(VW.ap[:, :, :, 64:128], 1.0), writes=[t_vones])

            for tt in range(16):
                ps, pt = next_ps()
                tg = tt // 4
                for k in range(KC):
                    S.op("pe", (lambda e, ps=ps, k=k, tt=tt: e.matmul(ps[:], H.ap[:, k, tt * 128:(tt + 1) * 128], vv_[:, k, :, :].rearrange("p s c -> p (s c)"),
                                                                    start=(k == 0), stop=(k == KC - 1))), reads=[vt_, H.t(k, tg)], writes=[pt])
                S.op("act", (lambda e, ps=ps, tt=tt: e.activation(out=VS.ap[:, tt, :, 0:64], in_=ps[:, 0:256].rearrange("p (g d) -> p g d", g=4), func=AF.Copy)),
                     reads=[pt, t_vones], writes=[VS.t(tt)])
                S.op("dve", (lambda e, ps=ps, tt=tt: e.tensor_copy(out=VW.ap[:, tt, :, 0:64], in_=ps[:, 256:512].rearrange("p (g d) -> p g d", g=4))),
                     reads=[pt, t_vones], writes=[VW.t(tt)])
            chk("v", [(AB[:, 0:8192], d_y[:, 0:4, :].rearrange("p a b -> p (a b)")), (AB[:, 8192:16384], d_y[:, 4:8, :].rearrange("p a b -> p (a b)"))])
            wo_v, wo_t = load_w_std(d_w_o[0], 0, 1024)
            S.barrier()

            POOLS["M"] = [7]
            HB = HR[:]
            OTG = TT(HB[:, 0:4096].rearrange("p (h t) -> p h t", h=8))
            GBC = HB[:, 4096:7168].rearrange("p (h b t) -> p h b t", h=2, b=3)
            t_gbc = Tok()
            XST = TT(HB[:, 7168:15360].bitcast(F32).rearrange("p (k t) -> p k t", k=8))
            BIAS = [HB[:, 15360 + i * 256:15360 + (i + 1) * 256].rearrange("p (r q) -> p r q", r=2) for i in range(2)]
            bias_t = [Tok(), Tok()]
            for i in range(2):
                S.op("dve", (lambda e, i=i: e.memset(BIAS[i], 0.0)), writes=[bias_t[i]])
            pend = {"tk": None}
            ptc = {"n": 0, "b": 0, "a": 0, "c": 0}
            cmb_lo = [Tok(), Tok()]
            cmb_hi = [Tok(), Tok()]
            cmb_on = [Tok(), Tok()]

            M2 = BIAS_EXTRA
            tri2 = M2[:, 0:256]
            anti2 = M2[:, 256:512]
            t_m2 = Tok()
            S.op("dve", lambda e: e.tensor_copy(out=tri2.rearrange("p (r q) -> p r q", r=2), in_=tri_b.unsqueeze(1).to_broadcast([128, 2, 128])), reads=[t_ac], writes=[t_m2])
            S.op("dve", lambda e: e.tensor_copy(out=anti2.rearrange("p (r q) -> p r q", r=2), in_=anti_b.unsqueeze(1).to_broadcast([128, 2, 128])), reads=[t_ac], writes=[t_m2])
            CM2 = [M2[:, 512 + i * 256:512 + (i + 1) * 256] for i in range(2)]
            cm2_t = [Tok(), Tok()]

            def emit_cm2(ci, qt):
                S.op("dve", (lambda e: e.tensor_copy(out=CM2[ci].rearrange("p (r q) -> p r q", r=2),
                                                     in_=cmpm[:, qt * 128:(qt + 1) * 128].unsqueeze(1).to_broadcast([128, 2, 128]))),
                     reads=[t_ac], writes=[cm2_t[ci]])

            def next_pt():
                i = ptc["n"] % 3
                ptc["n"] += 1
                return PTS[i], pts_t[i]

            def qk_tile(bank, bt, colbase, KT, g, kt0, nk, par, qt, mask, bias_i, phase):
                lhs_k = KT.ap[par * 64:(par + 1) * 64, g, kt0:kt0 + nk] if KT is not None else kcT[par * 64:(par + 1) * 64, g, 0:127]
                k_reads = [KT.t(g, kt0 // TGW)] if KT is not None else [t_kc[g]]
                qsl = slice(qt * 128, (qt + 1) * 128)
                q_reads = [QT.t(2 * g, qt // 4), QT.t(2 * g + 1, qt // 4)]
                if phase == 1:
                    S.op("pe", (lambda e: e.matmul(bank[0:nk, colbase:colbase + 256], lhs_k, QT.ap[par * 64:(par + 1) * 64, 2 * g:2 * g + 2, qsl],
                                                   start=(mask is None), stop=True)), reads=k_reads + q_reads, writes=[bt])
                elif mask is None:
                    pass
                elif mask == "bias":
                    kt = kt0 // 128
                    S.op("pe", (lambda e: e.matmul(bank[:, colbase:colbase + 256], e128[:, kt, :], BIAS[bias_i].rearrange("p r q -> p (r q)"), start=True, stop=False)),
                         reads=[t_ac, bias_t[bias_i]], writes=[bt])
                else:
                    mask_ap, mask_reads = mask
                    S.op("pe", (lambda e: e.matmul(bank[0:nk, colbase:colbase + 256], ident_b[:, 0:nk], mask_ap, start=True, stop=False)),
                         reads=[t_l1c] + mask_reads, writes=[bt])

            def branch_steps(g, qt, kind, bias_i, pos, br, acc, acct):
                ps_o, pt_o = next_ps("O")
                out = []
                if kind == "cmp":
                    bA, tA = next_ps("S")
                    bB, tB = next_ps("S")
                    PT, ptt = next_pt()

                    ci = ptc["a"] % 2
                    ptc["a"] += 1

                    def qk():
                        if g == 0 and qt == 0:
                            emit_cm2(ci, qt)
                        for phase in range(2):
                            for par, bank, bt in ((0, bA, tA), (1, bB, tB)):
                                qk_tile(bank, bt, 0, None, g, 0, 127, par, qt, (CM2[ci], [cm2_t[ci]]), None, phase)
                        for par, bank, bt in ((0, bA, tA), (1, bB, tB)):
                            S.op("act", (lambda e, par=par, bank=bank: e.activation(out=PT[0:127, 0, par, :], in_=bank[0:127, 0:256], func=AF.Exp)),
                                 reads=[bt], writes=[ptt])
                        nqt = qt + 1 if qt % 4 != 3 else (qt - 3 if g < 3 else qt + 1)
                        if nqt < 16:
                            emit_cm2(1 - ci, nqt)
                        if qt >= 8:
                            pend["tk"] = (lambda: topk_a(g, qt, PT, ptt))

                    def pv():
                        S.op("pe", (lambda e: e.matmul(ps_o[:], vcA[0:127, g, :], PT[0:127, 0, :, :].rearrange("p r c -> p (r c)"), start=True, stop=True)),
                             reads=[ptt, t_vc[g], t_vc_ones], writes=[pt_o])
                        combine(g, qt, pos, br, ps_o, pt_o, acc, acct)
                    return [(qk, pv)]
                KT, V = (KST, VS) if kind == "sel" else (KWT, VW)
                kts = list(range(0, qt + 1)) if kind == "sel" else list(range(max(0, qt - 4), qt + 1))
                for pi in range(0, len(kts), 2):
                    pair = kts[pi:pi + 2]
                    banks = (next_ps("S"), next_ps("S"))
                    PTp = next_pt()

                    def qk(pair=pair, banks=banks, PTp=PTp):
                        PT, ptt = PTp
                        npair = len(pair)
                        if kind == "sel" and qt >= 8 and pair[0] == 0:
                            topk_b(bias_i)
                        for ktp, kt in enumerate(pair):
                            if kt == qt:
                                mask = (tri2, [t_m2])
                            elif kind == "win" and kt == qt - 4:
                                mask = (anti2, [t_m2])
                            elif kind == "sel" and qt >= 8:
                                mask = "bias"
                            else:
                                mask = None
                            for phase in range(2):
                                for par, (bank, bt) in enumerate(banks):
                                    qk_tile(bank, bt, ktp * 256, KT, g, kt * 128, 128, par, qt, mask, bias_i, phase)
                        for par, (bank, bt) in enumerate(banks):
                            S.op("act", (lambda e, par=par, bank=bank: e.activation(
                                out=PT[:, 0:npair, par, :], in_=bank[:, 0:npair * 256].rearrange("p (k c) -> p k c", k=npair), func=AF.Exp)),
                                reads=[bt, banks[1][1]], writes=[ptt])
                        if pend["tk"] is not None:
                            pend["tk"]()
                            pend["tk"] = None

                    def pv(pair=pair, PTp=PTp, last=(pi + 2 >= len(kts)), first=(pi == 0)):
                        PT, ptt = PTp
                        if FILLER and not first:
                            S.op("pe", (lambda e: e.matmul(ps_o[:], zeros_b[:], ident_b[:].unsqueeze(1).to_broadcast([128, 4, 128]), start=False, stop=False)),
                                 reads=[t_l1c, t_zero], writes=[pt_o])
                        for ktp, kt in enumerate(pair):
                            S.op("pe", (lambda e, ktp=ktp, kt=kt: e.matmul(ps_o[:], V.ap[:, kt, g, :], PT[:, ktp, :, :].rearrange("p r c -> p (r c)"),
                                                                         start=(kt == kts[0]), stop=(kt == kts[-1]))),
                                 reads=[ptt, V.t(kt), t_vones], writes=[pt_o])
                        if last:
                            combine(g, qt, pos, br, ps_o, pt_o, acc, acct)
                    out.append((qk, pv))
                return out

            def combine(g, qt, pos, br, ps_o, pt_o, acc, acct):
                qi = qt % 4
                ci = ptc["c"] % 2
                ptc["c"] += 1
                T, tlo, thi = TMPF[ci], cmb_lo[ci], cmb_hi[ci]
                if pos == 1:
                    S.op("dve", (lambda e: e.reciprocal(out=T[64:128, :], in_=ps_o[64:128, :])), reads=[pt_o], writes=[thi, tlo])
                else:
                    S.op("act", (lambda e: e.activation(out=T[0:64, :], in_=ps_o[64:128, :], func=AF.Ln, bias=eps_t[64:128, 1:2], scale=1.0)), reads=[pt_o, t_eps], writes=[tlo])
                    S.op("act", (lambda e: e.activation(out=T[64:128, :], in_=T[0:64, :], func=AF.Exp, scale=-1.0)), reads=[tlo], writes=[thi])
                on, ont = TMPF[2][:, ci * 256:(ci + 1) * 256], cmb_on[ci]
                for par in range(2):
                    S.op("dve", (lambda e, par=par: e.tensor_tensor(out=on[par * 64:(par + 1) * 64, :], in0=ps_o[0:64, par * 256:(par + 1) * 256],
                                                                     in1=T[64:128, par * 256:(par + 1) * 256], op=ALU.mult)), reads=[pt_o, thi], writes=[ont])
                onv = on.rearrange("p (h q) -> p h q", h=2)
                accv = acc[:].rearrange("p (h q) -> p h q", h=2)
                Gv = GBC[:, :, br, qi * 128:(qi + 1) * 128]
                ce = "pool"
                if pos == 0:
                    S.op(ce, (lambda e: e.tensor_tensor(out=accv, in0=onv, in1=Gv, op=ALU.mult)), reads=[ont, t_gbc], writes=[acct])
                else:
                    S.op(ce, (lambda e: e.tensor_tensor(out=onv, in0=onv, in1=Gv, op=ALU.mult)), reads=[ont, t_gbc], writes=[ont])
                    if pos == 1:
                        S.op(ce, (lambda e: e.tensor_tensor(out=accv, in0=accv, in1=onv, op=ALU.add)), reads=[ont, acct], writes=[acct])
                    else:
                        S.op(ce, (lambda e: e.tensor_tensor(out=OTG.ap[:, 2 * g:2 * g + 2, qi * 128:(qi + 1) * 128], in0=accv, in1=onv, op=ALU.add)),
                             reads=[ont, acct], writes=[OTG.t(2 * g, 0), OTG.t(2 * g + 1, 0)])

            def topk_a(g, qt, PT, ptt):
                ps_i, pt_i = next_ps("M")
                for c in range(4):
                    S.op("pe", (lambda e, c=c: e.matmul(ps_i[:, c * 33:(c + 1) * 33], PT[0:127, 0, c // 2, (c % 2) * 128:(c % 2 + 1) * 128], cmap[0:127, 0:33],
                                                          start=True, stop=True)), reads=[ptt, t_ac], writes=[pt_i])
                psv = ps_i[:, 0:132].rearrange("p (c j) -> p c j", c=4)
                S.op("dve", (lambda e: e.reciprocal(out=rd4[:], in_=psv[:, :, 32])), reads=[pt_i], writes=[t_tk])
                S.op("dve", (lambda e: e.tensor_scalar(out=scA[:], in0=psv[:, 0, 0:32], scalar1=rd4[:, 0:1], scalar2=None, op0=ALU.mult)), reads=[pt_i, t_tk], writes=[t_tk])
                for c in range(1, 4):
                    S.op("dve", (lambda e, c=c: e.scalar_tensor_tensor(out=scA[:], in0=psv[:, c, 0:32], scalar=rd4[:, c:c + 1], in1=scA[:], op0=ALU.mult, op1=ALU.add)),
                         reads=[pt_i, t_tk], writes=[t_tk])
                S.op("dve", (lambda e: e.tensor_tensor(out=scA[:], in0=scA[:], in1=force[:, qt - 8, :], op=ALU.add)), reads=[t_tk, t_ac], writes=[t_tk])
                S.op("dve", (lambda e: e.max(out=m8a[:], in_=scA[:])), reads=[t_tk], writes=[t_tk])
                S.op("dve", (lambda e: e.match_replace(out=scB[:], in_to_replace=m8a[:], in_values=scA[:], imm_value=-1e30)), reads=[t_tk], writes=[t_tk])
                S.op("dve", (lambda e: e.max(out=m8b[:], in_=scB[:])), reads=[t_tk], writes=[t_tk])
                S.op("dve", (lambda e: e.tensor_scalar(out=selm[:], in0=scA[:], scalar1=m8b[:, 7:8], scalar2=1.0, op0=ALU.is_ge, op1=ALU.subtract)), reads=[t_tk], writes=[t_tk])

            def topk_b(bias_i):
                ps_m, pt_m = next_ps("M")
                S.op("pe", (lambda e: e.matmul(ps_m[0:32, 0:128], selm[:], ident_b[:], start=True, stop=True)), reads=[t_tk, t_l1c], writes=[pt_m])
                S.op("act", (lambda e: e.activation(out=BIAS[bias_i][0:32, :, :], in_=ps_m[0:32, 0:128].unsqueeze(1).to_broadcast([32, 2, 128]), func=AF.Copy, scale=30000.0)),
                     reads=[pt_m], writes=[bias_t[bias_i]])

            on_t = [Tok(), Tok()]
            acc_t = [Tok(), Tok()]
            t_tk = Tok()
            g1 = mod_part(1, 2)
            steps = []

            def emit_xreload(tg):
                for kh in range(2):
                    ks = slice(kh * 4, kh * 4 + 4)
                    S.op("sp", (lambda e, ks=ks: e.dma_start(out=XST.ap[:, ks, :], in_=d_xs[:, ks, tsl(tg)])),
                         reads=[xs_t[tg]], writes=[XST.t(k) for k in range(kh * 4, kh * 4 + 4)], dma="xr")

            def emit_gbc(tg, g):
                for hpl in range(2):
                    for br in range(3):
                        ps, pt = next_ps("OM")
                        S.op("pe", (lambda e, ps=ps, hpl=hpl, br=br: e.matmul(ps[:], gsel[:, 2 * g + hpl, br, :], SIGG.ap[:, tsl(tg)], start=True, stop=True)),
                             reads=[t_ac, SIGG.t(tg)], writes=[pt])
                        S.op("dve", (lambda e, ps=ps, hpl=hpl, br=br: e.tensor_copy(out=GBC[:, hpl, br, :], in_=ps[:])), reads=[pt], writes=[t_gbc])

            def emit_wo(tg):
                def ev_o(oc, tg_, ps, pt):
                    S.op("dve", (lambda e: e.scalar_tensor_tensor(out=XST.ap[:, oc, :], in0=ps[:], scalar=g1[:, oc:oc + 1], in1=XST.ap[:, oc, :], op0=ALU.mult, op1=ALU.add)),
                         reads=[pt, XST.t(oc), t_modv[1]], writes=[XST.t(oc)])
                defpool["v"] = "OM"
                proj(wo_v, wo_t, OTG, 8, ev_o, tgs=[0])
                defpool["v"] = "P6"
                for kh in range(2):
                    ks = slice(kh * 4, kh * 4 + 4)
                    S.op("sp", (lambda e, ks=ks: e.dma_start(out=d_xs[:, ks, tsl(tg)], in_=XST.ap[:, ks, :])),
                         reads=[XST.t(k) for k in range(kh * 4, kh * 4 + 4)], writes=[xs_t[tg]], dma="xw")

            for tg in range(NTG):
                steps.append((None, (lambda tg=tg: emit_xreload(tg))))
                for g in range(G4):
                    steps.append((None, (lambda tg=tg, g=g: emit_gbc(tg, g))))
                    for qi in range(4):
                        qt = tg * 4 + qi
                        ai = ptc["b"] % 2
                        ptc["b"] += 1
                        acc, acct = ACC[ai], acc_t[ai]
                        steps += branch_steps(g, qt, "cmp", ai, 0, 0, acc, acct)
                        steps += branch_steps(g, qt, "win", ai, 1, 2, acc, acct)
                        steps += branch_steps(g, qt, "sel", ai, 2, 1, acc, acct)
                steps.append((None, (lambda tg=tg: emit_wo(tg))))
            prev_pv = None
            for qk_fn, pv_fn in steps:
                if qk_fn is not None:
                    qk_fn()
                if prev_pv is not None:
                    prev_pv()
                prev_pv = pv_fn
            if prev_pv is not None:
                prev_pv()
            S.barrier()
            for tg in range(NTG):
                for kh in range(2):
                    ks = slice(kh * 4, kh * 4 + 4)
                    S.op("sp", (lambda e, tg=tg, ks=ks: e.dma_start(out=X.ap[:, ks, tsl(tg)], in_=d_xs[:, ks, tsl(tg)])),
                         reads=[xs_t[tg]], writes=[X.t(k, tg) for k in range(kh * 4, kh * 4 + 4)], dma="x2")
            defpool["v"] = "ALL"
            if stop != "mix1":
                mlp(1, 1)
          except StopBuild:
            S.emit(final_dma_keys=["out"])
            return nc

        for tg in range(NTG):
            for kh in range(2):
                ks = slice(kh * 4, kh * 4 + 4)
                S.op("sp", (lambda e, tg=tg, ks=ks: e.dma_start(out=d_y[:, ks, tsl(tg)], in_=X.ap[:, ks, tsl(tg)])),
                     reads=[X.t(k, tg) for k in range(kh * 4, kh * 4 + 4)], dma="out")
        S.emit(final_dma_keys=["out"])
    return nc


def _fm(v):
    return np.ascontiguousarray(v.reshape(KC, 128).T)


def _const_tables():
    f32 = np.float32
    NEGM = -30000.0
    j = np.arange(128)[:, None]
    t = np.arange(128)[None, :]
    tri = np.where(j <= t, 0.0, NEGM)
    anti = np.where(j > t, 0.0, NEGM)
    n = np.arange(128)[:, None]
    tt = np.arange(T)[None, :]
    cmpm = np.where((16 * n + 31 <= tt) & (n < 127), 0.0, NEGM)
    amask = np.concatenate([tri, anti, cmpm], axis=1).astype(f32)
    e128 = np.zeros((128, 16, 128), f32)
    for kt in range(16):
        for jj in range(128):
            e128[2 * kt + jj // 64, kt, jj] = 1.0
    cmap = np.zeros((128, 40), f32)
    c0 = np.arange(127)[:, None] * 16
    s0 = np.arange(32)[None, :] * 64
    ov = np.minimum(c0 + 32, s0 + 64) - np.maximum(c0, s0)
    cmap[:127, :32] = np.clip(ov, 0, None) / 32.0
    cmap[:127, 32] = 1.0
    force = np.zeros((128, 8, 32), f32)
    for q in range(8):
        tq = 128 * (q + 8) + np.arange(128)
        cur = tq // 64
        jb = np.arange(32)[None, :]
        forced = (jb == 0) | (jb == cur[:, None]) | (jb == cur[:, None] - 1)
        force[:, q, :] = np.where(forced, 1e4, np.where(jb > cur[:, None], -1e4, 0.0))
    gsel = np.zeros((48, 8, 3, 128), f32)
    for hp in range(8):
        for br in range(3):
            for m in range(128):
                gsel[(2 * hp + m // 64) * 3 + br, hp, br, m] = 1.0
    p = np.arange(128)
    bd = (p[:, None] // 64 == p[None, :] // 64).astype(f32)
    return {"amask": amask, "e128": e128.reshape(128, 2048), "cmap": cmap, "force": force.reshape(128, 256),
            "gsel": gsel.reshape(48, 3072), "bdones": bd}


def prep_inputs(inputs):
    f32 = np.float32
    g = {k: np.asarray(v, dtype=f32) for k, v in inputs.items()}
    vecs = np.zeros((128, NV, KC), f32)
    for i in range(2):
        for j in range(2):
            vecs[:, V_NG + 2 * i + j, :] = _fm(g["norm_gain"][i, j])
        for part in range(6):
            vecs[:, V_BADA + 6 * i + part, :] = _fm(g["b_ada"][i, part * D:(part + 1) * D])
    vecs[:, V_KVG, :] = _fm(g["kv_norm_gain"])
    for part in range(2):
        vecs[:, V_BKV + part, :] = _fm(g["b_ada_kv"][part * D:(part + 1) * D])
    for j in range(3):
        vecs[:, V_CONV + j, :] = _fm(g["conv_w"][0, j])
    consts = np.concatenate([np.eye(128, dtype=f32), np.ones((128, 128), f32)], axis=1)
    p64 = np.arange(128) % 64
    hvecs = np.zeros((128, 4), f32)
    hvecs[:, 0] = g["q_gain"][0, p64]
    for i in range(3):
        hvecs[:, 1 + i] = g["k_gain"][i, p64]
    peT = np.zeros((128, 64), f32)
    for kv in range(2):
        peT[:, kv * 32:(kv + 1) * 32] = g["cmp_pe"][kv][:, p64].T
    shared = {
        "vecs": vecs, "consts": consts, "hvecs": hvecs, "peT": peT,
        "w_ada": g["w_ada"], "w_a_in": g["w_a_in"], "w_a_out": g["w_a_out"],
        "w_mlp1": g["w_mlp1"], "w_mlp2": g["w_mlp2"],
        "w_ada_kv": g["w_ada_kv"], "w_kv": g["w_kv"], "w_qg": g["w_qg"], "w_o": g["w_o"],
        "cmp_w1": g["cmp_w1"], "cmp_w2": g["cmp_w2"],
    }
    shared.update(_const_tables())
    in_maps = []
    for b in range(N_CORES):
        m = dict(shared)
        xT = g["x"][b].T.reshape(KC, 128, T).transpose(1, 0, 2)
        m["xT"] = np.ascontiguousarray(xT)
        m["cT"] = _fm(g["c"][b])
        in_maps.append(m)
    return in_maps


def post_outputs(results):
    outs = []
    for r in results:
        yT = np.asarray(r["yT"])
        outs.append(yT.transpose(2, 1, 0).reshape(T, D))
    return np.stack(outs, axis=0).astype(np.float32)


def kernel(**inputs):
    in_maps = prep_inputs(inputs)
    nc = build_program(DEBUG_STOP)
    res = run_bass_kernel_spmd(nc, in_maps, core_ids=list(range(N_CORES)))
    return post_outputs(res.results)
```

```python
import numpy as np
from contextlib import ExitStack
import concourse.bass as bass
import concourse.mybir as mybir
from concourse.bass_utils import run_bass_kernel_spmd

F32 = mybir.dt.float32
BF16 = mybir.dt.bfloat16
AF = mybir.ActivationFunctionType
ALU = mybir.AluOpType

D = 1024
T = 2048
KC = 8
NTG = 4
TGW = 512
EPS = 1e-6
N_CORES = 8

V_NG = 0
V_KVG = 4
V_BADA = 5
V_BKV = 17
V_CONV = 19
NV = 22

DEBUG_STOP = None
FILLER = False


class Tok:
    __slots__ = ("ws", "rs", "rdma", "excl")

    def __init__(self, excl=False):
        self.ws = []
        self.rs = {}
        self.rdma = []
        self.excl = excl


class Op:
    __slots__ = ("eng", "fn", "deps", "dma", "signal", "sigval", "dmaval", "dsem")

    def __init__(self, eng, fn, dma):
        self.eng = eng
        self.fn = fn
        self.dma = dma
        self.deps = ()
        self.signal = False
        self.sigval = 0
        self.dmaval = 0
        self.dsem = None


ENGS = ["pe", "act", "dve", "pool", "sp"]
DMA_SEMS = 16


class Sched:
    def __init__(self, nc):
        self.nc = nc
        self.ops = {e: [] for e in ENGS}
        self.dma_hist = {e: [] for e in ENGS}

    def op(self, eng, fn, reads=(), writes=(), dma=None):
        o = Op(eng, fn, dma)
        ex = [t for t in reads if t.excl]
        if ex:
            reads = [t for t in reads if not t.excl]
            writes = list(writes) + ex
        deps = set()
        for t in reads:
            deps.update(t.ws)
        for t in writes:
            deps.update(t.ws)
            deps.update(t.rs.values())
            deps.update(t.rdma)
        if dma is not None:
            deps = {d for d in deps if d.dma != dma}
            hist = self.dma_hist[eng]
            n = len(hist)
            o.dsem = (eng, n % DMA_SEMS)
            o.dmaval = 16 * (n // DMA_SEMS + 1)
            if n >= DMA_SEMS:
                deps.add(hist[n - DMA_SEMS])
            hist.append(o)
        o.deps = tuple(deps)
        for t in reads:
            if dma is not None:
                t.rdma.append(o)
            else:
                t.rs[eng] = o
        for t in writes:
            if dma is not None and t.ws and not t.rs and not t.rdma and all(w.dma == dma for w in t.ws):
                t.ws.append(o)
            else:
                t.ws = [o]
            t.rs = {}
            t.rdma = []
        self.ops[eng].append(o)
        return o

    def barrier(self, engines=("pe", "act", "dve", "sp", "pool")):
        lasts = []
        for e in ENGS:
            for o in reversed(self.ops[e]):
                if o.dma is None and o.fn is not None:
                    lasts.append(o)
                    break
            lasts += self.dma_hist[e][-DMA_SEMS:]
        for e in engines:
            o = Op(e, None, None)
            o.deps = tuple(lasts)
            self.ops[e].append(o)

    def emit(self, final_dma_keys=()):
        nc = self.nc
        for e in ENGS:
            for o in self.ops[e]:
                for d in o.deps:
                    if d.dma is None:
                        if d.eng == "pe" and o.eng == "pe" and o.dma is None and o.fn is not None:
                            continue
                        d.signal = True
        for e in ENGS:
            c = 0
            for o in self.ops[e]:
                if o.dma is None and o.signal:
                    c += 1
                    o.sigval = c
        with ExitStack() as es:
            esem = {e: es.enter_context(nc.semaphore("s_" + e)) for e in ENGS}
            dsem = {}
            for e in ENGS:
                for i in range(min(DMA_SEMS, len(self.dma_hist[e]))):
                    dsem[(e, i)] = es.enter_context(nc.semaphore("d_%s%d" % (e, i)))
            block = es.enter_context(nc.Block())

            def run(e, eng):
                waited = {}
                for o in self.ops[e]:
                    need = {}
                    for d in o.deps:
                        if d.dma is not None:
                            key, sem, val = ("d",) + d.dsem, dsem[d.dsem], d.dmaval
                        else:
                            if d.eng == "pe" and e == "pe" and o.dma is None and o.fn is not None:
                                continue
                            key, sem, val = ("e", d.eng), esem[d.eng], d.sigval
                        if val > need.get(key, (None, 0))[1]:
                            need[key] = (sem, val)
                    for key, (sem, val) in need.items():
                        if waited.get(key, 0) >= val:
                            continue
                        waited[key] = val
                        eng.wait_ge(sem, val)
                    if o.fn is None:
                        continue
                    ins = o.fn(eng)
                    if o.dma is not None:
                        ins.then_inc(dsem[o.dsem], 16)
                    elif o.signal:
                        ins.then_inc(esem[e], 1)
                if e == "sp":
                    fin = {}
                    for q in ENGS:
                        for d in self.dma_hist[q]:
                            if d.dma in final_dma_keys:
                                fin[d.dsem] = max(fin.get(d.dsem, 0), d.dmaval)
                    for k, v in fin.items():
                        if waited.get(("d",) + k, 0) < v:
                            eng.wait_ge(dsem[k], v)

            block.sync(lambda eng: run("sp", eng))
            block.scalar(lambda eng: run("act", eng))
            block.vector(lambda eng: run("dve", eng))
            block.gpsimd(lambda eng: run("pool", eng))
            block.tensor(lambda eng: run("pe", eng))


class StopBuild(Exception):
    pass


class TT:
    def __init__(self, ap):
        self.ap = ap
        self.toks = {}

    def t(self, *key):
        tk = self.toks.get(key)
        if tk is None:
            tk = self.toks[key] = Tok()
        return tk

    def all(self):
        return list(self.toks.values())


def build_program(stop=None):
    nc = bass.Bass("TRN2", target_bir_lowering=False)

    def din(name, shape):
        return nc.dram_tensor(name, list(shape), F32, kind="ExternalInput").ap()

    d_x = din("xT", [128, KC, T])
    d_c = din("cT", [128, KC])
    d_vecs = din("vecs", [128, NV, KC])
    d_consts = din("consts", [128, 256])
    d_w_ada = din("w_ada", [2, D, 6 * D])
    d_w_a_in = din("w_a_in", [1, D, 3 * D])
    d_w_a_out = din("w_a_out", [1, D, D])
    d_w_mlp1 = din("w_mlp1", [2, D, 4 * D])
    d_w_mlp2 = din("w_mlp2", [2, 4 * D, D])
    d_w_ada_kv = din("w_ada_kv", [D, 2 * D])
    d_w_kv = din("w_kv", [D, 1536])
    d_w_qg = din("w_qg", [1, D, 1072])
    d_w_o = din("w_o", [1, D, D])
    d_cmp_w1 = din("cmp_w1", [2, 2048, 256])
    d_cmp_w2 = din("cmp_w2", [2, 256, 64])
    d_hvecs = din("hvecs", [128, 4])
    d_peT = din("peT", [128, 64])
    d_amask = din("amask", [128, 256 + 2048])
    d_e128 = din("e128", [128, 2048])
    d_cmap = din("cmap", [128, 40])
    d_force = din("force", [128, 256])
    d_gsel = din("gsel", [48, 3072])
    d_bd = din("bdones", [128, 128])
    d_y = nc.dram_tensor("yT", [128, KC, T], F32, kind="ExternalOutput").ap()
    d_xs = nc.dram_tensor("xs_scratch", [128, KC, T], F32).ap()

    S = Sched(nc)
    es = ExitStack()
    with es:
        def sb(name, shape, dt):
            return es.enter_context(nc.sbuf_tensor(name, list(shape), dt))

        XR = sb("XR", [128, KC * T], F32)
        HR = sb("HR", [128, KC * T], BF16)
        AR = sb("AR", [128, KC * T], BF16)
        RING = [sb("RING%d" % i, [128, 8192], BF16) for i in range(2)]
        ring_t = [Tok(), Tok()]
        rstd_s = sb("rstd", [128, T], F32)
        TMPF = [sb("tmpf%d" % i, [128, TGW], F32) for i in range(3)]
        tmpf_t = [Tok() for _ in TMPF]
        TMPB = [sb("tmpb%d" % i, [128, TGW], BF16) for i in range(3)]
        tmpb_t = [Tok() for _ in TMPB]
        SCR = sb("SCR", [128, 11048], BF16)
        ones_b = sb("ones_b", [128, 128], BF16)
        vecs = sb("vecs_s", [128, NV, KC], F32)
        c_f = sb("c_f", [128, KC], F32)
        cact = sb("cact", [128, KC], BF16)
        modv = sb("modv", [128, 3, 48], F32)
        der = sb("der", [128, 8, KC], F32)
        hvecs = sb("hvecs_s", [128, 4], F32)
        gq = sb("gq", [128, 1], F32)
        ident_b = sb("ident_b", [128, 128], BF16)
        bd_ones = sb("bd_ones", [128, 128], BF16)
        cbias = sb("cbias", [128, 4], F32)
        kcT = sb("kcT", [128, 4, 128], BF16)
        vcA = sb("vcA", [128, 4, 128], BF16)
        rd4 = sb("rd4", [128, 4], F32)
        scA = sb("scA", [128, 32], F32)
        scB = sb("scB", [128, 32], F32)
        m8a = sb("m8a", [128, 8], F32)
        m8b = sb("m8b", [128, 8], F32)
        selm = sb("selm", [128, 32], BF16)
        ACC = [sb("acc%d" % i, [128, 256], F32) for i in range(2)]
        BIAS_EXTRA = sb("m2", [128, 1024], BF16)
        zeros_b = sb("zeros_b", [128, 128], BF16)
        t_zero = Tok()
        S.op("dve", lambda e: e.memset(zeros_b[:], 0.0), writes=[t_zero])
        PS2 = [es.enter_context(nc.psum_tensor("psd%d" % i, [128, 2 * TGW], F32)) for i in range(2)]
        PS = [PS2[0][:, 0:TGW], PS2[0][:, TGW:2 * TGW], PS2[1][:, 0:TGW], PS2[1][:, TGW:2 * TGW]]
        PS += [es.enter_context(nc.psum_tensor("ps%d" % i, [128, TGW], F32)) for i in range(4, 8)]
        ps_t = [Tok(excl=True) for _ in PS]

        X = TT(XR[:].rearrange("p (k t) -> p k t", k=KC))
        H = TT(HR[:].rearrange("p (k t) -> p k t", k=KC))
        A = TT(AR[:].rearrange("p (k t) -> p k t", k=KC))
        t_const = Tok()
        eps_t = sb("eps_t", [128, 2], F32)
        t_eps = Tok()
        S.op("dve", lambda e: e.memset(eps_t[:, 0:1], EPS), writes=[t_eps])
        S.op("dve", lambda e: e.memset(eps_t[:, 1:2], 1e-30), writes=[t_eps])
        t_vecs = Tok()
        t_c = Tok()
        t_cact = Tok()
        t_modv = [Tok(), Tok(), Tok()]
        t_der = Tok()
        t_rstd = [Tok() for _ in range(NTG)]

        cnt = {"ps": 0, "ring": 0, "tf": 0, "tb": 0}

        POOLS = {"ALL": list(range(8)), "P6": [0, 1, 2, 3, 4, 5], "S": [0, 1, 2, 3], "O": [4, 5, 6], "M": [7], "OM": [4, 5, 6, 7]}
        pcnt = {k: 0 for k in POOLS}

        defpool = {"v": "ALL"}

        def next_ps(pool=None):
            pool = pool or defpool["v"]
            lst = POOLS[pool]
            i = lst[pcnt[pool] % len(lst)]
            pcnt[pool] += 1
            return PS[i], ps_t[i]

        def next_tf():
            i = cnt["tf"] % len(TMPF)
            cnt["tf"] += 1
            return TMPF[i], tmpf_t[i]

        def next_tb():
            i = cnt["tb"] % len(TMPB)
            cnt["tb"] += 1
            return TMPB[i], tmpb_t[i]

        def tsl(tg):
            return slice(tg * TGW, (tg + 1) * TGW)

        S.op("pool", lambda e: e.dma_start(out=ones_b[:], in_=d_consts[:, 128:256]), writes=[t_const], dma="cb")
        S.op("sp", lambda e: e.dma_start(out=vecs[:], in_=d_vecs), writes=[t_vecs], dma="c")
        S.op("sp", lambda e: e.dma_start(out=c_f[:], in_=d_c), writes=[t_c], dma="c")
        for tg in range(NTG):
            for kh in range(2):
                ks = slice(kh * 4, kh * 4 + 4)
                S.op("sp", (lambda e, tg=tg, ks=ks: e.dma_start(out=X.ap[:, ks, tsl(tg)], in_=d_x[:, ks, tsl(tg)])),
                     writes=[X.t(k, tg) for k in range(kh * 4, kh * 4 + 4)], dma="x")

        S.op("act", lambda e: e.activation(out=cact[:], in_=c_f[:], func=AF.Silu), reads=[t_c], writes=[t_cact])

        def load_w(src_aps, dst_view_fn):
            i = cnt["ring"] % 2
            cnt["ring"] += 1
            slot = RING[i]
            for dst_fn, src in src_aps:
                S.op("pool", (lambda e, dst_fn=dst_fn, src=src, slot=slot: e.dma_start(out=dst_fn(slot), in_=src)),
                     writes=[ring_t[i]], dma="r%d" % i)
            return dst_view_fn(slot), ring_t[i]

        def load_w_std(w2d, c0, ncols, k0=0):
            src = w2d.rearrange("(k p) n -> p k n", p=128)

            def view(slot):
                return slot[:, 0:8 * ncols].rearrange("p (k c) -> p k c", k=8)
            pieces = []
            for kh in range(2):
                ks = slice(kh * 4, kh * 4 + 4)
                pieces.append(((lambda slot, ks=ks: view(slot)[:, ks, :]), src[:, k0 + kh * 4:k0 + kh * 4 + 4, c0:c0 + ncols]))
            return load_w(pieces, view)

        def ada_matvec(w2d, ncb, bias_idx, mi):
            ps, pt = next_ps()
            for cb in range(ncb):
                wv, wt = load_w_std(w2d, cb * 1024, 1024)
                for oc in range(8):
                    col = cb * 8 + oc
                    for k in range(KC):
                        S.op("pe", (lambda e, ps=ps, wv=wv, oc=oc, k=k, col=col: e.matmul(
                            ps[:, col:col + 1], wv[:, k, oc * 128:(oc + 1) * 128], cact[:, k:k + 1],
                            start=(k == 0), stop=(k == KC - 1))), reads=[wt, t_cact], writes=[pt])
            n = ncb * 8
            S.op("dve", (lambda e, ps=ps, n=n: e.tensor_tensor(
                out=modv[:, mi, 0:n], in0=ps[:, 0:n],
                in1=vecs[:, bias_idx:bias_idx + ncb, :].rearrange("p a b -> p (a b)"), op=ALU.add)),
                reads=[pt, t_vecs], writes=[t_modv[mi]])

        def mod_part(mi, part):
            return modv[:, mi, part * 8:(part + 1) * 8]

        def derive(mi, part_sc, gain_idx, dst):
            S.op("dve", lambda e: e.tensor_scalar(out=der[:, dst, :], in0=mod_part(mi, part_sc), scalar1=1.0, scalar2=1.0,
                                                  op0=ALU.add, op1=ALU.mult), reads=[t_modv[mi]], writes=[t_der])
            S.op("dve", lambda e: e.tensor_tensor(out=der[:, dst, :], in0=der[:, dst, :], in1=vecs[:, gain_idx, :], op=ALU.mult),
                 reads=[t_der, t_vecs], writes=[t_der])

        def compute_rstd():
            for tg in range(NTG):
                ps, pt = next_ps()
                for k in range(KC):
                    tb, tbt = next_tb()
                    S.op("act", (lambda e, tb=tb, k=k, tg=tg: e.activation(out=tb[:], in_=X.ap[:, k, tsl(tg)], func=AF.Square)),
                         reads=[X.t(k, tg)], writes=[tbt])
                    S.op("pe", (lambda e, ps=ps, tb=tb, k=k: e.matmul(ps[:], ones_b[:], tb[:], start=(k == 0), stop=(k == KC - 1))),
                         reads=[tbt, t_const], writes=[pt])
                tf, tft = next_tf()
                S.op("act", (lambda e, ps=ps, tf=tf: e.activation(out=tf[:], in_=ps[:], func=AF.Ln, bias=eps_t[:, 0:1], scale=1.0 / D)),
                     reads=[pt, t_eps], writes=[tft])
                S.op("act", (lambda e, tf=tf, tg=tg: e.activation(out=rstd_s[:, tsl(tg)], in_=tf[:], func=AF.Exp, scale=-0.5)), reads=[tft], writes=[t_rstd[tg]])

        def norm_mod(dst, a_ap, b_ap, extra_reads):
            for tg in range(NTG):
                for k in range(KC):
                    tf, tft = next_tf()
                    S.op("dve", (lambda e, tf=tf, k=k, tg=tg: e.tensor_tensor(out=tf[:], in0=X.ap[:, k, tsl(tg)], in1=rstd_s[:, tsl(tg)], op=ALU.mult)),
                         reads=[X.t(k, tg), t_rstd[tg]], writes=[tft])
                    S.op("act", (lambda e, tf=tf, k=k, tg=tg: e.activation(out=dst.ap[:, k, tsl(tg)], in_=tf[:], func=AF.Identity,
                                                                            bias=b_ap[:, k:k + 1], scale=a_ap[:, k:k + 1])),
                         reads=[tft] + extra_reads, writes=[dst.t(k, tg)])

        def proj(wv, wt, src, n_oc, evac, tgs=range(NTG), oc_cols=None):
            for tg in tgs:
                for oc in range(n_oc):
                    ps, pt = next_ps()
                    for k in range(KC):
                        lhs = wv[:, k, oc * 128:(oc + 1) * 128] if oc_cols is None else oc_cols(wv, k, oc)
                        S.op("pe", (lambda e, ps=ps, lhs=lhs, k=k, tg=tg: e.matmul(ps[:], lhs, src.ap[:, k, tsl(tg)],
                                                                                   start=(k == 0), stop=(k == KC - 1))),
                             reads=[wt, src.t(k, tg)], writes=[pt])
                    evac(oc, tg, ps, pt)

        def resid_evac(g_ap, extra_reads):
            def ev(oc, tg, ps, pt):
                S.op("dve", (lambda e: e.scalar_tensor_tensor(out=X.ap[:, oc, tsl(tg)], in0=ps[:], scalar=g_ap[:, oc:oc + 1],
                                                              in1=X.ap[:, oc, tsl(tg)], op0=ALU.mult, op1=ALU.add)),
                     reads=[pt, X.t(oc, tg)] + extra_reads, writes=[X.t(oc, tg)])
            return ev

        mv_tasks = []
        mv_t = [Tok(), Tok()]
        mv_state = {"n": 0}

        def make_mv_tasks(w2d, n512, bias_idx, mi):
            src = w2d.rearrange("(k p) n -> p k n", p=128)
            nparts = (n512 * 4) // 8
            bias_flat = vecs[:, bias_idx:bias_idx + nparts, :].rearrange("p a b -> p (a b)")
            for cb in range(n512):
                def task(cb=cb):
                    n = mv_state["n"]
                    mv_state["n"] += 1
                    i = n % 2
                    slot = SCR[:, i * 4096:(i + 1) * 4096].rearrange("p (k c) -> p k c", k=8)
                    extra = ([t for row in t_gb for t in row] + [t for row in t_v for t in row] + [t_vhalo]) if n < 2 else []
                    S.op("pool", (lambda e: e.dma_start(out=slot, in_=src[:, :, cb * 512:(cb + 1) * 512])), writes=[mv_t[i]] + extra, dma="mv%d" % i)
                    ps, pt = next_ps()
                    for oc in range(4):
                        for k in range(KC):
                            S.op("pe", (lambda e, oc=oc, k=k: e.matmul(ps[:, oc:oc + 1], slot[:, k, oc * 128:(oc + 1) * 128], cact[:, k:k + 1],
                                                                      start=(k == 0), stop=(k == KC - 1))), reads=[mv_t[i], t_cact], writes=[pt])
                    c0 = cb * 4
                    S.op("dve", (lambda e: e.tensor_tensor(out=modv[:, mi, c0:c0 + 4], in0=ps[:, 0:4], in1=bias_flat[:, c0:c0 + 4], op=ALU.add)),
                         reads=[pt, t_vecs], writes=[t_modv[mi]])
                mv_tasks.append(task)

        def run_mv_tasks(n):
            for _ in range(n):
                if mv_tasks:
                    mv_tasks.pop(0)()

        def mlp(layer, mi):
            compute_rstd()
            derive(mi, 4, V_NG + 2 * layer + 1, 1)
            norm_mod(H, der[:, 1, :], mod_part(mi, 3), [t_der, t_modv[mi]])
            g2 = mod_part(mi, 5)
            for hb in range(4):
                wv, wt = load_w_std(d_w_mlp1[layer], hb * 1024, 1024)

                def ev1(oc, tg, ps, pt):
                    tf, tft = next_tf()
                    S.op("act", (lambda e: e.activation(out=tf[:], in_=ps[:], func=AF.Relu)), reads=[pt], writes=[tft])
                    S.op("dve", (lambda e: e.tensor_tensor(out=A.ap[:, oc, tsl(tg)], in0=tf[:], in1=tf[:], op=ALU.mult)),
                         reads=[tft], writes=[A.t(oc, tg)])
                proj(wv, wt, H, 8, ev1)
                wv2, wt2 = load_w_std(d_w_mlp2[layer], 0, 1024, k0=hb * 8)
                run_mv_tasks(2)
                proj(wv2, wt2, A, 8, resid_evac(g2, [t_modv[mi]]))
                run_mv_tasks(2)

        compute_rstd()
        ada_matvec(d_w_ada[0], 6, V_BADA, 0)
        derive(0, 1, V_NG + 0, 0)
        norm_mod(H, der[:, 0, :], mod_part(0, 0), [t_der, t_modv[0]])

        gbv = SCR[:, 0:4096].rearrange("p (j t) -> p j t", j=2)
        vv = SCR[:, 4096:4096 + 2 * 2056].rearrange("p (j t) -> p j t", j=2)
        t_gb = [[Tok() for _ in range(NTG)] for _ in range(2)]
        t_v = [[Tok() for _ in range(NTG)] for _ in range(2)]
        t_vhalo = Tok()
        S.op("dve", lambda e: e.memset(vv[:, :, 0:2], 0.0), writes=[t_vhalo])
        w_in_v = d_w_a_in[0].rearrange("(k p) (s c) -> p k s c", p=128, s=3)
        g1 = mod_part(0, 2)
        def mixer_j(wv, wt, jj, j):
            jb = j % 2
            for tg in range(NTG):
                pss = []
                for s in range(3):
                    ps, pt = next_ps()
                    for k in range(KC):
                        S.op("pe", (lambda e, ps=ps, s=s, k=k, tg=tg: e.matmul(ps[:], wv[:, k, s, jj * 128:(jj + 1) * 128], H.ap[:, k, tsl(tg)],
                                                                               start=(k == 0), stop=(k == KC - 1))),
                             reads=[wt, H.t(k, tg)], writes=[pt])
                    pss.append((ps, pt))
                (psb, ptb), (psc, ptc), (psu, ptu) = pss
                S.op("act", (lambda e, psb=psb, tg=tg: e.activation(out=gbv[:, jb, tsl(tg)], in_=psb[:], func=AF.Copy)),
                     reads=[ptb], writes=[t_gb[jb][tg]])
                tb, tbt = next_tb()
                S.op("act", (lambda e, psc=psc, tb=tb: e.activation(out=tb[:], in_=psc[:], func=AF.Copy)), reads=[ptc], writes=[tbt])
                S.op("dve", (lambda e, psu=psu, tb=tb, tg=tg: e.tensor_tensor(out=vv[:, jb, 2 + tg * TGW:2 + (tg + 1) * TGW], in0=psu[:], in1=tb[:], op=ALU.mult)),
                     reads=[ptu, tbt], writes=[t_v[jb][tg]])
            for tg in range(NTG):
                tf, tft = next_tf()
                rd = [t_v[jb][tg], t_vhalo, t_vecs] + ([t_v[jb][tg - 1]] if tg > 0 else [])
                b0 = tg * TGW
                S.op("dve", (lambda e, tf=tf, b0=b0: e.tensor_scalar(out=tf[:], in0=vv[:, jb, b0 + 2:b0 + 2 + TGW], scalar1=vecs[:, V_CONV + 2, j:j + 1], scalar2=None, op0=ALU.mult)),
                     reads=rd, writes=[tft])
                S.op("dve", (lambda e, tf=tf, b0=b0: e.scalar_tensor_tensor(out=tf[:], in0=vv[:, jb, b0 + 1:b0 + 1 + TGW], scalar=vecs[:, V_CONV + 1, j:j + 1], in1=tf[:], op0=ALU.mult, op1=ALU.add)),
                     reads=rd + [tft], writes=[tft])
                S.op("dve", (lambda e, tf=tf, b0=b0: e.scalar_tensor_tensor(out=tf[:], in0=vv[:, jb, b0:b0 + TGW], scalar=vecs[:, V_CONV + 0, j:j + 1], in1=tf[:], op0=ALU.mult, op1=ALU.add)),
                     reads=rd + [tft], writes=[tft])
                S.op("dve", (lambda e, tf=tf, tg=tg: e.tensor_tensor(out=A.ap[:, j, tsl(tg)], in0=tf[:], in1=gbv[:, jb, tsl(tg)], op=ALU.mult)),
                     reads=[tft, t_gb[jb][tg]], writes=[A.t(j, tg)])

        for jp in range(4):
            def view(slot):
                return slot[:, 0:6144].rearrange("p (k s c) -> p k s c", k=8, s=3)
            pieces = []
            for s3 in range(3):
                pieces.append(((lambda slot, s3=s3: view(slot)[:, :, s3, :]), w_in_v[:, :, s3, jp * 256:(jp + 1) * 256]))
            wv, wt = load_w(pieces, view)
            for jj in range(2):
                mixer_j(wv, wt, jj, 2 * jp + jj)
        wv, wt = load_w_std(d_w_a_out[0], 0, 1024)
        proj(wv, wt, A, 8, resid_evac(g1, [t_modv[0]]))
        if stop != "mix0":
            if stop not in ("mix0", "l0"):
                make_mv_tasks(d_w_ada[1], 12, V_BADA + 6, 1)
                make_mv_tasks(d_w_ada_kv, 4, V_BKV, 2)
            mlp(0, 0)
        def chk(name, dumps):
            if stop != name:
                return
            S.barrier()
            for ap, dst in dumps:
                S.op("pool", (lambda e, ap=ap, dst=dst: e.dma_start(out=dst, in_=ap)), dma="out")
            raise StopBuild()

        if stop not in ("mix0", "l0"):
          try:
            G4 = 4
            defpool["v"] = "P6"
            POOLS["P6"] = [0, 1, 2, 3, 4]
            POOLS["M"] = [5, 6, 7]
            t_l1c = Tok()
            wqg_g = SCR[:, 8192:8576].rearrange("p (k c) -> p k c", k=8)
            cw2k = SCR[:, 8576:8832].rearrange("p (c d) -> p c d", c=2)
            cw2v = SCR[:, 8832:8960].rearrange("p (c d) -> p c d", c=2)
            peT_b = SCR[:, 8960:9024]
            S.op("sp", lambda e: e.dma_start(out=hvecs[:], in_=d_hvecs), writes=[t_l1c], dma="c")
            S.op("pool", lambda e: e.dma_start(out=ident_b[:], in_=d_consts[:, 0:128]), writes=[t_l1c], dma="cb")
            S.op("pool", lambda e: e.dma_start(out=bd_ones[:], in_=d_bd), writes=[t_l1c], dma="cb")
            S.op("pool", lambda e: e.dma_start(out=peT_b, in_=d_peT), writes=[t_l1c], dma="cb")
            S.op("pool", lambda e: e.dma_start(out=cw2k[:, :, 0:64], in_=d_cmp_w2[0].rearrange("(c p) d -> p c d", p=128)), writes=[t_l1c], dma="cb")
            S.op("pool", lambda e: e.dma_start(out=cw2k[:, :, 64:128], in_=d_cmp_w2[0].rearrange("(c p) d -> p c d", p=128)), writes=[t_l1c], dma="cb")
            S.op("pool", lambda e: e.dma_start(out=cw2v, in_=d_cmp_w2[1].rearrange("(c p) d -> p c d", p=128)), writes=[t_l1c], dma="cb")
            S.op("pool", lambda e: e.dma_start(out=wqg_g, in_=d_w_qg[0].rearrange("(k p) n -> p k n", p=128)[:, :, 1024:1072]), writes=[t_l1c], dma="cb")
            t_gq = Tok()
            S.op("dve", lambda e: e.tensor_scalar(out=gq[:], in0=hvecs[:, 0:1], scalar1=0.125, scalar2=None, op0=ALU.mult), reads=[t_l1c], writes=[t_gq])

            run_mv_tasks(99)
            compute_rstd()
            derive(2, 1, V_KVG, 2)
            derive(1, 1, V_NG + 2, 3)
            norm_mod(H, der[:, 2, :], mod_part(2, 0), [t_der, t_modv[2]])
            norm_mod(A, der[:, 3, :], mod_part(1, 0), [t_der, t_modv[1]])
            xs_t = [Tok() for _ in range(NTG)]
            for tg in range(NTG):
                for kh in range(2):
                    ks = slice(kh * 4, kh * 4 + 4)
                    S.op("sp", (lambda e, tg=tg, ks=ks: e.dma_start(out=d_xs[:, ks, tsl(tg)], in_=X.ap[:, ks, tsl(tg)])),
                         reads=[X.t(k, tg) for k in range(kh * 4, kh * 4 + 4)], writes=[xs_t[tg]], dma="xs")

            def head_norm(ps, pt, ncol, gain_ap, gain_reads, dst_ap, dst_toks):
                tb, tbt = next_tb()
                S.op("act", (lambda e: e.activation(out=tb[:, 0:ncol], in_=ps[:, 0:ncol], func=AF.Square)), reads=[pt], writes=[tbt])
                ps2, pt2 = next_ps("M")
                S.op("pe", (lambda e: e.matmul(ps2[:, 0:ncol], bd_ones[:], tb[:, 0:ncol], start=True, stop=True)), reads=[tbt, t_l1c], writes=[pt2])
                tf, tft = next_tf()
                S.op("act", (lambda e: e.activation(out=tf[:, 0:ncol], in_=ps2[:, 0:ncol], func=AF.Ln, bias=eps_t[:, 0:1], scale=1.0 / 64)),
                     reads=[pt2, t_eps], writes=[tft])
                tf2, tft2 = next_tf()
                S.op("act", (lambda e: e.activation(out=tf2[:, 0:ncol], in_=tf[:, 0:ncol], func=AF.Exp, scale=-0.5)), reads=[tft], writes=[tft2])
                S.op("dve", (lambda e: e.scalar_tensor_tensor(out=dst_ap, in0=ps[:, 0:ncol], scalar=gain_ap, in1=tf2[:, 0:ncol], op0=ALU.mult, op1=ALU.mult)),
                     reads=[pt, tft2] + gain_reads, writes=dst_toks)

            RAW = TT(SCR[:, 0:8192].rearrange("p (c t) -> p c t", c=4))
            wv, wt = load_w_std(d_w_kv, 0, 512)

            def ev_raw(oc, tg, ps, pt):
                S.op("act", (lambda e: e.activation(out=RAW.ap[:, oc, tsl(tg)], in_=ps[:], func=AF.Copy)), reads=[pt], writes=[RAW.t(oc, tg)])
            proj(wv, wt, H, 4, ev_raw)

            chk("raw", [(RAW.ap, d_y[:, 0:4, :])])
            t_kc = [Tok() for _ in range(G4)]
            t_vc = [Tok() for _ in range(G4)]
            t_vc_ones = Tok()
            S.op("dve", lambda e: e.memset(vcA[:, :, 64:128], 1.0), writes=[t_vc_ones])
            t_cb = Tok()
            for kv in range(2):
                def view1(slot):
                    return slot[:, 0:8192].rearrange("p (l h) -> p l h", l=32)
                src1 = d_cmp_w1[kv].rearrange("(l d) h -> d l h", d=64)
                pieces = [((lambda slot: view1(slot)[0:64, :, :]), src1), ((lambda slot: view1(slot)[64:128, :, :]), src1)]
                cwv, cwt = load_w(pieces, view1)
                psb, ptb = next_ps()
                for hc in range(2):
                    for l in range(32):
                        S.op("pe", (lambda e, hc=hc, l=l, cwv=cwv, psb=psb, kv=kv: e.matmul(
                            psb[:, hc:hc + 1], cwv[0:64, l, hc * 128:(hc + 1) * 128], peT_b[0:64, kv * 32 + l:kv * 32 + l + 1],
                            start=(l == 0), stop=(l == 31))), reads=[cwt, t_l1c], writes=[ptb])
                S.op("dve", (lambda e, psb=psb, kv=kv: e.tensor_copy(out=cbias[:, 2 * kv:2 * kv + 2], in_=psb[:, 0:2])), reads=[ptb], writes=[t_cb])
                for g in range(G4):
                    base = (g % 2) * 64
                    c = kv * 2 + g // 2
                    hids = []
                    for hc in range(2):
                        ps, pt = next_ps()
                        for l in range(32):
                            S.op("pe", (lambda e, ps=ps, l=l, hc=hc, cwv=cwv, base=base, c=c: e.matmul(
                                ps[:, 0:127], cwv[base:base + 64, l, hc * 128:(hc + 1) * 128],
                                RAW.ap[base:base + 64, c, l:l + 16 * 126 + 1:16], start=(l == 0), stop=(l == 31))),
                                reads=[cwt] + [RAW.t(c, tg) for tg in range(NTG)], writes=[pt])
                        z, zt = next_tf()
                        S.op("act", (lambda e, ps=ps, z=z, hc=hc, kv=kv: e.activation(out=z[:, 0:127], in_=ps[:, 0:127], func=AF.Identity,
                                                                                      bias=cbias[:, 2 * kv + hc:2 * kv + hc + 1], scale=1.0)),
                             reads=[pt, t_cb], writes=[zt])
                        u, ut = next_tf()
                        S.op("dve", (lambda e, z=z, u=u: e.tensor_tensor(out=u[:, 0:127], in0=z[:, 0:127], in1=z[:, 0:127], op=ALU.mult)), reads=[zt], writes=[ut])
                        S.op("dve", (lambda e, u=u: e.tensor_scalar(out=u[:, 0:127], in0=u[:, 0:127], scalar1=0.044715, scalar2=1.0, op0=ALU.mult, op1=ALU.add)),
                             reads=[ut], writes=[ut])
                        S.op("dve", (lambda e, z=z, u=u: e.tensor_tensor(out=u[:, 0:127], in0=u[:, 0:127], in1=z[:, 0:127], op=ALU.mult)), reads=[ut, zt], writes=[ut])
                        S.op("act", (lambda e, u=u: e.activation(out=u[:, 0:127], in_=u[:, 0:127], func=AF.Sigmoid, scale=1.5957691216057308)), reads=[ut], writes=[ut])
                        hb_, hbt = next_tb()
                        S.op("dve", (lambda e, z=z, u=u, hb_=hb_: e.tensor_tensor(out=hb_[:, 0:127], in0=u[:, 0:127], in1=z[:, 0:127], op=ALU.mult)), reads=[ut, zt], writes=[hbt])
                        hids.append((hb_, hbt))
                    if kv == 0:
                        ps, pt = next_ps()
                        for hc in range(2):
                            S.op("pe", (lambda e, ps=ps, hc=hc, hb_=hids[hc][0]: e.matmul(ps[:, 0:127], cw2k[:, hc, :], hb_[:, 0:127], start=(hc == 0), stop=(hc == 1))),
                                 reads=[hids[hc][1], t_l1c], writes=[pt])
                        head_norm(ps, pt, 127, hvecs[:, 1:2], [t_l1c], kcT[:, g, 0:127], [t_kc[g]])
                    else:
                        ps, pt = next_ps()
                        for hc in range(2):
                            S.op("pe", (lambda e, ps=ps, hc=hc, hb_=hids[hc][0]: e.matmul(ps[0:127, 0:64], hb_[:, 0:127], cw2v[:, hc, :], start=(hc == 0), stop=(hc == 1))),
                                 reads=[hids[hc][1], t_l1c], writes=[pt])
                        S.op("act", (lambda e, ps=ps, g=g: e.activation(out=vcA[0:127, g, 0:64], in_=ps[0:127, 0:64], func=AF.Copy)), reads=[pt], writes=[t_vc[g]])

            chk("cmp", [(kcT[:].rearrange("p g n -> p (g n)"), d_y[:, 0, 0:512]), (vcA[:].rearrange("p g n -> p (g n)"), d_y[:, 1, 0:512])])
            wv_q, wt_q = load_w_std(d_w_qg[0], 0, 1024)
            S.barrier()
            t_ac = Tok()
            tri_b = SCR[:, 0:128]
            anti_b = SCR[:, 128:256]
            cmpm = SCR[:, 256:2304]
            e128 = SCR[:, 2304:4352].rearrange("p (k j) -> p k j", k=16)
            cmap = SCR[:, 4352:4392]
            force = SCR[:, 4392:4904].bitcast(F32).rearrange("p (q j) -> p q j", q=8)
            gsel = SCR[0:48, 4904:7976].rearrange("p (h b m) -> p h b m", h=8, b=3)
            PTS = [SCR[:, 7976 + i * 1024:7976 + (i + 1) * 1024].rearrange("p (k r c) -> p k r c", k=2, r=2) for i in range(3)]
            pts_t = [Tok() for _ in PTS]
            S.op("pool", lambda e: e.dma_start(out=SCR[:, 0:2304], in_=d_amask), writes=[t_ac], dma="cb")
            S.op("pool", lambda e: e.dma_start(out=SCR[:, 2304:4352], in_=d_e128), writes=[t_ac], dma="cb")
            S.op("pool", lambda e: e.dma_start(out=cmap, in_=d_cmap), writes=[t_ac], dma="cb")
            S.op("sp", lambda e: e.dma_start(out=SCR[:, 4392:4904].bitcast(F32), in_=d_force), writes=[t_ac], dma="c")
            S.op("pool", lambda e: e.dma_start(out=SCR[0:48, 4904:7976], in_=d_gsel), writes=[t_ac], dma="cb")
            XB = XR[:].bitcast(BF16)
            QT = TT(XB[:, 0:16384].rearrange("p (h t) -> p h t", h=8))
            KST = TT(XB[:, 16384:24576].rearrange("p (g t) -> p g t", g=4))
            KWT = TT(XB[:, 24576:32768].rearrange("p (g t) -> p g t", g=4))
            RB = rstd_s[:].bitcast(BF16)
            SIGG = TT(RB[0:48, 0:T])

            def ev_q(oc, tg, ps, pt):
                head_norm(ps, pt, TGW, gq[:, 0:1], [t_gq], QT.ap[:, oc, tsl(tg)], [QT.t(oc, tg)])
            proj(wv_q, wt_q, A, 8, ev_q)
            for tg in range(NTG):
                ps, pt = next_ps()
                for k in range(KC):
                    S.op("pe", (lambda e, ps=ps, k=k, tg=tg: e.matmul(ps[0:48, :], wqg_g[:, k, :], A.ap[:, k, tsl(tg)], start=(k == 0), stop=(k == KC - 1))),
                         reads=[t_l1c, A.t(k, tg)], writes=[pt])
                S.op("act", (lambda e, ps=ps, tg=tg: e.activation(out=SIGG.ap[:, tsl(tg)], in_=ps[0:48, :], func=AF.Sigmoid)), reads=[pt], writes=[SIGG.t(tg)])

            for typ, dst, gi in ((2, KST, 2), (4, KWT, 3)):
                def viewk(slot):
                    return slot[:, 0:4096].rearrange("p (k g r d) -> p k g r d", k=8, g=4, r=2)
                srck = d_w_kv.rearrange("(k p) n -> p k n", p=128)[:, :, typ * 256:(typ + 1) * 256].rearrange("p k (g d) -> p k g d", g=4)
                pieces = [((lambda slot, r=r, gg=gg: viewk(slot)[:, :, gg, r, :]), srck[:, :, gg, :]) for r in range(2) for gg in range(4)]
                kv_, kt_ = load_w(pieces, viewk)

                def ev_k(oc, tg, ps, pt, dst=dst, gi=gi):
                    head_norm(ps, pt, TGW, hvecs[:, gi:gi + 1], [t_l1c], dst.ap[:, oc, tsl(tg)], [dst.t(oc, tg)])
                proj(kv_, kt_, H, 4, ev_k, oc_cols=(lambda wv_, k, oc: wv_[:, k, oc, :, :].rearrange("p r d -> p (r d)")))
            chk("q", [(QT.ap, d_y)])
            chk("k", [(KST.ap, d_y[:, 0:4, :]), (KWT.ap, d_y[:, 4:8, :]), ])

            def viewv(slot):
                return slot[:, 0:4096].rearrange("p (k s c) -> p k s c", k=8, s=2)
            srcv = d_w_kv.rearrange("(k p) n -> p k n", p=128)
            pieces = [((lambda slot: viewv(slot)[:, :, 0, :]), srcv[:, :, 768:1024]), ((lambda slot: viewv(slot)[:, :, 1, :]), srcv[:, :, 1280:1536])]
            vv_, vt_ = load_w(pieces, viewv)
            S.barrier()
            AB = AR[:]
            VS = TT(AB[:, 0:8192].rearrange("p (t g c) -> p t g c", t=16, g=4))
            VW = TT(AB[:, 8192:16384].rearrange("p (t g c) -> p t g c", t=16, g=4))
            t_vones = Tok()
            S.op("dve", lambda e: e.memset(VS.ap[:, :, :, 64:128], 1.0), writes=[t_vones])
            S.op("dve", lambda e: e.memset(VW.ap[:, :, :, 64:128], 1.0), writes=[t_vones])

            for tt in range(16):
                ps, pt = next_ps()
                tg = tt // 4
                for k in range(KC):
                    S.op("pe", (lambda e, ps=ps, k=k, tt=tt: e.matmul(ps[:], H.ap[:, k, tt * 128:(tt + 1) * 128], vv_[:, k, :, :].rearrange("p s c -> p (s c)"),
                                                                    start=(k == 0), stop=(k == KC - 1))), reads=[vt_, H.t(k, tg)], writes=[pt])
                S.op("act", (lambda e, ps=ps, tt=tt: e.activation(out=VS.ap[:, tt, :, 0:64], in_=ps[:, 0:256].rearrange("p (g d) -> p g d", g=4), func=AF.Copy)),
                     reads=[pt, t_vones], writes=[VS.t(tt)])
                S.op("dve", (lambda e, ps=ps, tt=tt: e.tensor_copy(out=VW.ap[:, tt, :, 0:64], in_=ps[:, 256:512].rearrange("p (g d) -> p g d", g=4))),
                     reads=[pt, t_vones], writes=[VW.t(tt)])
            chk("v", [(AB[:, 0:8192], d_y[:, 0:4, :].rearrange("p a b -> p (a b)")), (AB[:, 8192:16384], d_y[:, 4:8, :].rearrange("p a b -> p (a b)"))])
            wo_v, wo_t = load_w_std(d_w_o[0], 0, 1024)
            S.barrier()

            POOLS["M"] = [7]
            HB = HR[:]
            OTG = TT(HB[:, 0:4096].rearrange("p (h t) -> p h t", h=8))
            GBC = HB[:, 4096:7168].rearrange("p (h b t) -> p h b t", h=2, b=3)
            t_gbc = Tok()
            XST = TT(HB[:, 7168:15360].bitcast(F32).rearrange("p (k t) -> p k t", k=8))
            BIAS = [HB[:, 15360 + i * 256:15360 + (i + 1) * 256].rearrange("p (r q) -> p r q", r=2) for i in range(2)]
            bias_t = [Tok(), Tok()]
            for i in range(2):
                S.op("dve", (lambda e, i=i: e.memset(BIAS[i], 0.0)), writes=[bias_t[i]])
            pend = {"tk": None}
            ptc = {"n": 0, "b": 0, "a": 0, "c": 0, "p": 0}
            cmb_lo = [Tok(), Tok()]
            cmb_hi = [Tok(), Tok()]
            cmb_on = [Tok(), Tok()]

            M2 = BIAS_EXTRA
            tri2 = M2[:, 0:256]
            anti2 = M2[:, 256:512]
            t_m2 = Tok()
            S.op("dve", lambda e: e.tensor_copy(out=tri2.rearrange("p (r q) -> p r q", r=2), in_=tri_b.unsqueeze(1).to_broadcast([128, 2, 128])), reads=[t_ac], writes=[t_m2])
            S.op("dve", lambda e: e.tensor_copy(out=anti2.rearrange("p (r q) -> p r q", r=2), in_=anti_b.unsqueeze(1).to_broadcast([128, 2, 128])), reads=[t_ac], writes=[t_m2])
            CM2 = [M2[:, 512 + i * 256:512 + (i + 1) * 256] for i in range(2)]
            cm2_t = [Tok(), Tok()]

            def next_pair():
                i = ptc["p"] % 2
                ptc["p"] += 1
                return PS2[i], ps_t[2 * i]

            def emit_cm2(ci, qt):
                S.op("dve", (lambda e: e.tensor_copy(out=CM2[ci].rearrange("p (r q) -> p r q", r=2),
                                                     in_=cmpm[:, qt * 128:(qt + 1) * 128].unsqueeze(1).to_broadcast([128, 2, 128]))),
                     reads=[t_ac], writes=[cm2_t[ci]])

            def next_pt():
                i = ptc["n"] % 3
                ptc["n"] += 1
                return PTS[i], pts_t[i]

            def qk_tile(bank, bt, colbase, KT, g, kt0, nk, par, qt, mask, bias_i, phase):
                lhs_k = KT.ap[par * 64:(par + 1) * 64, g, kt0:kt0 + nk] if KT is not None else kcT[par * 64:(par + 1) * 64, g, 0:127]
                k_reads = [KT.t(g, kt0 // TGW)] if KT is not None else [t_kc[g]]
                qsl = slice(qt * 128, (qt + 1) * 128)
                q_reads = [QT.t(2 * g, qt // 4), QT.t(2 * g + 1, qt // 4)]
                if phase == 1:
                    S.op("pe", (lambda e: e.matmul(bank[0:nk, colbase:colbase + 256], lhs_k, QT.ap[par * 64:(par + 1) * 64, 2 * g:2 * g + 2, qsl],
                                                   start=(mask is None), stop=True)), reads=k_reads + q_reads, writes=[bt])
                elif mask is None:
                    pass
                elif mask == "bias":
                    kt = kt0 // 128
                    S.op("pe", (lambda e: e.matmul(bank[:, colbase:colbase + 256], e128[:, kt, :], BIAS[bias_i].rearrange("p r q -> p (r q)"), start=True, stop=False)),
                         reads=[t_ac, bias_t[bias_i]], writes=[bt])
                else:
                    mask_ap, mask_reads = mask
                    S.op("pe", (lambda e: e.matmul(bank[0:nk, colbase:colbase + 256], ident_b[:, 0:nk], mask_ap, start=True, stop=False)),
                         reads=[t_l1c] + mask_reads, writes=[bt])

            def branch_steps(g, qt, kind, bias_i, pos, br, acc, acct):
                ps_o, pt_o = next_ps("O")
                out = []
                if kind == "cmp":
                    PT, ptt = next_pt()

                    ci = ptc["a"] % 2
                    ptc["a"] += 1

                    def qk():
                        P2, tP = next_pair()
                        bA, bB, tA, tB = P2[:, 0:TGW], P2[:, TGW:2 * TGW], tP, tP
                        if g == 0 and qt == 0:
                            emit_cm2(ci, qt)
                        for phase in range(2):
                            for par, bank, bt in ((0, bA, tA), (1, bB, tB)):
                                qk_tile(bank, bt, 0, None, g, 0, 127, par, qt, (CM2[ci], [cm2_t[ci]]), None, phase)
                        S.op("act", (lambda e: e.activation(out=PT[0:127, 0, :, :], in_=P2[0:127, :].rearrange("p (r c) -> p r c", r=2)[:, :, 0:256], func=AF.Exp)),
                             reads=[tP], writes=[ptt])
                        nqt = qt + 1 if qt % 4 != 3 else (qt - 3 if g < 3 else qt + 1)
                        if nqt < 16:
                            emit_cm2(1 - ci, nqt)
                        if qt >= 8:
                            pend["tk"] = (lambda: topk_a(g, qt, PT, ptt))

                    def pv():
                        S.op("pe", (lambda e: e.matmul(ps_o[:], vcA[0:127, g, :], PT[0:127, 0, :, :].rearrange("p r c -> p (r c)"), start=True, stop=True)),
                             reads=[ptt, t_vc[g], t_vc_ones], writes=[pt_o])
                        combine(g, qt, pos, br, ps_o, pt_o, acc, acct)
                    return [(qk, pv)]
                KT, V = (KST, VS) if kind == "sel" else (KWT, VW)
                kts = list(range(0, qt + 1)) if kind == "sel" else list(range(max(0, qt - 4), qt + 1))
                for pi in range(0, len(kts), 2):
                    pair = kts[pi:pi + 2]
                    PTp = next_pt()

                    def qk(pair=pair, PTp=PTp):
                        PT, ptt = PTp
                        npair = len(pair)
                        P2, tP = next_pair()
                        banks = ((P2[:, 0:TGW], tP), (P2[:, TGW:2 * TGW], tP))
                        if kind == "sel" and qt >= 8 and pair[0] == 0:
                            topk_b(bias_i)
                        for ktp, kt in enumerate(pair):
                            if kt == qt:
                                mask = (tri2, [t_m2])
                            elif kind == "win" and kt == qt - 4:
                                mask = (anti2, [t_m2])
                            elif kind == "sel" and qt >= 8:
                                mask = "bias"
                            else:
                                mask = None
                            for phase in range(2):
                                for par, (bank, bt) in enumerate(banks):
                                    qk_tile(bank, bt, ktp * 256, KT, g, kt * 128, 128, par, qt, mask, bias_i, phase)
                        S.op("act", (lambda e: e.activation(
                            out=PT.rearrange("p k r c -> p r k c")[:, :, 0:npair, :],
                            in_=P2[:].rearrange("p (r k c) -> p r k c", r=2, k=2)[:, :, 0:npair, :], func=AF.Exp)),
                            reads=[tP], writes=[ptt])
                        if pend["tk"] is not None:
                            pend["tk"]()
                            pend["tk"] = None

                    def pv(pair=pair, PTp=PTp, last=(pi + 2 >= len(kts)), first=(pi == 0)):
                        PT, ptt = PTp
                        if FILLER and not first:
                            S.op("pe", (lambda e: e.matmul(ps_o[:], zeros_b[:], ident_b[:].unsqueeze(1).to_broadcast([128, 4, 128]), start=False, stop=False)),
                                 reads=[t_l1c, t_zero], writes=[pt_o])
                        for ktp, kt in enumerate(pair):
                            S.op("pe", (lambda e, ktp=ktp, kt=kt: e.matmul(ps_o[:], V.ap[:, kt, g, :], PT[:, ktp, :, :].rearrange("p r c -> p (r c)"),
                                                                         start=(kt == kts[0]), stop=(kt == kts[-1]))),
                                 reads=[ptt, V.t(kt), t_vones], writes=[pt_o])
                        if last:
                            combine(g, qt, pos, br, ps_o, pt_o, acc, acct)
                    out.append((qk, pv))
                return out

            def combine(g, qt, pos, br, ps_o, pt_o, acc, acct):
                qi = qt % 4
                ci = ptc["c"] % 2
                ptc["c"] += 1
                T, tlo, thi = TMPF[ci], cmb_lo[ci], cmb_hi[ci]
                if pos == 1:
                    S.op("dve", (lambda e: e.reciprocal(out=T[64:128, :], in_=ps_o[64:128, :])), reads=[pt_o], writes=[thi, tlo])
                else:
                    S.op("act", (lambda e: e.activation(out=T[0:64, :], in_=ps_o[64:128, :], func=AF.Ln, bias=eps_t[64:128, 1:2], scale=1.0)), reads=[pt_o, t_eps], writes=[tlo])
                    S.op("act", (lambda e: e.activation(out=T[64:128, :], in_=T[0:64, :], func=AF.Exp, scale=-1.0)), reads=[tlo], writes=[thi])
                on, ont = TMPF[2][:, ci * 256:(ci + 1) * 256], cmb_on[ci]
                for par in range(2):
                    S.op("dve", (lambda e, par=par: e.tensor_tensor(out=on[par * 64:(par + 1) * 64, :], in0=ps_o[0:64, par * 256:(par + 1) * 256],
                                                                     in1=T[64:128, par * 256:(par + 1) * 256], op=ALU.mult)), reads=[pt_o, thi], writes=[ont])
                onv = on.rearrange("p (h q) -> p h q", h=2)
                accv = acc[:].rearrange("p (h q) -> p h q", h=2)
                Gv = GBC[:, :, br, qi * 128:(qi + 1) * 128]
                ce = "pool"
                if pos == 0:
                    S.op(ce, (lambda e: e.tensor_tensor(out=accv, in0=onv, in1=Gv, op=ALU.mult)), reads=[ont, t_gbc], writes=[acct])
                else:
                    S.op(ce, (lambda e: e.tensor_tensor(out=onv, in0=onv, in1=Gv, op=ALU.mult)), reads=[ont, t_gbc], writes=[ont])
                    if pos == 1:
                        S.op(ce, (lambda e: e.tensor_tensor(out=accv, in0=accv, in1=onv, op=ALU.add)), reads=[ont, acct], writes=[acct])
                    else:
                        S.op(ce, (lambda e: e.tensor_tensor(out=OTG.ap[:, 2 * g:2 * g + 2, qi * 128:(qi + 1) * 128], in0=accv, in1=onv, op=ALU.add)),
                             reads=[ont, acct], writes=[OTG.t(2 * g, 0), OTG.t(2 * g + 1, 0)])

            def topk_a(g, qt, PT, ptt):
                ps_i, pt_i = next_ps("M")
                for c in range(4):
                    S.op("pe", (lambda e, c=c: e.matmul(ps_i[:, c * 33:(c + 1) * 33], PT[0:127, 0, c // 2, (c % 2) * 128:(c % 2 + 1) * 128], cmap[0:127, 0:33],
                                                          start=True, stop=True)), reads=[ptt, t_ac], writes=[pt_i])
                psv = ps_i[:, 0:132].rearrange("p (c j) -> p c j", c=4)
                S.op("dve", (lambda e: e.reciprocal(out=rd4[:], in_=psv[:, :, 32])), reads=[pt_i], writes=[t_tk])
                S.op("dve", (lambda e: e.tensor_scalar(out=scA[:], in0=psv[:, 0, 0:32], scalar1=rd4[:, 0:1], scalar2=None, op0=ALU.mult)), reads=[pt_i, t_tk], writes=[t_tk])
                for c in range(1, 4):
                    S.op("dve", (lambda e, c=c: e.scalar_tensor_tensor(out=scA[:], in0=psv[:, c, 0:32], scalar=rd4[:, c:c + 1], in1=scA[:], op0=ALU.mult, op1=ALU.add)),
                         reads=[pt_i, t_tk], writes=[t_tk])
                S.op("dve", (lambda e: e.tensor_tensor(out=scA[:], in0=scA[:], in1=force[:, qt - 8, :], op=ALU.add)), reads=[t_tk, t_ac], writes=[t_tk])
                S.op("dve", (lambda e: e.max(out=m8a[:], in_=scA[:])), reads=[t_tk], writes=[t_tk])
                S.op("dve", (lambda e: e.match_replace(out=scB[:], in_to_replace=m8a[:], in_values=scA[:], imm_value=-1e30)), reads=[t_tk], writes=[t_tk])
                S.op("dve", (lambda e: e.max(out=m8b[:], in_=scB[:])), reads=[t_tk], writes=[t_tk])
                S.op("dve", (lambda e: e.tensor_scalar(out=selm[:], in0=scA[:], scalar1=m8b[:, 7:8], scalar2=1.0, op0=ALU.is_ge, op1=ALU.subtract)), reads=[t_tk], writes=[t_tk])

            def topk_b(bias_i):
                ps_m, pt_m = next_ps("M")
                S.op("pe", (lambda e: e.matmul(ps_m[0:32, 0:128], selm[:], ident_b[:], start=True, stop=True)), reads=[t_tk, t_l1c], writes=[pt_m])
                S.op("act", (lambda e: e.activation(out=BIAS[bias_i][0:32, :, :], in_=ps_m[0:32, 0:128].unsqueeze(1).to_broadcast([32, 2, 128]), func=AF.Copy, scale=30000.0)),
                     reads=[pt_m], writes=[bias_t[bias_i]])

            on_t = [Tok(), Tok()]
            acc_t = [Tok(), Tok()]
            t_tk = Tok()
            g1 = mod_part(1, 2)
            steps = []

            def emit_xreload(tg):
                for kh in range(2):
                    ks = slice(kh * 4, kh * 4 + 4)
                    S.op("sp", (lambda e, ks=ks: e.dma_start(out=XST.ap[:, ks, :], in_=d_xs[:, ks, tsl(tg)])),
                         reads=[xs_t[tg]], writes=[XST.t(k) for k in range(kh * 4, kh * 4 + 4)], dma="xr")

            def emit_gbc(tg, g):
                for hpl in range(2):
                    for br in range(3):
                        ps, pt = next_ps("OM")
                        S.op("pe", (lambda e, ps=ps, hpl=hpl, br=br: e.matmul(ps[:], gsel[:, 2 * g + hpl, br, :], SIGG.ap[:, tsl(tg)], start=True, stop=True)),
                             reads=[t_ac, SIGG.t(tg)], writes=[pt])
                        S.op("dve", (lambda e, ps=ps, hpl=hpl, br=br: e.tensor_copy(out=GBC[:, hpl, br, :], in_=ps[:])), reads=[pt], writes=[t_gbc])

            def emit_wo(tg):
                def ev_o(oc, tg_, ps, pt):
                    S.op("dve", (lambda e: e.scalar_tensor_tensor(out=XST.ap[:, oc, :], in0=ps[:], scalar=g1[:, oc:oc + 1], in1=XST.ap[:, oc, :], op0=ALU.mult, op1=ALU.add)),
                         reads=[pt, XST.t(oc), t_modv[1]], writes=[XST.t(oc)])
                defpool["v"] = "OM"
                proj(wo_v, wo_t, OTG, 8, ev_o, tgs=[0])
                defpool["v"] = "P6"
                for kh in range(2):
                    ks = slice(kh * 4, kh * 4 + 4)
                    S.op("sp", (lambda e, ks=ks: e.dma_start(out=d_xs[:, ks, tsl(tg)], in_=XST.ap[:, ks, :])),
                         reads=[XST.t(k) for k in range(kh * 4, kh * 4 + 4)], writes=[xs_t[tg]], dma="xw")

            for tg in range(NTG):
                steps.append((None, (lambda tg=tg: emit_xreload(tg))))
                for g in range(G4):
                    steps.append((None, (lambda tg=tg, g=g: emit_gbc(tg, g))))
                    for qi in range(4):
                        qt = tg * 4 + qi
                        ai = ptc["b"] % 2
                        ptc["b"] += 1
                        acc, acct = ACC[ai], acc_t[ai]
                        steps += branch_steps(g, qt, "cmp", ai, 0, 0, acc, acct)
                        steps += branch_steps(g, qt, "win", ai, 1, 2, acc, acct)
                        steps += branch_steps(g, qt, "sel", ai, 2, 1, acc, acct)
                steps.append((None, (lambda tg=tg: emit_wo(tg))))
            prev_pv = None
            for qk_fn, pv_fn in steps:
                if qk_fn is not None:
                    qk_fn()
                if prev_pv is not None:
                    prev_pv()
                prev_pv = pv_fn
            if prev_pv is not None:
                prev_pv()
            S.barrier()
            for tg in range(NTG):
                for kh in range(2):
                    ks = slice(kh * 4, kh * 4 + 4)
                    S.op("sp", (lambda e, tg=tg, ks=ks: e.dma_start(out=X.ap[:, ks, tsl(tg)], in_=d_xs[:, ks, tsl(tg)])),
                         reads=[xs_t[tg]], writes=[X.t(k, tg) for k in range(kh * 4, kh * 4 + 4)], dma="x2")
            defpool["v"] = "ALL"
            if stop != "mix1":
                mlp(1, 1)
          except StopBuild:
            S.emit(final_dma_keys=["out"])
            return nc

        for tg in range(NTG):
            for kh in range(2):
                ks = slice(kh * 4, kh * 4 + 4)
                S.op("sp", (lambda e, tg=tg, ks=ks: e.dma_start(out=d_y[:, ks, tsl(tg)], in_=X.ap[:, ks, tsl(tg)])),
                     reads=[X.t(k, tg) for k in range(kh * 4, kh * 4 + 4)], dma="out")
        S.emit(final_dma_keys=["out"])
    return nc


def _fm(v):
    return np.ascontiguousarray(v.reshape(KC, 128).T)


def _const_tables():
    f32 = np.float32
    NEGM = -30000.0
    j = np.arange(128)[:, None]
    t = np.arange(128)[None, :]
    tri = np.where(j <= t, 0.0, NEGM)
    anti = np.where(j > t, 0.0, NEGM)
    n = np.arange(128)[:, None]
    tt = np.arange(T)[None, :]
    cmpm = np.where((16 * n + 31 <= tt) & (n < 127), 0.0, NEGM)
    amask = np.concatenate([tri, anti, cmpm], axis=1).astype(f32)
    e128 = np.zeros((128, 16, 128), f32)
    for kt in range(16):
        for jj in range(128):
            e128[2 * kt + jj // 64, kt, jj] = 1.0
    cmap = np.zeros((128, 40), f32)
    c0 = np.arange(127)[:, None] * 16
    s0 = np.arange(32)[None, :] * 64
    ov = np.minimum(c0 + 32, s0 + 64) - np.maximum(c0, s0)
    cmap[:127, :32] = np.clip(ov, 0, None) / 32.0
    cmap[:127, 32] = 1.0
    force = np.zeros((128, 8, 32), f32)
    for q in range(8):
        tq = 128 * (q + 8) + np.arange(128)
        cur = tq // 64
        jb = np.arange(32)[None, :]
        forced = (jb == 0) | (jb == cur[:, None]) | (jb == cur[:, None] - 1)
        force[:, q, :] = np.where(forced, 1e4, np.where(jb > cur[:, None], -1e4, 0.0))
    gsel = np.zeros((48, 8, 3, 128), f32)
    for hp in range(8):
        for br in range(3):
            for m in range(128):
                gsel[(2 * hp + m // 64) * 3 + br, hp, br, m] = 1.0
    p = np.arange(128)
    bd = (p[:, None] // 64 == p[None, :] // 64).astype(f32)
    return {"amask": amask, "e128": e128.reshape(128, 2048), "cmap": cmap, "force": force.reshape(128, 256),
            "gsel": gsel.reshape(48, 3072), "bdones": bd}


def prep_inputs(inputs):
    f32 = np.float32
    g = {k: np.asarray(v, dtype=f32) for k, v in inputs.items()}
    vecs = np.zeros((128, NV, KC), f32)
    for i in range(2):
        for j in range(2):
            vecs[:, V_NG + 2 * i + j, :] = _fm(g["norm_gain"][i, j])
        for part in range(6):
            vecs[:, V_BADA + 6 * i + part, :] = _fm(g["b_ada"][i, part * D:(part + 1) * D])
    vecs[:, V_KVG, :] = _fm(g["kv_norm_gain"])
    for part in range(2):
        vecs[:, V_BKV + part, :] = _fm(g["b_ada_kv"][part * D:(part + 1) * D])
    for j in range(3):
        vecs[:, V_CONV + j, :] = _fm(g["conv_w"][0, j])
    consts = np.concatenate([np.eye(128, dtype=f32), np.ones((128, 128), f32)], axis=1)
    p64 = np.arange(128) % 64
    hvecs = np.zeros((128, 4), f32)
    hvecs[:, 0] = g["q_gain"][0, p64]
    for i in range(3):
        hvecs[:, 1 + i] = g["k_gain"][i, p64]
    peT = np.zeros((128, 64), f32)
    for kv in range(2):
        peT[:, kv * 32:(kv + 1) * 32] = g["cmp_pe"][kv][:, p64].T
    shared = {
        "vecs": vecs, "consts": consts, "hvecs": hvecs, "peT": peT,
        "w_ada": g["w_ada"], "w_a_in": g["w_a_in"], "w_a_out": g["w_a_out"],
        "w_mlp1": g["w_mlp1"], "w_mlp2": g["w_mlp2"],
        "w_ada_kv": g["w_ada_kv"], "w_kv": g["w_kv"], "w_qg": g["w_qg"], "w_o": g["w_o"],
        "cmp_w1": g["cmp_w1"], "cmp_w2": g["cmp_w2"],
    }
    shared.update(_const_tables())
    in_maps = []
    for b in range(N_CORES):
        m = dict(shared)
        xT = g["x"][b].T.reshape(KC, 128, T).transpose(1, 0, 2)
        m["xT"] = np.ascontiguousarray(xT)
        m["cT"] = _fm(g["c"][b])
        in_maps.append(m)
    return in_maps


def post_outputs(results):
    outs = []
    for r in results:
        yT = np.asarray(r["yT"])
        outs.append(yT.transpose(2, 1, 0).reshape(T, D))
    return np.stack(outs, axis=0).astype(np.float32)


def kernel(**inputs):
    in_maps = prep_inputs(inputs)
    nc = build_program(DEBUG_STOP)
    res = run_bass_kernel_spmd(nc, in_maps, core_ids=list(range(N_CORES)))
    return post_outputs(res.results)
```

```python
import numpy as np
from contextlib import ExitStack
import concourse.bass as bass
import concourse.mybir as mybir
from concourse.bass_utils import run_bass_kernel_spmd

F32 = mybir.dt.float32
BF16 = mybir.dt.bfloat16
AF = mybir.ActivationFunctionType
ALU = mybir.AluOpType

D = 1024
T = 2048
KC = 8
NTG = 4
TGW = 512
EPS = 1e-6
N_CORES = 8

V_NG = 0
V_KVG = 4
V_BADA = 5
V_BKV = 17
V_CONV = 19
NV = 22

DEBUG_STOP = None
FILLER = False


class Tok:
    __slots__ = ("ws", "rs", "rdma", "excl")

    def __init__(self, excl=False):
        self.ws = []
        self.rs = {}
        self.rdma = []
        self.excl = excl


class Op:
    __slots__ = ("eng", "fn", "deps", "dma", "signal", "sigval", "dmaval", "dsem")

    def __init__(self, eng, fn, dma):
        self.eng = eng
        self.fn = fn
        self.dma = dma
        self.deps = ()
        self.signal = False
        self.sigval = 0
        self.dmaval = 0
        self.dsem = None


ENGS = ["pe", "act", "dve", "pool", "sp"]
DMA_SEMS = 16


class Sched:
    def __init__(self, nc):
        self.nc = nc
        self.ops = {e: [] for e in ENGS}
        self.dma_hist = {e: [] for e in ENGS}

    def op(self, eng, fn, reads=(), writes=(), dma=None):
        o = Op(eng, fn, dma)
        ex = [t for t in reads if t.excl]
        if ex:
            reads = [t for t in reads if not t.excl]
            writes = list(writes) + ex
        deps = set()
        for t in reads:
            deps.update(t.ws)
        for t in writes:
            deps.update(t.ws)
            deps.update(t.rs.values())
            deps.update(t.rdma)
        if dma is not None:
            deps = {d for d in deps if d.dma != dma}
            hist = self.dma_hist[eng]
            n = len(hist)
            o.dsem = (eng, n % DMA_SEMS)
            o.dmaval = 16 * (n // DMA_SEMS + 1)
            if n >= DMA_SEMS:
                deps.add(hist[n - DMA_SEMS])
            hist.append(o)
        o.deps = tuple(deps)
        for t in reads:
            if dma is not None:
                t.rdma.append(o)
            else:
                t.rs[eng] = o
        for t in writes:
            if dma is not None and t.ws and not t.rs and not t.rdma and all(w.dma == dma for w in t.ws):
                t.ws.append(o)
            else:
                t.ws = [o]
            t.rs = {}
            t.rdma = []
        self.ops[eng].append(o)
        return o

    def barrier(self, engines=("pe", "act", "dve", "sp", "pool")):
        lasts = []
        for e in ENGS:
            for o in reversed(self.ops[e]):
                if o.dma is None and o.fn is not None:
                    lasts.append(o)
                    break
            lasts += self.dma_hist[e][-DMA_SEMS:]
        for e in engines:
            o = Op(e, None, None)
            o.deps = tuple(lasts)
            self.ops[e].append(o)

    def emit(self, final_dma_keys=()):
        nc = self.nc
        for e in ENGS:
            for o in self.ops[e]:
                for d in o.deps:
                    if d.dma is None:
                        if d.eng == "pe" and o.eng == "pe" and o.dma is None and o.fn is not None:
                            continue
                        d.signal = True
        for e in ENGS:
            c = 0
            for o in self.ops[e]:
                if o.dma is None and o.signal:
                    c += 1
                    o.sigval = c
        with ExitStack() as es:
            esem = {e: es.enter_context(nc.semaphore("s_" + e)) for e in ENGS}
            dsem = {}
            for e in ENGS:
                for i in range(min(DMA_SEMS, len(self.dma_hist[e]))):
                    dsem[(e, i)] = es.enter_context(nc.semaphore("d_%s%d" % (e, i)))
            block = es.enter_context(nc.Block())

            def run(e, eng):
                waited = {}
                for o in self.ops[e]:
                    need = {}
                    for d in o.deps:
                        if d.dma is not None:
                            key, sem, val = ("d",) + d.dsem, dsem[d.dsem], d.dmaval
                        else:
                            if d.eng == "pe" and e == "pe" and o.dma is None and o.fn is not None:
                                continue
                            key, sem, val = ("e", d.eng), esem[d.eng], d.sigval
                        if val > need.get(key, (None, 0))[1]:
                            need[key] = (sem, val)
                    for key, (sem, val) in need.items():
                        if waited.get(key, 0) >= val:
                            continue
                        waited[key] = val
                        eng.wait_ge(sem, val)
                    if o.fn is None:
                        continue
                    ins = o.fn(eng)
                    if o.dma is not None:
                        ins.then_inc(dsem[o.dsem], 16)
                    elif o.signal:
                        ins.then_inc(esem[e], 1)
                if e == "sp":
                    fin = {}
                    for q in ENGS:
                        for d in self.dma_hist[q]:
                            if d.dma in final_dma_keys:
                                fin[d.dsem] = max(fin.get(d.dsem, 0), d.dmaval)
                    for k, v in fin.items():
                        if waited.get(("d",) + k, 0) < v:
                            eng.wait_ge(dsem[k], v)

            block.sync(lambda eng: run("sp", eng))
            block.scalar(lambda eng: run("act", eng))
            block.vector(lambda eng: run("dve", eng))
            block.gpsimd(lambda eng: run("pool", eng))
            block.tensor(lambda eng: run("pe", eng))


class StopBuild(Exception):
    pass


class TT:
    def __init__(self, ap):
        self.ap = ap
        self.toks = {}

    def t(self, *key):
        tk = self.toks.get(key)
        if tk is None:
            tk = self.toks[key] = Tok()
        return tk

    def all(self):
        return list(self.toks.values())


def build_program(stop=None):
    nc = bass.Bass("TRN2", target_bir_lowering=False)

    def din(name, shape):
        return nc.dram_tensor(name, list(shape), F32, kind="ExternalInput").ap()

    d_x = din("xT", [128, KC, T])
    d_c = din("cT", [128, KC])
    d_vecs = din("vecs", [128, NV, KC])
    d_consts = din("consts", [128, 256])
    d_w_ada = din("w_ada", [2, D, 6 * D])
    d_w_a_in = din("w_a_in", [1, D, 3 * D])
    d_w_a_out = din("w_a_out", [1, D, D])
    d_w_mlp1 = din("w_mlp1", [2, D, 4 * D])
    d_w_mlp2 = din("w_mlp2", [2, 4 * D, D])
    d_w_ada_kv = din("w_ada_kv", [D, 2 * D])
    d_w_kv = din("w_kv", [D, 1536])
    d_w_qg = din("w_qg", [1, D, 1072])
    d_w_o = din("w_o", [1, D, D])
    d_cmp_w1 = din("cmp_w1", [2, 2048, 256])
    d_cmp_w2 = din("cmp_w2", [2, 256, 64])
    d_hvecs = din("hvecs", [128, 4])
    d_peT = din("peT", [128, 64])
    d_amask = din("amask", [128, 256 + 2048])
    d_e128 = din("e128", [128, 2048])
    d_cmap = din("cmap", [128, 40])
    d_force = din("force", [128, 256])
    d_gsel = din("gsel", [48, 3072])
    d_bd = din("bdones", [128, 128])
    d_y = nc.dram_tensor("yT", [128, KC, T], F32, kind="ExternalOutput").ap()
    d_xs = nc.dram_tensor("xs_scratch", [128, KC, T], F32).ap()

    S = Sched(nc)
    es = ExitStack()
    with es:
        def sb(name, shape, dt):
            return es.enter_context(nc.sbuf_tensor(name, list(shape), dt))

        XR = sb("XR", [128, KC * T], F32)
        HR = sb("HR", [128, KC * T], BF16)
        AR = sb("AR", [128, KC * T], BF16)
        RING = [sb("RING%d" % i, [128, 8192], BF16) for i in range(2)]
        ring_t = [Tok(), Tok()]
        rstd_s = sb("rstd", [128, T], F32)
        TMPF = [sb("tmpf%d" % i, [128, TGW], F32) for i in range(3)]
        tmpf_t = [Tok() for _ in TMPF]
        TMPB = [sb("tmpb%d" % i, [128, TGW], BF16) for i in range(3)]
        tmpb_t = [Tok() for _ in TMPB]
        SCR = sb("SCR", [128, 11048], BF16)
        ones_b = sb("ones_b", [128, 128], BF16)
        vecs = sb("vecs_s", [128, NV, KC], F32)
        c_f = sb("c_f", [128, KC], F32)
        cact = sb("cact", [128, KC], BF16)
        modv = sb("modv", [128, 3, 48], F32)
        der = sb("der", [128, 8, KC], F32)
        hvecs = sb("hvecs_s", [128, 4], F32)
        gq = sb("gq", [128, 1], F32)
        ident_b = sb("ident_b", [128, 128], BF16)
        bd_ones = sb("bd_ones", [128, 128], BF16)
        cbias = sb("cbias", [128, 4], F32)
        kcT = sb("kcT", [128, 4, 128], BF16)
        vcA = sb("vcA", [128, 4, 128], BF16)
        rd4 = sb("rd4", [128, 4], F32)
        scA = sb("scA", [128, 32], F32)
        scB = sb("scB", [128, 32], F32)
        m8a = sb("m8a", [128, 8], F32)
        m8b = sb("m8b", [128, 8], F32)
        selm = sb("selm", [128, 32], BF16)
        ACC = [sb("acc%d" % i, [128, 256], F32) for i in range(2)]
        BIAS_EXTRA = sb("m2", [128, 1024], BF16)
        zeros_b = sb("zeros_b", [128, 128], BF16)
        t_zero = Tok()
        S.op("dve", lambda e: e.memset(zeros_b[:], 0.0), writes=[t_zero])
        PS2 = [es.enter_context(nc.psum_tensor("psd%d" % i, [128, 2 * TGW], F32)) for i in range(2)]
        PS = [PS2[0][:, 0:TGW], PS2[0][:, TGW:2 * TGW], PS2[1][:, 0:TGW], PS2[1][:, TGW:2 * TGW]]
        PS += [es.enter_context(nc.psum_tensor("ps%d" % i, [128, TGW], F32)) for i in range(4, 8)]
        ps_t = [Tok(excl=True) for _ in PS]

        X = TT(XR[:].rearrange("p (k t) -> p k t", k=KC))
        H = TT(HR[:].rearrange("p (k t) -> p k t", k=KC))
        A = TT(AR[:].rearrange("p (k t) -> p k t", k=KC))
        t_const = Tok()
        eps_t = sb("eps_t", [128, 2], F32)
        t_eps = Tok()
        S.op("dve", lambda e: e.memset(eps_t[:, 0:1], EPS), writes=[t_eps])
        S.op("dve", lambda e: e.memset(eps_t[:, 1:2], 1e-30), writes=[t_eps])
        t_vecs = Tok()
        t_c = Tok()
        t_cact = Tok()
        t_modv = [Tok(), Tok(), Tok()]
        t_der = Tok()
        t_rstd = [Tok() for _ in range(NTG)]

        cnt = {"ps": 0, "ring": 0, "tf": 0, "tb": 0}

        POOLS = {"ALL": list(range(8)), "P6": [0, 1, 2, 3, 4, 5], "S": [0, 1, 2, 3], "O": [4, 5, 6], "M": [7], "OM": [4, 5, 6, 7]}
        pcnt = {k: 0 for k in POOLS}

        defpool = {"v": "ALL"}

        def next_ps(pool=None):
            pool = pool or defpool["v"]
            lst = POOLS[pool]
            i = lst[pcnt[pool] % len(lst)]
            pcnt[pool] += 1
            return PS[i], ps_t[i]

        def next_tf():
            i = cnt["tf"] % len(TMPF)
            cnt["tf"] += 1
            return TMPF[i], tmpf_t[i]

        def next_tb():
            i = cnt["tb"] % len(TMPB)
            cnt["tb"] += 1
            return TMPB[i], tmpb_t[i]

        def tsl(tg):
            return slice(tg * TGW, (tg + 1) * TGW)

        S.op("pool", lambda e: e.dma_start(out=ones_b[:], in_=d_consts[:, 128:256]), writes=[t_const], dma="cb")
        S.op("sp", lambda e: e.dma_start(out=vecs[:], in_=d_vecs), writes=[t_vecs], dma="c")
        S.op("sp", lambda e: e.dma_start(out=c_f[:], in_=d_c), writes=[t_c], dma="c")
        for tg in range(NTG):
            for kh in range(2):
                ks = slice(kh * 4, kh * 4 + 4)
                S.op("sp", (lambda e, tg=tg, ks=ks: e.dma_start(out=X.ap[:, ks, tsl(tg)], in_=d_x[:, ks, tsl(tg)])),
                     writes=[X.t(k, tg) for k in range(kh * 4, kh * 4 + 4)], dma="x")

        S.op("act", lambda e: e.activation(out=cact[:], in_=c_f[:], func=AF.Silu), reads=[t_c], writes=[t_cact])

        def load_w(src_aps, dst_view_fn):
            i = cnt["ring"] % 2
            cnt["ring"] += 1
            slot = RING[i]
            for dst_fn, src in src_aps:
                S.op("pool", (lambda e, dst_fn=dst_fn, src=src, slot=slot: e.dma_start(out=dst_fn(slot), in_=src)),
                     writes=[ring_t[i]], dma="r%d" % i)
            return dst_view_fn(slot), ring_t[i]

        def load_w_std(w2d, c0, ncols, k0=0):
            src = w2d.rearrange("(k p) n -> p k n", p=128)

            def view(slot):
                return slot[:, 0:8 * ncols].rearrange("p (k c) -> p k c", k=8)
            pieces = []
            for kh in range(2):
                ks = slice(kh * 4, kh * 4 + 4)
                pieces.append(((lambda slot, ks=ks: view(slot)[:, ks, :]), src[:, k0 + kh * 4:k0 + kh * 4 + 4, c0:c0 + ncols]))
            return load_w(pieces, view)

        def ada_matvec(w2d, ncb, bias_idx, mi):
            ps, pt = next_ps()
            for cb in range(ncb):
                wv, wt = load_w_std(w2d, cb * 1024, 1024)
                for oc in range(8):
                    col = cb * 8 + oc
                    for k in range(KC):
                        S.op("pe", (lambda e, ps=ps, wv=wv, oc=oc, k=k, col=col: e.matmul(
                            ps[:, col:col + 1], wv[:, k, oc * 128:(oc + 1) * 128], cact[:, k:k + 1],
                            start=(k == 0), stop=(k == KC - 1))), reads=[wt, t_cact], writes=[pt])
            n = ncb * 8
            S.op("dve", (lambda e, ps=ps, n=n: e.tensor_tensor(
                out=modv[:, mi, 0:n], in0=ps[:, 0:n],
                in1=vecs[:, bias_idx:bias_idx + ncb, :].rearrange("p a b -> p (a b)"), op=ALU.add)),
                reads=[pt, t_vecs], writes=[t_modv[mi]])

        def mod_part(mi, part):
            return modv[:, mi, part * 8:(part + 1) * 8]

        def derive(mi, part_sc, gain_idx, dst):
            S.op("dve", lambda e: e.tensor_scalar(out=der[:, dst, :], in0=mod_part(mi, part_sc), scalar1=1.0, scalar2=1.0,
                                                  op0=ALU.add, op1=ALU.mult), reads=[t_modv[mi]], writes=[t_der])
            S.op("dve", lambda e: e.tensor_tensor(out=der[:, dst, :], in0=der[:, dst, :], in1=vecs[:, gain_idx, :], op=ALU.mult),
                 reads=[t_der, t_vecs], writes=[t_der])

        def compute_rstd():
            for tg in range(NTG):
                ps, pt = next_ps()
                for k in range(KC):
                    tb, tbt = next_tb()
                    S.op("act", (lambda e, tb=tb, k=k, tg=tg: e.activation(out=tb[:], in_=X.ap[:, k, tsl(tg)], func=AF.Square)),
                         reads=[X.t(k, tg)], writes=[tbt])
                    S.op("pe", (lambda e, ps=ps, tb=tb, k=k: e.matmul(ps[:], ones_b[:], tb[:], start=(k == 0), stop=(k == KC - 1))),
                         reads=[tbt, t_const], writes=[pt])
                tf, tft = next_tf()
                S.op("act", (lambda e, ps=ps, tf=tf: e.activation(out=tf[:], in_=ps[:], func=AF.Ln, bias=eps_t[:, 0:1], scale=1.0 / D)),
                     reads=[pt, t_eps], writes=[tft])
                S.op("act", (lambda e, tf=tf, tg=tg: e.activation(out=rstd_s[:, tsl(tg)], in_=tf[:], func=AF.Exp, scale=-0.5)), reads=[tft], writes=[t_rstd[tg]])

        def norm_mod(dst, a_ap, b_ap, extra_reads):
            for tg in range(NTG):
                for k in range(KC):
                    tf, tft = next_tf()
                    S.op("dve", (lambda e, tf=tf, k=k, tg=tg: e.tensor_tensor(out=tf[:], in0=X.ap[:, k, tsl(tg)], in1=rstd_s[:, tsl(tg)], op=ALU.mult)),
                         reads=[X.t(k, tg), t_rstd[tg]], writes=[tft])
                    S.op("act", (lambda e, tf=tf, k=k, tg=tg: e.activation(out=dst.ap[:, k, tsl(tg)], in_=tf[:], func=AF.Identity,
                                                                            bias=b_ap[:, k:k + 1], scale=a_ap[:, k:k + 1])),
                         reads=[tft] + extra_reads, writes=[dst.t(k, tg)])

        def proj(wv, wt, src, n_oc, evac, tgs=range(NTG), oc_cols=None):
            for tg in tgs:
                for oc in range(n_oc):
                    ps, pt = next_ps()
                    for k in range(KC):
                        lhs = wv[:, k, oc * 128:(oc + 1) * 128] if oc_cols is None else oc_cols(wv, k, oc)
                        S.op("pe", (lambda e, ps=ps, lhs=lhs, k=k, tg=tg: e.matmul(ps[:], lhs, src.ap[:, k, tsl(tg)],
                                                                                   start=(k == 0), stop=(k == KC - 1))),
                             reads=[wt, src.t(k, tg)], writes=[pt])
                    evac(oc, tg, ps, pt)

        def resid_evac(g_ap, extra_reads):
            def ev(oc, tg, ps, pt):
                S.op("dve", (lambda e: e.scalar_tensor_tensor(out=X.ap[:, oc, tsl(tg)], in0=ps[:], scalar=g_ap[:, oc:oc + 1],
                                                              in1=X.ap[:, oc, tsl(tg)], op0=ALU.mult, op1=ALU.add)),
                     reads=[pt, X.t(oc, tg)] + extra_reads, writes=[X.t(oc, tg)])
            return ev

        mv_tasks = []
        mv_t = [Tok(), Tok()]
        mv_state = {"n": 0}

        def make_mv_tasks(w2d, n512, bias_idx, mi):
            src = w2d.rearrange("(k p) n -> p k n", p=128)
            nparts = (n512 * 4) // 8
            bias_flat = vecs[:, bias_idx:bias_idx + nparts, :].rearrange("p a b -> p (a b)")
            for cb in range(n512):
                def task(cb=cb):
                    n = mv_state["n"]
                    mv_state["n"] += 1
                    i = n % 2
                    slot = SCR[:, i * 4096:(i + 1) * 4096].rearrange("p (k c) -> p k c", k=8)
                    extra = ([t for row in t_gb for t in row] + [t for row in t_v for t in row] + [t_vhalo]) if n < 2 else []
                    S.op("pool", (lambda e: e.dma_start(out=slot, in_=src[:, :, cb * 512:(cb + 1) * 512])), writes=[mv_t[i]] + extra, dma="mv%d" % i)
                    ps, pt = next_ps()
                    for oc in range(4):
                        for k in range(KC):
                            S.op("pe", (lambda e, oc=oc, k=k: e.matmul(ps[:, oc:oc + 1], slot[:, k, oc * 128:(oc + 1) * 128], cact[:, k:k + 1],
                                                                      start=(k == 0), stop=(k == KC - 1))), reads=[mv_t[i], t_cact], writes=[pt])
                    c0 = cb * 4
                    S.op("dve", (lambda e: e.tensor_tensor(out=modv[:, mi, c0:c0 + 4], in0=ps[:, 0:4], in1=bias_flat[:, c0:c0 + 4], op=ALU.add)),
                         reads=[pt, t_vecs], writes=[t_modv[mi]])
                mv_tasks.append(task)

        def run_mv_tasks(n):
            for _ in range(n):
                if mv_tasks:
                    mv_tasks.pop(0)()

        def mlp(layer, mi, first_w=None):
            compute_rstd()
            derive(mi, 4, V_NG + 2 * layer + 1, 1)
            norm_mod(H, der[:, 1, :], mod_part(mi, 3), [t_der, t_modv[mi]])
            g2 = mod_part(mi, 5)
            for hb in range(4):
                wv, wt = first_w if (hb == 0 and first_w is not None) else load_w_std(d_w_mlp1[layer], hb * 1024, 1024)

                def ev1(oc, tg, ps, pt):
                    tf, tft = next_tf()
                    S.op("act", (lambda e: e.activation(out=tf[:], in_=ps[:], func=AF.Relu)), reads=[pt], writes=[tft])
                    S.op("dve", (lambda e: e.tensor_tensor(out=A.ap[:, oc, tsl(tg)], in0=tf[:], in1=tf[:], op=ALU.mult)),
                         reads=[tft], writes=[A.t(oc, tg)])
                proj(wv, wt, H, 8, ev1)
                wv2, wt2 = load_w_std(d_w_mlp2[layer], 0, 1024, k0=hb * 8)
                run_mv_tasks(2)
                proj(wv2, wt2, A, 8, resid_evac(g2, [t_modv[mi]]))
                run_mv_tasks(2)

        compute_rstd()
        ada_matvec(d_w_ada[0], 6, V_BADA, 0)
        derive(0, 1, V_NG + 0, 0)
        norm_mod(H, der[:, 0, :], mod_part(0, 0), [t_der, t_modv[0]])

        gbv = SCR[:, 0:4096].rearrange("p (j t) -> p j t", j=2)
        vv = SCR[:, 4096:4096 + 2 * 2056].rearrange("p (j t) -> p j t", j=2)
        t_gb = [[Tok() for _ in range(NTG)] for _ in range(2)]
        t_v = [[Tok() for _ in range(NTG)] for _ in range(2)]
        t_vhalo = Tok()
        S.op("dve", lambda e: e.memset(vv[:, :, 0:2], 0.0), writes=[t_vhalo])
        w_in_v = d_w_a_in[0].rearrange("(k p) (s c) -> p k s c", p=128, s=3)
        g1 = mod_part(0, 2)
        def mixer_j(wv, wt, jj, j):
            jb = j % 2
            for tg in range(NTG):
                pss = []
                for s in range(3):
                    ps, pt = next_ps()
                    for k in range(KC):
                        S.op("pe", (lambda e, ps=ps, s=s, k=k, tg=tg: e.matmul(ps[:], wv[:, k, s, jj * 128:(jj + 1) * 128], H.ap[:, k, tsl(tg)],
                                                                               start=(k == 0), stop=(k == KC - 1))),
                             reads=[wt, H.t(k, tg)], writes=[pt])
                    pss.append((ps, pt))
                (psb, ptb), (psc, ptc), (psu, ptu) = pss
                S.op("act", (lambda e, psb=psb, tg=tg: e.activation(out=gbv[:, jb, tsl(tg)], in_=psb[:], func=AF.Copy)),
                     reads=[ptb], writes=[t_gb[jb][tg]])
                tb, tbt = next_tb()
                S.op("act", (lambda e, psc=psc, tb=tb: e.activation(out=tb[:], in_=psc[:], func=AF.Copy)), reads=[ptc], writes=[tbt])
                S.op("dve", (lambda e, psu=psu, tb=tb, tg=tg: e.tensor_tensor(out=vv[:, jb, 2 + tg * TGW:2 + (tg + 1) * TGW], in0=psu[:], in1=tb[:], op=ALU.mult)),
                     reads=[ptu, tbt], writes=[t_v[jb][tg]])
            for tg in range(NTG):
                tf, tft = next_tf()
                rd = [t_v[jb][tg], t_vhalo, t_vecs] + ([t_v[jb][tg - 1]] if tg > 0 else [])
                b0 = tg * TGW
                S.op("dve", (lambda e, tf=tf, b0=b0: e.tensor_scalar(out=tf[:], in0=vv[:, jb, b0 + 2:b0 + 2 + TGW], scalar1=vecs[:, V_CONV + 2, j:j + 1], scalar2=None, op0=ALU.mult)),
                     reads=rd, writes=[tft])
                S.op("dve", (lambda e, tf=tf, b0=b0: e.scalar_tensor_tensor(out=tf[:], in0=vv[:, jb, b0 + 1:b0 + 1 + TGW], scalar=vecs[:, V_CONV + 1, j:j + 1], in1=tf[:], op0=ALU.mult, op1=ALU.add)),
                     reads=rd + [tft], writes=[tft])
                S.op("dve", (lambda e, tf=tf, b0=b0: e.scalar_tensor_tensor(out=tf[:], in0=vv[:, jb, b0:b0 + TGW], scalar=vecs[:, V_CONV + 0, j:j + 1], in1=tf[:], op0=ALU.mult, op1=ALU.add)),
                     reads=rd + [tft], writes=[tft])
                S.op("dve", (lambda e, tf=tf, tg=tg: e.tensor_tensor(out=A.ap[:, j, tsl(tg)], in0=tf[:], in1=gbv[:, jb, tsl(tg)], op=ALU.mult)),
                     reads=[tft, t_gb[jb][tg]], writes=[A.t(j, tg)])

        for jp in range(4):
            def view(slot):
                return slot[:, 0:6144].rearrange("p (k s c) -> p k s c", k=8, s=3)
            pieces = []
            for s3 in range(3):
                pieces.append(((lambda slot, s3=s3: view(slot)[:, :, s3, :]), w_in_v[:, :, s3, jp * 256:(jp + 1) * 256]))
            wv, wt = load_w(pieces, view)
            for jj in range(2):
                mixer_j(wv, wt, jj, 2 * jp + jj)
        wv, wt = load_w_std(d_w_a_out[0], 0, 1024)
        proj(wv, wt, A, 8, resid_evac(g1, [t_modv[0]]))
        if stop != "mix0":
            if stop not in ("mix0", "l0"):
                make_mv_tasks(d_w_ada[1], 12, V_BADA + 6, 1)
                make_mv_tasks(d_w_ada_kv, 4, V_BKV, 2)
            mlp(0, 0)
        def chk(name, dumps):
            if stop != name:
                return
            S.barrier()
            for ap, dst in dumps:
                S.op("pool", (lambda e, ap=ap, dst=dst: e.dma_start(out=dst, in_=ap)), dma="out")
            raise StopBuild()

        if stop not in ("mix0", "l0"):
          try:
            G4 = 4
            defpool["v"] = "P6"
            POOLS["P6"] = [0, 1, 2, 3, 4]
            POOLS["M"] = [5, 6, 7]
            t_l1c = Tok()
            wqg_g = SCR[:, 8192:8576].rearrange("p (k c) -> p k c", k=8)
            cw2k = SCR[:, 8576:8832].rearrange("p (c d) -> p c d", c=2)
            cw2v = SCR[:, 8832:8960].rearrange("p (c d) -> p c d", c=2)
            peT_b = SCR[:, 8960:9024]
            S.op("sp", lambda e: e.dma_start(out=hvecs[:], in_=d_hvecs), writes=[t_l1c], dma="c")
            S.op("pool", lambda e: e.dma_start(out=ident_b[:], in_=d_consts[:, 0:128]), writes=[t_l1c], dma="cb")
            S.op("pool", lambda e: e.dma_start(out=bd_ones[:], in_=d_bd), writes=[t_l1c], dma="cb")
            S.op("pool", lambda e: e.dma_start(out=peT_b, in_=d_peT), writes=[t_l1c], dma="cb")
            S.op("pool", lambda e: e.dma_start(out=cw2k[:, :, 0:64], in_=d_cmp_w2[0].rearrange("(c p) d -> p c d", p=128)), writes=[t_l1c], dma="cb")
            S.op("pool", lambda e: e.dma_start(out=cw2k[:, :, 64:128], in_=d_cmp_w2[0].rearrange("(c p) d -> p c d", p=128)), writes=[t_l1c], dma="cb")
            S.op("pool", lambda e: e.dma_start(out=cw2v, in_=d_cmp_w2[1].rearrange("(c p) d -> p c d", p=128)), writes=[t_l1c], dma="cb")
            S.op("pool", lambda e: e.dma_start(out=wqg_g, in_=d_w_qg[0].rearrange("(k p) n -> p k n", p=128)[:, :, 1024:1072]), writes=[t_l1c], dma="cb")
            t_gq = Tok()
            S.op("dve", lambda e: e.tensor_scalar(out=gq[:], in0=hvecs[:, 0:1], scalar1=0.125, scalar2=None, op0=ALU.mult), reads=[t_l1c], writes=[t_gq])

            run_mv_tasks(99)
            compute_rstd()
            derive(2, 1, V_KVG, 2)
            derive(1, 1, V_NG + 2, 3)
            norm_mod(H, der[:, 2, :], mod_part(2, 0), [t_der, t_modv[2]])
            norm_mod(A, der[:, 3, :], mod_part(1, 0), [t_der, t_modv[1]])
            xs_t = [Tok() for _ in range(NTG)]
            for tg in range(NTG):
                for kh in range(2):
                    ks = slice(kh * 4, kh * 4 + 4)
                    S.op("sp", (lambda e, tg=tg, ks=ks: e.dma_start(out=d_xs[:, ks, tsl(tg)], in_=X.ap[:, ks, tsl(tg)])),
                         reads=[X.t(k, tg) for k in range(kh * 4, kh * 4 + 4)], writes=[xs_t[tg]], dma="xs")

            def head_norm(ps, pt, ncol, gain_ap, gain_reads, dst_ap, dst_toks):
                tb, tbt = next_tb()
                S.op("act", (lambda e: e.activation(out=tb[:, 0:ncol], in_=ps[:, 0:ncol], func=AF.Square)), reads=[pt], writes=[tbt])
                ps2, pt2 = next_ps("M")
                S.op("pe", (lambda e: e.matmul(ps2[:, 0:ncol], bd_ones[:], tb[:, 0:ncol], start=True, stop=True)), reads=[tbt, t_l1c], writes=[pt2])
                tf, tft = next_tf()
                S.op("act", (lambda e: e.activation(out=tf[:, 0:ncol], in_=ps2[:, 0:ncol], func=AF.Ln, bias=eps_t[:, 0:1], scale=1.0 / 64)),
                     reads=[pt2, t_eps], writes=[tft])
                tf2, tft2 = next_tf()
                S.op("act", (lambda e: e.activation(out=tf2[:, 0:ncol], in_=tf[:, 0:ncol], func=AF.Exp, scale=-0.5)), reads=[tft], writes=[tft2])
                S.op("dve", (lambda e: e.scalar_tensor_tensor(out=dst_ap, in0=ps[:, 0:ncol], scalar=gain_ap, in1=tf2[:, 0:ncol], op0=ALU.mult, op1=ALU.mult)),
                     reads=[pt, tft2] + gain_reads, writes=dst_toks)

            RAW = TT(SCR[:, 0:8192].rearrange("p (c t) -> p c t", c=4))
            wv, wt = load_w_std(d_w_kv, 0, 512)

            def ev_raw(oc, tg, ps, pt):
                S.op("act", (lambda e: e.activation(out=RAW.ap[:, oc, tsl(tg)], in_=ps[:], func=AF.Copy)), reads=[pt], writes=[RAW.t(oc, tg)])
            proj(wv, wt, H, 4, ev_raw)

            chk("raw", [(RAW.ap, d_y[:, 0:4, :])])
            t_kc = [Tok() for _ in range(G4)]
            t_vc = [Tok() for _ in range(G4)]
            t_vc_ones = Tok()
            S.op("dve", lambda e: e.memset(vcA[:, :, 64:128], 1.0), writes=[t_vc_ones])
            t_cb = Tok()
            for kv in range(2):
                def view1(slot):
                    return slot[:, 0:8192].rearrange("p (l h) -> p l h", l=32)
                src1 = d_cmp_w1[kv].rearrange("(l d) h -> d l h", d=64)
                pieces = [((lambda slot: view1(slot)[0:64, :, :]), src1), ((lambda slot: view1(slot)[64:128, :, :]), src1)]
                cwv, cwt = load_w(pieces, view1)
                psb, ptb = next_ps()
                for hc in range(2):
                    for l in range(32):
                        S.op("pe", (lambda e, hc=hc, l=l, cwv=cwv, psb=psb, kv=kv: e.matmul(
                            psb[:, hc:hc + 1], cwv[0:64, l, hc * 128:(hc + 1) * 128], peT_b[0:64, kv * 32 + l:kv * 32 + l + 1],
                            start=(l == 0), stop=(l == 31))), reads=[cwt, t_l1c], writes=[ptb])
                S.op("dve", (lambda e, psb=psb, kv=kv: e.tensor_copy(out=cbias[:, 2 * kv:2 * kv + 2], in_=psb[:, 0:2])), reads=[ptb], writes=[t_cb])
                for g in range(G4):
                    base = (g % 2) * 64
                    c = kv * 2 + g // 2
                    hids = []
                    for hc in range(2):
                        ps, pt = next_ps()
                        for l in range(32):
                            S.op("pe", (lambda e, ps=ps, l=l, hc=hc, cwv=cwv, base=base, c=c: e.matmul(
                                ps[:, 0:127], cwv[base:base + 64, l, hc * 128:(hc + 1) * 128],
                                RAW.ap[base:base + 64, c, l:l + 16 * 126 + 1:16], start=(l == 0), stop=(l == 31))),
                                reads=[cwt] + [RAW.t(c, tg) for tg in range(NTG)], writes=[pt])
                        z, zt = next_tf()
                        S.op("act", (lambda e, ps=ps, z=z, hc=hc, kv=kv: e.activation(out=z[:, 0:127], in_=ps[:, 0:127], func=AF.Identity,
                                                                                      bias=cbias[:, 2 * kv + hc:2 * kv + hc + 1], scale=1.0)),
                             reads=[pt, t_cb], writes=[zt])
                        u, ut = next_tf()
                        S.op("dve", (lambda e, z=z, u=u: e.tensor_tensor(out=u[:, 0:127], in0=z[:, 0:127], in1=z[:, 0:127], op=ALU.mult)), reads=[zt], writes=[ut])
                        S.op("dve", (lambda e, u=u: e.tensor_scalar(out=u[:, 0:127], in0=u[:, 0:127], scalar1=0.044715, scalar2=1.0, op0=ALU.mult, op1=ALU.add)),
                             reads=[ut], writes=[ut])
                        S.op("dve", (lambda e, z=z, u=u: e.tensor_tensor(out=u[:, 0:127], in0=u[:, 0:127], in1=z[:, 0:127], op=ALU.mult)), reads=[ut, zt], writes=[ut])
                        S.op("act", (lambda e, u=u: e.activation(out=u[:, 0:127], in_=u[:, 0:127], func=AF.Sigmoid, scale=1.5957691216057308)), reads=[ut], writes=[ut])
                        hb_, hbt = next_tb()
                        S.op("dve", (lambda e, z=z, u=u, hb_=hb_: e.tensor_tensor(out=hb_[:, 0:127], in0=u[:, 0:127], in1=z[:, 0:127], op=ALU.mult)), reads=[ut, zt], writes=[hbt])
                        hids.append((hb_, hbt))
                    if kv == 0:
                        ps, pt = next_ps()
                        for hc in range(2):
                            S.op("pe", (lambda e, ps=ps, hc=hc, hb_=hids[hc][0]: e.matmul(ps[:, 0:127], cw2k[:, hc, :], hb_[:, 0:127], start=(hc == 0), stop=(hc == 1))),
                                 reads=[hids[hc][1], t_l1c], writes=[pt])
                        head_norm(ps, pt, 127, hvecs[:, 1:2], [t_l1c], kcT[:, g, 0:127], [t_kc[g]])
                    else:
                        ps, pt = next_ps()
                        for hc in range(2):
                            S.op("pe", (lambda e, ps=ps, hc=hc, hb_=hids[hc][0]: e.matmul(ps[0:127, 0:64], hb_[:, 0:127], cw2v[:, hc, :], start=(hc == 0), stop=(hc == 1))),
                                 reads=[hids[hc][1], t_l1c], writes=[pt])
                        S.op("act", (lambda e, ps=ps, g=g: e.activation(out=vcA[0:127, g, 0:64], in_=ps[0:127, 0:64], func=AF.Copy)), reads=[pt], writes=[t_vc[g]])

            chk("cmp", [(kcT[:].rearrange("p g n -> p (g n)"), d_y[:, 0, 0:512]), (vcA[:].rearrange("p g n -> p (g n)"), d_y[:, 1, 0:512])])
            wv_q, wt_q = load_w_std(d_w_qg[0], 0, 1024)
            S.barrier()
            t_ac = Tok()
            tri_b = SCR[:, 0:128]
            anti_b = SCR[:, 128:256]
            cmpm = SCR[:, 256:2304]
            e128 = SCR[:, 2304:4352].rearrange("p (k j) -> p k j", k=16)
            cmap = SCR[:, 4352:4392]
            force = SCR[:, 4392:4904].bitcast(F32).rearrange("p (q j) -> p q j", q=8)
            gsel = SCR[0:48, 4904:7976].rearrange("p (h b m) -> p h b m", h=8, b=3)
            PTS = [SCR[:, 7976 + i * 1024:7976 + (i + 1) * 1024].rearrange("p (k r c) -> p k r c", k=2, r=2) for i in range(3)]
            pts_t = [Tok() for _ in PTS]
            S.op("pool", lambda e: e.dma_start(out=SCR[:, 0:2304], in_=d_amask), writes=[t_ac], dma="cb")
            S.op("pool", lambda e: e.dma_start(out=SCR[:, 2304:4352], in_=d_e128), writes=[t_ac], dma="cb")
            S.op("pool", lambda e: e.dma_start(out=cmap, in_=d_cmap), writes=[t_ac], dma="cb")
            S.op("sp", lambda e: e.dma_start(out=SCR[:, 4392:4904].bitcast(F32), in_=d_force), writes=[t_ac], dma="c")
            S.op("pool", lambda e: e.dma_start(out=SCR[0:48, 4904:7976], in_=d_gsel), writes=[t_ac], dma="cb")
            XB = XR[:].bitcast(BF16)
            QT = TT(XB[:, 0:16384].rearrange("p (h t) -> p h t", h=8))
            KST = TT(XB[:, 16384:24576].rearrange("p (g t) -> p g t", g=4))
            KWT = TT(XB[:, 24576:32768].rearrange("p (g t) -> p g t", g=4))
            RB = rstd_s[:].bitcast(BF16)
            SIGG = TT(RB[0:48, 0:T])

            def ev_q(oc, tg, ps, pt):
                head_norm(ps, pt, TGW, gq[:, 0:1], [t_gq], QT.ap[:, oc, tsl(tg)], [QT.t(oc, tg)])
            proj(wv_q, wt_q, A, 8, ev_q)
            for tg in range(NTG):
                ps, pt = next_ps()
                for k in range(KC):
                    S.op("pe", (lambda e, ps=ps, k=k, tg=tg: e.matmul(ps[0:48, :], wqg_g[:, k, :], A.ap[:, k, tsl(tg)], start=(k == 0), stop=(k == KC - 1))),
                         reads=[t_l1c, A.t(k, tg)], writes=[pt])
                S.op("act", (lambda e, ps=ps, tg=tg: e.activation(out=SIGG.ap[:, tsl(tg)], in_=ps[0:48, :], func=AF.Sigmoid)), reads=[pt], writes=[SIGG.t(tg)])

            for typ, dst, gi in ((2, KST, 2), (4, KWT, 3)):
                def viewk(slot):
                    return slot[:, 0:4096].rearrange("p (k g r d) -> p k g r d", k=8, g=4, r=2)
                srck = d_w_kv.rearrange("(k p) n -> p k n", p=128)[:, :, typ * 256:(typ + 1) * 256].rearrange("p k (g d) -> p k g d", g=4)
                pieces = [((lambda slot, r=r, gg=gg: viewk(slot)[:, :, gg, r, :]), srck[:, :, gg, :]) for r in range(2) for gg in range(4)]
                kv_, kt_ = load_w(pieces, viewk)

                def ev_k(oc, tg, ps, pt, dst=dst, gi=gi):
                    head_norm(ps, pt, TGW, hvecs[:, gi:gi + 1], [t_l1c], dst.ap[:, oc, tsl(tg)], [dst.t(oc, tg)])
                proj(kv_, kt_, H, 4, ev_k, oc_cols=(lambda wv_, k, oc: wv_[:, k, oc, :, :].rearrange("p r d -> p (r d)")))
            chk("q", [(QT.ap, d_y)])
            chk("k", [(KST.ap, d_y[:, 0:4, :]), (KWT.ap, d_y[:, 4:8, :]), ])

            def viewv(slot):
                return slot[:, 0:4096].rearrange("p (k s c) -> p k s c", k=8, s=2)
            srcv = d_w_kv.rearrange("(k p) n -> p k n", p=128)
            pieces = [((lambda slot: viewv(slot)[:, :, 0, :]), srcv[:, :, 768:1024]), ((lambda slot: viewv(slot)[:, :, 1, :]), srcv[:, :, 1280:1536])]
            vv_, vt_ = load_w(pieces, viewv)
            S.barrier()
            AB = AR[:]
            VS = TT(AB[:, 0:8192].rearrange("p (t g c) -> p t g c", t=16, g=4))
            VW = TT(AB[:, 8192:16384].rearrange("p (t g c) -> p t g c", t=16, g=4))
            t_vones = Tok()
            S.op("dve", lambda e: e.memset(VS.ap[:, :, :, 64:128], 1.0), writes=[t_vones])
            S.op("dve", lambda e: e.memset(VW.ap[:, :, :, 64:128], 1.0), writes=[t_vones])

            for tt in range(16):
                ps, pt = next_ps()
                tg = tt // 4
                for k in range(KC):
                    S.op("pe", (lambda e, ps=ps, k=k, tt=tt: e.matmul(ps[:], H.ap[:, k, tt * 128:(tt + 1) * 128], vv_[:, k, :, :].rearrange("p s c -> p (s c)"),
                                                                    start=(k == 0), stop=(k == KC - 1))), reads=[vt_, H.t(k, tg)], writes=[pt])
                S.op("act", (lambda e, ps=ps, tt=tt: e.activation(out=VS.ap[:, tt, :, 0:64], in_=ps[:, 0:256].rearrange("p (g d) -> p g d", g=4), func=AF.Copy)),
                     reads=[pt, t_vones], writes=[VS.t(tt)])
                S.op("dve", (lambda e, ps=ps, tt=tt: e.tensor_copy(out=VW.ap[:, tt, :, 0:64], in_=ps[:, 256:512].rearrange("p (g d) -> p g d", g=4))),
                     reads=[pt, t_vones], writes=[VW.t(tt)])
            chk("v", [(AB[:, 0:8192], d_y[:, 0:4, :].rearrange("p a b -> p (a b)")), (AB[:, 8192:16384], d_y[:, 4:8, :].rearrange("p a b -> p (a b)"))])
            wo_v, wo_t = load_w_std(d_w_o[0], 0, 1024)
            w1_pre = load_w_std(d_w_mlp1[1], 0, 1024)
            S.barrier()

            POOLS["M"] = [7]
            HB = HR[:]
            OTG = TT(HB[:, 0:4096].rearrange("p (h t) -> p h t", h=8))
            GBC = HB[:, 4096:7168].rearrange("p (h b t) -> p h b t", h=2, b=3)
            t_gbc = Tok()
            XST = TT(HB[:, 7168:15360].bitcast(F32).rearrange("p (k t) -> p k t", k=8))
            BIAS = [HB[:, 15360 + i * 256:15360 + (i + 1) * 256].rearrange("p (r q) -> p r q", r=2) for i in range(2)]
            bias_t = [Tok(), Tok()]
            for i in range(2):
                S.op("dve", (lambda e, i=i: e.memset(BIAS[i], 0.0)), writes=[bias_t[i]])
            pend = {"tk": None}
            ptc = {"n": 0, "b": 0, "a": 0, "c": 0, "p": 0}
            cmb_lo = [Tok(), Tok()]
            cmb_hi = [Tok(), Tok()]
            cmb_on = [Tok(), Tok()]

            M2 = BIAS_EXTRA
            tri2 = M2[:, 0:256]
            anti2 = M2[:, 256:512]
            t_m2 = Tok()
            S.op("dve", lambda e: e.tensor_copy(out=tri2.rearrange("p (r q) -> p r q", r=2), in_=tri_b.unsqueeze(1).to_broadcast([128, 2, 128])), reads=[t_ac], writes=[t_m2])
            S.op("dve", lambda e: e.tensor_copy(out=anti2.rearrange("p (r q) -> p r q", r=2), in_=anti_b.unsqueeze(1).to_broadcast([128, 2, 128])), reads=[t_ac], writes=[t_m2])
            CM2 = [M2[:, 512 + i * 256:512 + (i + 1) * 256] for i in range(2)]
            cm2_t = [Tok(), Tok()]

            def next_pair():
                i = ptc["p"] % 2
                ptc["p"] += 1
                return PS2[i], ps_t[2 * i]

            def emit_cm2(ci, qt):
                S.op("dve", (lambda e: e.tensor_copy(out=CM2[ci].rearrange("p (r q) -> p r q", r=2),
                                                     in_=cmpm[:, qt * 128:(qt + 1) * 128].unsqueeze(1).to_broadcast([128, 2, 128]))),
                     reads=[t_ac], writes=[cm2_t[ci]])

            def next_pt():
                i = ptc["n"] % 3
                ptc["n"] += 1
                return PTS[i], pts_t[i]

            def qk_tile(bank, bt, colbase, KT, g, kt0, nk, par, qt, mask, bias_i, phase):
                lhs_k = KT.ap[par * 64:(par + 1) * 64, g, kt0:kt0 + nk] if KT is not None else kcT[par * 64:(par + 1) * 64, g, 0:127]
                k_reads = [KT.t(g, kt0 // TGW)] if KT is not None else [t_kc[g]]
                qsl = slice(qt * 128, (qt + 1) * 128)
                q_reads = [QT.t(2 * g, qt // 4), QT.t(2 * g + 1, qt // 4)]
                if phase == 1:
                    S.op("pe", (lambda e: e.matmul(bank[0:nk, colbase:colbase + 256], lhs_k, QT.ap[par * 64:(par + 1) * 64, 2 * g:2 * g + 2, qsl],
                                                   start=(mask is None), stop=True)), reads=k_reads + q_reads, writes=[bt])
                elif mask is None:
                    pass
                elif mask == "bias":
                    kt = kt0 // 128
                    S.op("pe", (lambda e: e.matmul(bank[:, colbase:colbase + 256], e128[:, kt, :], BIAS[bias_i].rearrange("p r q -> p (r q)"), start=True, stop=False)),
                         reads=[t_ac, bias_t[bias_i]], writes=[bt])
                else:
                    mask_ap, mask_reads = mask
                    S.op("pe", (lambda e: e.matmul(bank[0:nk, colbase:colbase + 256], ident_b[:, 0:nk], mask_ap, start=True, stop=False)),
                         reads=[t_l1c] + mask_reads, writes=[bt])

            def branch_steps(g, qt, kind, bias_i, pos, br, acc, acct):
                ps_o, pt_o = next_ps("O")
                out = []
                if kind == "cmp":
                    PT, ptt = next_pt()

                    ci = ptc["a"] % 2
                    ptc["a"] += 1

                    def qk():
                        P2, tP = next_pair()
                        bA, bB, tA, tB = P2[:, 0:TGW], P2[:, TGW:2 * TGW], tP, tP
                        if g == 0 and qt == 0:
                            emit_cm2(ci, qt)
                        for phase in range(2):
                            for par, bank, bt in ((0, bA, tA), (1, bB, tB)):
                                qk_tile(bank, bt, 0, None, g, 0, 127, par, qt, (CM2[ci], [cm2_t[ci]]), None, phase)
                        S.op("act", (lambda e: e.activation(out=PT[0:127, 0, :, :], in_=P2[0:127, :].rearrange("p (r c) -> p r c", r=2)[:, :, 0:256], func=AF.Exp)),
                             reads=[tP], writes=[ptt])
                        nqt = qt + 1 if qt % 4 != 3 else (qt - 3 if g < 3 else qt + 1)
                        if nqt < 16:
                            emit_cm2(1 - ci, nqt)
                        if qt >= 8:
                            pend["tk"] = (lambda: topk_a(g, qt, PT, ptt))

                    def pv():
                        S.op("pe", (lambda e: e.matmul(ps_o[:], vcA[0:127, g, :], PT[0:127, 0, :, :].rearrange("p r c -> p (r c)"), start=True, stop=True)),
                             reads=[ptt, t_vc[g], t_vc_ones], writes=[pt_o])
                        combine(g, qt, pos, br, ps_o, pt_o, acc, acct)
                    return [(qk, pv)]
                KT, V = (KST, VS) if kind == "sel" else (KWT, VW)
                kts = list(range(0, qt + 1)) if kind == "sel" else list(range(max(0, qt - 4), qt + 1))
                for pi in range(0, len(kts), 2):
                    pair = kts[pi:pi + 2]
                    PTp = next_pt()

                    def qk(pair=pair, PTp=PTp):
                        PT, ptt = PTp
                        npair = len(pair)
                        P2, tP = next_pair()
                        banks = ((P2[:, 0:TGW], tP), (P2[:, TGW:2 * TGW], tP))
                        if kind == "sel" and qt >= 8 and pair[0] == 0:
                            topk_b(bias_i)
                        for ktp, kt in enumerate(pair):
                            if kt == qt:
                                mask = (tri2, [t_m2])
                            elif kind == "win" and kt == qt - 4:
                                mask = (anti2, [t_m2])
                            elif kind == "sel" and qt >= 8:
                                mask = "bias"
                            else:
                                mask = None
                            for phase in range(2):
                                for par, (bank, bt) in enumerate(banks):
                                    qk_tile(bank, bt, ktp * 256, KT, g, kt * 128, 128, par, qt, mask, bias_i, phase)
                        S.op("act", (lambda e: e.activation(
                            out=PT.rearrange("p k r c -> p r k c")[:, :, 0:npair, :],
                            in_=P2[:].rearrange("p (r k c) -> p r k c", r=2, k=2)[:, :, 0:npair, :], func=AF.Exp)),
                            reads=[tP], writes=[ptt])
                        if pend["tk"] is not None:
                            pend["tk"]()
                            pend["tk"] = None

                    def pv(pair=pair, PTp=PTp, last=(pi + 2 >= len(kts)), first=(pi == 0)):
                        PT, ptt = PTp
                        if FILLER and not first:
                            S.op("pe", (lambda e: e.matmul(ps_o[:], zeros_b[:], ident_b[:].unsqueeze(1).to_broadcast([128, 4, 128]), start=False, stop=False)),
                                 reads=[t_l1c, t_zero], writes=[pt_o])
                        for ktp, kt in enumerate(pair):
                            S.op("pe", (lambda e, ktp=ktp, kt=kt: e.matmul(ps_o[:], V.ap[:, kt, g, :], PT[:, ktp, :, :].rearrange("p r c -> p (r c)"),
                                                                         start=(kt == kts[0]), stop=(kt == kts[-1]))),
                                 reads=[ptt, V.t(kt), t_vones], writes=[pt_o])
                        if last:
                            combine(g, qt, pos, br, ps_o, pt_o, acc, acct)
                    out.append((qk, pv))
                return out

            def combine(g, qt, pos, br, ps_o, pt_o, acc, acct):
                qi = qt % 4
                ci = ptc["c"] % 2
                ptc["c"] += 1
                T, tlo, thi = TMPF[ci], cmb_lo[ci], cmb_hi[ci]
                if pos == 1 or (pos == 0 and qt >= 1):
                    S.op("dve", (lambda e: e.reciprocal(out=T[64:128, :], in_=ps_o[64:128, :])), reads=[pt_o], writes=[thi, tlo])
                else:
                    S.op("act", (lambda e: e.activation(out=T[0:64, :], in_=ps_o[64:128, :], func=AF.Ln, bias=eps_t[64:128, 1:2], scale=1.0)), reads=[pt_o, t_eps], writes=[tlo])
                    S.op("act", (lambda e: e.activation(out=T[64:128, :], in_=T[0:64, :], func=AF.Exp, scale=-1.0)), reads=[tlo], writes=[thi])
                on, ont = TMPF[2][:, ci * 256:(ci + 1) * 256], cmb_on[ci]
                for par in range(2):
                    S.op("dve", (lambda e, par=par: e.tensor_tensor(out=on[par * 64:(par + 1) * 64, :], in0=ps_o[0:64, par * 256:(par + 1) * 256],
                                                                     in1=T[64:128, par * 256:(par + 1) * 256], op=ALU.mult)), reads=[pt_o, thi], writes=[ont])
                onv = on.rearrange("p (h q) -> p h q", h=2)
                accv = acc[:].rearrange("p (h q) -> p h q", h=2)
                Gv = GBC[:, :, br, qi * 128:(qi + 1) * 128]
                ce = "pool"
                if pos == 0:
                    S.op(ce, (lambda e: e.tensor_tensor(out=accv, in0=onv, in1=Gv, op=ALU.mult)), reads=[ont, t_gbc], writes=[acct])
                else:
                    S.op(ce, (lambda e: e.tensor_tensor(out=onv, in0=onv, in1=Gv, op=ALU.mult)), reads=[ont, t_gbc], writes=[ont])
                    if pos == 1:
                        S.op(ce, (lambda e: e.tensor_tensor(out=accv, in0=accv, in1=onv, op=ALU.add)), reads=[ont, acct], writes=[acct])
                    else:
                        S.op(ce, (lambda e: e.tensor_tensor(out=OTG.ap[:, 2 * g:2 * g + 2, qi * 128:(qi + 1) * 128], in0=accv, in1=onv, op=ALU.add)),
                             reads=[ont, acct], writes=[OTG.t(2 * g, 0), OTG.t(2 * g + 1, 0)])

            def topk_a(g, qt, PT, ptt):
                ps_i, pt_i = next_ps("M")
                for c in range(4):
                    S.op("pe", (lambda e, c=c: e.matmul(ps_i[:, c * 33:(c + 1) * 33], PT[0:127, 0, c // 2, (c % 2) * 128:(c % 2 + 1) * 128], cmap[0:127, 0:33],
                                                          start=True, stop=True)), reads=[ptt, t_ac], writes=[pt_i])
                psv = ps_i[:, 0:132].rearrange("p (c j) -> p c j", c=4)
                S.op("dve", (lambda e: e.reciprocal(out=rd4[:], in_=psv[:, :, 32])), reads=[pt_i], writes=[t_tk])
                S.op("dve", (lambda e: e.tensor_scalar(out=scA[:], in0=psv[:, 0, 0:32], scalar1=rd4[:, 0:1], scalar2=None, op0=ALU.mult)), reads=[pt_i, t_tk], writes=[t_tk])
                for c in range(1, 4):
                    S.op("dve", (lambda e, c=c: e.scalar_tensor_tensor(out=scA[:], in0=psv[:, c, 0:32], scalar=rd4[:, c:c + 1], in1=scA[:], op0=ALU.mult, op1=ALU.add)),
                         reads=[pt_i, t_tk], writes=[t_tk])
                S.op("dve", (lambda e: e.tensor_tensor(out=scA[:], in0=scA[:], in1=force[:, qt - 8, :], op=ALU.add)), reads=[t_tk, t_ac], writes=[t_tk])
                S.op("dve", (lambda e: e.max(out=m8a[:], in_=scA[:])), reads=[t_tk], writes=[t_tk])
                S.op("dve", (lambda e: e.match_replace(out=scB[:], in_to_replace=m8a[:], in_values=scA[:], imm_value=-1e30)), reads=[t_tk], writes=[t_tk])
                S.op("dve", (lambda e: e.max(out=m8b[:], in_=scB[:])), reads=[t_tk], writes=[t_tk])
                S.op("dve", (lambda e: e.tensor_scalar(out=selm[:], in0=scA[:], scalar1=m8b[:, 7:8], scalar2=1.0, op0=ALU.is_ge, op1=ALU.subtract)), reads=[t_tk], writes=[t_tk])

            def topk_b(bias_i):
                ps_m, pt_m = next_ps("M")
                S.op("pe", (lambda e: e.matmul(ps_m[0:32, 0:128], selm[:], ident_b[:], start=True, stop=True)), reads=[t_tk, t_l1c], writes=[pt_m])
                S.op("act", (lambda e: e.activation(out=BIAS[bias_i][0:32, :, :], in_=ps_m[0:32, 0:128].unsqueeze(1).to_broadcast([32, 2, 128]), func=AF.Copy, scale=30000.0)),
                     reads=[pt_m], writes=[bias_t[bias_i]])

            on_t = [Tok(), Tok()]
            acc_t = [Tok(), Tok()]
            t_tk = Tok()
            g1 = mod_part(1, 2)
            steps = []

            def emit_xreload(tg):
                for kh in range(2):
                    ks = slice(kh * 4, kh * 4 + 4)
                    S.op("sp", (lambda e, ks=ks: e.dma_start(out=XST.ap[:, ks, :], in_=d_xs[:, ks, tsl(tg)])),
                         reads=[xs_t[tg]], writes=[XST.t(k) for k in range(kh * 4, kh * 4 + 4)], dma="xr")

            def emit_gbc(tg, g):
                for hpl in range(2):
                    for br in range(3):
                        ps, pt = next_ps("OM")
                        S.op("pe", (lambda e, ps=ps, hpl=hpl, br=br: e.matmul(ps[:], gsel[:, 2 * g + hpl, br, :], SIGG.ap[:, tsl(tg)], start=True, stop=True)),
                             reads=[t_ac, SIGG.t(tg)], writes=[pt])
                        S.op("dve", (lambda e, ps=ps, hpl=hpl, br=br: e.tensor_copy(out=GBC[:, hpl, br, :], in_=ps[:])), reads=[pt], writes=[t_gbc])

            def emit_wo(tg):
                def ev_o(oc, tg_, ps, pt):
                    S.op("dve", (lambda e: e.scalar_tensor_tensor(out=XST.ap[:, oc, :], in0=ps[:], scalar=g1[:, oc:oc + 1], in1=XST.ap[:, oc, :], op0=ALU.mult, op1=ALU.add)),
                         reads=[pt, XST.t(oc), t_modv[1]], writes=[XST.t(oc)])
                defpool["v"] = "OM"
                proj(wo_v, wo_t, OTG, 8, ev_o, tgs=[0])
                defpool["v"] = "P6"
                for kh in range(2):
                    ks = slice(kh * 4, kh * 4 + 4)
                    S.op("sp", (lambda e, ks=ks: e.dma_start(out=d_xs[:, ks, tsl(tg)], in_=XST.ap[:, ks, :])),
                         reads=[XST.t(k) for k in range(kh * 4, kh * 4 + 4)], writes=[xs_t[tg]], dma="xw")

            for tg in range(NTG):
                steps.append((None, (lambda tg=tg: emit_xreload(tg))))
                for g in range(G4):
                    steps.append((None, (lambda tg=tg, g=g: emit_gbc(tg, g))))
                    for qi in range(4):
                        qt = tg * 4 + qi
                        ai = ptc["b"] % 2
                        ptc["b"] += 1
                        acc, acct = ACC[ai], acc_t[ai]
                        steps += branch_steps(g, qt, "cmp", ai, 0, 0, acc, acct)
                        steps += branch_steps(g, qt, "win", ai, 1, 2, acc, acct)
                        steps += branch_steps(g, qt, "sel", ai, 2, 1, acc, acct)
                steps.append((None, (lambda tg=tg: emit_wo(tg))))
            prev_pv = None
            for qk_fn, pv_fn in steps:
                if qk_fn is not None:
                    qk_fn()
                if prev_pv is not None:
                    prev_pv()
                prev_pv = pv_fn
            if prev_pv is not None:
                prev_pv()
            S.barrier()
            for tg in range(NTG):
                for kh in range(2):
                    ks = slice(kh * 4, kh * 4 + 4)
                    S.op("sp", (lambda e, tg=tg, ks=ks: e.dma_start(out=X.ap[:, ks, tsl(tg)], in_=d_xs[:, ks, tsl(tg)])),
                         reads=[xs_t[tg]], writes=[X.t(k, tg) for k in range(kh * 4, kh * 4 + 4)], dma="x2")
            defpool["v"] = "ALL"
            if stop != "mix1":
                mlp(1, 1, first_w=w1_pre)
          except StopBuild:
            S.emit(final_dma_keys=["out"])
            return nc

        for tg in range(NTG):
            for kh in range(2):
                ks = slice(kh * 4, kh * 4 + 4)
                S.op("sp", (lambda e, tg=tg, ks=ks: e.dma_start(out=d_y[:, ks, tsl(tg)], in_=X.ap[:, ks, tsl(tg)])),
                     reads=[X.t(k, tg) for k in range(kh * 4, kh * 4 + 4)], dma="out")
        S.emit(final_dma_keys=["out"])
    return nc


def _fm(v):
    return np.ascontiguousarray(v.reshape(KC, 128).T)


def _const_tables():
    f32 = np.float32
    NEGM = -30000.0
    j = np.arange(128)[:, None]
    t = np.arange(128)[None, :]
    tri = np.where(j <= t, 0.0, NEGM)
    anti = np.where(j > t, 0.0, NEGM)
    n = np.arange(128)[:, None]
    tt = np.arange(T)[None, :]
    cmpm = np.where((16 * n + 31 <= tt) & (n < 127), 0.0, NEGM)
    amask = np.concatenate([tri, anti, cmpm], axis=1).astype(f32)
    e128 = np.zeros((128, 16, 128), f32)
    for kt in range(16):
        for jj in range(128):
            e128[2 * kt + jj // 64, kt, jj] = 1.0
    cmap = np.zeros((128, 40), f32)
    c0 = np.arange(127)[:, None] * 16
    s0 = np.arange(32)[None, :] * 64
    ov = np.minimum(c0 + 32, s0 + 64) - np.maximum(c0, s0)
    cmap[:127, :32] = np.clip(ov, 0, None) / 32.0
    cmap[:127, 32] = 1.0
    force = np.zeros((128, 8, 32), f32)
    for q in range(8):
        tq = 128 * (q + 8) + np.arange(128)
        cur = tq // 64
        jb = np.arange(32)[None, :]
        forced = (jb == 0) | (jb == cur[:, None]) | (jb == cur[:, None] - 1)
        force[:, q, :] = np.where(forced, 1e4, np.where(jb > cur[:, None], -1e4, 0.0))
    gsel = np.zeros((48, 8, 3, 128), f32)
    for hp in range(8):
        for br in range(3):
            for m in range(128):
                gsel[(2 * hp + m // 64) * 3 + br, hp, br, m] = 1.0
    p = np.arange(128)
    bd = (p[:, None] // 64 == p[None, :] // 64).astype(f32)
    return {"amask": amask, "e128": e128.reshape(128, 2048), "cmap": cmap, "force": force.reshape(128, 256),
            "gsel": gsel.reshape(48, 3072), "bdones": bd}


def prep_inputs(inputs):
    f32 = np.float32
    g = {k: np.asarray(v, dtype=f32) for k, v in inputs.items()}
    vecs = np.zeros((128, NV, KC), f32)
    for i in range(2):
        for j in range(2):
            vecs[:, V_NG + 2 * i + j, :] = _fm(g["norm_gain"][i, j])
        for part in range(6):
            vecs[:, V_BADA + 6 * i + part, :] = _fm(g["b_ada"][i, part * D:(part + 1) * D])
    vecs[:, V_KVG, :] = _fm(g["kv_norm_gain"])
    for part in range(2):
        vecs[:, V_BKV + part, :] = _fm(g["b_ada_kv"][part * D:(part + 1) * D])
    for j in range(3):
        vecs[:, V_CONV + j, :] = _fm(g["conv_w"][0, j])
    consts = np.concatenate([np.eye(128, dtype=f32), np.ones((128, 128), f32)], axis=1)
    p64 = np.arange(128) % 64
    hvecs = np.zeros((128, 4), f32)
    hvecs[:, 0] = g["q_gain"][0, p64]
    for i in range(3):
        hvecs[:, 1 + i] = g["k_gain"][i, p64]
    peT = np.zeros((128, 64), f32)
    for kv in range(2):
        peT[:, kv * 32:(kv + 1) * 32] = g["cmp_pe"][kv][:, p64].T
    shared = {
        "vecs": vecs, "consts": consts, "hvecs": hvecs, "peT": peT,
        "w_ada": g["w_ada"], "w_a_in": g["w_a_in"], "w_a_out": g["w_a_out"],
        "w_mlp1": g["w_mlp1"], "w_mlp2": g["w_mlp2"],
        "w_ada_kv": g["w_ada_kv"], "w_kv": g["w_kv"], "w_qg": g["w_qg"], "w_o": g["w_o"],
        "cmp_w1": g["cmp_w1"], "cmp_w2": g["cmp_w2"],
    }
    shared.update(_const_tables())
    in_maps = []
    for b in range(N_CORES):
        m = dict(shared)
        xT = g["x"][b].T.reshape(KC, 128, T).transpose(1, 0, 2)
        m["xT"] = np.ascontiguousarray(xT)
        m["cT"] = _fm(g["c"][b])
        in_maps.append(m)
    return in_maps


def post_outputs(results):
    outs = []
    for r in results:
        yT = np.asarray(r["yT"])
        outs.append(yT.transpose(2, 1, 0).reshape(T, D))
    return np.stack(outs, axis=0).astype(np.float32)


def kernel(**inputs):
    in_maps = prep_inputs(inputs)
    nc = build_program(DEBUG_STOP)
    res = run_bass_kernel_spmd(nc, in_maps, core_ids=list(range(N_CORES)))
    return post_outputs(res.results)
```

```python
import numpy as np
from contextlib import ExitStack
import concourse.bass as bass
import concourse.mybir as mybir
from concourse.bass_utils import run_bass_kernel_spmd

F32 = mybir.dt.float32
BF16 = mybir.dt.bfloat16
AF = mybir.ActivationFunctionType
ALU = mybir.AluOpType

D = 1024
T = 2048
KC = 8
NTG = 4
TGW = 512
EPS = 1e-6
N_CORES = 8

V_NG = 0
V_KVG = 4
V_BADA = 5
V_BKV = 17
V_CONV = 19
NV = 22

DEBUG_STOP = None
FILLER = False


class Tok:
    __slots__ = ("ws", "rs", "rdma", "excl")

    def __init__(self, excl=False):
        self.ws = []
        self.rs = {}
        self.rdma = []
        self.excl = excl


class Op:
    __slots__ = ("eng", "fn", "deps", "dma", "signal", "sigval", "dmaval", "dsem")

    def __init__(self, eng, fn, dma):
        self.eng = eng
        self.fn = fn
        self.dma = dma
        self.deps = ()
        self.signal = False
        self.sigval = 0
        self.dmaval = 0
        self.dsem = None


ENGS = ["pe", "act", "dve", "pool", "sp"]
DMA_SEMS = 16


class Sched:
    def __init__(self, nc):
        self.nc = nc
        self.ops = {e: [] for e in ENGS}
        self.dma_hist = {e: [] for e in ENGS}

    def op(self, eng, fn, reads=(), writes=(), dma=None):
        o = Op(eng, fn, dma)
        ex = [t for t in reads if t.excl]
        if ex:
            reads = [t for t in reads if not t.excl]
            writes = list(writes) + ex
        deps = set()
        for t in reads:
            deps.update(t.ws)
        for t in writes:
            deps.update(t.ws)
            deps.update(t.rs.values())
            deps.update(t.rdma)
        if dma is not None:
            deps = {d for d in deps if d.dma != dma}
            hist = self.dma_hist[eng]
            n = len(hist)
            o.dsem = (eng, n % DMA_SEMS)
            o.dmaval = 16 * (n // DMA_SEMS + 1)
            if n >= DMA_SEMS:
                deps.add(hist[n - DMA_SEMS])
            hist.append(o)
        o.deps = tuple(deps)
        for t in reads:
            if dma is not None:
                t.rdma.append(o)
            else:
                t.rs[eng] = o
        for t in writes:
            if dma is not None and t.ws and not t.rs and not t.rdma and all(w.dma == dma for w in t.ws):
                t.ws.append(o)
            else:
                t.ws = [o]
            t.rs = {}
            t.rdma = []
        self.ops[eng].append(o)
        return o

    def barrier(self, engines=("pe", "act", "dve", "sp", "pool")):
        lasts = []
        for e in ENGS:
            for o in reversed(self.ops[e]):
                if o.dma is None and o.fn is not None:
                    lasts.append(o)
                    break
            lasts += self.dma_hist[e][-DMA_SEMS:]
        for e in engines:
            o = Op(e, None, None)
            o.deps = tuple(lasts)
            self.ops[e].append(o)

    def emit(self, final_dma_keys=()):
        nc = self.nc
        for e in ENGS:
            for o in self.ops[e]:
                for d in o.deps:
                    if d.dma is None:
                        if d.eng == "pe" and o.eng == "pe" and o.dma is None and o.fn is not None:
                            continue
                        d.signal = True
        for e in ENGS:
            c = 0
            for o in self.ops[e]:
                if o.dma is None and o.signal:
                    c += 1
                    o.sigval = c
        with ExitStack() as es:
            esem = {e: es.enter_context(nc.semaphore("s_" + e)) for e in ENGS}
            dsem = {}
            for e in ENGS:
                for i in range(min(DMA_SEMS, len(self.dma_hist[e]))):
                    dsem[(e, i)] = es.enter_context(nc.semaphore("d_%s%d" % (e, i)))
            block = es.enter_context(nc.Block())

            def run(e, eng):
                waited = {}
                for o in self.ops[e]:
                    need = {}
                    for d in o.deps:
                        if d.dma is not None:
                            key, sem, val = ("d",) + d.dsem, dsem[d.dsem], d.dmaval
                        else:
                            if d.eng == "pe" and e == "pe" and o.dma is None and o.fn is not None:
                                continue
                            key, sem, val = ("e", d.eng), esem[d.eng], d.sigval
                        if val > need.get(key, (None, 0))[1]:
                            need[key] = (sem, val)
                    for key, (sem, val) in need.items():
                        if waited.get(key, 0) >= val:
                            continue
                        waited[key] = val
                        eng.wait_ge(sem, val)
                    if o.fn is None:
                        continue
                    ins = o.fn(eng)
                    if o.dma is not None:
                        ins.then_inc(dsem[o.dsem], 16)
                    elif o.signal:
                        ins.then_inc(esem[e], 1)
                if e == "sp":
                    fin = {}
                    for q in ENGS:
                        for d in self.dma_hist[q]:
                            if d.dma in final_dma_keys:
                                fin[d.dsem] = max(fin.get(d.dsem, 0), d.dmaval)
                    for k, v in fin.items():
                        if waited.get(("d",) + k, 0) < v:
                            eng.wait_ge(dsem[k], v)

            block.sync(lambda eng: run("sp", eng))
            block.scalar(lambda eng: run("act", eng))
            block.vector(lambda eng: run("dve", eng))
            block.gpsimd(lambda eng: run("pool", eng))
            block.tensor(lambda eng: run("pe", eng))


class StopBuild(Exception):
    pass


class TT:
    def __init__(self, ap):
        self.ap = ap
        self.toks = {}

    def t(self, *key):
        tk = self.toks.get(key)
        if tk is None:
            tk = self.toks[key] = Tok()
        return tk

    def all(self):
        return list(self.toks.values())


def build_program(stop=None):
    nc = bass.Bass("TRN2", target_bir_lowering=False)

    def din(name, shape):
        return nc.dram_tensor(name, list(shape), F32, kind="ExternalInput").ap()

    d_x = din("xT", [128, KC, T])
    d_c = din("cT", [128, KC])
    d_vecs = din("vecs", [128, NV, KC])
    d_consts = din("consts", [128, 256])
    d_w_ada = din("w_ada", [2, D, 6 * D])
    d_w_a_in = din("w_a_in", [1, D, 3 * D])
    d_w_a_out = din("w_a_out", [1, D, D])
    d_w_mlp1 = din("w_mlp1", [2, D, 4 * D])
    d_w_mlp2 = din("w_mlp2", [2, 4 * D, D])
    d_w_ada_kv = din("w_ada_kv", [D, 2 * D])
    d_w_kv = din("w_kv", [D, 1536])
    d_w_qg = din("w_qg", [1, D, 1072])
    d_w_o = din("w_o", [1, D, D])
    d_cmp_w1 = din("cmp_w1", [2, 2048, 256])
    d_cmp_w2 = din("cmp_w2", [2, 256, 64])
    d_hvecs = din("hvecs", [128, 4])
    d_peT = din("peT", [128, 64])
    d_amask = din("amask", [128, 256 + 2048])
    d_e128 = din("e128", [128, 2048])
    d_cmap = din("cmap", [128, 40])
    d_force = din("force", [128, 256])
    d_gsel = din("gsel", [48, 3072])
    d_bd = din("bdones", [128, 128])
    d_y = nc.dram_tensor("yT", [128, KC, T], F32, kind="ExternalOutput").ap()
    d_xs = nc.dram_tensor("xs_scratch", [128, KC, T], F32).ap()

    S = Sched(nc)
    es = ExitStack()
    with es:
        def sb(name, shape, dt):
            return es.enter_context(nc.sbuf_tensor(name, list(shape), dt))

        XR = sb("XR", [128, KC * T], F32)
        HR = sb("HR", [128, KC * T], BF16)
        AR = sb("AR", [128, KC * T], BF16)
        RING = [sb("RING%d" % i, [128, 8192], BF16) for i in range(2)]
        ring_t = [Tok(), Tok()]
        rstd_s = sb("rstd", [128, T], F32)
        TMPF = [sb("tmpf%d" % i, [128, TGW], F32) for i in range(3)]
        tmpf_t = [Tok() for _ in TMPF]
        TMPB = [sb("tmpb%d" % i, [128, TGW], BF16) for i in range(3)]
        tmpb_t = [Tok() for _ in TMPB]
        SCR = sb("SCR", [128, 11048], BF16)
        ones_b = sb("ones_b", [128, 128], BF16)
        vecs = sb("vecs_s", [128, NV, KC], F32)
        c_f = sb("c_f", [128, KC], F32)
        cact = sb("cact", [128, KC], BF16)
        modv = sb("modv", [128, 3, 48], F32)
        der = sb("der", [128, 8, KC], F32)
        hvecs = sb("hvecs_s", [128, 4], F32)
        gq = sb("gq", [128, 1], F32)
        ident_b = sb("ident_b", [128, 128], BF16)
        bd_ones = sb("bd_ones", [128, 128], BF16)
        cbias = sb("cbias", [128, 4], F32)
        kcT = sb("kcT", [128, 4, 128], BF16)
        vcA = sb("vcA", [128, 4, 128], BF16)
        rd4 = sb("rd4", [128, 4], F32)
        scA = sb("scA", [128, 32], F32)
        scB = sb("scB", [128, 32], F32)
        m8a = sb("m8a", [128, 8], F32)
        m8b = sb("m8b", [128, 8], F32)
        selm = sb("selm", [128, 32], BF16)
        ACC = [sb("acc%d" % i, [128, 256], F32) for i in range(2)]
        BIAS_EXTRA = sb("m2", [128, 1024], BF16)
        zeros_b = sb("zeros_b", [128, 128], BF16)
        t_zero = Tok()
        S.op("dve", lambda e: e.memset(zeros_b[:], 0.0), writes=[t_zero])
        PS2 = [es.enter_context(nc.psum_tensor("psd%d" % i, [128, 2 * TGW], F32)) for i in range(2)]
        PS = [PS2[0][:, 0:TGW], PS2[0][:, TGW:2 * TGW], PS2[1][:, 0:TGW], PS2[1][:, TGW:2 * TGW]]
        PS += [es.enter_context(nc.psum_tensor("ps%d" % i, [128, TGW], F32)) for i in range(4, 8)]
        ps_t = [Tok(excl=True) for _ in PS]

        X = TT(XR[:].rearrange("p (k t) -> p k t", k=KC))
        H = TT(HR[:].rearrange("p (k t) -> p k t", k=KC))
        A = TT(AR[:].rearrange("p (k t) -> p k t", k=KC))
        t_const = Tok()
        eps_t = sb("eps_t", [128, 2], F32)
        t_eps = Tok()
        S.op("dve", lambda e: e.memset(eps_t[:, 0:1], EPS), writes=[t_eps])
        S.op("dve", lambda e: e.memset(eps_t[:, 1:2], 1e-30), writes=[t_eps])
        t_vecs = Tok()
        t_c = Tok()
        t_cact = Tok()
        t_modv = [Tok(), Tok(), Tok()]
        t_der = Tok()
        t_rstd = [Tok() for _ in range(NTG)]

        cnt = {"ps": 0, "ring": 0, "tf": 0, "tb": 0}

        POOLS = {"ALL": list(range(8)), "P6": [0, 1, 2, 3, 4, 5], "S": [0, 1, 2, 3], "O": [4, 5, 6], "M": [7], "OM": [4, 5, 6, 7]}
        pcnt = {k: 0 for k in POOLS}

        defpool = {"v": "ALL"}

        def next_ps(pool=None):
            pool = pool or defpool["v"]
            lst = POOLS[pool]
            i = lst[pcnt[pool] % len(lst)]
            pcnt[pool] += 1
            return PS[i], ps_t[i]

        def next_tf():
            i = cnt["tf"] % len(TMPF)
            cnt["tf"] += 1
            return TMPF[i], tmpf_t[i]

        def next_tb():
            i = cnt["tb"] % len(TMPB)
            cnt["tb"] += 1
            return TMPB[i], tmpb_t[i]

        def tsl(tg):
            return slice(tg * TGW, (tg + 1) * TGW)

        S.op("pool", lambda e: e.dma_start(out=ones_b[:], in_=d_consts[:, 128:256]), writes=[t_const], dma="cb")
        S.op("sp", lambda e: e.dma_start(out=vecs[:], in_=d_vecs), writes=[t_vecs], dma="c")
        S.op("sp", lambda e: e.dma_start(out=c_f[:], in_=d_c), writes=[t_c], dma="c")
        for tg in range(NTG):
            for kh in range(2):
                ks = slice(kh * 4, kh * 4 + 4)
                S.op("sp", (lambda e, tg=tg, ks=ks: e.dma_start(out=X.ap[:, ks, tsl(tg)], in_=d_x[:, ks, tsl(tg)])),
                     writes=[X.t(k, tg) for k in range(kh * 4, kh * 4 + 4)], dma="x")

        S.op("act", lambda e: e.activation(out=cact[:], in_=c_f[:], func=AF.Silu), reads=[t_c], writes=[t_cact])

        def load_w(src_aps, dst_view_fn):
            i = cnt["ring"] % 2
            cnt["ring"] += 1
            slot = RING[i]
            for dst_fn, src in src_aps:
                S.op("pool", (lambda e, dst_fn=dst_fn, src=src, slot=slot: e.dma_start(out=dst_fn(slot), in_=src)),
                     writes=[ring_t[i]], dma="r%d" % i)
            return dst_view_fn(slot), ring_t[i]

        def load_w_std(w2d, c0, ncols, k0=0):
            src = w2d.rearrange("(k p) n -> p k n", p=128)

            def view(slot):
                return slot[:, 0:8 * ncols].rearrange("p (k c) -> p k c", k=8)
            pieces = []
            for kh in range(2):
                ks = slice(kh * 4, kh * 4 + 4)
                pieces.append(((lambda slot, ks=ks: view(slot)[:, ks, :]), src[:, k0 + kh * 4:k0 + kh * 4 + 4, c0:c0 + ncols]))
            return load_w(pieces, view)

        def ada_matvec(w2d, ncb, bias_idx, mi):
            ps, pt = next_ps()
            for cb in range(ncb):
                wv, wt = load_w_std(w2d, cb * 1024, 1024)
                for oc in range(8):
                    col = cb * 8 + oc
                    for k in range(KC):
                        S.op("pe", (lambda e, ps=ps, wv=wv, oc=oc, k=k, col=col: e.matmul(
                            ps[:, col:col + 1], wv[:, k, oc * 128:(oc + 1) * 128], cact[:, k:k + 1],
                            start=(k == 0), stop=(k == KC - 1))), reads=[wt, t_cact], writes=[pt])
            n = ncb * 8
            S.op("dve", (lambda e, ps=ps, n=n: e.tensor_tensor(
                out=modv[:, mi, 0:n], in0=ps[:, 0:n],
                in1=vecs[:, bias_idx:bias_idx + ncb, :].rearrange("p a b -> p (a b)"), op=ALU.add)),
                reads=[pt, t_vecs], writes=[t_modv[mi]])

        def mod_part(mi, part):
            return modv[:, mi, part * 8:(part + 1) * 8]

        def derive(mi, part_sc, gain_idx, dst):
            S.op("dve", lambda e: e.tensor_scalar(out=der[:, dst, :], in0=mod_part(mi, part_sc), scalar1=1.0, scalar2=1.0,
                                                  op0=ALU.add, op1=ALU.mult), reads=[t_modv[mi]], writes=[t_der])
            S.op("dve", lambda e: e.tensor_tensor(out=der[:, dst, :], in0=der[:, dst, :], in1=vecs[:, gain_idx, :], op=ALU.mult),
                 reads=[t_der, t_vecs], writes=[t_der])

        def compute_rstd(tgs=range(NTG)):
            for tg in tgs:
                ps, pt = next_ps()
                for k in range(KC):
                    tb, tbt = next_tb()
                    S.op("act", (lambda e, tb=tb, k=k, tg=tg: e.activation(out=tb[:], in_=X.ap[:, k, tsl(tg)], func=AF.Square)),
                         reads=[X.t(k, tg)], writes=[tbt])
                    S.op("pe", (lambda e, ps=ps, tb=tb, k=k: e.matmul(ps[:], ones_b[:], tb[:], start=(k == 0), stop=(k == KC - 1))),
                         reads=[tbt, t_const], writes=[pt])
                tf, tft = next_tf()
                S.op("act", (lambda e, ps=ps, tf=tf: e.activation(out=tf[:], in_=ps[:], func=AF.Ln, bias=eps_t[:, 0:1], scale=1.0 / D)),
                     reads=[pt, t_eps], writes=[tft])
                S.op("act", (lambda e, tf=tf, tg=tg: e.activation(out=rstd_s[:, tsl(tg)], in_=tf[:], func=AF.Exp, scale=-0.5)), reads=[tft], writes=[t_rstd[tg]])

        def norm_mod(dst, a_ap, b_ap, extra_reads, tgs=range(NTG)):
            for tg in tgs:
                for k in range(KC):
                    tf, tft = next_tf()
                    S.op("dve", (lambda e, tf=tf, k=k, tg=tg: e.tensor_tensor(out=tf[:], in0=X.ap[:, k, tsl(tg)], in1=rstd_s[:, tsl(tg)], op=ALU.mult)),
                         reads=[X.t(k, tg), t_rstd[tg]], writes=[tft])
                    S.op("act", (lambda e, tf=tf, k=k, tg=tg: e.activation(out=dst.ap[:, k, tsl(tg)], in_=tf[:], func=AF.Identity,
                                                                            bias=b_ap[:, k:k + 1], scale=a_ap[:, k:k + 1])),
                         reads=[tft] + extra_reads, writes=[dst.t(k, tg)])

        def proj(wv, wt, src, n_oc, evac, tgs=range(NTG), oc_cols=None, pre_tg=None):
            for tg in tgs:
                if pre_tg is not None:
                    pre_tg(tg)
                for oc in range(n_oc):
                    ps, pt = next_ps()
                    for k in range(KC):
                        lhs = wv[:, k, oc * 128:(oc + 1) * 128] if oc_cols is None else oc_cols(wv, k, oc)
                        S.op("pe", (lambda e, ps=ps, lhs=lhs, k=k, tg=tg: e.matmul(ps[:], lhs, src.ap[:, k, tsl(tg)],
                                                                                   start=(k == 0), stop=(k == KC - 1))),
                             reads=[wt, src.t(k, tg)], writes=[pt])
                    evac(oc, tg, ps, pt)

        def resid_evac(g_ap, extra_reads):
            def ev(oc, tg, ps, pt):
                S.op("dve", (lambda e: e.scalar_tensor_tensor(out=X.ap[:, oc, tsl(tg)], in0=ps[:], scalar=g_ap[:, oc:oc + 1],
                                                              in1=X.ap[:, oc, tsl(tg)], op0=ALU.mult, op1=ALU.add)),
                     reads=[pt, X.t(oc, tg)] + extra_reads, writes=[X.t(oc, tg)])
            return ev

        mv_tasks = []
        mv_t = [Tok(), Tok()]
        mv_state = {"n": 0}

        def make_mv_tasks(w2d, n512, bias_idx, mi):
            src = w2d.rearrange("(k p) n -> p k n", p=128)
            nparts = (n512 * 4) // 8
            bias_flat = vecs[:, bias_idx:bias_idx + nparts, :].rearrange("p a b -> p (a b)")
            for cb in range(n512):
                def task(cb=cb):
                    n = mv_state["n"]
                    mv_state["n"] += 1
                    i = n % 2
                    slot = SCR[:, i * 4096:(i + 1) * 4096].rearrange("p (k c) -> p k c", k=8)
                    extra = ([t for row in t_gb for t in row] + [t for row in t_v for t in row] + [t_vhalo]) if n < 2 else []
                    S.op("pool", (lambda e: e.dma_start(out=slot, in_=src[:, :, cb * 512:(cb + 1) * 512])), writes=[mv_t[i]] + extra, dma="mv%d" % i)
                    ps, pt = next_ps()
                    for oc in range(4):
                        for k in range(KC):
                            S.op("pe", (lambda e, oc=oc, k=k: e.matmul(ps[:, oc:oc + 1], slot[:, k, oc * 128:(oc + 1) * 128], cact[:, k:k + 1],
                                                                      start=(k == 0), stop=(k == KC - 1))), reads=[mv_t[i], t_cact], writes=[pt])
                    c0 = cb * 4
                    S.op("dve", (lambda e: e.tensor_tensor(out=modv[:, mi, c0:c0 + 4], in0=ps[:, 0:4], in1=bias_flat[:, c0:c0 + 4], op=ALU.add)),
                         reads=[pt, t_vecs], writes=[t_modv[mi]])
                mv_tasks.append(task)

        def run_mv_tasks(n):
            for _ in range(n):
                if mv_tasks:
                    mv_tasks.pop(0)()

        def mlp(layer, mi, first_w=None):
            derive(mi, 4, V_NG + 2 * layer + 1, 1)

            def norm_tg(tg):
                if tg < NTG:
                    compute_rstd([tg])
                    norm_mod(H, der[:, 1, :], mod_part(mi, 3), [t_der, t_modv[mi]], [tg])
            norm_tg(0)
            g2 = mod_part(mi, 5)
            for hb in range(4):
                wv, wt = first_w if (hb == 0 and first_w is not None) else load_w_std(d_w_mlp1[layer], hb * 1024, 1024)

                def ev1(oc, tg, ps, pt):
                    tf, tft = next_tf()
                    S.op("act", (lambda e: e.activation(out=tf[:], in_=ps[:], func=AF.Relu)), reads=[pt], writes=[tft])
                    S.op("dve", (lambda e: e.tensor_tensor(out=A.ap[:, oc, tsl(tg)], in0=tf[:], in1=tf[:], op=ALU.mult)),
                         reads=[tft], writes=[A.t(oc, tg)])
                proj(wv, wt, H, 8, ev1, pre_tg=((lambda tg: norm_tg(tg + 1)) if hb == 0 else None))
                wv2, wt2 = load_w_std(d_w_mlp2[layer], 0, 1024, k0=hb * 8)
                run_mv_tasks(2)
                proj(wv2, wt2, A, 8, resid_evac(g2, [t_modv[mi]]))
                run_mv_tasks(2)

        compute_rstd()
        ada_matvec(d_w_ada[0], 6, V_BADA, 0)
        derive(0, 1, V_NG + 0, 0)
        norm_mod(H, der[:, 0, :], mod_part(0, 0), [t_der, t_modv[0]])

        gbv = SCR[:, 0:4096].rearrange("p (j t) -> p j t", j=2)
        vv = SCR[:, 4096:4096 + 2 * 2056].rearrange("p (j t) -> p j t", j=2)
        t_gb = [[Tok() for _ in range(NTG)] for _ in range(2)]
        t_v = [[Tok() for _ in range(NTG)] for _ in range(2)]
        t_vhalo = Tok()
        S.op("dve", lambda e: e.memset(vv[:, :, 0:2], 0.0), writes=[t_vhalo])
        w_in_v = d_w_a_in[0].rearrange("(k p) (s c) -> p k s c", p=128, s=3)
        g1 = mod_part(0, 2)
        def mixer_j(wv, wt, jj, j):
            jb = j % 2
            for tg in range(NTG):
                pss = []
                for s in range(3):
                    ps, pt = next_ps()
                    for k in range(KC):
                        S.op("pe", (lambda e, ps=ps, s=s, k=k, tg=tg: e.matmul(ps[:], wv[:, k, s, jj * 128:(jj + 1) * 128], H.ap[:, k, tsl(tg)],
                                                                               start=(k == 0), stop=(k == KC - 1))),
                             reads=[wt, H.t(k, tg)], writes=[pt])
                    pss.append((ps, pt))
                (psb, ptb), (psc, ptc), (psu, ptu) = pss
                S.op("act", (lambda e, psb=psb, tg=tg: e.activation(out=gbv[:, jb, tsl(tg)], in_=psb[:], func=AF.Copy)),
                     reads=[ptb], writes=[t_gb[jb][tg]])
                tb, tbt = next_tb()
                S.op("act", (lambda e, psc=psc, tb=tb: e.activation(out=tb[:], in_=psc[:], func=AF.Copy)), reads=[ptc], writes=[tbt])
                S.op("dve", (lambda e, psu=psu, tb=tb, tg=tg: e.tensor_tensor(out=vv[:, jb, 2 + tg * TGW:2 + (tg + 1) * TGW], in0=psu[:], in1=tb[:], op=ALU.mult)),
                     reads=[ptu, tbt], writes=[t_v[jb][tg]])
            for tg in range(NTG):
                tf, tft = next_tf()
                rd = [t_v[jb][tg], t_vhalo, t_vecs] + ([t_v[jb][tg - 1]] if tg > 0 else [])
                b0 = tg * TGW
                S.op("dve", (lambda e, tf=tf, b0=b0: e.tensor_scalar(out=tf[:], in0=vv[:, jb, b0 + 2:b0 + 2 + TGW], scalar1=vecs[:, V_CONV + 2, j:j + 1], scalar2=None, op0=ALU.mult)),
                     reads=rd, writes=[tft])
                S.op("dve", (lambda e, tf=tf, b0=b0: e.scalar_tensor_tensor(out=tf[:], in0=vv[:, jb, b0 + 1:b0 + 1 + TGW], scalar=vecs[:, V_CONV + 1, j:j + 1], in1=tf[:], op0=ALU.mult, op1=ALU.add)),
                     reads=rd + [tft], writes=[tft])
                S.op("dve", (lambda e, tf=tf, b0=b0: e.scalar_tensor_tensor(out=tf[:], in0=vv[:, jb, b0:b0 + TGW], scalar=vecs[:, V_CONV + 0, j:j + 1], in1=tf[:], op0=ALU.mult, op1=ALU.add)),
                     reads=rd + [tft], writes=[tft])
                S.op("dve", (lambda e, tf=tf, tg=tg: e.tensor_tensor(out=A.ap[:, j, tsl(tg)], in0=tf[:], in1=gbv[:, jb, tsl(tg)], op=ALU.mult)),
                     reads=[tft, t_gb[jb][tg]], writes=[A.t(j, tg)])

        for jp in range(4):
            def view(slot):
                return slot[:, 0:6144].rearrange("p (k s c) -> p k s c", k=8, s=3)
            pieces = []
            for s3 in range(3):
                pieces.append(((lambda slot, s3=s3: view(slot)[:, :, s3, :]), w_in_v[:, :, s3, jp * 256:(jp + 1) * 256]))
            wv, wt = load_w(pieces, view)
            for jj in range(2):
                mixer_j(wv, wt, jj, 2 * jp + jj)
        wv, wt = load_w_std(d_w_a_out[0], 0, 1024)
        proj(wv, wt, A, 8, resid_evac(g1, [t_modv[0]]))
        if stop != "mix0":
            if stop not in ("mix0", "l0"):
                make_mv_tasks(d_w_ada[1], 12, V_BADA + 6, 1)
                make_mv_tasks(d_w_ada_kv, 4, V_BKV, 2)
            mlp(0, 0)
        def chk(name, dumps):
            if stop != name:
                return
            S.barrier()
            for ap, dst in dumps:
                S.op("pool", (lambda e, ap=ap, dst=dst: e.dma_start(out=dst, in_=ap)), dma="out")
            raise StopBuild()

        if stop not in ("mix0", "l0"):
          try:
            G4 = 4
            defpool["v"] = "P6"
            POOLS["P6"] = [0, 1, 2, 3, 4]
            POOLS["M"] = [5, 6, 7]
            t_l1c = Tok()
            wqg_g = SCR[:, 8192:8576].rearrange("p (k c) -> p k c", k=8)
            cw2k = SCR[:, 8576:8832].rearrange("p (c d) -> p c d", c=2)
            cw2v = SCR[:, 8832:8960].rearrange("p (c d) -> p c d", c=2)
            peT_b = SCR[:, 8960:9024]
            S.op("sp", lambda e: e.dma_start(out=hvecs[:], in_=d_hvecs), writes=[t_l1c], dma="c")
            S.op("pool", lambda e: e.dma_start(out=ident_b[:], in_=d_consts[:, 0:128]), writes=[t_l1c], dma="cb")
            S.op("pool", lambda e: e.dma_start(out=bd_ones[:], in_=d_bd), writes=[t_l1c], dma="cb")
            S.op("pool", lambda e: e.dma_start(out=peT_b, in_=d_peT), writes=[t_l1c], dma="cb")
            S.op("pool", lambda e: e.dma_start(out=cw2k[:, :, 0:64], in_=d_cmp_w2[0].rearrange("(c p) d -> p c d", p=128)), writes=[t_l1c], dma="cb")
            S.op("pool", lambda e: e.dma_start(out=cw2k[:, :, 64:128], in_=d_cmp_w2[0].rearrange("(c p) d -> p c d", p=128)), writes=[t_l1c], dma="cb")
            S.op("pool", lambda e: e.dma_start(out=cw2v, in_=d_cmp_w2[1].rearrange("(c p) d -> p c d", p=128)), writes=[t_l1c], dma="cb")
            S.op("pool", lambda e: e.dma_start(out=wqg_g, in_=d_w_qg[0].rearrange("(k p) n -> p k n", p=128)[:, :, 1024:1072]), writes=[t_l1c], dma="cb")
            t_gq = Tok()
            S.op("dve", lambda e: e.tensor_scalar(out=gq[:], in0=hvecs[:, 0:1], scalar1=0.125, scalar2=None, op0=ALU.mult), reads=[t_l1c], writes=[t_gq])

            run_mv_tasks(99)
            compute_rstd()
            derive(2, 1, V_KVG, 2)
            derive(1, 1, V_NG + 2, 3)
            xs_t = [Tok() for _ in range(NTG)]

            def l1_norm_tg(tg_kv, tg_h1):
                if tg_kv is not None and tg_kv < NTG:
                    norm_mod(H, der[:, 2, :], mod_part(2, 0), [t_der, t_modv[2]], [tg_kv])
                if tg_h1 is not None:
                    norm_mod(A, der[:, 3, :], mod_part(1, 0), [t_der, t_modv[1]], [tg_h1])
                    for kh in range(2):
                        ks = slice(kh * 4, kh * 4 + 4)
                        S.op("sp", (lambda e, tg=tg_h1, ks=ks: e.dma_start(out=d_xs[:, ks, tsl(tg)], in_=X.ap[:, ks, tsl(tg)])),
                             reads=[X.t(k, tg_h1) for k in range(kh * 4, kh * 4 + 4)], writes=[xs_t[tg_h1]], dma="xs")
            l1_norm_tg(0, None)

            def head_norm(ps, pt, ncol, gain_ap, gain_reads, dst_ap, dst_toks):
                tb, tbt = next_tb()
                S.op("act", (lambda e: e.activation(out=tb[:, 0:ncol], in_=ps[:, 0:ncol], func=AF.Square)), reads=[pt], writes=[tbt])
                ps2, pt2 = next_ps("M")
                S.op("pe", (lambda e: e.matmul(ps2[:, 0:ncol], bd_ones[:], tb[:, 0:ncol], start=True, stop=True)), reads=[tbt, t_l1c], writes=[pt2])
                tf, tft = next_tf()
                S.op("act", (lambda e: e.activation(out=tf[:, 0:ncol], in_=ps2[:, 0:ncol], func=AF.Ln, bias=eps_t[:, 0:1], scale=1.0 / 64)),
                     reads=[pt2, t_eps], writes=[tft])
                tf2, tft2 = next_tf()
                S.op("act", (lambda e: e.activation(out=tf2[:, 0:ncol], in_=tf[:, 0:ncol], func=AF.Exp, scale=-0.5)), reads=[tft], writes=[tft2])
                S.op("dve", (lambda e: e.scalar_tensor_tensor(out=dst_ap, in0=ps[:, 0:ncol], scalar=gain_ap, in1=tf2[:, 0:ncol], op0=ALU.mult, op1=ALU.mult)),
                     reads=[pt, tft2] + gain_reads, writes=dst_toks)

            RAW = TT(SCR[:, 0:8192].rearrange("p (c t) -> p c t", c=4))
            wv, wt = load_w_std(d_w_kv, 0, 512)

            def ev_raw(oc, tg, ps, pt):
                S.op("act", (lambda e: e.activation(out=RAW.ap[:, oc, tsl(tg)], in_=ps[:], func=AF.Copy)), reads=[pt], writes=[RAW.t(oc, tg)])
            proj(wv, wt, H, 4, ev_raw, pre_tg=(lambda tg: l1_norm_tg(tg + 1, tg)))

            chk("raw", [(RAW.ap, d_y[:, 0:4, :])])
            t_kc = [Tok() for _ in range(G4)]
            t_vc = [Tok() for _ in range(G4)]
            t_vc_ones = Tok()
            S.op("dve", lambda e: e.memset(vcA[:, :, 64:128], 1.0), writes=[t_vc_ones])
            t_cb = Tok()
            for kv in range(2):
                def view1(slot):
                    return slot[:, 0:8192].rearrange("p (l h) -> p l h", l=32)
                src1 = d_cmp_w1[kv].rearrange("(l d) h -> d l h", d=64)
                pieces = [((lambda slot: view1(slot)[0:64, :, :]), src1), ((lambda slot: view1(slot)[64:128, :, :]), src1)]
                cwv, cwt = load_w(pieces, view1)
                psb, ptb = next_ps()
                for hc in range(2):
                    for l in range(32):
                        S.op("pe", (lambda e, hc=hc, l=l, cwv=cwv, psb=psb, kv=kv: e.matmul(
                            psb[:, hc:hc + 1], cwv[0:64, l, hc * 128:(hc + 1) * 128], peT_b[0:64, kv * 32 + l:kv * 32 + l + 1],
                            start=(l == 0), stop=(l == 31))), reads=[cwt, t_l1c], writes=[ptb])
                S.op("dve", (lambda e, psb=psb, kv=kv: e.tensor_copy(out=cbias[:, 2 * kv:2 * kv + 2], in_=psb[:, 0:2])), reads=[ptb], writes=[t_cb])
                for g in range(G4):
                    base = (g % 2) * 64
                    c = kv * 2 + g // 2
                    hids = []
                    for hc in range(2):
                        ps, pt = next_ps()
                        for l in range(32):
                            S.op("pe", (lambda e, ps=ps, l=l, hc=hc, cwv=cwv, base=base, c=c: e.matmul(
                                ps[:, 0:127], cwv[base:base + 64, l, hc * 128:(hc + 1) * 128],
                                RAW.ap[base:base + 64, c, l:l + 16 * 126 + 1:16], start=(l == 0), stop=(l == 31))),
                                reads=[cwt] + [RAW.t(c, tg) for tg in range(NTG)], writes=[pt])
                        z, zt = next_tf()
                        S.op("act", (lambda e, ps=ps, z=z, hc=hc, kv=kv: e.activation(out=z[:, 0:127], in_=ps[:, 0:127], func=AF.Identity,
                                                                                      bias=cbias[:, 2 * kv + hc:2 * kv + hc + 1], scale=1.0)),
                             reads=[pt, t_cb], writes=[zt])
                        u, ut = next_tf()
                        S.op("dve", (lambda e, z=z, u=u: e.tensor_tensor(out=u[:, 0:127], in0=z[:, 0:127], in1=z[:, 0:127], op=ALU.mult)), reads=[zt], writes=[ut])
                        S.op("dve", (lambda e, u=u: e.tensor_scalar(out=u[:, 0:127], in0=u[:, 0:127], scalar1=0.044715, scalar2=1.0, op0=ALU.mult, op1=ALU.add)),
                             reads=[ut], writes=[ut])
                        S.op("dve", (lambda e, z=z, u=u: e.tensor_tensor(out=u[:, 0:127], in0=u[:, 0:127], in1=z[:, 0:127], op=ALU.mult)), reads=[ut, zt], writes=[ut])
                        S.op("act", (lambda e, u=u: e.activation(out=u[:, 0:127], in_=u[:, 0:127], func=AF.Sigmoid, scale=1.5957691216057308)), reads=[ut], writes=[ut])
                        hb_, hbt = next_tb()
                        S.op("dve", (lambda e, z=z, u=u, hb_=hb_: e.tensor_tensor(out=hb_[:, 0:127], in0=u[:, 0:127], in1=z[:, 0:127], op=ALU.mult)), reads=[ut, zt], writes=[hbt])
                        hids.append((hb_, hbt))
                    if kv == 0:
                        ps, pt = next_ps()
                        for hc in range(2):
                            S.op("pe", (lambda e, ps=ps, hc=hc, hb_=hids[hc][0]: e.matmul(ps[:, 0:127], cw2k[:, hc, :], hb_[:, 0:127], start=(hc == 0), stop=(hc == 1))),
                                 reads=[hids[hc][1], t_l1c], writes=[pt])
                        head_norm(ps, pt, 127, hvecs[:, 1:2], [t_l1c], kcT[:, g, 0:127], [t_kc[g]])
                    else:
                        ps, pt = next_ps()
                        for hc in range(2):
                            S.op("pe", (lambda e, ps=ps, hc=hc, hb_=hids[hc][0]: e.matmul(ps[0:127, 0:64], hb_[:, 0:127], cw2v[:, hc, :], start=(hc == 0), stop=(hc == 1))),
                                 reads=[hids[hc][1], t_l1c], writes=[pt])
                        S.op("act", (lambda e, ps=ps, g=g: e.activation(out=vcA[0:127, g, 0:64], in_=ps[0:127, 0:64], func=AF.Copy)), reads=[pt], writes=[t_vc[g]])

            chk("cmp", [(kcT[:].rearrange("p g n -> p (g n)"), d_y[:, 0, 0:512]), (vcA[:].rearrange("p g n -> p (g n)"), d_y[:, 1, 0:512])])
            wv_q, wt_q = load_w_std(d_w_qg[0], 0, 1024)
            S.barrier()
            t_ac = Tok()
            tri_b = SCR[:, 0:128]
            anti_b = SCR[:, 128:256]
            cmpm = SCR[:, 256:2304]
            e128 = SCR[:, 2304:4352].rearrange("p (k j) -> p k j", k=16)
            cmap = SCR[:, 4352:4392]
            force = SCR[:, 4392:4904].bitcast(F32).rearrange("p (q j) -> p q j", q=8)
            gsel = SCR[0:48, 4904:7976].rearrange("p (h b m) -> p h b m", h=8, b=3)
            PTS = [SCR[:, 7976 + i * 1024:7976 + (i + 1) * 1024].rearrange("p (k r c) -> p k r c", k=2, r=2) for i in range(3)]
            pts_t = [Tok() for _ in PTS]
            S.op("pool", lambda e: e.dma_start(out=SCR[:, 0:2304], in_=d_amask), writes=[t_ac], dma="cb")
            S.op("pool", lambda e: e.dma_start(out=SCR[:, 2304:4352], in_=d_e128), writes=[t_ac], dma="cb")
            S.op("pool", lambda e: e.dma_start(out=cmap, in_=d_cmap), writes=[t_ac], dma="cb")
            S.op("sp", lambda e: e.dma_start(out=SCR[:, 4392:4904].bitcast(F32), in_=d_force), writes=[t_ac], dma="c")
            S.op("pool", lambda e: e.dma_start(out=SCR[0:48, 4904:7976], in_=d_gsel), writes=[t_ac], dma="cb")
            XB = XR[:].bitcast(BF16)
            QT = TT(XB[:, 0:16384].rearrange("p (h t) -> p h t", h=8))
            KST = TT(XB[:, 16384:24576].rearrange("p (g t) -> p g t", g=4))
            KWT = TT(XB[:, 24576:32768].rearrange("p (g t) -> p g t", g=4))
            RB = rstd_s[:].bitcast(BF16)
            SIGG = TT(RB[0:48, 0:T])

            def ev_q(oc, tg, ps, pt):
                head_norm(ps, pt, TGW, gq[:, 0:1], [t_gq], QT.ap[:, oc, tsl(tg)], [QT.t(oc, tg)])
            proj(wv_q, wt_q, A, 8, ev_q)
            for tg in range(NTG):
                ps, pt = next_ps()
                for k in range(KC):
                    S.op("pe", (lambda e, ps=ps, k=k, tg=tg: e.matmul(ps[0:48, :], wqg_g[:, k, :], A.ap[:, k, tsl(tg)], start=(k == 0), stop=(k == KC - 1))),
                         reads=[t_l1c, A.t(k, tg)], writes=[pt])
                S.op("act", (lambda e, ps=ps, tg=tg: e.activation(out=SIGG.ap[:, tsl(tg)], in_=ps[0:48, :], func=AF.Sigmoid)), reads=[pt], writes=[SIGG.t(tg)])

            for typ, dst, gi in ((2, KST, 2), (4, KWT, 3)):
                def viewk(slot):
                    return slot[:, 0:4096].rearrange("p (k g r d) -> p k g r d", k=8, g=4, r=2)
                srck = d_w_kv.rearrange("(k p) n -> p k n", p=128)[:, :, typ * 256:(typ + 1) * 256].rearrange("p k (g d) -> p k g d", g=4)
                pieces = [((lambda slot, r=r, gg=gg: viewk(slot)[:, :, gg, r, :]), srck[:, :, gg, :]) for r in range(2) for gg in range(4)]
                kv_, kt_ = load_w(pieces, viewk)

                def ev_k(oc, tg, ps, pt, dst=dst, gi=gi):
                    head_norm(ps, pt, TGW, hvecs[:, gi:gi + 1], [t_l1c], dst.ap[:, oc, tsl(tg)], [dst.t(oc, tg)])
                proj(kv_, kt_, H, 4, ev_k, oc_cols=(lambda wv_, k, oc: wv_[:, k, oc, :, :].rearrange("p r d -> p (r d)")))
            chk("q", [(QT.ap, d_y)])
            chk("k", [(KST.ap, d_y[:, 0:4, :]), (KWT.ap, d_y[:, 4:8, :]), ])

            def viewv(slot):
                return slot[:, 0:4096].rearrange("p (k s c) -> p k s c", k=8, s=2)
            srcv = d_w_kv.rearrange("(k p) n -> p k n", p=128)
            pieces = [((lambda slot: viewv(slot)[:, :, 0, :]), srcv[:, :, 768:1024]), ((lambda slot: viewv(slot)[:, :, 1, :]), srcv[:, :, 1280:1536])]
            vv_, vt_ = load_w(pieces, viewv)
            S.barrier()
            AB = AR[:]
            VS = TT(AB[:, 0:8192].rearrange("p (t g c) -> p t g c", t=16, g=4))
            VW = TT(AB[:, 8192:16384].rearrange("p (t g c) -> p t g c", t=16, g=4))
            t_vones = Tok()
            S.op("dve", lambda e: e.memset(VS.ap[:, :, :, 64:128], 1.0), writes=[t_vones])
            S.op("dve", lambda e: e.memset(VW.ap[:, :, :, 64:128], 1.0), writes=[t_vones])

            for tt in range(16):
                ps, pt = next_ps()
                tg = tt // 4
                for k in range(KC):
                    S.op("pe", (lambda e, ps=ps, k=k, tt=tt: e.matmul(ps[:], H.ap[:, k, tt * 128:(tt + 1) * 128], vv_[:, k, :, :].rearrange("p s c -> p (s c)"),
                                                                    start=(k == 0), stop=(k == KC - 1))), reads=[vt_, H.t(k, tg)], writes=[pt])
                S.op("act", (lambda e, ps=ps, tt=tt: e.activation(out=VS.ap[:, tt, :, 0:64], in_=ps[:, 0:256].rearrange("p (g d) -> p g d", g=4), func=AF.Copy)),
                     reads=[pt, t_vones], writes=[VS.t(tt)])
                S.op("dve", (lambda e, ps=ps, tt=tt: e.tensor_copy(out=VW.ap[:, tt, :, 0:64], in_=ps[:, 256:512].rearrange("p (g d) -> p g d", g=4))),
                     reads=[pt, t_vones], writes=[VW.t(tt)])
            chk("v", [(AB[:, 0:8192], d_y[:, 0:4, :].rearrange("p a b -> p (a b)")), (AB[:, 8192:16384], d_y[:, 4:8, :].rearrange("p a b -> p (a b)"))])
            wo_v, wo_t = load_w_std(d_w_o[0], 0, 1024)
            w1_pre = load_w_std(d_w_mlp1[1], 0, 1024)
            S.barrier()

            POOLS["M"] = [7]
            HB = HR[:]
            OTG = TT(HB[:, 0:4096].rearrange("p (h t) -> p h t", h=8))
            GBC = HB[:, 4096:7168].rearrange("p (h b t) -> p h b t", h=2, b=3)
            t_gbc = Tok()
            XST = TT(HB[:, 7168:15360].bitcast(F32).rearrange("p (k t) -> p k t", k=8))
            BIAS = [HB[:, 15360 + i * 256:15360 + (i + 1) * 256].rearrange("p (r q) -> p r q", r=2) for i in range(2)]
            bias_t = [Tok(), Tok()]
            for i in range(2):
                S.op("dve", (lambda e, i=i: e.memset(BIAS[i], 0.0)), writes=[bias_t[i]])
            pend = {"tk": None}
            ptc = {"n": 0, "b": 0, "a": 0, "c": 0, "p": 0}
            cmb_lo = [Tok(), Tok()]
            cmb_hi = [Tok(), Tok()]
            cmb_on = [Tok(), Tok()]

            M2 = BIAS_EXTRA
            tri2 = M2[:, 0:256]
            anti2 = M2[:, 256:512]
            t_m2 = Tok()
            S.op("dve", lambda e: e.tensor_copy(out=tri2.rearrange("p (r q) -> p r q", r=2), in_=tri_b.unsqueeze(1).to_broadcast([128, 2, 128])), reads=[t_ac], writes=[t_m2])
            S.op("dve", lambda e: e.tensor_copy(out=anti2.rearrange("p (r q) -> p r q", r=2), in_=anti_b.unsqueeze(1).to_broadcast([128, 2, 128])), reads=[t_ac], writes=[t_m2])
            CM2 = [M2[:, 512 + i * 256:512 + (i + 1) * 256] for i in range(2)]
            cm2_t = [Tok(), Tok()]

            def next_pair():
                i = ptc["p"] % 2
                ptc["p"] += 1
                return PS2[i], ps_t[2 * i]

            def emit_cm2(ci, qt):
                S.op("dve", (lambda e: e.tensor_copy(out=CM2[ci].rearrange("p (r q) -> p r q", r=2),
                                                     in_=cmpm[:, qt * 128:(qt + 1) * 128].unsqueeze(1).to_broadcast([128, 2, 128]))),
                     reads=[t_ac], writes=[cm2_t[ci]])

            def next_pt():
                i = ptc["n"] % 3
                ptc["n"] += 1
                return PTS[i], pts_t[i]

            def qk_tile(bank, bt, colbase, KT, g, kt0, nk, par, qt, mask, bias_i, phase):
                lhs_k = KT.ap[par * 64:(par + 1) * 64, g, kt0:kt0 + nk] if KT is not None else kcT[par * 64:(par + 1) * 64, g, 0:127]
                k_reads = [KT.t(g, kt0 // TGW)] if KT is not None else [t_kc[g]]
                qsl = slice(qt * 128, (qt + 1) * 128)
                q_reads = [QT.t(2 * g, qt // 4), QT.t(2 * g + 1, qt // 4)]
                if phase == 1:
                    S.op("pe", (lambda e: e.matmul(bank[0:nk, colbase:colbase + 256], lhs_k, QT.ap[par * 64:(par + 1) * 64, 2 * g:2 * g + 2, qsl],
                                                   start=(mask is None), stop=True)), reads=k_reads + q_reads, writes=[bt])
                elif mask is None:
                    pass
                elif mask == "bias":
                    kt = kt0 // 128
                    S.op("pe", (lambda e: e.matmul(bank[:, colbase:colbase + 256], e128[:, kt, :], BIAS[bias_i].rearrange("p r q -> p (r q)"), start=True, stop=False)),
                         reads=[t_ac, bias_t[bias_i]], writes=[bt])
                else:
                    mask_ap, mask_reads = mask
                    S.op("pe", (lambda e: e.matmul(bank[0:nk, colbase:colbase + 256], ident_b[:, 0:nk], mask_ap, start=True, stop=False)),
                         reads=[t_l1c] + mask_reads, writes=[bt])

            def branch_steps(g, qt, kind, bias_i, pos, br, acc, acct):
                ps_o, pt_o = next_ps("O")
                out = []
                if kind == "cmp":
                    PT, ptt = next_pt()

                    ci = ptc["a"] % 2
                    ptc["a"] += 1

                    def qk():
                        P2, tP = next_pair()
                        bA, bB, tA, tB = P2[:, 0:TGW], P2[:, TGW:2 * TGW], tP, tP
                        if g == 0 and qt == 0:
                            emit_cm2(ci, qt)
                        for phase in range(2):
                            for par, bank, bt in ((0, bA, tA), (1, bB, tB)):
                                qk_tile(bank, bt, 0, None, g, 0, 127, par, qt, (CM2[ci], [cm2_t[ci]]), None, phase)
                        S.op("act", (lambda e: e.activation(out=PT[0:127, 0, :, :], in_=P2[0:127, :].rearrange("p (r c) -> p r c", r=2)[:, :, 0:256], func=AF.Exp)),
                             reads=[tP], writes=[ptt])
                        nqt = qt + 1 if qt % 4 != 3 else (qt - 3 if g < 3 else qt + 1)
                        if nqt < 16:
                            emit_cm2(1 - ci, nqt)
                        if qt >= 8:
                            pend["tk"] = (lambda: topk_a(g, qt, PT, ptt))

                    def pv():
                        S.op("pe", (lambda e: e.matmul(ps_o[:], vcA[0:127, g, :], PT[0:127, 0, :, :].rearrange("p r c -> p (r c)"), start=True, stop=True)),
                             reads=[ptt, t_vc[g], t_vc_ones], writes=[pt_o])
                        combine(g, qt, pos, br, ps_o, pt_o, acc, acct)
                    return [(qk, pv)]
                KT, V = (KST, VS) if kind == "sel" else (KWT, VW)
                kts = list(range(0, qt + 1)) if kind == "sel" else list(range(max(0, qt - 4), qt + 1))
                for pi in range(0, len(kts), 2):
                    pair = kts[pi:pi + 2]
                    PTp = next_pt()

                    def qk(pair=pair, PTp=PTp):
                        PT, ptt = PTp
                        npair = len(pair)
                        P2, tP = next_pair()
                        banks = ((P2[:, 0:TGW], tP), (P2[:, TGW:2 * TGW], tP))
                        if kind == "sel" and qt >= 8 and pair[0] == 0:
                            topk_b(bias_i)
                        for ktp, kt in enumerate(pair):
                            if kt == qt:
                                mask = (tri2, [t_m2])
                            elif kind == "win" and kt == qt - 4:
                                mask = (anti2, [t_m2])
                            elif kind == "sel" and qt >= 8:
                                mask = "bias"
                            else:
                                mask = None
                            for phase in range(2):
                                for par, (bank, bt) in enumerate(banks):
                                    qk_tile(bank, bt, ktp * 256, KT, g, kt * 128, 128, par, qt, mask, bias_i, phase)
                        S.op("act", (lambda e: e.activation(
                            out=PT.rearrange("p k r c -> p r k c")[:, :, 0:npair, :],
                            in_=P2[:].rearrange("p (r k c) -> p r k c", r=2, k=2)[:, :, 0:npair, :], func=AF.Exp)),
                            reads=[tP], writes=[ptt])
                        if pend["tk"] is not None:
                            pend["tk"]()
                            pend["tk"] = None

                    def pv(pair=pair, PTp=PTp, last=(pi + 2 >= len(kts)), first=(pi == 0)):
                        PT, ptt = PTp
                        if FILLER and not first:
                            S.op("pe", (lambda e: e.matmul(ps_o[:], zeros_b[:], ident_b[:].unsqueeze(1).to_broadcast([128, 4, 128]), start=False, stop=False)),
                                 reads=[t_l1c, t_zero], writes=[pt_o])
                        for ktp, kt in enumerate(pair):
                            S.op("pe", (lambda e, ktp=ktp, kt=kt: e.matmul(ps_o[:], V.ap[:, kt, g, :], PT[:, ktp, :, :].rearrange("p r c -> p (r c)"),
                                                                         start=(kt == kts[0]), stop=(kt == kts[-1]))),
                                 reads=[ptt, V.t(kt), t_vones], writes=[pt_o])
                        if last:
                            combine(g, qt, pos, br, ps_o, pt_o, acc, acct)
                    out.append((qk, pv))
                return out

            def combine(g, qt, pos, br, ps_o, pt_o, acc, acct):
                qi = qt % 4
                ci = ptc["c"] % 2
                ptc["c"] += 1
                T, tlo, thi = TMPF[ci], cmb_lo[ci], cmb_hi[ci]
                if pos == 1 or (pos == 0 and qt >= 1):
                    S.op("dve", (lambda e: e.reciprocal(out=T[64:128, :], in_=ps_o[64:128, :])), reads=[pt_o], writes=[thi, tlo])
                else:
                    S.op("act", (lambda e: e.activation(out=T[0:64, :], in_=ps_o[64:128, :], func=AF.Ln, bias=eps_t[64:128, 1:2], scale=1.0)), reads=[pt_o, t_eps], writes=[tlo])
                    S.op("act", (lambda e: e.activation(out=T[64:128, :], in_=T[0:64, :], func=AF.Exp, scale=-1.0)), reads=[tlo], writes=[thi])
                on, ont = TMPF[2][:, ci * 256:(ci + 1) * 256], cmb_on[ci]
                for par in range(2):
                    S.op("dve", (lambda e, par=par: e.tensor_tensor(out=on[par * 64:(par + 1) * 64, :], in0=ps_o[0:64, par * 256:(par + 1) * 256],
                                                                     in1=T[64:128, par * 256:(par + 1) * 256], op=ALU.mult)), reads=[pt_o, thi], writes=[ont])
                onv = on.rearrange("p (h q) -> p h q", h=2)
                accv = acc[:].rearrange("p (h q) -> p h q", h=2)
                Gv = GBC[:, :, br, qi * 128:(qi + 1) * 128]
                ce = "pool"
                if pos == 0:
                    S.op(ce, (lambda e: e.tensor_tensor(out=accv, in0=onv, in1=Gv, op=ALU.mult)), reads=[ont, t_gbc], writes=[acct])
                else:
                    S.op(ce, (lambda e: e.tensor_tensor(out=onv, in0=onv, in1=Gv, op=ALU.mult)), reads=[ont, t_gbc], writes=[ont])
                    if pos == 1:
                        S.op(ce, (lambda e: e.tensor_tensor(out=accv, in0=accv, in1=onv, op=ALU.add)), reads=[ont, acct], writes=[acct])
                    else:
                        S.op(ce, (lambda e: e.tensor_tensor(out=OTG.ap[:, 2 * g:2 * g + 2, qi * 128:(qi + 1) * 128], in0=accv, in1=onv, op=ALU.add)),
                             reads=[ont, acct], writes=[OTG.t(2 * g, 0), OTG.t(2 * g + 1, 0)])

            def topk_a(g, qt, PT, ptt):
                ps_i, pt_i = next_ps("M")
                for c in range(4):
                    S.op("pe", (lambda e, c=c: e.matmul(ps_i[:, c * 33:(c + 1) * 33], PT[0:127, 0, c // 2, (c % 2) * 128:(c % 2 + 1) * 128], cmap[0:127, 0:33],
                                                          start=True, stop=True)), reads=[ptt, t_ac], writes=[pt_i])
                psv = ps_i[:, 0:132].rearrange("p (c j) -> p c j", c=4)
                S.op("dve", (lambda e: e.reciprocal(out=rd4[:], in_=psv[:, :, 32])), reads=[pt_i], writes=[t_tk])
                S.op("dve", (lambda e: e.tensor_scalar(out=scA[:], in0=psv[:, 0, 0:32], scalar1=rd4[:, 0:1], scalar2=None, op0=ALU.mult)), reads=[pt_i, t_tk], writes=[t_tk])
                for c in range(1, 4):
                    S.op("dve", (lambda e, c=c: e.scalar_tensor_tensor(out=scA[:], in0=psv[:, c, 0:32], scalar=rd4[:, c:c + 1], in1=scA[:], op0=ALU.mult, op1=ALU.add)),
                         reads=[pt_i, t_tk], writes=[t_tk])
                S.op("dve", (lambda e: e.tensor_tensor(out=scA[:], in0=scA[:], in1=force[:, qt - 8, :], op=ALU.add)), reads=[t_tk, t_ac], writes=[t_tk])
                S.op("dve", (lambda e: e.max(out=m8a[:], in_=scA[:])), reads=[t_tk], writes=[t_tk])
                S.op("dve", (lambda e: e.match_replace(out=scB[:], in_to_replace=m8a[:], in_values=scA[:], imm_value=-1e30)), reads=[t_tk], writes=[t_tk])
                S.op("dve", (lambda e: e.max(out=m8b[:], in_=scB[:])), reads=[t_tk], writes=[t_tk])
                S.op("dve", (lambda e: e.tensor_scalar(out=selm[:], in0=scA[:], scalar1=m8b[:, 7:8], scalar2=1.0, op0=ALU.is_ge, op1=ALU.subtract)), reads=[t_tk], writes=[t_tk])

            def topk_b(bias_i):
                ps_m, pt_m = next_ps("M")
                S.op("pe", (lambda e: e.matmul(ps_m[0:32, 0:128], selm[:], ident_b[:], start=True, stop=True)), reads=[t_tk, t_l1c], writes=[pt_m])
                S.op("act", (lambda e: e.activation(out=BIAS[bias_i][0:32, :, :], in_=ps_m[0:32, 0:128].unsqueeze(1).to_broadcast([32, 2, 128]), func=AF.Copy, scale=30000.0)),
                     reads=[pt_m], writes=[bias_t[bias_i]])

            on_t = [Tok(), Tok()]
            acc_t = [Tok(), Tok()]
            t_tk = Tok()
            g1 = mod_part(1, 2)
            steps = []

            def emit_xreload(tg):
                for kh in range(2):
                    ks = slice(kh * 4, kh * 4 + 4)
                    S.op("sp", (lambda e, ks=ks: e.dma_start(out=XST.ap[:, ks, :], in_=d_xs[:, ks, tsl(tg)])),
                         reads=[xs_t[tg]], writes=[XST.t(k) for k in range(kh * 4, kh * 4 + 4)], dma="xr")

            def emit_gbc(tg, g):
                for hpl in range(2):
                    for br in range(3):
                        ps, pt = next_ps("OM")
                        S.op("pe", (lambda e, ps=ps, hpl=hpl, br=br: e.matmul(ps[:], gsel[:, 2 * g + hpl, br, :], SIGG.ap[:, tsl(tg)], start=True, stop=True)),
                             reads=[t_ac, SIGG.t(tg)], writes=[pt])
                        S.op("dve", (lambda e, ps=ps, hpl=hpl, br=br: e.tensor_copy(out=GBC[:, hpl, br, :], in_=ps[:])), reads=[pt], writes=[t_gbc])

            def emit_wo(tg):
                def ev_o(oc, tg_, ps, pt):
                    S.op("dve", (lambda e: e.scalar_tensor_tensor(out=XST.ap[:, oc, :], in0=ps[:], scalar=g1[:, oc:oc + 1], in1=XST.ap[:, oc, :], op0=ALU.mult, op1=ALU.add)),
                         reads=[pt, XST.t(oc), t_modv[1]], writes=[XST.t(oc)])
                defpool["v"] = "OM"
                proj(wo_v, wo_t, OTG, 8, ev_o, tgs=[0])
                defpool["v"] = "P6"
                for kh in range(2):
                    ks = slice(kh * 4, kh * 4 + 4)
                    S.op("sp", (lambda e, ks=ks: e.dma_start(out=d_xs[:, ks, tsl(tg)], in_=XST.ap[:, ks, :])),
                         reads=[XST.t(k) for k in range(kh * 4, kh * 4 + 4)], writes=[xs_t[tg]], dma="xw")

            for tg in range(NTG):
                steps.append((None, (lambda tg=tg: emit_xreload(tg))))
                for g in range(G4):
                    steps.append((None, (lambda tg=tg, g=g: emit_gbc(tg, g))))
                    for qi in range(4):
                        qt = tg * 4 + qi
                        ai = ptc["b"] % 2
                        ptc["b"] += 1
                        acc, acct = ACC[ai], acc_t[ai]
                        steps += branch_steps(g, qt, "cmp", ai, 0, 0, acc, acct)
                        steps += branch_steps(g, qt, "win", ai, 1, 2, acc, acct)
                        steps += branch_steps(g, qt, "sel", ai, 2, 1, acc, acct)
                steps.append((None, (lambda tg=tg: emit_wo(tg))))
            prev_pv = None
            for qk_fn, pv_fn in steps:
                if qk_fn is not None:
                    qk_fn()
                if prev_pv is not None:
                    prev_pv()
                prev_pv = pv_fn
            if prev_pv is not None:
                prev_pv()
            S.barrier()
            for tg in range(NTG):
                for kh in range(2):
                    ks = slice(kh * 4, kh * 4 + 4)
                    S.op("sp", (lambda e, tg=tg, ks=ks: e.dma_start(out=X.ap[:, ks, tsl(tg)], in_=d_xs[:, ks, tsl(tg)])),
                         reads=[xs_t[tg]], writes=[X.t(k, tg) for k in range(kh * 4, kh * 4 + 4)], dma="x2")
            defpool["v"] = "ALL"
            if stop != "mix1":
                mlp(1, 1, first_w=w1_pre)
          except StopBuild:
            S.emit(final_dma_keys=["out"])
            return nc

        for tg in range(NTG):
            for kh in range(2):
                ks = slice(kh * 4, kh * 4 + 4)
                S.op("sp", (lambda e, tg=tg, ks=ks: e.dma_start(out=d_y[:, ks, tsl(tg)], in_=X.ap[:, ks, tsl(tg)])),
                     reads=[X.t(k, tg) for k in range(kh * 4, kh * 4 + 4)], dma="out")
        S.emit(final_dma_keys=["out"])
    return nc


def _fm(v):
    return np.ascontiguousarray(v.reshape(KC, 128).T)


def _const_tables():
    f32 = np.float32
    NEGM = -30000.0
    j = np.arange(128)[:, None]
    t = np.arange(128)[None, :]
    tri = np.where(j <= t, 0.0, NEGM)
    anti = np.where(j > t, 0.0, NEGM)
    n = np.arange(128)[:, None]
    tt = np.arange(T)[None, :]
    cmpm = np.where((16 * n + 31 <= tt) & (n < 127), 0.0, NEGM)
    amask = np.concatenate([tri, anti, cmpm], axis=1).astype(f32)
    e128 = np.zeros((128, 16, 128), f32)
    for kt in range(16):
        for jj in range(128):
            e128[2 * kt + jj // 64, kt, jj] = 1.0
    cmap = np.zeros((128, 40), f32)
    c0 = np.arange(127)[:, None] * 16
    s0 = np.arange(32)[None, :] * 64
    ov = np.minimum(c0 + 32, s0 + 64) - np.maximum(c0, s0)
    cmap[:127, :32] = np.clip(ov, 0, None) / 32.0
    cmap[:127, 32] = 1.0
    force = np.zeros((128, 8, 32), f32)
    for q in range(8):
        tq = 128 * (q + 8) + np.arange(128)
        cur = tq // 64
        jb = np.arange(32)[None, :]
        forced = (jb == 0) | (jb == cur[:, None]) | (jb == cur[:, None] - 1)
        force[:, q, :] = np.where(forced, 1e4, np.where(jb > cur[:, None], -1e4, 0.0))
    gsel = np.zeros((48, 8, 3, 128), f32)
    for hp in range(8):
        for br in range(3):
            for m in range(128):
                gsel[(2 * hp + m // 64) * 3 + br, hp, br, m] = 1.0
    p = np.arange(128)
    bd = (p[:, None] // 64 == p[None, :] // 64).astype(f32)
    return {"amask": amask, "e128": e128.reshape(128, 2048), "cmap": cmap, "force": force.reshape(128, 256),
            "gsel": gsel.reshape(48, 3072), "bdones": bd}


def prep_inputs(inputs):
    f32 = np.float32
    g = {k: np.asarray(v, dtype=f32) for k, v in inputs.items()}
    vecs = np.zeros((128, NV, KC), f32)
    for i in range(2):
        for j in range(2):
            vecs[:, V_NG + 2 * i + j, :] = _fm(g["norm_gain"][i, j])
        for part in range(6):
            vecs[:, V_BADA + 6 * i + part, :] = _fm(g["b_ada"][i, part * D:(part + 1) * D])
    vecs[:, V_KVG, :] = _fm(g["kv_norm_gain"])
    for part in range(2):
        vecs[:, V_BKV + part, :] = _fm(g["b_ada_kv"][part * D:(part + 1) * D])
    for j in range(3):
        vecs[:, V_CONV + j, :] = _fm(g["conv_w"][0, j])
    consts = np.concatenate([np.eye(128, dtype=f32), np.ones((128, 128), f32)], axis=1)
    p64 = np.arange(128) % 64
    hvecs = np.zeros((128, 4), f32)
    hvecs[:, 0] = g["q_gain"][0, p64]
    for i in range(3):
        hvecs[:, 1 + i] = g["k_gain"][i, p64]
    peT = np.zeros((128, 64), f32)
    for kv in range(2):
        peT[:, kv * 32:(kv + 1) * 32] = g["cmp_pe"][kv][:, p64].T
    shared = {
        "vecs": vecs, "consts": consts, "hvecs": hvecs, "peT": peT,
        "w_ada": g["w_ada"], "w_a_in": g["w_a_in"], "w_a_out": g["w_a_out"],
        "w_mlp1": g["w_mlp1"], "w_mlp2": g["w_mlp2"],
        "w_ada_kv": g["w_ada_kv"], "w_kv": g["w_kv"], "w_qg": g["w_qg"], "w_o": g["w_o"],
        "cmp_w1": g["cmp_w1"], "cmp_w2": g["cmp_w2"],
    }
    shared.update(_const_tables())
    in_maps = []
    for b in range(N_CORES):
        m = dict(shared)
        xT = g["x"][b].T.reshape(KC, 128, T).transpose(1, 0, 2)
        m["xT"] = np.ascontiguousarray(xT)
        m["cT"] = _fm(g["c"][b])
        in_maps.append(m)
    return in_maps


def post_outputs(results):
    outs = []
    for r in results:
        yT = np.asarray(r["yT"])
        outs.append(yT.transpose(2, 1, 0).reshape(T, D))
    return np.stack(outs, axis=0).astype(np.float32)


def kernel(**inputs):
    in_maps = prep_inputs(inputs)
    nc = build_program(DEBUG_STOP)
    res = run_bass_kernel_spmd(nc, in_maps, core_ids=list(range(N_CORES)))
    return post_outputs(res.results)
```
